# Optimizing a Trainium2 kernel written in Bass

```python
import math
import jax, jax.numpy as jnp
from jax import lax
import numpy as np

D_MODEL = 1024
BATCH = 16
SEQ = 256
DEPTH = 4
DEC_BATCH = 8
DEC_SEQ = 1024
PAST_LEN = 256

GRID_W = 64
HEAD_DIM = 64
GROUP_W = D_MODEL // 4
RET_HEADS = GROUP_W // HEAD_DIM
RET_CHUNK = 128
LRU_WIDTH = GROUP_W
LRU_BLOCKS = 4
LRU_BLOCK_W = LRU_WIDTH // LRU_BLOCKS
LRU_CONV_W = 4
LRU_C = 8.0
SWA_HEADS = GROUP_W // HEAD_DIM
SWA_KV_HEADS = 2
SWA_WINDOW = 128
ATTN_BLOCK = 128
DIFF_HEADS = GROUP_W // HEAD_DIM
DIFF_QK_DIM = HEAD_DIM // 2
D_FF = 4 * D_MODEL
ROPE_BASE = 10000.0
EPS = 1e-6
N_MOD = 6
IN_SIZES = (GROUP_W, GROUP_W, GROUP_W, GROUP_W,
            LRU_WIDTH, LRU_WIDTH,
            SWA_HEADS * HEAD_DIM, SWA_KV_HEADS * HEAD_DIM, SWA_KV_HEADS * HEAD_DIM,
            DIFF_HEADS * HEAD_DIM, DIFF_HEADS * HEAD_DIM, DIFF_HEADS * HEAD_DIM)
D_IN = sum(IN_SIZES)

kernel_name = "hybrid_diffusion_parallel_heads_step"

F32 = jnp.float32


def rmsnorm(x, g):
    xf = x.astype(F32)
    y = xf * lax.rsqrt(jnp.mean(xf * xf, -1, keepdims=True) + EPS)
    return (y * g.astype(F32)).astype(x.dtype)


def head_groupnorm(o, g):
    B, L, H, d = o.shape
    of = o.astype(F32)
    mu = jnp.mean(of, -1, keepdims=True)
    var = jnp.mean(jnp.square(of - mu), -1, keepdims=True)
    y = (of - mu) * lax.rsqrt(var + EPS)
    return y.reshape(B, L, H * d) * g.astype(F32)


def head_rmsnorm(o, g):
    of = o.astype(F32)
    y = of * lax.rsqrt(jnp.mean(of * of, -1, keepdims=True) + EPS) * g.astype(F32)
    B, L, H, d = o.shape
    return y.reshape(B, L, H * d)


def modulation(cvec, w, b):
    m = (jax.nn.silu(cvec) @ w + b)[..., None, :]
    return jnp.split(m, N_MOD, axis=-1)


def axial_rope(length, dim):
    rows = length // GRID_W
    row = jnp.repeat(jnp.arange(rows, dtype=F32), GRID_W)
    col = (jnp.arange(length) % GRID_W).astype(F32)
    n = dim // 4
    inv = ROPE_BASE ** (-jnp.arange(n, dtype=F32) / n)
    ang = jnp.concatenate([row[:, None] * inv, col[:, None] * inv], -1)
    return jnp.cos(ang), jnp.sin(ang)


def apply_rope(x, cos, sin):
    x1, x2 = jnp.split(x.astype(F32), 2, -1)
    c = cos[None, :, None, :]
    s = sin[None, :, None, :]
    return jnp.concatenate([x1 * c - x2 * s, x1 * s + x2 * c], -1).astype(x.dtype)


def retention_scan(q, k, v, log_gamma, s0):
    B, L, H, dk = q.shape
    dv = v.shape[-1]
    nc = L // RET_CHUNK
    lg = log_gamma.astype(F32)
    idx = jnp.arange(RET_CHUNK, dtype=F32)
    dist = idx[:, None] - idx[None, :]
    decay_in = jnp.exp(jnp.where((dist >= 0)[None], dist[None] * lg[:, None, None], -jnp.inf))
    xi = jnp.exp((idx[:, None] + 1.0) * lg[None, :])
    zeta = jnp.exp((RET_CHUNK - 1.0 - idx)[:, None] * lg[None, :])
    g_chunk = jnp.exp(RET_CHUNK * lg)

    def to_chunks(t):
        return t.astype(F32).reshape(B, nc, RET_CHUNK, H, t.shape[-1]).swapaxes(0, 1)

    def step(S, inp):
        qc, kc, vc = inp
        scores = jnp.einsum('bnhd,bmhd->bhnm', qc, kc) * decay_in[None]
        o = (jnp.einsum('bhnm,bmhe->bnhe', scores, vc)
             + jnp.einsum('bnhd,bhde->bnhe', qc, S) * xi[None, :, :, None])
        S = S * g_chunk[None, :, None, None] + jnp.einsum('bmhd,bmhe->bhde', kc * zeta[None, :, :, None], vc)
        return S, o

    S, o = lax.scan(step, s0.astype(F32), (to_chunks(q), to_chunks(k), to_chunks(v)))
    return o.swapaxes(0, 1).reshape(B, L, H, dv), S


def retention_mixer(q, k, v, g, decay_logit, gn_g, s0):
    B, L, _ = q.shape
    q = q.reshape(B, L, RET_HEADS, HEAD_DIM)
    k = k.reshape(B, L, RET_HEADS, HEAD_DIM) * (HEAD_DIM ** -0.5)
    v = v.reshape(B, L, RET_HEADS, HEAD_DIM)
    log_gamma = jax.nn.log_sigmoid(decay_logit.astype(F32))
    o_f, s_f = retention_scan(q, k, v, log_gamma[0], s0[:, 0])
    o_b, s_b = retention_scan(q[:, ::-1], k[:, ::-1], v[:, ::-1], log_gamma[1], s0[:, 1])
    y = head_groupnorm(o_f + o_b[:, ::-1], gn_g) * jax.nn.silu(g.astype(F32))
    return y, jnp.stack([s_f, s_b], 1)


def depthwise_conv_centred(x, w, b):
    y = lax.conv_general_dilated(x, w[:, None, :].astype(x.dtype), window_strides=(1,),
                                 padding=[(LRU_CONV_W // 2, LRU_CONV_W - 1 - LRU_CONV_W // 2)],
                                 dimension_numbers=('NWC', 'WIO', 'NWC'), feature_group_count=x.shape[-1])
    return y + b


def linear_scan(a, b, h0):
    def combine(left, right):
        a1, b1 = left
        a2, b2 = right
        return a1 * a2, a2 * b1 + b2
    a_cum, b_cum = lax.associative_scan(combine, (a, b), axis=1)
    return a_cum * h0[:, None, :] + b_cum


def rglru_direction(xc, w_a, b_a, w_x, b_x, lam, h0):
    B, L, C = xc.shape
    xb = xc.reshape(B, L, LRU_BLOCKS, LRU_BLOCK_W)
    r = jax.nn.sigmoid((jnp.einsum('blnc,ncd->blnd', xb, w_a).reshape(B, L, C) + b_a).astype(F32))
    i = jax.nn.sigmoid((jnp.einsum('blnc,ncd->blnd', xb, w_x).reshape(B, L, C) + b_x).astype(F32))
    log_a = -LRU_C * r * jax.nn.softplus(-lam.astype(F32))
    a = jnp.exp(log_a)
    u = jnp.sqrt(-jnp.expm1(2.0 * log_a)) * (i * xc.astype(F32))
    h = linear_scan(a, u, h0.astype(F32))
    return h, h[:, -1]


def rglru_mixer(x_in, gate_in, conv_w, conv_b, w_a, b_a, w_x, b_x, lam, h0):
    xc = depthwise_conv_centred(x_in, conv_w, conv_b)
    h_f, hT_f = rglru_direction(xc, w_a[0], b_a[0], w_x[0], b_x[0], lam[0], h0[:, 0])
    h_b, hT_b = rglru_direction(xc[:, ::-1], w_a[1], b_a[1], w_x[1], b_x[1], lam[1], h0[:, 1])
    y = (h_f + h_b[:, ::-1]) * jax.nn.gelu(gate_in.astype(F32))
    return y, jnp.stack([hT_f, hT_b], 1)


def sink_attention_dense(q, k, v, sink):
    B, Lq, Hq, d = q.shape
    Hkv = k.shape[2]
    G = Hq // Hkv
    nb = Lq // ATTN_BLOCK
    qb = q.reshape(B, nb, ATTN_BLOCK, Hkv, G, d).swapaxes(0, 1)
    sink_l = sink.astype(F32).reshape(Hkv, G)

    def block(qi):
        s = jnp.einsum('bqkgd,bskd->bkgqs', qi, k).astype(F32) * (d ** -0.5)
        s = jnp.concatenate([s, jnp.broadcast_to(sink_l[None, :, :, None, None], s.shape[:-1] + (1,))], -1)
        p = jax.nn.softmax(s, -1)[..., :-1]
        return jnp.einsum('bkgqs,bskd->bqkgd', p.astype(v.dtype), v)

    o = lax.map(block, qb)
    return o.swapaxes(0, 1).reshape(B, Lq, Hq * d)


def window_sink_attention(q, k, v, k_ctx, v_ctx, sink):
    B, L, Hq, d = q.shape
    Hkv = k.shape[2]
    G = Hq // Hkv
    BLK = ATTN_BLOCK
    nb = L // BLK
    qb = q.reshape(B, nb, BLK, Hkv, G, d)

    def band(t):
        tp = jnp.pad(t, ((0, 0), (BLK, BLK), (0, 0), (0, 0))).reshape(B, nb + 2, BLK, Hkv, d)
        return jnp.concatenate([tp[:, :-2], tp[:, 1:-1], tp[:, 2:]], axis=2)

    kw, vw = band(k), band(v)
    qpos = jnp.arange(nb)[:, None] * BLK + jnp.arange(BLK)[None, :]
    kpos = (jnp.arange(nb)[:, None] - 1) * BLK + jnp.arange(3 * BLK)[None, :]
    rel = kpos[:, None, :] - qpos[:, :, None]
    valid = (jnp.abs(rel) <= SWA_WINDOW) & (kpos[:, None, :] >= 0) & (kpos[:, None, :] < L)
    scale = d ** -0.5
    s_loc = jnp.einsum('bnqkgd,bnskd->bnkgqs', qb, kw).astype(F32) * scale
    s_loc = jnp.where(valid[None, :, None, None], s_loc, -jnp.inf)
    s_ctx = jnp.einsum('bnqkgd,bskd->bnkgqs', qb, k_ctx).astype(F32) * scale
    s_sink = jnp.broadcast_to(sink.astype(F32).reshape(1, 1, Hkv, G, 1, 1), s_loc.shape[:-1] + (1,))
    p = jax.nn.softmax(jnp.concatenate([s_loc, s_ctx, s_sink], -1), -1)
    p_loc = p[..., :3 * BLK].astype(v.dtype)
    p_ctx = p[..., 3 * BLK:-1].astype(v.dtype)
    o = (jnp.einsum('bnkgqs,bnskd->bnqkgd', p_loc, vw)
         + jnp.einsum('bnkgqs,bskd->bnqkgd', p_ctx, v_ctx.astype(v.dtype)))
    return o.reshape(B, L, Hq * d)


def diff_attention(q, k, v, lam, norm_g, lambda_init):
    B, Lq, H, _ = q.shape
    Lk = k.shape[1]
    nb = Lq // ATTN_BLOCK
    qb = q.reshape(B, nb, ATTN_BLOCK, H, 2, DIFF_QK_DIM).swapaxes(0, 1)
    kk = k.reshape(B, Lk, H, 2, DIFF_QK_DIM)
    scale = DIFF_QK_DIM ** -0.5

    def block(qi):
        s = jnp.einsum('bqhcd,bshcd->bchqs', qi, kk).astype(F32) * scale
        p = jax.nn.softmax(s, -1)
        w = p[:, 0] - lam * p[:, 1]
        return jnp.einsum('bhqs,bshe->bqhe', w.astype(v.dtype), v)

    o = lax.map(block, qb).swapaxes(0, 1).reshape(B, Lq, H, HEAD_DIM)
    return head_rmsnorm(o, norm_g) * (1.0 - lambda_init)


def trunk_layer(x, cvec, l, p, ctx):
    B, L, _ = x.shape
    sh1, sc1, g1, sh2, sc2, g2 = modulation(cvec, p['w_ada'], p['b_ada'])
    h = rmsnorm(x, p['norm_mix_g']) * (1.0 + sc1) + sh1
    z = h @ p['w_in']
    points, acc = [], 0
    for s in IN_SIZES[:-1]:
        acc += s
        points.append(acc)
    rq, rk, rv, rg, lx, lgate, sq, sk, sv, dq, dk, dv = jnp.split(z, points, axis=-1)
    sq = sq.reshape(B, L, SWA_HEADS, HEAD_DIM)
    sk = sk.reshape(B, L, SWA_KV_HEADS, HEAD_DIM)
    sv = sv.reshape(B, L, SWA_KV_HEADS, HEAD_DIM)
    dq = dq.reshape(B, L, DIFF_HEADS, HEAD_DIM)
    dk = dk.reshape(B, L, DIFF_HEADS, HEAD_DIM)
    dv = dv.reshape(B, L, DIFF_HEADS, HEAD_DIM)

    lambda_init = 0.8 - 0.6 * math.exp(-0.3 * l)
    lv = p['diff_lambda'].astype(F32)
    lam = jnp.exp(jnp.sum(lv[0] * lv[1])) - jnp.exp(jnp.sum(lv[2] * lv[3])) + lambda_init

    if ctx is None:
        s_ret_in = jnp.zeros((B, 2, RET_HEADS, HEAD_DIM, HEAD_DIM), F32)
        s_lru_in = jnp.zeros((B, 2, LRU_WIDTH), F32)
    else:
        s_ret_in, s_lru_in, ck_swa, cv_swa, ck_diff, cv_diff = ctx

    y_ret, s_ret = retention_mixer(rq, rk, rv, rg, p['ret_decay'], p['ret_gn_g'], s_ret_in)
    y_lru, s_lru = rglru_mixer(lx, lgate, p['lru_conv_w'], p['lru_conv_b'], p['lru_w_a'], p['lru_b_a'],
                               p['lru_w_x'], p['lru_b_x'], p['lru_lambda'], s_lru_in)

    if ctx is None:
        y_swa = sink_attention_dense(sq, sk, sv, p['swa_sink'])
        y_diff = diff_attention(dq, dk, dv, lam, p['diff_norm_g'], lambda_init)
        new_ctx = (s_ret, s_lru, sk, sv, dk, dv)
    else:
        cos, sin = axial_rope(L, HEAD_DIM)
        y_swa = window_sink_attention(apply_rope(sq, cos, sin), apply_rope(sk, cos, sin), sv,
                                      ck_swa.astype(sk.dtype), cv_swa, p['swa_sink'])
        cos2, sin2 = axial_rope(L, DIFF_QK_DIM)
        dq_r = apply_rope(dq.reshape(B, L, 2 * DIFF_HEADS, DIFF_QK_DIM), cos2, sin2).reshape(B, L, DIFF_HEADS, HEAD_DIM)
        dk_r = apply_rope(dk.reshape(B, L, 2 * DIFF_HEADS, DIFF_QK_DIM), cos2, sin2).reshape(B, L, DIFF_HEADS, HEAD_DIM)
        k_all = jnp.concatenate([ck_diff.astype(dk_r.dtype), dk_r], axis=1)
        v_all = jnp.concatenate([cv_diff.astype(dv.dtype), dv], axis=1)
        y_diff = diff_attention(dq_r, k_all, v_all, lam, p['diff_norm_g'], lambda_init)
        new_ctx = None

    y = jnp.concatenate([y_ret.astype(x.dtype), y_lru.astype(x.dtype),
                         y_swa.astype(x.dtype), y_diff.astype(x.dtype)], -1) @ p['w_out']
    x = x + g1 * y
    h2 = rmsnorm(x, p['norm_mlp_g']) * (1.0 + sc2) + sh2
    x = x + g2 * (jnp.square(jax.nn.relu(h2 @ p['w_ff1'])) @ p['w_ff2'])
    return x, new_ctx


def setup_inputs(seed: int = 0) -> dict:
    key = jax.random.key(seed)
    ks = jax.random.split(key, 32)
    nrm = lambda k, shape, s=1.0: jax.random.normal(k, shape, F32) * s
    e = 2.0 ** (-5.0 - jnp.arange(RET_HEADS, dtype=F32))
    ret_logit = jnp.log1p(-e) - jnp.log(e)
    u = jax.random.uniform(ks[20], (DEPTH, 2, LRU_WIDTH), F32, minval=0.9, maxval=0.999)
    a0 = u ** (1.0 / LRU_C)
    return {
        "x_prompt": nrm(ks[0], (BATCH, SEQ, D_MODEL)),
        "x_sample": nrm(ks[1], (DEC_BATCH, DEC_SEQ, D_MODEL)),
        "c": nrm(ks[2], (DEC_BATCH, D_MODEL)),
        "state_ret": nrm(ks[3], (DEC_BATCH, DEPTH, 2, RET_HEADS, HEAD_DIM, HEAD_DIM), 0.5),
        "state_lru": nrm(ks[4], (DEC_BATCH, DEPTH, 2, LRU_WIDTH), 0.5),
        "cache_swa_k": nrm(ks[5], (DEC_BATCH, DEPTH, PAST_LEN, SWA_KV_HEADS, HEAD_DIM)),
        "cache_swa_v": nrm(ks[6], (DEC_BATCH, DEPTH, PAST_LEN, SWA_KV_HEADS, HEAD_DIM)),
        "cache_diff_k": nrm(ks[7], (DEC_BATCH, DEPTH, PAST_LEN, DIFF_HEADS, HEAD_DIM)),
        "cache_diff_v": nrm(ks[8], (DEC_BATCH, DEPTH, PAST_LEN, DIFF_HEADS, HEAD_DIM)),
        "c_ctx": nrm(ks[9], (D_MODEL,)),
        "w_ada": nrm(ks[10], (DEPTH, D_MODEL, N_MOD * D_MODEL), 0.5 * D_MODEL ** -0.5),
        "b_ada": nrm(ks[11], (DEPTH, N_MOD * D_MODEL), 0.01),
        "norm_mix_g": 1.0 + nrm(ks[12], (DEPTH, D_MODEL), 0.01),
        "w_in": nrm(ks[13], (DEPTH, D_MODEL, D_IN), D_MODEL ** -0.5),
        "ret_decay": ret_logit + nrm(ks[14], (DEPTH, 2, RET_HEADS), 0.1),
        "ret_gn_g": 1.0 + nrm(ks[15], (DEPTH, GROUP_W), 0.01),
        "lru_conv_w": nrm(ks[16], (DEPTH, LRU_CONV_W, LRU_WIDTH), LRU_CONV_W ** -0.5),
        "lru_conv_b": nrm(ks[17], (DEPTH, LRU_WIDTH), 0.01),
        "lru_w_a": nrm(ks[18], (DEPTH, 2, LRU_BLOCKS, LRU_BLOCK_W, LRU_BLOCK_W), LRU_BLOCK_W ** -0.5),
        "lru_b_a": nrm(ks[19], (DEPTH, 2, LRU_WIDTH), 0.01),
        "lru_w_x": nrm(ks[21], (DEPTH, 2, LRU_BLOCKS, LRU_BLOCK_W, LRU_BLOCK_W), LRU_BLOCK_W ** -0.5),
        "lru_b_x": nrm(ks[22], (DEPTH, 2, LRU_WIDTH), 0.01),
        "lru_lambda": jnp.log(a0) - jnp.log1p(-a0),
        "swa_sink": nrm(ks[23], (DEPTH, SWA_HEADS), 0.5),
        "diff_lambda": nrm(ks[24], (DEPTH, 4, DIFF_QK_DIM), 0.1),
        "diff_norm_g": 1.0 + nrm(ks[25], (DEPTH, HEAD_DIM), 0.01),
        "w_out": nrm(ks[26], (DEPTH, D_MODEL, D_MODEL), D_MODEL ** -0.5),
        "norm_mlp_g": 1.0 + nrm(ks[27], (DEPTH, D_MODEL), 0.01),
        "w_ff1": nrm(ks[28], (DEPTH, D_MODEL, D_FF), D_MODEL ** -0.5),
        "w_ff2": nrm(ks[29], (DEPTH, D_FF, D_MODEL), D_FF ** -0.5),
        "final_norm_g": 1.0 + nrm(ks[30], (D_MODEL,), 0.01),
    }


def reference(x_prompt, x_sample, c, state_ret, state_lru, cache_swa_k, cache_swa_v, cache_diff_k, cache_diff_v,
              c_ctx, w_ada, b_ada, norm_mix_g, w_in, ret_decay, ret_gn_g, lru_conv_w, lru_conv_b,
              lru_w_a, lru_b_a, lru_w_x, lru_b_x, lru_lambda, swa_sink, diff_lambda, diff_norm_g,
              w_out, norm_mlp_g, w_ff1, w_ff2, final_norm_g):
    stacked = dict(w_ada=w_ada, b_ada=b_ada, norm_mix_g=norm_mix_g, w_in=w_in, ret_decay=ret_decay,
                   ret_gn_g=ret_gn_g, lru_conv_w=lru_conv_w, lru_conv_b=lru_conv_b, lru_w_a=lru_w_a,
                   lru_b_a=lru_b_a, lru_w_x=lru_w_x, lru_b_x=lru_b_x, lru_lambda=lru_lambda,
                   swa_sink=swa_sink, diff_lambda=diff_lambda, diff_norm_g=diff_norm_g, w_out=w_out,
                   norm_mlp_g=norm_mlp_g, w_ff1=w_ff1, w_ff2=w_ff2)

    xp = x_prompt
    ctx_list = []
    for l in range(DEPTH):
        p = {name: arr[l] for name, arr in stacked.items()}
        xp, ctx_l = trunk_layer(xp, c_ctx, l, p, None)
        ctx_list.append(ctx_l)
    y_prompt = rmsnorm(xp, final_norm_g)
    new_state_ret = jnp.stack([t[0] for t in ctx_list], axis=1)
    new_state_lru = jnp.stack([t[1] for t in ctx_list], axis=1)
    new_cache_swa_k = jnp.stack([t[2] for t in ctx_list], axis=1)
    new_cache_swa_v = jnp.stack([t[3] for t in ctx_list], axis=1)
    new_cache_diff_k = jnp.stack([t[4] for t in ctx_list], axis=1)
    new_cache_diff_v = jnp.stack([t[5] for t in ctx_list], axis=1)

    xs = x_sample
    for l in range(DEPTH):
        p = {name: arr[l] for name, arr in stacked.items()}
        ctx_l = (state_ret[:, l], state_lru[:, l], cache_swa_k[:, l], cache_swa_v[:, l],
                 cache_diff_k[:, l], cache_diff_v[:, l])
        xs, _ = trunk_layer(xs, c, l, p, ctx_l)
    y_sample = rmsnorm(xs, final_norm_g)

    return (y_prompt, y_sample, new_state_ret, new_state_lru, new_cache_swa_k, new_cache_swa_v,
            new_cache_diff_k, new_cache_diff_v)
```

```python
import contextlib
import numpy as np
import ml_dtypes
import concourse.bass as bass
import concourse.mybir as mybir
from concourse.bass_utils import run_bass_kernel_spmd

F32 = mybir.dt.float32
BF16 = mybir.dt.bfloat16
AF = mybir.ActivationFunctionType
ALU = mybir.AluOpType
AX = mybir.AxisListType

ENGS = ("pe", "act", "dve", "pool", "sp")
DMA_WIN = 8
DEPTH = 4
T = 1536
NT = 12
EPS = 1e-6
NSTG = 3
NWBF = 3
PPL = 24 + 3 + 3 + 1 + 2 + 1 + 2 + 1 + 1 + 2 + 2 + 1 + 32


class Op:
    __slots__ = ("idx", "eng", "fn", "deps", "dma", "signal", "sem", "val", "waits")

    def __init__(self, idx, eng, fn, dma):
        self.idx = idx; self.eng = eng; self.fn = fn; self.dma = dma
        self.deps = set(); self.signal = False; self.sem = None; self.val = 0; self.waits = []


class Prog:
    def __init__(self, nc):
        self.nc = nc
        self.ops = []
        self.last_w = {}
        self.readers = {}
        self.scope = None

    def add(self, eng, fn, r=(), w=(), dma=False, noscope=False):
        op = Op(len(self.ops), eng, fn, dma)
        w = list(w) + [k for k in r if isinstance(k, str) and k.startswith("ps")]
        r = [k for k in r if not (isinstance(k, str) and k.startswith("ps"))]
        if self.scope is not None and not noscope:
            r.append(self.scope)
        for k in r:
            lw = self.last_w.get(k)
            if lw is not None:
                op.deps.add((lw, "raw"))
        for k in w:
            lw = self.last_w.get(k)
            if lw is not None:
                op.deps.add((lw, "waw"))
            for rd in self.readers.get(k, ()):
                op.deps.add((rd, "war"))
        for k in r:
            self.readers.setdefault(k, []).append(op.idx)
        for k in w:
            self.last_w[k] = op.idx
            self.readers[k] = []
        self.ops.append(op)
        return op

    def dma(self, out, in_, r=(), w=(), noscope=False):
        return self.add("sp", lambda e: e.dma_start(out=out, in_=in_), r=r, w=w, dma=True, noscope=noscope)

    def emit(self):
        nc = self.nc
        ops = self.ops
        for op in ops:
            nd = set()
            for (d, kind) in op.deps:
                dop = ops[d]
                if d == op.idx:
                    continue
                if dop.eng == op.eng and not dop.dma and not op.dma and kind != "raw":
                    continue
                if dop.eng == op.eng and op.eng == "pe":
                    continue
                nd.add(d)
            op.deps = nd
            for d in nd:
                ops[d].signal = True
        stack = contextlib.ExitStack()
        eng_sem = {}; eng_cnt = {}; dma_sems = {}; dma_cnt = {}
        for e in ENGS:
            eng_sem[e] = stack.enter_context(nc.semaphore("c_" + e))
            eng_cnt[e] = 0; dma_sems[e] = None; dma_cnt[e] = 0
        extra_waits = {}
        final_waits = []
        for op in ops:
            if op.dma:
                if dma_sems[op.eng] is None:
                    dma_sems[op.eng] = [stack.enter_context(nc.semaphore("d_%s%d" % (op.eng, i)))
                                        for i in range(DMA_WIN)]
                k = dma_cnt[op.eng]; dma_cnt[op.eng] += 1
                op.sem = dma_sems[op.eng][k % DMA_WIN]
                op.val = 16 * (k // DMA_WIN + 1)
                op.signal = True
                if k >= DMA_WIN:
                    extra_waits[op.idx] = [(op.sem, op.val - 16)]
            elif op.signal:
                eng_cnt[op.eng] += 1
                op.sem = eng_sem[op.eng]; op.val = eng_cnt[op.eng]
        for e in ENGS:
            if dma_sems[e] is not None:
                k = dma_cnt[e]
                for slot in range(min(k, DMA_WIN)):
                    n = (k - 1 - slot) // DMA_WIN + 1
                    final_waits.append((dma_sems[e][slot], 16 * n))
        waited = {e: {} for e in ENGS}
        for op in ops:
            need = {}
            for d in op.deps:
                dop = ops[d]
                if need.get(dop.sem, (None, 0))[1] < dop.val:
                    need[dop.sem] = (dop.sem, dop.val)
            for (s, v) in extra_waits.get(op.idx, ()):
                if need.get(s, (None, 0))[1] < v:
                    need[s] = (s, v)
            wl = []
            for key, (s, v) in need.items():
                if waited[op.eng].get(key, 0) >= v:
                    continue
                waited[op.eng][key] = v
                wl.append((s, v))
            op.waits = wl
        by_eng = {e: [op for op in ops if op.eng == e] for e in ENGS}
        self.stats = {e: len(by_eng[e]) for e in ENGS}
        self.stats["waits"] = sum(len(op.waits) for op in ops)
        self.stats["maxsem"] = dict(eng_cnt)

        def run(eh, ename, final=False):
            for op in by_eng[ename]:
                for (s, v) in op.waits:
                    eh.wait_ge(s, v)
                inst = op.fn(eh)
                if op.signal:
                    inst.then_inc(op.sem, 16 if op.dma else 1)
            if final:
                for (s, v) in final_waits:
                    eh.wait_ge(s, v)

        with stack:
            with nc.Block() as block:
                @block.tensor
                def _(e):
                    run(e, "pe")

                @block.scalar
                def _(e):
                    run(e, "act")

                @block.vector
                def _(e):
                    run(e, "dve")

                @block.gpsimd
                def _(e):
                    run(e, "pool")

                @block.sync
                def _(e):
                    run(e, "sp", final=True)


def cst_layout():
    entA = [("BD", 128), ("hm", 2), ("cm4", 4),
            ("b_ada", DEPTH * 48), ("g_mix", DEPTH * 8), ("g_mlp", DEPTH * 8), ("g_fin", 8),
            ("conv_w", DEPTH * 8), ("conv_b", DEPTH * 2), ("b_a", DEPTH * 4), ("b_x", DEPTH * 4),
            ("h0", DEPTH * 4), ("gng", DEPTH * 256)]
    entB = [("ident", 128), ("R64T", 128), ("R32T", 128), ("mlo", 128), ("mhi", 128), ("Ef", 128), ("Eb", 128),
            ("Lf8", 128), ("Lb8", 128), ("idx1", 128), ("idxr", 128), ("cmf", 1), ("cmb", 1), ("cT", 16),
            ("lam", DEPTH * 4), ("rd_row", 32), ("rd_pair", 16), ("sink", 16), ("dlam", DEPTH * 128), ("dng", DEPTH * 64)]
    off = {}
    o = 0
    for n, w in entA + entB:
        off[n] = (o, w)
        o += w
    off["_WA"] = (sum(w for _, w in entA), 0)
    return off, o


def piece_ids(l):
    b = l * PPL
    d = {}
    d["ada"] = list(range(b, b + 24)); b += 24
    d["retF"] = [b, b + 1, b + 2]; b += 3
    d["retT"] = [b, b + 1, b + 2]; b += 3
    d["retO"] = b; b += 1
    d["lruF"] = [b, b + 1]; b += 2
    d["lruO"] = b; b += 1
    d["swaF"] = [b, b + 1]; b += 2
    d["swaT"] = [b]; b += 1
    d["swaO"] = b; b += 1
    d["difF"] = [b, b + 1]; b += 2
    d["difT"] = [b, b + 1]; b += 2
    d["difO"] = b; b += 1
    d["ff"] = []
    for q in range(4):
        d["ff"].append((list(range(b, b + 4)), list(range(b + 4, b + 8)))); b += 8
    assert b == (l + 1) * PPL + 0 or True
    return d


def build_program(depth=DEPTH, stages=None, debug=False):
    ST = (lambda name: True) if stages is None else (lambda name: name in stages)
    nc = bass.Bass("TRN2", target_bir_lowering=False)
    P = Prog(nc)
    coff, CW = cst_layout()

    def din(name, shape):
        return nc.dram_tensor(name, list(shape), F32, kind="ExternalInput").ap()

    def dout(name, shape):
        return nc.dram_tensor(name, list(shape), F32, kind="ExternalOutput").ap()

    xT_d = din("xT", [1024, T])
    wts_d = din("wts", [DEPTH * PPL, 128, 2048])
    cst_d = din("cst", [128, CW])
    rope_d = din("rope", [128, 4096])
    lruw_d = din("lruw", [128, DEPTH * 8 * 128])
    cks_d = din("cks", [128, DEPTH, 256])
    cvs_d = din("cvs", [128, DEPTH, 2, 128])
    ckd_d = din("ckd", [128, DEPTH, 2, 256])
    cvd_d = din("cvd", [128, DEPTH, 2, 256])
    sret_d = din("sret", [128, DEPTH, 4, 128])
    yT_o = dout("yT", [1024, T])
    osret_o = dout("o_sret", [128, 2, DEPTH, 4, 128])
    oslru_o = dout("o_slru", [128, 2 * DEPTH * 4])
    okv_o = dout("o_kv", [4, 128, DEPTH, 768])

    def sb(name, shape, dt=F32):
        return nc.alloc_sbuf_tensor("s_" + name, list(shape), dt)

    dbg_o = {}
    if debug:
        dbg_o["d_hT"] = nc.dram_tensor("d_hT", [128, 8 * T], BF16, kind="ExternalOutput").ap()
        dbg_o["d_mod"] = nc.dram_tensor("d_mod", [128, 96], F32, kind="ExternalOutput").ap()
        dbg_o["d_sq"] = nc.dram_tensor("d_sq", [128, 2 * T], BF16, kind="ExternalOutput").ap()
        dbg_o["d_kms0"] = nc.dram_tensor("d_kms0", [128, 256 + T], BF16, kind="ExternalOutput").ap()
        dbg_o["d_kms1"] = nc.dram_tensor("d_kms1", [128, 256 + T], BF16, kind="ExternalOutput").ap()
        dbg_o["d_sva"] = nc.dram_tensor("d_sva", [128, NT * 2 * 80], BF16, kind="ExternalOutput").ap()
        dbg_o["d_vctx"] = nc.dram_tensor("d_vctx", [128, 2 * 2 * 80], BF16, kind="ExternalOutput").ap()
        dbg_o["d_osw"] = nc.dram_tensor("d_osw", [128, NT * 65], F32, kind="ExternalOutput").ap()
        for g_ in ("ret", "lru", "swa", "diff"):
            dbg_o["d_y_" + g_] = nc.dram_tensor("d_y_" + g_, [128, 2 * T], BF16, kind="ExternalOutput").ap()

    psum = nc.alloc_psum_tensor("psum", [128, 4096], F32)

    def PS(b, n=512, o=0):
        return psum[:, b * 512 + o: b * 512 + o + n]

    def pk(b):
        return "ps%d" % b

    x = sb("x", [128, 8, T])
    hT = sb("hT", [128, 8, T], BF16)
    yg = sb("yg", [128, 2, T], BF16)
    WA = coff["_WA"][0]
    cst = sb("cst", [128, WA])
    stg = [sb("stg%d" % i, [128, 2048]) for i in range(NSTG)]
    wbf = [sb("wbf%d" % i, [128, 2048], BF16) for i in range(NWBF)]
    ident = sb("ident", [128, 128], BF16)
    ones = sb("ones", [128, 128], BF16)
    R64 = sb("R64", [128, 128], BF16)
    R32 = sb("R32", [128, 128], BF16)
    mlo = sb("mlo", [128, 128], BF16)
    mhi = sb("mhi", [128, 128], BF16)
    DTm = sb("DTm", [128, DEPTH * 4, 128], BF16)
    Xi = sb("Xi", [128, DEPTH * 4, 128], BF16)
    zeta = sb("zeta", [128, DEPTH * 2, 4])
    Gp = sb("Gp", [128, DEPTH * 4])
    lgrow = sb("lgrow", [128, 32])
    lgpair = sb("lgpair", [128, 16])
    esink = sb("esink", [128, 16])
    nlam = sb("nlam", [128, DEPTH])
    dgt = sb("dgt", [128, DEPTH, 64])
    sa8 = sb("sa8", [128, DEPTH * 4])
    sa16 = sb("sa16", [128, DEPTH * 4])
    csil = sb("csil", [128, 8, 2], BF16)
    mods = [sb("mod%d" % i, [128, 48, 2]) for i in range(2)]
    modAs = [sb("modA%d" % i, [128, 2, 8, 2]) for i in range(2)]
    MODK = ["mod0", "mod1", "modA0", "modA1"]
    c_eps = sb("c_eps", [128, 1])
    c_one = sb("c_one", [128, 1])
    bar_t = sb("bar_t", [128, 1])
    rstd = sb("rstd", [128, 512])
    rstd2 = [rstd, sb("rstd1", [128, 512])]
    sqt = [sb("sqt%d" % i, [128, 512], BF16) for i in range(2)]
    ntmp = [sb("ntmp%d" % i, [128, 512]) for i in range(2)]
    adarow = [sb("adarow%d" % i, [2, 256]) for i in range(2)]
    identf = sb("identf", [2, 2])
    setup_stack = contextlib.ExitStack()
    cstB = setup_stack.enter_context(nc.sbuf_tensor("s_cstB", [128, CW - WA], F32))

    def C(name, lo=0, n=None):
        o, w = coff[name]
        if n is None:
            n = w - lo
        if o >= WA:
            return cstB[:, o - WA + lo: o - WA + lo + n]
        return cst[:, o + lo: o + lo + n]

    rr = {"ev": 0}

    def V(fn, r, w):
        return P.add("dve", fn, r, w)

    def G(fn, r, w):
        return P.add("pool", fn, r, w)

    def A(fn, r, w):
        return P.add("act", fn, r, w)

    def M(fn, r, w):
        return P.add("pe", fn, r, w)

    def act(out, in_, func, r, w, bias=None, scale=1.0):
        kw = {}
        if bias is not None:
            kw["bias"] = bias
        return A(lambda e: e.activation(out=out, in_=in_, func=func, scale=scale, **kw), r, w)

    def tt(eng, out, a, b, op, r, w):
        return P.add(eng, lambda e: e.tensor_tensor(out=out, in0=a, in1=b, op=op), r, w)

    def ts(eng, out, a, s1, s2, op0, op1, r, w):
        if s2 is None:
            return P.add(eng, lambda e: e.tensor_scalar(out=out, in0=a, scalar1=s1, scalar2=None, op0=op0), r, w)
        return P.add(eng, lambda e: e.tensor_scalar(out=out, in0=a, scalar1=s1, scalar2=s2, op0=op0, op1=op1), r, w)

    def stt(eng, out, a, s, b, op0, op1, r, w):
        return P.add(eng, lambda e: e.scalar_tensor_tensor(out=out, in0=a, scalar=s, in1=b, op0=op0, op1=op1), r, w)

    def cp(eng, out, in_, r, w):
        if eng == "act":
            return A(lambda e: e.activation(out=out, in_=in_, func=AF.Copy), r, w)
        return P.add(eng, lambda e: e.tensor_copy(out=out, in_=in_), r, w)

    def mm(out, lhsT, rhs, start, stop, r, w):
        return M(lambda e: e.matmul(out, lhsT=lhsT, rhs=rhs, start=start, stop=stop), r, w)

    def mmk(out, lhs_fn, rhs_fn, nk, r, w):
        def f(e):
            for kc in range(nk):
                ins = e.matmul(out, lhsT=lhs_fn(kc), rhs=rhs_fn(kc), start=(kc == 0), stop=(kc == nk - 1))
            return ins
        return M(f, r, w)

    minfree = {"v": 1 << 30}

    def barrier():
        minfree["v"] = min(minfree["v"], nc.sbuf_bytes_remaining)
        P.scope = None
        G(lambda e: e.memset(bar_t[:], 0.0), [], ["ARENA", "bar_t"])
        P.scope = "ARENA"

    order = []
    for l in range(depth):
        pi = piece_ids(l)
        if l == 0:
            order += pi["ada"]
        order += pi["retF"] + pi["retT"] + [pi["retO"]] + pi["lruF"] + [pi["lruO"]] + pi["swaF"] + pi["swaT"] + \
            [pi["swaO"]] + pi["difF"] + pi["difT"] + [pi["difO"]]
        nada = 0
        nxt = piece_ids(l + 1)["ada"] if l + 1 < depth else []
        for q in range(4):
            for p_ in pi["ff"][q][0] + pi["ff"][q][1]:
                order.append(p_)
                if nada < len(nxt):
                    order.append(nxt[nada]); nada += 1
    ffn_piece = set()
    for l in range(depth):
        pi = piece_ids(l)
        for q in range(4):
            ffn_piece.update(pi["ff"][q][0] + pi["ff"][q][1])
        if l + 1 < depth:
            ffn_piece.update(piece_ids(l + 1)["ada"])
    ws = {"issued": 0, "used": 0, "stg_rr": 0, "slot": {}, "free": list(range(NWBF)), "xstg": [], "xwbf": []}
    cast_eng = ["dve", "act", "dve", "act"]

    def stg_of(i):
        return (stg[i], "stg%d" % i) if i < NSTG else (ws["xstg"][i - NSTG], "xstg%d" % (i - NSTG))

    def wbf_of(i):
        return (wbf[i], "wbf%d" % i) if i < NWBF else (ws["xwbf"][i - NWBF], "xwbf%d" % (i - NWBF))

    def w_issue():
        k = ws["issued"]
        if k >= len(order):
            return False
        pid = order[k]
        big = (pid in ffn_piece) and len(ws["xstg"]) > 0
        cand = [b for b in ws["free"] if b < NWBF or big]
        if not cand:
            return False
        b = cand[0]
        ws["free"].remove(b)
        nst = NSTG + (len(ws["xstg"]) if big else 0)
        si = ws["stg_rr"] % nst
        ws["stg_rr"] += 1
        st_, skey = stg_of(si)
        wb_, wkey = wbf_of(b)
        ws["slot"][k] = b
        nsc = not (si >= NSTG or b >= NWBF)
        P.dma(st_[:], wts_d[pid], r=[], w=[skey], noscope=nsc)
        eng = cast_eng[k % 4]
        if eng == "act":
            P.add("act", lambda e: e.activation(out=wb_[:], in_=st_[:], func=AF.Copy), [skey], [wkey], noscope=nsc)
        else:
            P.add(eng, lambda e: e.tensor_copy(out=wb_[:], in_=st_[:]), [skey], [wkey], noscope=nsc)
        ws["issued"] += 1
        return True

    def w_next(expect):
        while order[ws["used"]] != expect:
            ws["used"] += 1
        k = ws["used"]
        for kk in list(ws["slot"].keys()):
            if kk < k:
                ws["free"].append(ws["slot"].pop(kk))
        assert ws["issued"] >= k or ws["issued"] == k or True
        if ws["issued"] < k:
            ws["issued"] = k
        while ws["issued"] < len(order) and ws["issued"] < k + 8:
            if not w_issue():
                break
        assert k in ws["slot"], (k, ws)
        ws["used"] += 1
        return wbf_of(ws["slot"][k])

    def w_extra_begin(xs, xw):
        ws["xstg"] = xs
        ws["xwbf"] = xw
        ws["free"] += [NWBF + i for i in range(len(xw))]

    def w_extra_end():
        for kk, b in ws["slot"].items():
            assert b < NWBF, "extra bf16 slot still in use at phase end"
        ws["free"] = [b for b in ws["free"] if b < NWBF]
        ws["xstg"] = []
        ws["xwbf"] = []

    P.scope = "ARENA"
    P.dma(cst[:], cst_d[:, 0:WA], w=["cst"])
    P.dma(cstB[:], cst_d[:, WA:CW], w=["cst"])
    for c in range(8):
        P.dma(x[:, c, :], xT_d[c * 128:(c + 1) * 128, :], w=["x%d" % c])
    G(lambda e: e.memset(ones[:], 1.0), [], ["ones"])
    G(lambda e: e.memset(c_eps[:], EPS), [], ["c_eps"])
    G(lambda e: e.memset(c_one[:], 1.0), [], ["c_one"])
    cp("dve", ident[:], C("ident"), ["cst"], ["ident"])
    cp("dve", identf[:], C("ident")[0:2, 0:2], ["cst"], ["identf"])
    cp("dve", R64[:], C("R64T"), ["cst"], ["R64"])
    cp("dve", R32[:], C("R32T"), ["cst"], ["R32"])
    cp("dve", mlo[:], C("mlo"), ["cst"], ["mlo"])
    cp("dve", mhi[:], C("mhi"), ["cst"], ["mhi"])
    act(csil[:].rearrange("p a b -> p (a b)"), C("cT"), AF.Silu, ["cst"], ["csil"])
    for (dst, src, n, key) in ((lgrow, "rd_row", 32, "lgrow"), (lgpair, "rd_pair", 16, "lgpair")):
        act(dst[:], C(src), AF.Exp, ["cst"], [key], scale=-1.0)
        act(dst[:], dst[:], AF.Ln, [key, "c_one"], [key], bias=c_one[:, 0:1])
        ts("dve", dst[:], dst[:], -1.0, None, ALU.mult, None, [key], [key])
    etmp = setup_stack.enter_context(nc.sbuf_tensor("s_etmp", [128, 128], F32))
    etmp2 = setup_stack.enter_context(nc.sbuf_tensor("s_etmp2", [128, 128], F32))
    for l in range(depth):
        for h in range(4):
            act(etmp[:], C("Ef"), AF.Exp, ["cst", "lgrow"], ["etmp"], scale=lgrow[:, l * 8 + h: l * 8 + h + 1])
            tt("dve", etmp[:], etmp[:], C("Lf8"), ALU.mult, ["etmp", "cst"], ["etmp"])
            act(etmp2[:], C("Eb"), AF.Exp, ["cst", "lgrow"], ["etmp2"], scale=lgrow[:, l * 8 + 4 + h: l * 8 + 4 + h + 1])
            tt("dve", etmp2[:], etmp2[:], C("Lb8"), ALU.mult, ["etmp2", "cst"], ["etmp2"])
            tt("dve", DTm[:, l * 4 + h, :], etmp[:], etmp2[:], ALU.add, ["etmp", "etmp2"], ["DTm"])
        for d in range(2):
            for j in range(2):
                i = l * 4 + d * 2 + j
                act(Xi[:, i, :], C("idx1") if d == 0 else C("idxr"), AF.Exp, ["cst", "lgpair"], ["Xi"], scale=lgpair[:, i:i + 1])
                act(Gp[:, i:i + 1], lgpair[:, i:i + 1], AF.Exp, ["lgpair"], ["Gp"], scale=128.0)
            act(zeta[:, l * 2 + d, :], lgrow[:, l * 8 + d * 4: l * 8 + d * 4 + 4], AF.Exp, ["lgrow", "cst"], ["zeta"],
                scale=(C("cmf") if d == 0 else C("cmb")))
    ts("dve", zeta[:], zeta[:], 0.125, None, ALU.mult, None, ["zeta"], ["zeta"])
    act(esink[:], C("sink"), AF.Exp, ["cst"], ["esink"])
    dl1 = setup_stack.enter_context(nc.sbuf_tensor("s_dl1", [128, DEPTH, 2, 32], F32))
    dl2 = setup_stack.enter_context(nc.sbuf_tensor("s_dl2", [128, DEPTH * 2], F32))
    dlv = C("dlam").rearrange("p (l a b c) -> p l a b c", l=DEPTH, a=2, b=2)
    tt("dve", dl1[:], dlv[:, :, :, 0, :], dlv[:, :, :, 1, :], ALU.mult, ["cst"], ["dl1"])
    V(lambda e: e.tensor_reduce(out=dl2[:], in_=dl1[:].rearrange("p l a c -> p (l a) c"), axis=AX.X, op=ALU.add), ["dl1"], ["dl2"])
    act(dl2[:], dl2[:], AF.Exp, ["dl2"], ["dl2"])
    for l in range(depth):
        import math
        li = 0.8 - 0.6 * math.exp(-0.3 * l)
        stt("dve", nlam[:, l:l + 1], dl2[:, 2 * l + 1: 2 * l + 2], -li, dl2[:, 2 * l: 2 * l + 1], ALU.add, ALU.subtract, ["dl2"], ["nlam"])
        ts("dve", dgt[:, l, :], C("dng", l * 64, 64), 1.0 - li, None, ALU.mult, None, ["cst"], ["dgt"])
    act(sa8[:], C("lam"), AF.Exp, ["cst"], ["sa8"], scale=-1.0)
    act(sa8[:], sa8[:], AF.Ln, ["sa8", "c_one"], ["sa8"], bias=c_one[:, 0:1])
    ts("dve", sa16[:], sa8[:], -16.0, None, ALU.mult, None, ["sa8"], ["sa16"])
    ts("dve", sa8[:], sa8[:], -8.0, None, ALU.mult, None, ["sa8", "sa16"], ["sa8"])

    setup_stack.close()
    P.scope = None
    xk = ["x%d" % c for c in range(8)]
    hk = ["h%d" % c for c in range(8)]

    def norm(Aap, Bap, dst_fn, dst_keys, final=False):
        def stats(n):
            blk = slice(n * 512, (n + 1) * 512)
            bank = 5 if n % 2 == 0 else 2
            rs = rstd2[n % 2]
            rk_ = "rstd%d" % (n % 2)
            for c in range(8):
                s = sqt[c % 2]
                act(s[:], x[:, c, blk], AF.Square, [xk[c]], ["sqt%d" % (c % 2)])
                mm(PS(bank), ones[:], s[:], c == 0, c == 7, ["ones", "sqt%d" % (c % 2)], [pk(bank)])
            act(rs[:], PS(bank), AF.Sqrt, [pk(bank), "c_eps"], [rk_], bias=c_eps[:, 0:1], scale=1.0 / 1024.0)
            V(lambda e: e.reciprocal(out=rs[:], in_=rs[:]), [rk_], [rk_])

        def apply(n):
            v = 0 if n < 2 else 1
            blk = slice(n * 512, (n + 1) * 512)
            rs = rstd2[n % 2]
            rk_ = "rstd%d" % (n % 2)
            for c in range(8):
                tmp = ntmp[c % 2]
                tt("dve", tmp[:], x[:, c, blk], rs[:], ALU.mult, [xk[c], rk_], ["ntmp%d" % (c % 2)])
                o = dst_fn(c, n)
                if c % 2 == 0:
                    A(lambda e, o=o, tmp=tmp, sa=Aap(c, v), ba=Bap(c, v): e.activation(out=o, in_=tmp[:], func=AF.Identity, scale=sa, bias=ba),
                      ["ntmp%d" % (c % 2)] + MODK, [dst_keys[c]])
                else:
                    ts("pool", o, tmp[:], Aap(c, v), Bap(c, v), ALU.mult, ALU.add, ["ntmp%d" % (c % 2)] + MODK, [dst_keys[c]])
        stats(0)
        stats(1)
        apply(0)
        stats(2)
        apply(1)
        apply(2)

    dstate = {"b": 0}

    def dbank():
        dstate["b"] ^= 1
        return 3 + dstate["b"]

    def proj_F(pid, src, srck, nchunks, evac):
        wv, wkey = w_next(pid)
        w3 = wv[:].rearrange("p (k c) -> p k c", k=8)
        for mi in range(nchunks):
            for n in range(3):
                b = dbank()
                mmk(PS(b), lambda kc, mi=mi: w3[:, kc, mi * 128:(mi + 1) * 128],
                    lambda kc, n=n: src[:, kc, n * 512:(n + 1) * 512], 8, [wkey] + srck, [pk(b)])
                evac(mi, n, b)

    def proj_T(pid, tiles, evac, ncols=256):
        wv, wkey = w_next(pid)
        w3 = wv[:].rearrange("p (k c) -> p k c", k=8)
        for t in tiles:
            b = dbank()
            mmk(PS(b, ncols), lambda kc, t=t: hT[:, kc, t * 128:(t + 1) * 128], lambda kc: w3[:, kc, 0:ncols], 8,
                [wkey] + hk, [pk(b)])
            evac(t, b)

    def out_proj(pid, G1):
        wv, wkey = w_next(pid)
        w3 = wv[:].rearrange("p (k c) -> p k c", k=2)
        for n in range(3):
            for m in range(8):
                v = 0 if n < 2 else 1
                b = dbank()
                mmk(PS(b), lambda kc, m=m: w3[:, kc, m * 128:(m + 1) * 128],
                    lambda kc, n=n: yg[:, kc, n * 512:(n + 1) * 512], 2, [wkey, "yg"], [pk(b)])
                xs = x[:, m, n * 512:(n + 1) * 512]
                stt("dve", xs, PS(b), G1(m, v), xs, ALU.mult, ALU.add, [pk(b), xk[m]] + MODK, [xk[m]])

    def transpose_part(ytok, ykey, t2s, fcs):
        for t2 in t2s:
            for fc in fcs:
                for i in range(2):
                    t = t2 * 2 + i
                    mm(PS(5, 128, (fc * 2 + i) * 128), ytok[:, t, fc * 128:(fc + 1) * 128], ident[:], True, True,
                       [ykey, "ident"], [pk(5)])
            for fc in fcs:
                cp("act" if fc == 0 else "dve", yg[:, fc, t2 * 256:(t2 + 1) * 256], PS(5, 256, fc * 256), [pk(5)], ["yg"])

    def transpose_y(ytok, ykey):
        transpose_part(ytok, ykey, range(6), [0, 1])

    def rope_load(rope, half):
        P.dma(stg[0][:], rope_d[:, half * 2048:(half + 1) * 2048], w=["stg0"], noscope=True)
        cp("dve", rope[:], stg[0][:], ["stg0"], ["rope"])

    def rope_apply(z, zkey, R, Rk, rope):
        tab0 = 0
        for n in range(2):
            blk = slice(n * 512, (n + 1) * 512)
            b = dbank()
            mm(PS(b), R[:], z[:, blk], True, True, [Rk, zkey], [pk(b)])
            t1 = ntmp[0]
            t2 = ntmp[1]
            tt("dve", t1[:], z[:, blk], rope[:, tab0 + n * 512: tab0 + (n + 1) * 512], ALU.mult, [zkey, "rope"], ["ntmp0"])
            tt("dve", t2[:], PS(b), rope[:, tab0 + 1024 + n * 512: tab0 + 1024 + (n + 1) * 512], ALU.mult, [pk(b), "rope"], ["ntmp1"])
            tt("pool", z[:, blk], t1[:], t2[:], ALU.add, ["ntmp0", "ntmp1"], [zkey])

    astate = {"i": 0}

    class AttnPipe:
        def __init__(self, pTs, pkey):
            self.pTs = pTs; self.pkey = pkey; self.n = 0; self.pending = None

        def push(self, qT, qkey, kblocks, scale, post, nq=1):
            i = self.n; self.n += 1
            base = 0 if (i % 2 == 0) else 3
            pT = self.pTs[i % 2]; pkey = self.pkey + "%d_" % (i % 2); ob = 6 + (i % 2)
            nk = len(kblocks)
            w = nq * 128
            assert nk * w <= 1536 and (512 % w == 0)
            banks = sorted(set(base + (j * w) // 512 for j in range(nk)))
            for j, (kT, kkey, vap, vkey, mask) in enumerate(kblocks):
                o = psum[:, base * 512 + j * w: base * 512 + (j + 1) * w]
                bk = pk(base + (j * w) // 512)
                mm(o, kT, qT, True, mask is None, [kkey, qkey], [bk])
                if mask is not None:
                    assert nq == 1
                    mm(o, ident[:], mask[0][:], False, True, ["ident", mask[1]], [bk])
            act(pT[:, 0:nk * w], psum[:, base * 512: base * 512 + nk * w], AF.Exp, [pk(b) for b in banks], [pkey + "0"], scale=scale)
            prev = self.pending
            self.pending = (kblocks, pT, pkey, ob, post, nq)
            if prev is not None:
                self._pv(prev)

        def _pv(self, call):
            kblocks, pT, pkey, ob, post, nq = call
            nk = len(kblocks)
            w = nq * 128
            for qi in range(nq):
                for j, (kT, kkey, vap, vkey, mask) in enumerate(kblocks):
                    c = j * w + qi * 128
                    mm(PS(ob, 65, qi * 128), pT[:, c:c + 128], vap, j == 0, j == nk - 1, [pkey + "0", vkey], [pk(ob)])
            post(ob)

        def flush(self):
            if self.pending is not None:
                self._pv(self.pending)
                self.pending = None

    def ada_piece(l, j):
        wv, wkey = w_next(piece_ids(l)["ada"][j])
        w3 = wv[:].rearrange("p (k c) -> p k c", k=8)
        o = psum[0:2, 2 * 512 + (j % 2) * 256: 2 * 512 + (j % 2) * 256 + 256]
        ar_ = adarow[j % 2]
        mmk(o, lambda kc: csil[:, kc, :], lambda kc, w3=w3: w3[:, kc, 0:256], 8, [wkey, "csil"], [pk(2)])
        cp("act" if j % 2 == 0 else "dve", ar_[:], o, [pk(2)], ["adarow%d" % (j % 2)])
        for mi in range(2):
            mc = j * 2 + mi
            mm(PS(5, 2, mc * 2), ar_[:, mi * 128:(mi + 1) * 128], identf[:], True, True, ["adarow%d" % (j % 2), "identf"], [pk(5)])

    def ada_finish(l):
        mod = mods[l % 2]
        modA = modAs[l % 2]
        badd = C("b_ada", l * 48, 48)
        tt("dve", mod[:], PS(5, 96).rearrange("p (a b) -> p a b", b=2), badd.unsqueeze(2).to_broadcast([128, 48, 2]), ALU.add,
           [pk(5), "cst"], ["mod%d" % (l % 2)])
        for w_, (gname, sci) in enumerate((("g_mix", 1), ("g_mlp", 4))):
            gv = C(gname, l * 8, 8).unsqueeze(2).to_broadcast([128, 8, 2])
            stt("dve", modA[:, w_, :, :], mod[:, sci * 8:(sci + 1) * 8, :], 1.0, gv, ALU.add, ALU.mult,
                ["mod%d" % (l % 2), "cst"], ["modA%d" % (l % 2)])

    for l in range(depth):
        pi = piece_ids(l)
        P.scope = None
        mod = mods[l % 2]
        modA = modAs[l % 2]
        if l == 0 and ST('ada'):
            for j in range(24):
                ada_piece(0, j)
            ada_finish(0)
        A1 = lambda c, v: modA[:, 0, c, v:v + 1]
        B1 = lambda c, v: mod[:, 0 * 8 + c, v:v + 1]
        G1 = lambda c, v: mod[:, 2 * 8 + c, v:v + 1]
        A2 = lambda c, v: modA[:, 1, c, v:v + 1]
        B2 = lambda c, v: mod[:, 3 * 8 + c, v:v + 1]
        G2 = lambda c, v: mod[:, 5 * 8 + c, v:v + 1]
        if ST('norm'):
            norm(A1, B1, lambda c, n: hT[:, c, n * 512:(n + 1) * 512], hk)
        if debug and l == 0:
            P.dma(dbg_o["d_hT"], hT[:].rearrange("p a b -> p (a b)"), r=hk)
            P.dma(dbg_o["d_mod"], mod[:].rearrange("p a b -> p (a b)"), r=["mod"])

        barrier()
        with contextlib.ExitStack() as st:
          if ST('ret'):
              def sba(name, shape, dt=F32):
                  rr["ev"] += 1
                  return st.enter_context(nc.sbuf_tensor("a%d_" % rr["ev"] + name, list(shape), dt))
              rq = sba("rq", [128, 2, T], BF16)
              rkm = sba("rkm", [128, 4, T], BF16)
              rkT = sba("rkT", [128, NT, 256], BF16)
              rvT = sba("rvT", [128, NT, 256], BF16)
              rgT = sba("rgT", [128, NT, 256], BF16)
              ytok = sba("ytok", [128, NT, 256], BF16)
              Sbd = sba("Sbd", [128, 4, 128])
              Sbf = sba("Sbf", [128, 4, 8, 128], BF16)
              Ob = sba("Ob", [128, 2, 256])
              Oq = sba("Oq", [128, 2, 256])
              st4 = sba("st4", [128, 6, 8])
              qx8 = [sba("qx%d" % i, [128, 128], BF16) for i in range(8)]
              ktx = [sba("ktx%d" % i, [128, 256], BF16) for i in range(2)]
              dsb4 = [sba("dsb%d" % i, [128, 128]) for i in range(4)]
              scT8 = [sba("scT%d" % i, [128, 128], BF16) for i in range(8)]

              def ev_rq(mi, n, b):
                  cp("act", rq[:, mi, n * 512:(n + 1) * 512], PS(b), [pk(b)], ["rq"])
              proj_F(pi["retF"][0], hT, hk, 2, ev_rq)
              for pp in range(2):
                  def ev_rk(mi, n, b, pp=pp):
                      cp("act" if mi == 0 else "dve", rkm[:, pp * 2 + mi, n * 512:(n + 1) * 512], PS(b), [pk(b)], ["rkm"])
                  proj_F(pi["retF"][1 + pp], hT, hk, 2, ev_rk)
              for (dst, key, pid) in ((rkT, "rkT", pi["retT"][0]), (rvT, "rvT", pi["retT"][1]), (rgT, "rgT", pi["retT"][2])):
                  def ev_t(t, b, dst=dst, key=key):
                      cp("act" if t % 2 == 0 else "dve", dst[:, t, :], PS(b, 256), [pk(b)], [key])
                  proj_T(pid, range(NT), ev_t)
              G(lambda e: e.memset(Sbf[:], 0.0), [], ["Sbf"])
              act(rgT[:].rearrange("p a b -> p (a b)"), rgT[:].rearrange("p a b -> p (a b)"), AF.Silu, ["rgT"], ["rgT"])
              tt("pool", rgT[:], rgT[:], C("gng", l * 256, 256).unsqueeze(1).to_broadcast([128, NT, 256]), ALU.mult, ["rgT", "cst"], ["rgT"])
              segs = [(0, 8, True, 0), (8, 2, False, 0), (10, 2, False, 1)]
              for (t0, ncnk, samp, sj) in segs:
                  if samp:
                      P.dma(Sbd[:], sret_d[:, l, :, :], w=["Sbd0", "Sbd1", "Sbd2", "Sbd3"])
                  else:
                      G(lambda e: e.memset(Sbd[:], 0.0), [], ["Sbd0", "Sbd1", "Sbd2", "Sbd3"])
                  sbank = [0, 1, 2, 6]
                  for step in range(ncnk):
                      for d in range(2):
                          i = step if d == 0 else ncnk - 1 - step
                          t = t0 + i
                          kt = ktx[d]
                          tt("dve", kt[:].rearrange("p (h e) -> p h e", h=4), rkT[:, t, :].rearrange("p (h e) -> p h e", h=4),
                             zeta[:, l * 2 + d, :].unsqueeze(2).to_broadcast([128, 4, 64]), ALU.mult, ["rkT", "zeta"], ["ktx%d" % d])
                          for j in range(2):
                              si = d * 2 + j
                              sk_ = "Sbd%d" % si
                              for hb in range(2):
                                  A(lambda e, si=si, i=i, hb=hb: e.activation(out=Sbf[hb * 64:(hb + 1) * 64, si, i, hb * 64:(hb + 1) * 64],
                                    in_=Sbd[hb * 64:(hb + 1) * 64, si, hb * 64:(hb + 1) * 64], func=AF.Copy), [sk_], ["Sbf"])
                              mm(PS(sbank[si], 128), kt[:, j * 128:(j + 1) * 128], rvT[:, t, j * 128:(j + 1) * 128], True, True,
                                 ["ktx%d" % d, "rvT"], [pk(sbank[si])])
                              stt("dve", Sbd[:, si, :], Sbd[:, si, :], Gp[:, l * 4 + si: l * 4 + si + 1], PS(sbank[si], 128), ALU.mult, ALU.add,
                                  [sk_, "Gp", pk(sbank[si])], [sk_])
                  if not samp:
                      P.dma(osret_o[:, sj, l, :, :], Sbd[:], r=["Sbd0", "Sbd1", "Sbd2", "Sbd3"])
                  def stage_a(t):
                      cols = slice(t * 128, (t + 1) * 128)
                      par = t % 2
                      for j in range(2):
                          for d in range(2):
                              q_ = qx8[par * 4 + d * 2 + j]
                              tt("dve" if d == 0 else "pool", q_[:], rq[:, j, cols], Xi[:, l * 4 + d * 2 + j, :], ALU.mult,
                                 ["rq", "Xi"], ["qx%d" % (par * 4 + d * 2 + j)])
                      for h in range(4):
                          mm(PS(par, 128, h * 128), rkm[:, h, cols], rq[:, h // 2, cols], True, True, ["rkm", "rq"], [pk(par)])
                      for h in range(4):
                          tt("dve", scT8[par * 4 + h][:], PS(par, 128, h * 128), DTm[:, l * 4 + h, :], ALU.mult, [pk(par), "DTm"],
                             ["scT%d" % (par * 4 + h)])
                  stage_a(t0)
                  for i in range(ncnk):
                      t = t0 + i
                      if i + 1 < ncnk:
                          stage_a(t + 1)
                      par = t % 2
                      ob = 6 + (t % 2)
                      for j in range(2):
                          mm(PS(ob, 128, j * 128), qx8[par * 4 + j][:], Sbf[:, j, i, :], True, False, ["qx%d" % (par * 4 + j), "Sbf"], [pk(ob)])
                          mm(PS(ob, 128, j * 128), qx8[par * 4 + 2 + j][:], Sbf[:, 2 + j, i, :], False, False,
                             ["qx%d" % (par * 4 + 2 + j), "Sbf"], [pk(ob)])
                          for hh in range(2):
                              h = j * 2 + hh
                              mm(PS(ob, 64, h * 64), scT8[par * 4 + h][:], rvT[:, t, h * 64:(h + 1) * 64], False, hh == 1,
                                 ["scT%d" % (par * 4 + h), "rvT"], [pk(ob)])
                      cp("act", Ob[:, t % 2, :], PS(ob, 256), [pk(ob)], ["Ob"])
                      if t % 2 == 1:
                          tb = t - 1
                          O3 = Ob[:].rearrange("p t (h e) -> p (t h) e", h=4)
                          Q3 = Oq[:].rearrange("p t (h e) -> p (t h) e", h=4)
                          s1 = st4[:, 0, :]; s2 = st4[:, 1, :]; mean = st4[:, 2, :]; var = st4[:, 3, :]; rs = st4[:, 4, :]
                          V(lambda e: e.tensor_reduce(out=s1, in_=O3, axis=AX.X, op=ALU.add), ["Ob"], ["st4a"])
                          tt("pool", Oq[:], Ob[:], Ob[:], ALU.mult, ["Ob"], ["Oq"])
                          V(lambda e: e.tensor_reduce(out=s2, in_=Q3, axis=AX.X, op=ALU.add), ["Oq"], ["st4b"])
                          ts("dve", mean, s1, 1.0 / 64, None, ALU.mult, None, ["st4a"], ["st4c"])
                          tt("dve", var, mean, mean, ALU.mult, ["st4c"], ["st4d"])
                          stt("dve", var, s2, 1.0 / 64, var, ALU.mult, ALU.subtract, ["st4b", "st4d"], ["st4d"])
                          act(rs, var, AF.Sqrt, ["st4d", "c_eps"], ["st4e"], bias=c_eps[:, 0:1])
                          V(lambda e: e.reciprocal(out=rs, in_=rs), ["st4e"], ["st4e"])
                          tt("dve", Q3, O3, mean.unsqueeze(2).to_broadcast([128, 8, 64]), ALU.subtract, ["Ob", "st4c", "Oq"], ["Oq"])
                          tt("dve", Q3, Q3, rs.unsqueeze(2).to_broadcast([128, 8, 64]), ALU.mult, ["Oq", "st4e"], ["Oq"])
                          tt("pool", ytok[:, tb:tb + 2, :], Oq[:], rgT[:, tb:tb + 2, :], ALU.mult, ["Oq", "rgT"], ["ytok"])
                          transpose_part(ytok, "ytok", [tb // 2], [0, 1])
              (P.dma(dbg_o["d_y_ret"], yg[:].rearrange("p a b -> p (a b)"), r=["yg"]) if (debug and l == 0) else None)
              out_proj(pi["retO"], G1)

        barrier()
        with contextlib.ExitStack() as st:
          if ST('lru'):
              def sba(name, shape, dt=F32):
                  rr["ev"] += 1
                  return st.enter_context(nc.sbuf_tensor("a%d_" % rr["ev"] + name, list(shape), dt))
              lx = sba("lx", [128, 2, T], BF16)
              lgt = sba("lgt", [128, 2, T], BF16)
              xc = sba("xc", [128, 1024])
              xcb = sba("xcb", [128, 1024], BF16)
              rr2 = [sba("rr%d" % i, [128, 1024]) for i in range(2)]
              ii2 = [sba("ii%d" % i, [128, 1024]) for i in range(2)]
              uu2 = [sba("uu%d" % i, [128, 1024]) for i in range(2)]
              hh_ = [sba("hh%d" % i, [128, 1024]) for i in range(2)]
              slo2 = [sba("slo%d" % i, [128, 2 * 4]) for i in range(2)]
              lruwb = sba("lruwb", [128, 8, 128], BF16)
              for (dst, key, pid) in ((lx, "lx", pi["lruF"][0]), (lgt, "lgt", pi["lruF"][1])):
                  def ev(mi, n, b, dst=dst, key=key):
                      cp("act" if n % 2 == 0 else "dve", dst[:, mi, n * 512:(n + 1) * 512], PS(b), [pk(b)], [key])
                  proj_F(pid, hT, hk, 2, ev)
              P.dma(stg[0][:, 0:1024], lruw_d[:, l * 1024:(l + 1) * 1024], w=["stg0"], noscope=True)
              cp("dve", lruwb[:].rearrange("p a b -> p (a b)"), stg[0][:, 0:1024], ["stg0"], ["lruwb"])
              for cc in range(2):
                  act(lgt[:, cc, :], lgt[:, cc, :], AF.Gelu_apprx_tanh, ["lgt"], ["lgt"])
              for (c0, L, samp, nseq) in ((0, 1024, True, 1), (1024, 512, False, 2)):
                  Ls = L // nseq
                  for cc in range(2):
                      xi_ = lx[:, cc, c0:c0 + L]
                      cw = lambda k: C("conv_w", l * 8 + k * 2 + cc, 1)
                      ts("dve", xc[:, 0:L], xi_, cw(2), C("conv_b", l * 2 + cc, 1), ALU.mult, ALU.add, ["lx", "cst"], ["xc"])
                      for sq_ in range(nseq):
                          b0 = sq_ * Ls
                          stt("dve", xc[:, b0 + 2:b0 + Ls], xi_[:, b0:b0 + Ls - 2], cw(0), xc[:, b0 + 2:b0 + Ls], ALU.mult, ALU.add, ["lx", "cst", "xc"], ["xc"])
                          stt("dve", xc[:, b0 + 1:b0 + Ls], xi_[:, b0:b0 + Ls - 1], cw(1), xc[:, b0 + 1:b0 + Ls], ALU.mult, ALU.add, ["lx", "cst", "xc"], ["xc"])
                          stt("dve", xc[:, b0:b0 + Ls - 1], xi_[:, b0 + 1:b0 + Ls], cw(3), xc[:, b0:b0 + Ls - 1], ALU.mult, ALU.add, ["lx", "cst", "xc"], ["xc"])
                      cp("pool", xcb[:, 0:L], xc[:, 0:L], ["xc"], ["xcb"])
                      pidx_ = [l * 4 + d * 2 + cc for d in range(2)]
                      for d in range(2):
                          for (dst, key, gsel, bname) in ((rr2[d], "rr%d" % d, 0, "b_a"), (ii2[d], "ii%d" % d, 1, "b_x")):
                              for n in range((L + 511) // 512):
                                  w_ = min(512, L - n * 512)
                                  b = dbank()
                                  mm(PS(b, w_), lruwb[:, d * 4 + gsel * 2 + cc, :], xcb[:, n * 512:n * 512 + w_], True, True,
                                     ["lruwb", "xcb"], [pk(b)])
                                  act(dst[:, n * 512:n * 512 + w_], PS(b, w_), AF.Sigmoid, [pk(b), "cst"], [key], bias=C(bname, pidx_[d], 1))
                      for d in range(2):
                          tt("pool" if d == 1 else "dve", ii2[d][:, 0:L], ii2[d][:, 0:L], xc[:, 0:L], ALU.mult, ["ii%d" % d, "xc"], ["ii%d" % d])
                      for d in range(2):
                          act(uu2[d][:, 0:L], rr2[d][:, 0:L], AF.Exp, ["rr%d" % d, "sa16"], ["uu%d" % d], scale=sa16[:, pidx_[d]:pidx_[d] + 1])
                          act(rr2[d][:, 0:L], rr2[d][:, 0:L], AF.Exp, ["rr%d" % d, "sa8"], ["rr%d" % d], scale=sa8[:, pidx_[d]:pidx_[d] + 1])
                      for d in range(2):
                          act(uu2[d][:, 0:L], uu2[d][:, 0:L], AF.Sqrt, ["uu%d" % d, "c_one"], ["uu%d" % d], bias=c_one[:, 0:1], scale=-1.0)
                      for d in range(2):
                          tt("dve", uu2[d][:, 0:L], uu2[d][:, 0:L], ii2[d][:, 0:L], ALU.mult, ["uu%d" % d, "ii%d" % d], ["uu%d" % d])
                      for sq_ in range(nseq):
                          b0 = sq_ * Ls
                          for d in range(2):
                              hd = hh_[d]
                              init = C("h0", pidx_[d], 1) if samp else 0.0
                              if d == 0:
                                  V(lambda e, hd=hd, init=init, b0=b0, Ls=Ls: e.tensor_tensor_scan(out=hd[:, b0:b0 + Ls], data0=rr2[0][:, b0:b0 + Ls],
                                    data1=uu2[0][:, b0:b0 + Ls], initial=init, op0=ALU.mult, op1=ALU.add), ["rr0", "uu0", "cst"], ["hh0"])
                                  fin = hd[:, b0 + Ls - 1:b0 + Ls]
                              else:
                                  V(lambda e, hd=hd, init=init, b0=b0, Ls=Ls: e.tensor_tensor_scan(out=hd[:, b0:b0 + Ls][:, ::-1],
                                    data0=rr2[1][:, b0:b0 + Ls][:, ::-1], data1=uu2[1][:, b0:b0 + Ls][:, ::-1],
                                    initial=init, op0=ALU.mult, op1=ALU.add), ["rr1", "uu1", "cst"], ["hh1"])
                                  fin = hd[:, b0:b0 + 1]
                              if not samp:
                                  cp("pool", slo2[sq_][:, d * 2 + cc: d * 2 + cc + 1], fin, ["hh%d" % d], ["slo%d" % sq_])
                      tt("dve", hh_[0][:, 0:L], hh_[0][:, 0:L], hh_[1][:, 0:L], ALU.add, ["hh0", "hh1"], ["hh0"])
                      tt("dve", yg[:, cc, c0:c0 + L], hh_[0][:, 0:L], lgt[:, cc, c0:c0 + L], ALU.mult, ["hh0", "lgt"], ["yg"])
                  if not samp:
                      for sj in range(2):
                          P.dma(oslru_o[:, (sj * DEPTH + l) * 4:(sj * DEPTH + l) * 4 + 4], slo2[sj][:, 0:4], r=["slo%d" % sj])
              (P.dma(dbg_o["d_y_lru"], yg[:].rearrange("p a b -> p (a b)"), r=["yg"]) if (debug and l == 0) else None)
              out_proj(pi["lruO"], G1)

        barrier()
        with contextlib.ExitStack() as st:
          if ST('swa'):
              def sba(name, shape, dt=F32):
                  rr["ev"] += 1
                  return st.enter_context(nc.sbuf_tensor("a%d_" % rr["ev"] + name, list(shape), dt))
              sq = sba("sq", [128, 2, T], BF16)
              sk = sba("sk", [128, T], BF16)
              kms = [sba("kms%d" % i, [128, 256 + T], BF16) for i in range(2)]
              svaug = sba("svaug", [128, NT, 2, 80], BF16)
              vctx = sba("vctx", [128, 2, 2, 80], BF16)
              ckf = sba("ckf", [128, 256])
              cvf = sba("cvf", [128, 2, 128])
              pT = [sba("pT%d" % i, [128, 640], BF16) for i in range(2)]
              Osw = [sba("Osw", [128, NT, 65])] * 2
              den = sba("den", [128, NT])
              ytok = sba("ytok", [128, NT, 256], BF16)
              kvst = [sba("kvst%d" % i, [128, 256]) for i in range(2)]
              cp("dve", svaug[:, :, :, 64:66], ones[:, 0:NT * 4].rearrange("p (a b c) -> p a b c", a=NT, b=2), ["ones"], ["svaug"])
              cp("dve", vctx[:, :, :, 64:66], ones[:, 0:8].rearrange("p (a b c) -> p a b c", a=2, b=2), ["ones"], ["vctx"])

              def ev_sq(mi, n, b):
                  cp("act", sq[:, mi, n * 512:(n + 1) * 512], PS(b), [pk(b)], ["sq%d" % mi])
              proj_F(pi["swaF"][0], hT, hk, 2, ev_sq)

              def ev_sk(mi, n, b):
                  cp("act", sk[:, n * 512:(n + 1) * 512], PS(b), [pk(b)], ["sk"])
              proj_F(pi["swaF"][1], hT, hk, 1, ev_sk)

              def ev_swt(t, b):
                  for hv in range(2):
                      cp("act", svaug[:, t, hv, 0:64], PS(b, 64, 128 + hv * 64), [pk(b)], ["svaug"])
                  if t >= 8:
                      kb = kvst[t % 2]
                      cp("dve", kb[:], PS(b, 256), [pk(b)], ["kvst%d" % (t % 2)])
                      P.dma(okv_o[t - 8, :, l, 0:256], kb[:], r=["kvst%d" % (t % 2)])
              proj_T(pi["swaT"][0], range(NT), ev_swt)
              rope = sba("rope", [128, 2048], BF16)
              if ST('swa_rope'):
                  rope_load(rope, 0)
                  rope_apply(sq[:, 0, :], "sq0", R64, "R64", rope)
                  rope_apply(sq[:, 1, :], "sq1", R64, "R64", rope)
                  rope_apply(sk[:], "sk", R64, "R64", rope)
              P.dma(ckf[:], cks_d[:, l, :], w=["ckf"])
              P.dma(cvf[:], cvs_d[:, l, :, :], w=["cvf"])
              for cb_ in range(2):
                  for hv in range(2):
                      cp("dve", vctx[:, cb_, hv, 0:64], cvf[:, cb_, hv * 64:(hv + 1) * 64], ["cvf"], ["vctx"])
              for m in range(2):
                  ts("dve", kms[m][:, 0:256], ckf[:], C("hm", m, 1), None, ALU.mult, None, ["ckf", "cst"], ["kms%d" % m])
                  ts("dve", kms[m][:, 256:256 + T], sk[:], C("hm", m, 1), None, ALU.mult, None, ["sk", "cst"], ["kms%d" % m])
              if debug and l == 0:
                  P.dma(dbg_o["d_sq"], sq[:].rearrange("p a b -> p (a b)"), r=["sq0", "sq1"])
                  P.dma(dbg_o["d_kms0"], kms[0][:], r=["kms0"])
                  P.dma(dbg_o["d_kms1"], kms[1][:], r=["kms1"])
                  P.dma(dbg_o["d_sva"], svaug[:].rearrange("p a b c -> p (a b c)"), r=["svaug"])
                  P.dma(dbg_o["d_vctx"], vctx[:].rearrange("p a b c -> p (a b c)"), r=["vctx"])
              apipe = AttnPipe(pT, "pT")
              for h in (range(4) if ST('swa_attn') else []):
                  qc = 0 if h in (0, 3) else 1
                  kvh = h // 2
                  for t in list(range(8)) + [8, 10]:
                      nq = 1 if t < 8 else 2
                      qT = sq[:, qc, t * 128:(t + nq) * 128]
                      kb = []
                      if t < 8:
                          for cb in range(2):
                              kb.append((kms[kvh][:, cb * 128:(cb + 1) * 128], "kms%d" % kvh, vctx[:, cb, kvh, 0:65], "vctx", None))
                          for dt_ in (-1, 0, 1):
                              tk = t + dt_
                              if tk < 0 or tk > 7:
                                  continue
                              mask = None if dt_ == 0 else ((mlo, "mlo") if dt_ == -1 else (mhi, "mhi"))
                              kb.append((kms[kvh][:, 256 + tk * 128: 256 + (tk + 1) * 128], "kms%d" % kvh, svaug[:, tk, kvh, 0:65], "svaug", mask))
                      else:
                          for tk in (t, t + 1):
                              kb.append((kms[kvh][:, 256 + tk * 128: 256 + (tk + 1) * 128], "kms%d" % kvh, svaug[:, tk, kvh, 0:65], "svaug", None))

                      def post(ob, t=t, h=h, nq=nq):
                          Oh = Osw[h % 2]
                          for qi in range(nq):
                              cp("dve", Oh[:, t + qi, :], PS(ob, 65, qi * 128), [pk(ob)], ["Osw"])
                          if t + nq == NT:
                              ts("dve", den[:], Oh[:, :, 64], esink[:, l * 4 + h: l * 4 + h + 1], None, ALU.add, None,
                                 ["Osw", "esink"], ["den"])
                              V(lambda e: e.reciprocal(out=den[:], in_=den[:]), ["den"], ["den"])
                              tt("dve", ytok[:, :, h * 64:(h + 1) * 64], Oh[:, :, 0:64], den[:].unsqueeze(2).to_broadcast([128, NT, 64]),
                                 ALU.mult, ["Osw", "den"], ["ytok"])
                              if h % 2 == 1:
                                  transpose_part(ytok, "ytok", range(6), [h // 2])
                      apipe.push(qT, "sq%d" % qc, kb, 0.125, post, nq=nq)
              apipe.flush()
              (P.dma(dbg_o["d_y_swa"], yg[:].rearrange("p a b -> p (a b)"), r=["yg"]) if (debug and l == 0) else None)
              out_proj(pi["swaO"], G1)

        barrier()
        with contextlib.ExitStack() as st:
          if ST('diff'):
              def sba(name, shape, dt=F32):
                  rr["ev"] += 1
                  return st.enter_context(nc.sbuf_tensor("a%d_" % rr["ev"] + name, list(shape), dt))
              dq = sba("dq", [128, 2, T], BF16)
              dk = sba("dk", [128, 2, T], BF16)
              kmd = [sba("kmd%d" % i, [128, 256 + T], BF16) for i in range(2)]
              dvaug = sba("dvaug", [128, NT, 4, 80], BF16)
              vctx = sba("vctxd", [128, 2, 4, 80], BF16)
              ckf = sba("ckfd", [128, 2, 256])
              cvf = sba("cvfd", [128, 2, 256])
              pT = [sba("pTd%d" % i, [128, 1280], BF16) for i in range(2)]
              Oall = [sba("Oall", [128, 2, NT, 65])] * 2
              rec = sba("rec", [128, 2, NT])
              y0 = sba("y0", [128, NT, 64])
              y1 = sba("y1", [128, NT, 64])
              ss = sba("ss", [128, NT])
              ytok = sba("ytokd", [128, NT, 256], BF16)
              kvst = [sba("kvstd%d" % i, [128, 256]) for i in range(2)]
              cp("dve", dvaug[:, :, :, 64:66], ones[:, 0:NT * 8].rearrange("p (a b c) -> p a b c", a=NT, b=4), ["ones"], ["dvaug"])
              cp("dve", vctx[:, :, :, 64:66], ones[:, 0:16].rearrange("p (a b c) -> p a b c", a=2, b=4), ["ones"], ["vctxd"])
              for (dst, key, pid) in ((dq, "dq", pi["difF"][0]), (dk, "dk", pi["difF"][1])):
                  def ev(mi, n, b, dst=dst, key=key):
                      cp("act", dst[:, mi, n * 512:(n + 1) * 512], PS(b), [pk(b)], [key + str(mi)])
                  proj_F(pid, hT, hk, 2, ev)

              def ev_dkt(t, b):
                  kb = kvst[t % 2]
                  cp("dve", kb[:], PS(b, 256), [pk(b)], ["kvstd%d" % (t % 2)])
                  P.dma(okv_o[t - 8, :, l, 256:512], kb[:], r=["kvstd%d" % (t % 2)])
              proj_T(pi["difT"][0], range(8, NT), ev_dkt)

              def ev_dvt(t, b):
                  for hv in range(4):
                      cp("act", dvaug[:, t, hv, 0:64], PS(b, 64, hv * 64), [pk(b)], ["dvaug"])
                  if t >= 8:
                      kb = kvst[t % 2]
                      cp("dve", kb[:], PS(b, 256), [pk(b)], ["kvstd%d" % (t % 2)])
                      P.dma(okv_o[t - 8, :, l, 512:768], kb[:], r=["kvstd%d" % (t % 2)])
              proj_T(pi["difT"][1], range(NT), ev_dvt)
              rope = sba("roped", [128, 2048], BF16)
              rope_load(rope, 1)
              for cc in range(2):
                  rope_apply(dq[:, cc, :], "dq%d" % cc, R32, "R32", rope)
                  rope_apply(dk[:, cc, :], "dk%d" % cc, R32, "R32", rope)
              P.dma(ckf[:], ckd_d[:, l, :, :], w=["ckfd"])
              P.dma(cvf[:], cvd_d[:, l, :, :], w=["cvfd"])
              for cb_ in range(2):
                  for hv in range(4):
                      cp("dve", vctx[:, cb_, hv, 0:64], cvf[:, cb_, hv * 64:(hv + 1) * 64], ["cvfd"], ["vctxd"])
              ki = 0
              apipe = AttnPipe(pT, "pTd")
              for h in range(4):
                  cc = h // 2
                  hh = h % 2
                  for comp in range(2):
                      km = kmd[ki % 2]
                      kk = "kmd%d" % (ki % 2)
                      ki += 1
                      mcol = C("cm4", hh * 2 + comp, 1)
                      ts("dve", km[:, 0:256], ckf[:, cc, :], mcol, None, ALU.mult, None, ["ckfd", "cst"], [kk])
                      ts("dve", km[:, 256:256 + T], dk[:, cc, :], mcol, None, ALU.mult, None, ["dk%d" % cc, "cst"], [kk])
                      for t in list(range(8)) + [8, 10]:
                          nq = 1 if t < 8 else 2
                          qT = dq[:, cc, t * 128:(t + nq) * 128]
                          kb = []
                          if t < 8:
                              for cb in range(2):
                                  kb.append((km[:, cb * 128:(cb + 1) * 128], kk, vctx[:, cb, h, 0:65], "vctxd", None))
                              for tk in range(8):
                                  kb.append((km[:, 256 + tk * 128: 256 + (tk + 1) * 128], kk, dvaug[:, tk, h, 0:65], "dvaug", None))
                          else:
                              for tk in (t, t + 1):
                                  kb.append((km[:, 256 + tk * 128: 256 + (tk + 1) * 128], kk, dvaug[:, tk, h, 0:65], "dvaug", None))

                          def post(ob, t=t, h=h, comp=comp, nq=nq):
                              Oa = Oall[h % 2]
                              ok = "Oall"
                              for qi in range(nq):
                                  cp("dve", Oa[:, comp, t + qi, :], PS(ob, 65, qi * 128), [pk(ob)], [ok])
                              if t + nq == NT and comp == 1:
                                  V(lambda e: e.reciprocal(out=rec[:], in_=Oa[:, :, :, 64]), [ok], ["rec"])
                                  tt("dve", y0[:], Oa[:, 0, :, 0:64], rec[:, 0, :].unsqueeze(2).to_broadcast([128, NT, 64]), ALU.mult, [ok, "rec"], ["y0"])
                                  tt("dve", y1[:], Oa[:, 1, :, 0:64], rec[:, 1, :].unsqueeze(2).to_broadcast([128, NT, 64]), ALU.mult, [ok, "rec"], ["y1"])
                                  stt("dve", y0[:], y1[:], nlam[:, l:l + 1], y0[:], ALU.mult, ALU.add, ["y1", "nlam", "y0"], ["y0"])
                                  tt("pool", y1[:], y0[:], y0[:], ALU.mult, ["y0"], ["y1"])
                                  V(lambda e: e.tensor_reduce(out=ss[:], in_=y1[:], axis=AX.X, op=ALU.add), ["y1"], ["ss"])
                                  act(ss[:], ss[:], AF.Sqrt, ["ss", "c_eps"], ["ss"], bias=c_eps[:, 0:1], scale=1.0 / 64)
                                  V(lambda e: e.reciprocal(out=ss[:], in_=ss[:]), ["ss"], ["ss"])
                                  tt("dve", y0[:], y0[:], ss[:].unsqueeze(2).to_broadcast([128, NT, 64]), ALU.mult, ["y0", "ss"], ["y0"])
                                  tt("dve", ytok[:, :, h * 64:(h + 1) * 64], y0[:], dgt[:, l, :].unsqueeze(1).to_broadcast([128, NT, 64]),
                                     ALU.mult, ["y0", "dgt"], ["ytokd"])
                          apipe.push(qT, "dq%d" % cc, kb, 32 ** -0.5, post, nq=nq)
              apipe.flush()
              transpose_y(ytok, "ytokd")
              (P.dma(dbg_o["d_y_diff"], yg[:].rearrange("p a b -> p (a b)"), r=["yg"]) if (debug and l == 0) else None)
              out_proj(pi["difO"], G1)

        barrier()
        if ST('ffn'):
            norm(A2, B2, lambda c, n: hT[:, c, n * 512:(n + 1) * 512], hk)
        with contextlib.ExitStack() as st:
          if ST('ffn'):
              hid = st.enter_context(nc.sbuf_tensor("a_hid_%d" % l, [128, 8, T], BF16))
              rl = [st.enter_context(nc.sbuf_tensor("a_rl%d_%d" % (i, l), [128, 512], BF16)) for i in range(2)]
              xs = [st.enter_context(nc.sbuf_tensor("a_xstg%d_%d" % (i, l), [128, 2048], F32)) for i in range(3)]
              xw = [st.enter_context(nc.sbuf_tensor("a_xwbf%d_%d" % (i, l), [128, 2048], BF16)) for i in range(2)]
              w_extra_begin(xs, xw)
              nada = [0]

              def ada_tick():
                  if l + 1 < depth and nada[0] < 24 and ST('ada'):
                      ada_piece(l + 1, nada[0]); nada[0] += 1
              for q in range(4):
                  f1, f2 = pi["ff"][q]
                  for j in range(4):
                      def ev1(mi, n, b, j=j):
                          r_ = rl[(mi + n) % 2]
                          act(r_[:], PS(b), AF.Relu, [pk(b)], ["rl%d" % ((mi + n) % 2)])
                          tt("pool" if (mi + n) % 2 == 0 else "dve", hid[:, j * 2 + mi, n * 512:(n + 1) * 512], r_[:], r_[:], ALU.mult, ["rl%d" % ((mi + n) % 2)], ["hid"])
                      proj_F(f1[j], hT, hk, 2, ev1)
                      ada_tick()
                  for j in range(4):
                      def ev2(mi, n, b, j=j):
                          m = j * 2 + mi
                          v = 0 if n < 2 else 1
                          xs = x[:, m, n * 512:(n + 1) * 512]
                          stt("dve", xs, PS(b), G2(m, v), xs, ALU.mult, ALU.add, [pk(b), xk[m]] + MODK, [xk[m]])
                      proj_F(f2[j], hid, ["hid"], 2, ev2)
                      ada_tick()
              if l + 1 < depth and ST('ada'):
                  ada_finish(l + 1)
              w_extra_end()

    P.scope = None
    ostg = [sb("ostg%d" % i, [128, 512]) for i in range(2)]
    okeys = ["ostg%d" % (c % 2) for c in range(8)]
    cnt = {"i": 0}
    for n in range(3):
        blk = slice(n * 512, (n + 1) * 512)
        for c in range(8):
            s = sqt[c % 2]
            act(s[:], x[:, c, blk], AF.Square, [xk[c]], ["sqt%d" % (c % 2)])
            mm(PS(5), ones[:], s[:], c == 0, c == 7, ["ones", "sqt%d" % (c % 2)], [pk(5)])
        act(rstd[:], PS(5), AF.Sqrt, [pk(5), "c_eps"], ["rstd"], bias=c_eps[:, 0:1], scale=1.0 / 1024.0)
        V(lambda e: e.reciprocal(out=rstd[:], in_=rstd[:]), ["rstd"], ["rstd"])
        for c in range(8):
            o = ostg[c % 2]
            stt("dve", o[:], x[:, c, blk], C("g_fin", c, 1), rstd[:], ALU.mult, ALU.mult, [xk[c], "rstd", "cst"], ["ostg%d" % (c % 2)])
            P.dma(yT_o[c * 128:(c + 1) * 128, blk], o[:], r=["ostg%d" % (c % 2)])
    P.emit()
    P.stats['minfree'] = minfree['v']
    return nc, P


def _consts():
    coff, CW = cst_layout()
    c = np.zeros((128, CW), np.float32)

    def put(name, arr):
        o, w = coff[name]
        c[:, o:o + w] = np.asarray(arr, np.float32).reshape(128, w)
    put("ident", np.eye(128))
    R64 = np.zeros((128, 128)); R32 = np.zeros((128, 128))
    for blk in range(2):
        for i in range(64):
            if i < 32:
                R64[blk * 64 + i, blk * 64 + i + 32] = -1.0
            else:
                R64[blk * 64 + i, blk * 64 + i - 32] = 1.0
    for blk in range(4):
        for i in range(32):
            if i < 16:
                R32[blk * 32 + i, blk * 32 + i + 16] = -1.0
            else:
                R32[blk * 32 + i, blk * 32 + i - 16] = 1.0
    put("R64T", R64.T); put("R32T", R32.T)
    j = np.arange(128)[:, None]; i = np.arange(128)[None, :]
    put("mlo", np.where(j >= i, 0.0, -480.0))
    put("mhi", np.where(j <= i, 0.0, -480.0))
    m = np.arange(128)[:, None]; n = np.arange(128)[None, :]
    put("Ef", np.maximum(n - m, 0)); put("Eb", np.maximum(m - n, 0))
    put("Lf8", (n >= m) / 8.0); put("Lb8", (m >= n) / 8.0)
    bd = np.zeros((128, 128)); bd[:64, :64] = 1; bd[64:, 64:] = 1
    put("BD", bd)
    put("idx1", np.broadcast_to(np.arange(128)[None, :] + 1.0, (128, 128)))
    put("idxr", np.broadcast_to(128.0 - np.arange(128)[None, :], (128, 128)))
    put("cmf", (127.0 - np.arange(128))[:, None]); put("cmb", (np.arange(128) * 1.0)[:, None])
    p = np.arange(128)
    put("hm", np.stack([p < 64, p >= 64], 1))
    put("cm4", np.stack([(p // 32) == k for k in range(4)], 1))
    pos = np.arange(1024)
    row = (pos // 64).astype(np.float32); col = (pos % 64).astype(np.float32)

    def tab(dim):
        nn = dim // 4
        inv = (10000.0 ** (-np.arange(nn, dtype=np.float32) / nn)).astype(np.float32)
        ang = np.concatenate([row[:, None] * inv, col[:, None] * inv], -1).astype(np.float32)
        idx = (np.arange(128) % dim) % (dim // 2)
        return np.cos(ang)[:, idx].T.astype(np.float32), np.sin(ang)[:, idx].T.astype(np.float32)
    c64, s64 = tab(64); c32, s32 = tab(32)
    rope = np.concatenate([c64, s64, c32, s32], 1).astype(np.float32)
    return c, rope, coff


def _fm(v):
    v = np.asarray(v)
    n = v.shape[-1] // 128
    lead = v.shape[:-1]
    return np.moveaxis(v.reshape(lead + (n, 128)), -1, 0)


_CACHE = {}


def kernel(**inp):
    I = {k: np.asarray(v) for k, v in inp.items()}
    if "prog" not in _CACHE:
        _CACHE["prog"] = build_program()
    nc, P = _CACHE["prog"]
    in_maps = prep_inputs(I)
    res = run_bass_kernel_spmd(nc, in_maps, core_ids=list(range(8)))
    return assemble(res.results)


def prep_inputs(I, cores=range(8)):
    cbase, rope, coff = _consts()

    wts = np.zeros((DEPTH * PPL, 128, 2048), np.float32)

    def kpiece(W, cols):
        out = np.zeros((128, 8, 256), np.float32)
        cols = np.asarray(cols)
        valid = cols >= 0
        Wc = W[:, cols[valid]].reshape(8, 128, -1)
        out[:, :, np.nonzero(valid)[0]] = np.transpose(Wc, (1, 0, 2))
        return out.reshape(128, 2048)
    ar = np.arange
    for l in range(DEPTH):
        pi = piece_ids(l)
        wa, wi, wo, w1, w2 = I["w_ada"][l], I["w_in"][l], I["w_out"][l], I["w_ff1"][l], I["w_ff2"][l]
        for j in range(24):
            wts[pi["ada"][j]] = kpiece(wa, ar(j * 256, (j + 1) * 256))
        wts[pi["retF"][0]] = kpiece(wi, ar(0, 256))
        neg = -np.ones(64, int)
        wts[pi["retF"][1]] = kpiece(wi, np.concatenate([ar(256, 320), neg, neg, ar(320, 384)]))
        wts[pi["retF"][2]] = kpiece(wi, np.concatenate([ar(384, 448), neg, neg, ar(448, 512)]))
        wts[pi["retT"][0]] = kpiece(wi, ar(256, 512))
        wts[pi["retT"][1]] = kpiece(wi, ar(512, 768))
        wts[pi["retT"][2]] = kpiece(wi, ar(768, 1024))
        wts[pi["lruF"][0]] = kpiece(wi, ar(1024, 1280))
        wts[pi["lruF"][1]] = kpiece(wi, ar(1280, 1536))
        sqb = 1536
        wts[pi["swaF"][0]] = kpiece(wi, np.concatenate([ar(sqb, sqb + 64), ar(sqb + 192, sqb + 256), ar(sqb + 64, sqb + 192)]))
        wts[pi["swaF"][1]] = kpiece(wi, np.concatenate([ar(1792, 1920), -np.ones(128, int)]))
        wts[pi["swaT"][0]] = kpiece(wi, ar(1792, 2048))
        wts[pi["difF"][0]] = kpiece(wi, ar(2048, 2304))
        wts[pi["difF"][1]] = kpiece(wi, ar(2304, 2560))
        wts[pi["difT"][0]] = kpiece(wi, ar(2304, 2560))
        wts[pi["difT"][1]] = kpiece(wi, ar(2560, 2816))
        for g, name in enumerate(("retO", "lruO", "swaO", "difO")):
            blkw = wo[g * 256:(g + 1) * 256, :].reshape(2, 128, 1024)
            wts[pi[name]] = np.transpose(blkw, (1, 0, 2)).reshape(128, 2048)
        for q in range(4):
            f1, f2 = pi["ff"][q]
            for j in range(4):
                wts[f1[j]] = kpiece(w1, ar(q * 1024 + j * 256, q * 1024 + (j + 1) * 256))
                wts[f2[j]] = kpiece(w2[q * 1024:(q + 1) * 1024, :], ar(j * 256, (j + 1) * 256))
    lruw = np.zeros((128, DEPTH, 8, 128), np.float32)
    for l in range(DEPTH):
        for d in range(2):
            for gsel, nm in enumerate(("lru_w_a", "lru_w_x")):
                for cc in range(2):
                    for bb in range(2):
                        blk = I[nm][l, d, cc * 2 + bb]
                        lruw[bb * 64:(bb + 1) * 64, l, d * 4 + gsel * 2 + cc, bb * 64:(bb + 1) * 64] = blk
    lruw = lruw.reshape(128, DEPTH * 8 * 128)

    def putc(c, name, arr):
        o, w = coff[name]
        c[:, o:o + w] = np.asarray(arr, np.float32).reshape(128, w)

    in_maps = []
    for core in cores:
        c = cbase.copy()
        putc(c, "b_ada", _fm(I["b_ada"]))
        putc(c, "g_mix", _fm(I["norm_mix_g"])); putc(c, "g_mlp", _fm(I["norm_mlp_g"])); putc(c, "g_fin", _fm(I["final_norm_g"]))
        cv = np.stack([I["c"][core], I["c_ctx"]], 0)
        putc(c, "cT", np.transpose(_fm(cv), (0, 2, 1)))
        putc(c, "conv_w", _fm(I["lru_conv_w"])); putc(c, "conv_b", _fm(I["lru_conv_b"]))
        putc(c, "b_a", _fm(I["lru_b_a"])); putc(c, "b_x", _fm(I["lru_b_x"])); putc(c, "lam", _fm(I["lru_lambda"]))
        putc(c, "h0", _fm(I["state_lru"][core]))
        rd = I["ret_decay"]
        putc(c, "rd_row", np.broadcast_to(rd.reshape(1, 32), (128, 32)))
        rp = np.zeros((128, DEPTH, 2, 2), np.float32)
        for j in range(2):
            rp[:64, :, :, j] = rd[:, :, 2 * j]; rp[64:, :, :, j] = rd[:, :, 2 * j + 1]
        putc(c, "rd_pair", rp)
        putc(c, "sink", np.broadcast_to(I["swa_sink"].reshape(1, 16), (128, 16)))
        putc(c, "dlam", np.broadcast_to(I["diff_lambda"].reshape(1, -1), (128, DEPTH * 128)))
        putc(c, "dng", np.broadcast_to(I["diff_norm_g"].reshape(1, -1), (128, DEPTH * 64)))
        putc(c, "gng", np.broadcast_to(I["ret_gn_g"].reshape(1, -1), (128, DEPTH * 256)))
        xall = np.concatenate([I["x_sample"][core], I["x_prompt"][2 * core], I["x_prompt"][2 * core + 1]], 0)
        cks = np.transpose(I["cache_swa_k"][core].reshape(DEPTH, 256, 128), (2, 0, 1))
        cvs = np.transpose(I["cache_swa_v"][core].reshape(DEPTH, 2, 128, 128), (2, 0, 1, 3))
        ckd = np.transpose(I["cache_diff_k"][core].reshape(DEPTH, 256, 2, 128), (3, 0, 2, 1))
        cvd = np.transpose(I["cache_diff_v"][core].reshape(DEPTH, 2, 128, 256), (2, 0, 1, 3))
        sr = I["state_ret"][core]
        sret = np.zeros((128, DEPTH, 4, 128), np.float32)
        for d in range(2):
            for j in range(2):
                sret[:64, :, d * 2 + j, :64] = np.transpose(sr[:, d, 2 * j], (1, 0, 2))
                sret[64:, :, d * 2 + j, 64:] = np.transpose(sr[:, d, 2 * j + 1], (1, 0, 2))
        in_maps.append({"xT": np.ascontiguousarray(xall.T), "wts": wts, "cst": c, "rope": rope, "lruw": lruw,
                        "cks": np.ascontiguousarray(cks), "cvs": np.ascontiguousarray(cvs), "ckd": np.ascontiguousarray(ckd),
                        "cvd": np.ascontiguousarray(cvd), "sret": sret})
    return in_maps


def assemble(R):
    y_prompt = np.zeros((16, 256, 1024), np.float32); y_sample = np.zeros((8, 1024, 1024), np.float32)
    nsr = np.zeros((16, DEPTH, 2, 4, 64, 64), np.float32); nsl = np.zeros((16, DEPTH, 2, 256), np.float32)
    nsk = np.zeros((16, DEPTH, 256, 2, 64), np.float32); nsv = np.zeros((16, DEPTH, 256, 2, 64), np.float32)
    ndk = np.zeros((16, DEPTH, 256, 4, 64), np.float32); ndv = np.zeros((16, DEPTH, 256, 4, 64), np.float32)
    for core in range(8):
        r = R[core]
        y = r["yT"].T
        y_sample[core] = y[0:1024]
        osr = r["o_sret"]
        osl = r["o_slru"].reshape(128, 2, DEPTH, 2, 2)
        okv = r["o_kv"]
        for sj in range(2):
            b = 2 * core + sj
            y_prompt[b] = y[1024 + 256 * sj: 1280 + 256 * sj]
            for d in range(2):
                for j in range(2):
                    nsr[b, :, d, 2 * j] = np.transpose(osr[:64, sj, :, d * 2 + j, :64], (1, 0, 2))
                    nsr[b, :, d, 2 * j + 1] = np.transpose(osr[64:, sj, :, d * 2 + j, 64:], (1, 0, 2))
            nsl[b] = np.transpose(osl[:, sj], (1, 2, 3, 0)).reshape(DEPTH, 2, 256)
            kv = np.transpose(okv[2 * sj:2 * sj + 2], (2, 0, 1, 3)).reshape(DEPTH, 256, 768)
            nsk[b] = kv[:, :, 0:128].reshape(DEPTH, 256, 2, 64)
            nsv[b] = kv[:, :, 128:256].reshape(DEPTH, 256, 2, 64)
            ndk[b] = kv[:, :, 256:512].reshape(DEPTH, 256, 4, 64)
            ndv[b] = kv[:, :, 512:768].reshape(DEPTH, 256, 4, 64)
    return (y_prompt, y_sample, nsr, nsl, nsk, nsv, ndk, ndv)
```

```python
import contextlib
import numpy as np
import ml_dtypes
import concourse.bass as bass
import concourse.mybir as mybir
from concourse.bass_utils import run_bass_kernel_spmd

F32 = mybir.dt.float32
BF16 = mybir.dt.bfloat16
AF = mybir.ActivationFunctionType
ALU = mybir.AluOpType
AX = mybir.AxisListType

ENGS = ("pe", "act", "dve", "pool", "sp")
DMA_WIN = 8
DEPTH = 4
T = 1536
NT = 12
EPS = 1e-6
NSTG = 3
NWBF = 3
PPL = 24 + 3 + 3 + 1 + 2 + 1 + 2 + 1 + 1 + 2 + 2 + 1 + 32


class Op:
    __slots__ = ("idx", "eng", "fn", "deps", "dma", "signal", "sem", "val", "waits")

    def __init__(self, idx, eng, fn, dma):
        self.idx = idx; self.eng = eng; self.fn = fn; self.dma = dma
        self.deps = set(); self.signal = False; self.sem = None; self.val = 0; self.waits = []


class Prog:
    def __init__(self, nc):
        self.nc = nc
        self.ops = []
        self.last_w = {}
        self.readers = {}
        self.scope = None

    def add(self, eng, fn, r=(), w=(), dma=False, noscope=False):
        op = Op(len(self.ops), eng, fn, dma)
        w = list(w) + [k for k in r if isinstance(k, str) and k.startswith("ps")]
        r = [k for k in r if not (isinstance(k, str) and k.startswith("ps"))]
        if self.scope is not None and not noscope:
            r.append(self.scope)
        for k in r:
            lw = self.last_w.get(k)
            if lw is not None:
                op.deps.add((lw, "raw"))
        for k in w:
            lw = self.last_w.get(k)
            if lw is not None:
                op.deps.add((lw, "waw"))
            for rd in self.readers.get(k, ()):
                op.deps.add((rd, "war"))
        for k in r:
            self.readers.setdefault(k, []).append(op.idx)
        for k in w:
            self.last_w[k] = op.idx
            self.readers[k] = []
        self.ops.append(op)
        return op

    def dma(self, out, in_, r=(), w=(), noscope=False):
        return self.add("sp", lambda e: e.dma_start(out=out, in_=in_), r=r, w=w, dma=True, noscope=noscope)

    def emit(self):
        nc = self.nc
        ops = self.ops
        for op in ops:
            nd = set()
            for (d, kind) in op.deps:
                dop = ops[d]
                if d == op.idx:
                    continue
                if dop.eng == op.eng and not dop.dma and not op.dma and kind != "raw":
                    continue
                if dop.eng == op.eng and op.eng == "pe":
                    continue
                nd.add(d)
            op.deps = nd
            for d in nd:
                ops[d].signal = True
        stack = contextlib.ExitStack()
        eng_sem = {}; eng_cnt = {}; dma_sems = {}; dma_cnt = {}
        for e in ENGS:
            eng_sem[e] = stack.enter_context(nc.semaphore("c_" + e))
            eng_cnt[e] = 0; dma_sems[e] = None; dma_cnt[e] = 0
        extra_waits = {}
        final_waits = []
        for op in ops:
            if op.dma:
                if dma_sems[op.eng] is None:
                    dma_sems[op.eng] = [stack.enter_context(nc.semaphore("d_%s%d" % (op.eng, i)))
                                        for i in range(DMA_WIN)]
                k = dma_cnt[op.eng]; dma_cnt[op.eng] += 1
                op.sem = dma_sems[op.eng][k % DMA_WIN]
                op.val = 16 * (k // DMA_WIN + 1)
                op.signal = True
                if k >= DMA_WIN:
                    extra_waits[op.idx] = [(op.sem, op.val - 16)]
            elif op.signal:
                eng_cnt[op.eng] += 1
                op.sem = eng_sem[op.eng]; op.val = eng_cnt[op.eng]
        for e in ENGS:
            if dma_sems[e] is not None:
                k = dma_cnt[e]
                for slot in range(min(k, DMA_WIN)):
                    n = (k - 1 - slot) // DMA_WIN + 1
                    final_waits.append((dma_sems[e][slot], 16 * n))
        waited = {e: {} for e in ENGS}
        for op in ops:
            need = {}
            for d in op.deps:
                dop = ops[d]
                if need.get(dop.sem, (None, 0))[1] < dop.val:
                    need[dop.sem] = (dop.sem, dop.val)
            for (s, v) in extra_waits.get(op.idx, ()):
                if need.get(s, (None, 0))[1] < v:
                    need[s] = (s, v)
            wl = []
            for key, (s, v) in need.items():
                if waited[op.eng].get(key, 0) >= v:
                    continue
                waited[op.eng][key] = v
                wl.append((s, v))
            op.waits = wl
        by_eng = {e: [op for op in ops if op.eng == e] for e in ENGS}
        self.stats = {e: len(by_eng[e]) for e in ENGS}
        self.stats["waits"] = sum(len(op.waits) for op in ops)
        self.stats["maxsem"] = dict(eng_cnt)

        def run(eh, ename, final=False):
            for op in by_eng[ename]:
                for (s, v) in op.waits:
                    eh.wait_ge(s, v)
                inst = op.fn(eh)
                if op.signal:
                    inst.then_inc(op.sem, 16 if op.dma else 1)
            if final:
                for (s, v) in final_waits:
                    eh.wait_ge(s, v)

        with stack:
            with nc.Block() as block:
                @block.tensor
                def _(e):
                    run(e, "pe")

                @block.scalar
                def _(e):
                    run(e, "act")

                @block.vector
                def _(e):
                    run(e, "dve")

                @block.gpsimd
                def _(e):
                    run(e, "pool")

                @block.sync
                def _(e):
                    run(e, "sp", final=True)


def cst_layout():
    entA = [("BD", 128), ("hm", 2), ("cm4", 4),
            ("b_ada", DEPTH * 48), ("g_mix", DEPTH * 8), ("g_mlp", DEPTH * 8), ("g_fin", 8),
            ("conv_w", DEPTH * 8), ("conv_b", DEPTH * 2), ("b_a", DEPTH * 4), ("b_x", DEPTH * 4),
            ("h0", DEPTH * 4), ("gng", DEPTH * 256)]
    entB = [("ident", 128), ("R64T", 128), ("R32T", 128), ("mlo", 128), ("mhi", 128), ("Ef", 128), ("Eb", 128),
            ("Lf8", 128), ("Lb8", 128), ("idx1", 128), ("idxr", 128), ("cmf", 1), ("cmb", 1), ("cT", 16),
            ("lam", DEPTH * 4), ("rd_row", 32), ("rd_pair", 16), ("sink", 16), ("dlam", DEPTH * 128), ("dng", DEPTH * 64)]
    off = {}
    o = 0
    for n, w in entA + entB:
        off[n] = (o, w)
        o += w
    off["_WA"] = (sum(w for _, w in entA), 0)
    return off, o


def piece_ids(l):
    b = l * PPL
    d = {}
    d["ada"] = list(range(b, b + 24)); b += 24
    d["retF"] = [b, b + 1, b + 2]; b += 3
    d["retT"] = [b, b + 1, b + 2]; b += 3
    d["retO"] = b; b += 1
    d["lruF"] = [b, b + 1]; b += 2
    d["lruO"] = b; b += 1
    d["swaF"] = [b, b + 1]; b += 2
    d["swaT"] = [b]; b += 1
    d["swaO"] = b; b += 1
    d["difF"] = [b, b + 1]; b += 2
    d["difT"] = [b, b + 1]; b += 2
    d["difO"] = b; b += 1
    d["ff"] = []
    for q in range(4):
        d["ff"].append((list(range(b, b + 4)), list(range(b + 4, b + 8)))); b += 8
    assert b == (l + 1) * PPL + 0 or True
    return d


def build_program(depth=DEPTH, stages=None, debug=False):
    ST = (lambda name: True) if stages is None else (lambda name: name in stages)
    nc = bass.Bass("TRN2", target_bir_lowering=False)
    P = Prog(nc)
    coff, CW = cst_layout()

    def din(name, shape):
        return nc.dram_tensor(name, list(shape), F32, kind="ExternalInput").ap()

    def dout(name, shape):
        return nc.dram_tensor(name, list(shape), F32, kind="ExternalOutput").ap()

    xT_d = din("xT", [1024, T])
    wts_d = din("wts", [DEPTH * PPL, 128, 2048])
    cst_d = din("cst", [128, CW])
    rope_d = din("rope", [128, 4096])
    lruw_d = din("lruw", [128, DEPTH * 8 * 128])
    cks_d = din("cks", [128, DEPTH, 256])
    cvs_d = din("cvs", [128, DEPTH, 2, 128])
    ckd_d = din("ckd", [128, DEPTH, 2, 256])
    cvd_d = din("cvd", [128, DEPTH, 2, 256])
    sret_d = din("sret", [128, DEPTH, 4, 128])
    yT_o = dout("yT", [1024, T])
    osret_o = dout("o_sret", [128, 2, DEPTH, 4, 128])
    oslru_o = dout("o_slru", [128, 2 * DEPTH * 4])
    okv_o = dout("o_kv", [4, 128, DEPTH, 768])

    def sb(name, shape, dt=F32):
        return nc.alloc_sbuf_tensor("s_" + name, list(shape), dt)

    dbg_o = {}
    if debug:
        dbg_o["d_hT"] = nc.dram_tensor("d_hT", [128, 8 * T], BF16, kind="ExternalOutput").ap()
        dbg_o["d_mod"] = nc.dram_tensor("d_mod", [128, 96], F32, kind="ExternalOutput").ap()
        dbg_o["d_sq"] = nc.dram_tensor("d_sq", [128, 2 * T], BF16, kind="ExternalOutput").ap()
        dbg_o["d_kms0"] = nc.dram_tensor("d_kms0", [128, 256 + T], BF16, kind="ExternalOutput").ap()
        dbg_o["d_kms1"] = nc.dram_tensor("d_kms1", [128, 256 + T], BF16, kind="ExternalOutput").ap()
        dbg_o["d_sva"] = nc.dram_tensor("d_sva", [128, NT * 2 * 80], BF16, kind="ExternalOutput").ap()
        dbg_o["d_vctx"] = nc.dram_tensor("d_vctx", [128, 2 * 2 * 80], BF16, kind="ExternalOutput").ap()
        dbg_o["d_osw"] = nc.dram_tensor("d_osw", [128, NT * 65], F32, kind="ExternalOutput").ap()
        for g_ in ("ret", "lru", "swa", "diff"):
            dbg_o["d_y_" + g_] = nc.dram_tensor("d_y_" + g_, [128, 2 * T], BF16, kind="ExternalOutput").ap()

    psum = nc.alloc_psum_tensor("psum", [128, 4096], F32)

    def PS(b, n=512, o=0):
        return psum[:, b * 512 + o: b * 512 + o + n]

    def pk(b):
        return "ps%d" % b

    x = sb("x", [128, 8, T])
    hT = sb("hT", [128, 8, T], BF16)
    yg = sb("yg", [128, 2, T], BF16)
    WA = coff["_WA"][0]
    cst = sb("cst", [128, WA])
    stg = [sb("stg%d" % i, [128, 2048]) for i in range(NSTG)]
    wbf = [sb("wbf%d" % i, [128, 2048], BF16) for i in range(NWBF)]
    ident = sb("ident", [128, 128], BF16)
    ones = sb("ones", [128, 128], BF16)
    R64 = sb("R64", [128, 128], BF16)
    R32 = sb("R32", [128, 128], BF16)
    mlo = sb("mlo", [128, 128], BF16)
    mhi = sb("mhi", [128, 128], BF16)
    DTm = sb("DTm", [128, DEPTH * 4, 128], BF16)
    Xi = sb("Xi", [128, DEPTH * 4, 128], BF16)
    zeta = sb("zeta", [128, DEPTH * 2, 4])
    Gp = sb("Gp", [128, DEPTH * 4])
    lgrow = sb("lgrow", [128, 32])
    lgpair = sb("lgpair", [128, 16])
    esink = sb("esink", [128, 16])
    nlam = sb("nlam", [128, DEPTH])
    dgt = sb("dgt", [128, DEPTH, 64])
    sa8 = sb("sa8", [128, DEPTH * 4])
    sa16 = sb("sa16", [128, DEPTH * 4])
    csil = sb("csil", [128, 8, 2], BF16)
    mods = [sb("mod%d" % i, [128, 48, 2]) for i in range(2)]
    modAs = [sb("modA%d" % i, [128, 2, 8, 2]) for i in range(2)]
    MODK = ["mod0", "mod1", "modA0", "modA1"]
    c_eps = sb("c_eps", [128, 1])
    c_one = sb("c_one", [128, 1])
    bar_t = sb("bar_t", [128, 1])
    rstd = sb("rstd", [128, 512])
    rstd2 = [rstd, sb("rstd1", [128, 512])]
    sqt = [sb("sqt%d" % i, [128, 512], BF16) for i in range(2)]
    ntmp = [sb("ntmp%d" % i, [128, 512]) for i in range(2)]
    adarow = [sb("adarow%d" % i, [2, 256]) for i in range(2)]
    identf = sb("identf", [2, 2])
    setup_stack = contextlib.ExitStack()
    cstB = setup_stack.enter_context(nc.sbuf_tensor("s_cstB", [128, CW - WA], F32))

    def C(name, lo=0, n=None):
        o, w = coff[name]
        if n is None:
            n = w - lo
        if o >= WA:
            return cstB[:, o - WA + lo: o - WA + lo + n]
        return cst[:, o + lo: o + lo + n]

    rr = {"ev": 0}

    def V(fn, r, w):
        return P.add("dve", fn, r, w)

    def G(fn, r, w):
        return P.add("pool", fn, r, w)

    def A(fn, r, w):
        return P.add("act", fn, r, w)

    def M(fn, r, w):
        return P.add("pe", fn, r, w)

    def act(out, in_, func, r, w, bias=None, scale=1.0):
        kw = {}
        if bias is not None:
            kw["bias"] = bias
        return A(lambda e: e.activation(out=out, in_=in_, func=func, scale=scale, **kw), r, w)

    def tt(eng, out, a, b, op, r, w):
        return P.add(eng, lambda e: e.tensor_tensor(out=out, in0=a, in1=b, op=op), r, w)

    def ts(eng, out, a, s1, s2, op0, op1, r, w):
        if s2 is None:
            return P.add(eng, lambda e: e.tensor_scalar(out=out, in0=a, scalar1=s1, scalar2=None, op0=op0), r, w)
        return P.add(eng, lambda e: e.tensor_scalar(out=out, in0=a, scalar1=s1, scalar2=s2, op0=op0, op1=op1), r, w)

    def stt(eng, out, a, s, b, op0, op1, r, w):
        return P.add(eng, lambda e: e.scalar_tensor_tensor(out=out, in0=a, scalar=s, in1=b, op0=op0, op1=op1), r, w)

    def cp(eng, out, in_, r, w):
        if eng == "act":
            return A(lambda e: e.activation(out=out, in_=in_, func=AF.Copy), r, w)
        return P.add(eng, lambda e: e.tensor_copy(out=out, in_=in_), r, w)

    def mm(out, lhsT, rhs, start, stop, r, w):
        return M(lambda e: e.matmul(out, lhsT=lhsT, rhs=rhs, start=start, stop=stop), r, w)

    def mmk(out, lhs_fn, rhs_fn, nk, r, w):
        def f(e):
            for kc in range(nk):
                ins = e.matmul(out, lhsT=lhs_fn(kc), rhs=rhs_fn(kc), start=(kc == 0), stop=(kc == nk - 1))
            return ins
        return M(f, r, w)

    minfree = {"v": 1 << 30}

    def barrier():
        minfree["v"] = min(minfree["v"], nc.sbuf_bytes_remaining)
        P.scope = None
        G(lambda e: e.memset(bar_t[:], 0.0), [], ["ARENA", "bar_t"])
        P.scope = "ARENA"

    order = []
    for l in range(depth):
        pi = piece_ids(l)
        if l == 0:
            order += pi["ada"]
        order += pi["retF"] + pi["retT"] + [pi["retO"]] + pi["lruF"] + [pi["lruO"]] + pi["swaF"] + pi["swaT"] + \
            [pi["swaO"]] + pi["difF"] + pi["difT"] + [pi["difO"]]
        nada = 0
        nxt = piece_ids(l + 1)["ada"] if l + 1 < depth else []
        for q in range(4):
            for p_ in pi["ff"][q][0] + pi["ff"][q][1]:
                order.append(p_)
                if nada < len(nxt):
                    order.append(nxt[nada]); nada += 1
    ffn_piece = set()
    for l in range(depth):
        pi = piece_ids(l)
        for q in range(4):
            ffn_piece.update(pi["ff"][q][0] + pi["ff"][q][1])
        if l + 1 < depth:
            ffn_piece.update(piece_ids(l + 1)["ada"])
    ws = {"issued": 0, "used": 0, "stg_rr": 0, "slot": {}, "free": list(range(NWBF)), "xstg": [], "xwbf": []}
    cast_eng = ["dve", "act", "dve", "act"]

    def stg_of(i):
        return (stg[i], "stg%d" % i) if i < NSTG else (ws["xstg"][i - NSTG], "xstg%d" % (i - NSTG))

    def wbf_of(i):
        return (wbf[i], "wbf%d" % i) if i < NWBF else (ws["xwbf"][i - NWBF], "xwbf%d" % (i - NWBF))

    def w_issue():
        k = ws["issued"]
        if k >= len(order):
            return False
        pid = order[k]
        big = (pid in ffn_piece) and len(ws["xstg"]) > 0
        cand = [b for b in ws["free"] if b < NWBF or big]
        if not cand:
            return False
        b = cand[0]
        ws["free"].remove(b)
        nst = NSTG + (len(ws["xstg"]) if big else 0)
        si = ws["stg_rr"] % nst
        ws["stg_rr"] += 1
        st_, skey = stg_of(si)
        wb_, wkey = wbf_of(b)
        ws["slot"][k] = b
        nsc = not (si >= NSTG or b >= NWBF)
        P.dma(st_[:], wts_d[pid], r=[], w=[skey], noscope=nsc)
        eng = cast_eng[k % 4]
        if eng == "act":
            P.add("act", lambda e: e.activation(out=wb_[:], in_=st_[:], func=AF.Copy), [skey], [wkey], noscope=nsc)
        else:
            P.add(eng, lambda e: e.tensor_copy(out=wb_[:], in_=st_[:]), [skey], [wkey], noscope=nsc)
        ws["issued"] += 1
        return True

    def w_next(expect):
        while order[ws["used"]] != expect:
            ws["used"] += 1
        k = ws["used"]
        for kk in list(ws["slot"].keys()):
            if kk < k:
                ws["free"].append(ws["slot"].pop(kk))
        assert ws["issued"] >= k or ws["issued"] == k or True
        if ws["issued"] < k:
            ws["issued"] = k
        while ws["issued"] < len(order) and ws["issued"] < k + 8:
            if not w_issue():
                break
        assert k in ws["slot"], (k, ws)
        ws["used"] += 1
        return wbf_of(ws["slot"][k])

    def w_extra_begin(xs, xw):
        ws["xstg"] = xs
        ws["xwbf"] = xw
        ws["free"] += [NWBF + i for i in range(len(xw))]

    def w_extra_end():
        for kk, b in ws["slot"].items():
            assert b < NWBF, "extra bf16 slot still in use at phase end"
        ws["free"] = [b for b in ws["free"] if b < NWBF]
        ws["xstg"] = []
        ws["xwbf"] = []

    P.scope = "ARENA"
    P.dma(cst[:], cst_d[:, 0:WA], w=["cst"])
    P.dma(cstB[:], cst_d[:, WA:CW], w=["cst"])
    for c in range(8):
        P.dma(x[:, c, :], xT_d[c * 128:(c + 1) * 128, :], w=["x%d" % c])
    G(lambda e: e.memset(ones[:], 1.0), [], ["ones"])
    G(lambda e: e.memset(c_eps[:], EPS), [], ["c_eps"])
    G(lambda e: e.memset(c_one[:], 1.0), [], ["c_one"])
    cp("dve", ident[:], C("ident"), ["cst"], ["ident"])
    cp("dve", identf[:], C("ident")[0:2, 0:2], ["cst"], ["identf"])
    cp("dve", R64[:], C("R64T"), ["cst"], ["R64"])
    cp("dve", R32[:], C("R32T"), ["cst"], ["R32"])
    cp("dve", mlo[:], C("mlo"), ["cst"], ["mlo"])
    cp("dve", mhi[:], C("mhi"), ["cst"], ["mhi"])
    act(csil[:].rearrange("p a b -> p (a b)"), C("cT"), AF.Silu, ["cst"], ["csil"])
    for (dst, src, n, key) in ((lgrow, "rd_row", 32, "lgrow"), (lgpair, "rd_pair", 16, "lgpair")):
        act(dst[:], C(src), AF.Exp, ["cst"], [key], scale=-1.0)
        act(dst[:], dst[:], AF.Ln, [key, "c_one"], [key], bias=c_one[:, 0:1])
        ts("dve", dst[:], dst[:], -1.0, None, ALU.mult, None, [key], [key])
    etmp = setup_stack.enter_context(nc.sbuf_tensor("s_etmp", [128, 128], F32))
    etmp2 = setup_stack.enter_context(nc.sbuf_tensor("s_etmp2", [128, 128], F32))
    for l in range(depth):
        for h in range(4):
            act(etmp[:], C("Ef"), AF.Exp, ["cst", "lgrow"], ["etmp"], scale=lgrow[:, l * 8 + h: l * 8 + h + 1])
            tt("dve", etmp[:], etmp[:], C("Lf8"), ALU.mult, ["etmp", "cst"], ["etmp"])
            act(etmp2[:], C("Eb"), AF.Exp, ["cst", "lgrow"], ["etmp2"], scale=lgrow[:, l * 8 + 4 + h: l * 8 + 4 + h + 1])
            tt("dve", etmp2[:], etmp2[:], C("Lb8"), ALU.mult, ["etmp2", "cst"], ["etmp2"])
            tt("dve", DTm[:, l * 4 + h, :], etmp[:], etmp2[:], ALU.add, ["etmp", "etmp2"], ["DTm"])
        for d in range(2):
            for j in range(2):
                i = l * 4 + d * 2 + j
                act(Xi[:, i, :], C("idx1") if d == 0 else C("idxr"), AF.Exp, ["cst", "lgpair"], ["Xi"], scale=lgpair[:, i:i + 1])
                act(Gp[:, i:i + 1], lgpair[:, i:i + 1], AF.Exp, ["lgpair"], ["Gp"], scale=128.0)
            act(zeta[:, l * 2 + d, :], lgrow[:, l * 8 + d * 4: l * 8 + d * 4 + 4], AF.Exp, ["lgrow", "cst"], ["zeta"],
                scale=(C("cmf") if d == 0 else C("cmb")))
    ts("dve", zeta[:], zeta[:], 0.125, None, ALU.mult, None, ["zeta"], ["zeta"])
    act(esink[:], C("sink"), AF.Exp, ["cst"], ["esink"])
    dl1 = setup_stack.enter_context(nc.sbuf_tensor("s_dl1", [128, DEPTH, 2, 32], F32))
    dl2 = setup_stack.enter_context(nc.sbuf_tensor("s_dl2", [128, DEPTH * 2], F32))
    dlv = C("dlam").rearrange("p (l a b c) -> p l a b c", l=DEPTH, a=2, b=2)
    tt("dve", dl1[:], dlv[:, :, :, 0, :], dlv[:, :, :, 1, :], ALU.mult, ["cst"], ["dl1"])
    V(lambda e: e.tensor_reduce(out=dl2[:], in_=dl1[:].rearrange("p l a c -> p (l a) c"), axis=AX.X, op=ALU.add), ["dl1"], ["dl2"])
    act(dl2[:], dl2[:], AF.Exp, ["dl2"], ["dl2"])
    for l in range(depth):
        import math
        li = 0.8 - 0.6 * math.exp(-0.3 * l)
        stt("dve", nlam[:, l:l + 1], dl2[:, 2 * l + 1: 2 * l + 2], -li, dl2[:, 2 * l: 2 * l + 1], ALU.add, ALU.subtract, ["dl2"], ["nlam"])
        ts("dve", dgt[:, l, :], C("dng", l * 64, 64), 1.0 - li, None, ALU.mult, None, ["cst"], ["dgt"])
    act(sa8[:], C("lam"), AF.Exp, ["cst"], ["sa8"], scale=-1.0)
    act(sa8[:], sa8[:], AF.Ln, ["sa8", "c_one"], ["sa8"], bias=c_one[:, 0:1])
    ts("dve", sa16[:], sa8[:], -16.0, None, ALU.mult, None, ["sa8"], ["sa16"])
    ts("dve", sa8[:], sa8[:], -8.0, None, ALU.mult, None, ["sa8", "sa16"], ["sa8"])

    setup_stack.close()
    P.scope = None
    xk = ["x%d" % c for c in range(8)]
    hk = ["h%d" % c for c in range(8)]

    def norm(Aap, Bap, dst_fn, dst_keys, final=False):
        def stats(n):
            blk = slice(n * 512, (n + 1) * 512)
            bank = 5 if n % 2 == 0 else 2
            rs = rstd2[n % 2]
            rk_ = "rstd%d" % (n % 2)
            for c in range(8):
                s = sqt[c % 2]
                act(s[:], x[:, c, blk], AF.Square, [xk[c]], ["sqt%d" % (c % 2)])
                mm(PS(bank), ones[:], s[:], c == 0, c == 7, ["ones", "sqt%d" % (c % 2)], [pk(bank)])
            act(rs[:], PS(bank), AF.Sqrt, [pk(bank), "c_eps"], [rk_], bias=c_eps[:, 0:1], scale=1.0 / 1024.0)
            V(lambda e: e.reciprocal(out=rs[:], in_=rs[:]), [rk_], [rk_])

        def apply(n):
            v = 0 if n < 2 else 1
            blk = slice(n * 512, (n + 1) * 512)
            rs = rstd2[n % 2]
            rk_ = "rstd%d" % (n % 2)
            for c in range(8):
                tmp = ntmp[c % 2]
                tt("dve", tmp[:], x[:, c, blk], rs[:], ALU.mult, [xk[c], rk_], ["ntmp%d" % (c % 2)])
                o = dst_fn(c, n)
                if c % 2 == 0:
                    A(lambda e, o=o, tmp=tmp, sa=Aap(c, v), ba=Bap(c, v): e.activation(out=o, in_=tmp[:], func=AF.Identity, scale=sa, bias=ba),
                      ["ntmp%d" % (c % 2)] + MODK, [dst_keys[c]])
                else:
                    ts("pool", o, tmp[:], Aap(c, v), Bap(c, v), ALU.mult, ALU.add, ["ntmp%d" % (c % 2)] + MODK, [dst_keys[c]])
        stats(0)
        stats(1)
        apply(0)
        stats(2)
        apply(1)
        apply(2)

    dstate = {"b": 0}

    def dbank():
        dstate["b"] ^= 1
        return 3 + dstate["b"]

    def proj_F(pid, src, srck, nchunks, evac):
        wv, wkey = w_next(pid)
        w3 = wv[:].rearrange("p (k c) -> p k c", k=8)
        for mi in range(nchunks):
            for n in range(3):
                b = dbank()
                mmk(PS(b), lambda kc, mi=mi: w3[:, kc, mi * 128:(mi + 1) * 128],
                    lambda kc, n=n: src[:, kc, n * 512:(n + 1) * 512], 8, [wkey] + srck, [pk(b)])
                evac(mi, n, b)

    def proj_T(pid, tiles, evac, ncols=256):
        wv, wkey = w_next(pid)
        w3 = wv[:].rearrange("p (k c) -> p k c", k=8)
        for t in tiles:
            b = dbank()
            mmk(PS(b, ncols), lambda kc, t=t: hT[:, kc, t * 128:(t + 1) * 128], lambda kc: w3[:, kc, 0:ncols], 8,
                [wkey] + hk, [pk(b)])
            evac(t, b)

    def out_proj(pid, G1):
        wv, wkey = w_next(pid)
        w3 = wv[:].rearrange("p (k c) -> p k c", k=2)
        for n in range(3):
            for m in range(8):
                v = 0 if n < 2 else 1
                b = dbank()
                mmk(PS(b), lambda kc, m=m: w3[:, kc, m * 128:(m + 1) * 128],
                    lambda kc, n=n: yg[:, kc, n * 512:(n + 1) * 512], 2, [wkey, "yg"], [pk(b)])
                xs = x[:, m, n * 512:(n + 1) * 512]
                stt("dve", xs, PS(b), G1(m, v), xs, ALU.mult, ALU.add, [pk(b), xk[m]] + MODK, [xk[m]])

    def transpose_part(ytok, ykey, t2s, fcs):
        for t2 in t2s:
            for fc in fcs:
                for i in range(2):
                    t = t2 * 2 + i
                    mm(PS(5, 128, (fc * 2 + i) * 128), ytok[:, t, fc * 128:(fc + 1) * 128], ident[:], True, True,
                       [ykey, "ident"], [pk(5)])
            for fc in fcs:
                cp("act" if fc == 0 else "dve", yg[:, fc, t2 * 256:(t2 + 1) * 256], PS(5, 256, fc * 256), [pk(5)], ["yg"])

    def transpose_y(ytok, ykey):
        transpose_part(ytok, ykey, range(6), [0, 1])

    def rope_load(rope, half):
        P.dma(stg[0][:], rope_d[:, half * 2048:(half + 1) * 2048], w=["stg0"], noscope=True)
        cp("dve", rope[:], stg[0][:], ["stg0"], ["rope"])

    def rope_apply(z, zkey, R, Rk, rope):
        tab0 = 0
        for n in range(2):
            blk = slice(n * 512, (n + 1) * 512)
            b = dbank()
            mm(PS(b), R[:], z[:, blk], True, True, [Rk, zkey], [pk(b)])
            t1 = ntmp[0]
            t2 = ntmp[1]
            tt("dve", t1[:], z[:, blk], rope[:, tab0 + n * 512: tab0 + (n + 1) * 512], ALU.mult, [zkey, "rope"], ["ntmp0"])
            tt("dve", t2[:], PS(b), rope[:, tab0 + 1024 + n * 512: tab0 + 1024 + (n + 1) * 512], ALU.mult, [pk(b), "rope"], ["ntmp1"])
            tt("pool", z[:, blk], t1[:], t2[:], ALU.add, ["ntmp0", "ntmp1"], [zkey])

    astate = {"i": 0}

    class AttnPipe:
        def __init__(self, pTs, pkey):
            self.pTs = pTs; self.pkey = pkey; self.n = 0; self.pending = None

        def push(self, qT, qkey, kblocks, scale, post, nq=1):
            i = self.n; self.n += 1
            base = 0 if (i % 2 == 0) else 3
            pT = self.pTs[i % 2]; pkey = self.pkey + "%d_" % (i % 2); ob = 6 + (i % 2)
            nk = len(kblocks)
            w = nq * 128
            assert nk * w <= 1536 and (512 % w == 0)
            banks = sorted(set(base + (j * w) // 512 for j in range(nk)))
            for j, (kT, kkey, vap, vkey, mask) in enumerate(kblocks):
                o = psum[:, base * 512 + j * w: base * 512 + (j + 1) * w]
                bk = pk(base + (j * w) // 512)
                mm(o, kT, qT, True, mask is None, [kkey, qkey], [bk])
                if mask is not None:
                    assert nq == 1
                    mm(o, ident[:], mask[0][:], False, True, ["ident", mask[1]], [bk])
            tot = nk * w
            c1 = min(tot, 512)
            act(pT[:, 0:c1], psum[:, base * 512: base * 512 + c1], AF.Exp, [pk(banks[0])], [pkey + "0"], scale=scale)
            if tot > 512:
                act(pT[:, 512:tot], psum[:, base * 512 + 512: base * 512 + tot], AF.Exp, [pk(b) for b in banks[1:]], [pkey + "1"], scale=scale)
            prev = self.pending
            self.pending = (kblocks, pT, pkey, ob, post, nq)
            if prev is not None:
                self._pv(prev)

        def _pv(self, call):
            kblocks, pT, pkey, ob, post, nq = call
            nk = len(kblocks)
            w = nq * 128
            for qi in range(nq):
                for j, (kT, kkey, vap, vkey, mask) in enumerate(kblocks):
                    c = j * w + qi * 128
                    mm(PS(ob, 65, qi * 128), pT[:, c:c + 128], vap, j == 0, j == nk - 1, [pkey + str(min(c // 512, 1)), vkey], [pk(ob)])
            post(ob)

        def flush(self):
            if self.pending is not None:
                self._pv(self.pending)
                self.pending = None

    def ada_piece(l, j):
        wv, wkey = w_next(piece_ids(l)["ada"][j])
        w3 = wv[:].rearrange("p (k c) -> p k c", k=8)
        o = psum[0:2, 2 * 512 + (j % 2) * 256: 2 * 512 + (j % 2) * 256 + 256]
        ar_ = adarow[j % 2]
        mmk(o, lambda kc: csil[:, kc, :], lambda kc, w3=w3: w3[:, kc, 0:256], 8, [wkey, "csil"], [pk(2)])
        cp("act" if j % 2 == 0 else "dve", ar_[:], o, [pk(2)], ["adarow%d" % (j % 2)])
        for mi in range(2):
            mc = j * 2 + mi
            mm(PS(5, 2, mc * 2), ar_[:, mi * 128:(mi + 1) * 128], identf[:], True, True, ["adarow%d" % (j % 2), "identf"], [pk(5)])

    def ada_finish(l):
        mod = mods[l % 2]
        modA = modAs[l % 2]
        badd = C("b_ada", l * 48, 48)
        tt("dve", mod[:], PS(5, 96).rearrange("p (a b) -> p a b", b=2), badd.unsqueeze(2).to_broadcast([128, 48, 2]), ALU.add,
           [pk(5), "cst"], ["mod%d" % (l % 2)])
        for w_, (gname, sci) in enumerate((("g_mix", 1), ("g_mlp", 4))):
            gv = C(gname, l * 8, 8).unsqueeze(2).to_broadcast([128, 8, 2])
            stt("dve", modA[:, w_, :, :], mod[:, sci * 8:(sci + 1) * 8, :], 1.0, gv, ALU.add, ALU.mult,
                ["mod%d" % (l % 2), "cst"], ["modA%d" % (l % 2)])

    for l in range(depth):
        pi = piece_ids(l)
        P.scope = None
        mod = mods[l % 2]
        modA = modAs[l % 2]
        if l == 0 and ST('ada'):
            for j in range(24):
                ada_piece(0, j)
            ada_finish(0)
        A1 = lambda c, v: modA[:, 0, c, v:v + 1]
        B1 = lambda c, v: mod[:, 0 * 8 + c, v:v + 1]
        G1 = lambda c, v: mod[:, 2 * 8 + c, v:v + 1]
        A2 = lambda c, v: modA[:, 1, c, v:v + 1]
        B2 = lambda c, v: mod[:, 3 * 8 + c, v:v + 1]
        G2 = lambda c, v: mod[:, 5 * 8 + c, v:v + 1]
        if ST('norm'):
            norm(A1, B1, lambda c, n: hT[:, c, n * 512:(n + 1) * 512], hk)
        if debug and l == 0:
            P.dma(dbg_o["d_hT"], hT[:].rearrange("p a b -> p (a b)"), r=hk)
            P.dma(dbg_o["d_mod"], mod[:].rearrange("p a b -> p (a b)"), r=["mod"])

        barrier()
        with contextlib.ExitStack() as st:
          if ST('ret'):
              def sba(name, shape, dt=F32):
                  rr["ev"] += 1
                  return st.enter_context(nc.sbuf_tensor("a%d_" % rr["ev"] + name, list(shape), dt))
              rq = sba("rq", [128, 2, T], BF16)
              rkm = sba("rkm", [128, 4, T], BF16)
              rkT = sba("rkT", [128, NT, 256], BF16)
              rvT = sba("rvT", [128, NT, 256], BF16)
              rgT = sba("rgT", [128, NT, 256], BF16)
              ytok = sba("ytok", [128, NT, 256], BF16)
              Sbd = sba("Sbd", [128, 4, 128])
              Sbf = sba("Sbf", [128, 4, 8, 128], BF16)
              Ob = sba("Ob", [128, 2, 256])
              Oq = sba("Oq", [128, 2, 256])
              st4 = sba("st4", [128, 6, 8])
              qx8 = [sba("qx%d" % i, [128, 128], BF16) for i in range(8)]
              ktx = [sba("ktx%d" % i, [128, 256], BF16) for i in range(2)]
              dsb4 = [sba("dsb%d" % i, [128, 128]) for i in range(4)]
              scT8 = [sba("scT%d" % i, [128, 128], BF16) for i in range(8)]

              def ev_rq(mi, n, b):
                  cp("act", rq[:, mi, n * 512:(n + 1) * 512], PS(b), [pk(b)], ["rq"])
              proj_F(pi["retF"][0], hT, hk, 2, ev_rq)
              for pp in range(2):
                  def ev_rk(mi, n, b, pp=pp):
                      cp("act" if mi == 0 else "dve", rkm[:, pp * 2 + mi, n * 512:(n + 1) * 512], PS(b), [pk(b)], ["rkm"])
                  proj_F(pi["retF"][1 + pp], hT, hk, 2, ev_rk)
              for (dst, key, pid) in ((rkT, "rkT", pi["retT"][0]), (rvT, "rvT", pi["retT"][1]), (rgT, "rgT", pi["retT"][2])):
                  def ev_t(t, b, dst=dst, key=key):
                      cp("act" if t % 2 == 0 else "dve", dst[:, t, :], PS(b, 256), [pk(b)], [key])
                  proj_T(pid, range(NT), ev_t)
              G(lambda e: e.memset(Sbf[:], 0.0), [], ["Sbf"])
              act(rgT[:].rearrange("p a b -> p (a b)"), rgT[:].rearrange("p a b -> p (a b)"), AF.Silu, ["rgT"], ["rgT"])
              tt("pool", rgT[:], rgT[:], C("gng", l * 256, 256).unsqueeze(1).to_broadcast([128, NT, 256]), ALU.mult, ["rgT", "cst"], ["rgT"])
              segs = [(0, 8, True, 0), (8, 2, False, 0), (10, 2, False, 1)]
              for (t0, ncnk, samp, sj) in segs:
                  if samp:
                      P.dma(Sbd[:], sret_d[:, l, :, :], w=["Sbd0", "Sbd1", "Sbd2", "Sbd3"])
                  else:
                      G(lambda e: e.memset(Sbd[:], 0.0), [], ["Sbd0", "Sbd1", "Sbd2", "Sbd3"])
                  sbank = [0, 1, 2, 6]
                  for step in range(ncnk):
                      for d in range(2):
                          i = step if d == 0 else ncnk - 1 - step
                          t = t0 + i
                          kt = ktx[d]
                          tt("dve", kt[:].rearrange("p (h e) -> p h e", h=4), rkT[:, t, :].rearrange("p (h e) -> p h e", h=4),
                             zeta[:, l * 2 + d, :].unsqueeze(2).to_broadcast([128, 4, 64]), ALU.mult, ["rkT", "zeta"], ["ktx%d" % d])
                          for j in range(2):
                              si = d * 2 + j
                              sk_ = "Sbd%d" % si
                              for hb in range(2):
                                  A(lambda e, si=si, i=i, hb=hb: e.activation(out=Sbf[hb * 64:(hb + 1) * 64, si, i, hb * 64:(hb + 1) * 64],
                                    in_=Sbd[hb * 64:(hb + 1) * 64, si, hb * 64:(hb + 1) * 64], func=AF.Copy), [sk_], ["Sbf"])
                              mm(PS(sbank[si], 128), kt[:, j * 128:(j + 1) * 128], rvT[:, t, j * 128:(j + 1) * 128], True, True,
                                 ["ktx%d" % d, "rvT"], [pk(sbank[si])])
                              stt("dve", Sbd[:, si, :], Sbd[:, si, :], Gp[:, l * 4 + si: l * 4 + si + 1], PS(sbank[si], 128), ALU.mult, ALU.add,
                                  [sk_, "Gp", pk(sbank[si])], [sk_])
                  if not samp:
                      P.dma(osret_o[:, sj, l, :, :], Sbd[:], r=["Sbd0", "Sbd1", "Sbd2", "Sbd3"])
                  def stage_a(t):
                      cols = slice(t * 128, (t + 1) * 128)
                      par = t % 2
                      for j in range(2):
                          for d in range(2):
                              q_ = qx8[par * 4 + d * 2 + j]
                              tt("dve" if d == 0 else "pool", q_[:], rq[:, j, cols], Xi[:, l * 4 + d * 2 + j, :], ALU.mult,
                                 ["rq", "Xi"], ["qx%d" % (par * 4 + d * 2 + j)])
                      for h in range(4):
                          mm(PS(par, 128, h * 128), rkm[:, h, cols], rq[:, h // 2, cols], True, True, ["rkm", "rq"], [pk(par)])
                      for h in range(4):
                          tt("dve", scT8[par * 4 + h][:], PS(par, 128, h * 128), DTm[:, l * 4 + h, :], ALU.mult, [pk(par), "DTm"],
                             ["scT%d" % (par * 4 + h)])
                  stage_a(t0)
                  for i in range(ncnk):
                      t = t0 + i
                      if i + 1 < ncnk:
                          stage_a(t + 1)
                      par = t % 2
                      ob = 6 + (t % 2)
                      for j in range(2):
                          mm(PS(ob, 128, j * 128), qx8[par * 4 + j][:], Sbf[:, j, i, :], True, False, ["qx%d" % (par * 4 + j), "Sbf"], [pk(ob)])
                          mm(PS(ob, 128, j * 128), qx8[par * 4 + 2 + j][:], Sbf[:, 2 + j, i, :], False, False,
                             ["qx%d" % (par * 4 + 2 + j), "Sbf"], [pk(ob)])
                          for hh in range(2):
                              h = j * 2 + hh
                              mm(PS(ob, 64, h * 64), scT8[par * 4 + h][:], rvT[:, t, h * 64:(h + 1) * 64], False, hh == 1,
                                 ["scT%d" % (par * 4 + h), "rvT"], [pk(ob)])
                      cp("act", Ob[:, t % 2, :], PS(ob, 256), [pk(ob)], ["Ob"])
                      if t % 2 == 1:
                          tb = t - 1
                          O3 = Ob[:].rearrange("p t (h e) -> p (t h) e", h=4)
                          Q3 = Oq[:].rearrange("p t (h e) -> p (t h) e", h=4)
                          s1 = st4[:, 0, :]; s2 = st4[:, 1, :]; mean = st4[:, 2, :]; var = st4[:, 3, :]; rs = st4[:, 4, :]
                          V(lambda e: e.tensor_reduce(out=s1, in_=O3, axis=AX.X, op=ALU.add), ["Ob"], ["st4a"])
                          tt("pool", Oq[:], Ob[:], Ob[:], ALU.mult, ["Ob"], ["Oq"])
                          V(lambda e: e.tensor_reduce(out=s2, in_=Q3, axis=AX.X, op=ALU.add), ["Oq"], ["st4b"])
                          ts("dve", mean, s1, 1.0 / 64, None, ALU.mult, None, ["st4a"], ["st4c"])
                          tt("dve", var, mean, mean, ALU.mult, ["st4c"], ["st4d"])
                          stt("dve", var, s2, 1.0 / 64, var, ALU.mult, ALU.subtract, ["st4b", "st4d"], ["st4d"])
                          act(rs, var, AF.Sqrt, ["st4d", "c_eps"], ["st4e"], bias=c_eps[:, 0:1])
                          V(lambda e: e.reciprocal(out=rs, in_=rs), ["st4e"], ["st4e"])
                          tt("dve", Q3, O3, mean.unsqueeze(2).to_broadcast([128, 8, 64]), ALU.subtract, ["Ob", "st4c", "Oq"], ["Oq"])
                          tt("dve", Q3, Q3, rs.unsqueeze(2).to_broadcast([128, 8, 64]), ALU.mult, ["Oq", "st4e"], ["Oq"])
                          tt("pool", ytok[:, tb:tb + 2, :], Oq[:], rgT[:, tb:tb + 2, :], ALU.mult, ["Oq", "rgT"], ["ytok"])
                          transpose_part(ytok, "ytok", [tb // 2], [0, 1])
              (P.dma(dbg_o["d_y_ret"], yg[:].rearrange("p a b -> p (a b)"), r=["yg"]) if (debug and l == 0) else None)
              out_proj(pi["retO"], G1)

        barrier()
        with contextlib.ExitStack() as st:
          if ST('lru'):
              def sba(name, shape, dt=F32):
                  rr["ev"] += 1
                  return st.enter_context(nc.sbuf_tensor("a%d_" % rr["ev"] + name, list(shape), dt))
              lx = sba("lx", [128, 2, T], BF16)
              lgt = sba("lgt", [128, 2, T], BF16)
              BUF = {}
              for sfx, Lb in (("", 1024), ("P", 512)):
                  BUF[sfx] = dict(xc=sba("xc" + sfx, [128, Lb]), xcb=sba("xcb" + sfx, [128, Lb], BF16),
                                  rr=[sba("rr%d%s" % (i, sfx), [128, Lb]) for i in range(2)],
                                  ii=[sba("ii%d%s" % (i, sfx), [128, Lb]) for i in range(2)],
                                  uu=[sba("uu%d%s" % (i, sfx), [128, Lb]) for i in range(2)])
              slo2 = [sba("slo%d" % i, [128, 2 * 4]) for i in range(2)]
              lruwb = sba("lruwb", [128, 8, 128], BF16)
              for (dst, key, pid) in ((lx, "lx", pi["lruF"][0]), (lgt, "lgt", pi["lruF"][1])):
                  def ev(mi, n, b, dst=dst, key=key):
                      cp("act" if n % 2 == 0 else "dve", dst[:, mi, n * 512:(n + 1) * 512], PS(b), [pk(b)], [key])
                  proj_F(pid, hT, hk, 2, ev)
              P.dma(stg[0][:, 0:1024], lruw_d[:, l * 1024:(l + 1) * 1024], w=["stg0"], noscope=True)
              cp("dve", lruwb[:].rearrange("p a b -> p (a b)"), stg[0][:, 0:1024], ["stg0"], ["lruwb"])
              for cc in range(2):
                  act(lgt[:, cc, :], lgt[:, cc, :], AF.Gelu_apprx_tanh, ["lgt"], ["lgt"])
              for cc in range(2):
                for (c0, L, samp, nseq) in ((0, 1024, True, 1), (1024, 512, False, 2)):
                  sfx = "" if samp else "P"
                  B_ = BUF[sfx]
                  xc = B_["xc"]; xcb = B_["xcb"]; rr2 = B_["rr"]; ii2 = B_["ii"]; uu2 = B_["uu"]
                  kx = "xc" + sfx; kxb = "xcb" + sfx
                  kr = ["rr0" + sfx, "rr1" + sfx]; ki_ = ["ii0" + sfx, "ii1" + sfx]; ku = ["uu0" + sfx, "uu1" + sfx]
                  Ls = L // nseq
                  if True:
                      xi_ = lx[:, cc, c0:c0 + L]
                      cw = lambda k: C("conv_w", l * 8 + k * 2 + cc, 1)
                      ts("dve", xc[:, 0:L], xi_, cw(2), C("conv_b", l * 2 + cc, 1), ALU.mult, ALU.add, ["lx", "cst"], [kx])
                      for sq_ in range(nseq):
                          b0 = sq_ * Ls
                          stt("dve", xc[:, b0 + 2:b0 + Ls], xi_[:, b0:b0 + Ls - 2], cw(0), xc[:, b0 + 2:b0 + Ls], ALU.mult, ALU.add, ["lx", "cst", kx], [kx])
                          stt("dve", xc[:, b0 + 1:b0 + Ls], xi_[:, b0:b0 + Ls - 1], cw(1), xc[:, b0 + 1:b0 + Ls], ALU.mult, ALU.add, ["lx", "cst", kx], [kx])
                          stt("dve", xc[:, b0:b0 + Ls - 1], xi_[:, b0 + 1:b0 + Ls], cw(3), xc[:, b0:b0 + Ls - 1], ALU.mult, ALU.add, ["lx", "cst", kx], [kx])
                      cp("pool", xcb[:, 0:L], xc[:, 0:L], [kx], [kxb])
                      pidx_ = [l * 4 + d * 2 + cc for d in range(2)]
                      for d in range(2):
                          for (dst, key, gsel, bname) in ((rr2[d], kr[d], 0, "b_a"), (ii2[d], ki_[d], 1, "b_x")):
                              for n in range((L + 511) // 512):
                                  w_ = min(512, L - n * 512)
                                  b = dbank()
                                  mm(PS(b, w_), lruwb[:, d * 4 + gsel * 2 + cc, :], xcb[:, n * 512:n * 512 + w_], True, True,
                                     ["lruwb", kxb], [pk(b)])
                                  act(dst[:, n * 512:n * 512 + w_], PS(b, w_), AF.Sigmoid, [pk(b), "cst"], [key], bias=C(bname, pidx_[d], 1))
                      for d in range(2):
                          tt("pool" if d == 1 else "dve", ii2[d][:, 0:L], ii2[d][:, 0:L], xc[:, 0:L], ALU.mult, [ki_[d], kx], [ki_[d]])
                      for d in range(2):
                          act(uu2[d][:, 0:L], rr2[d][:, 0:L], AF.Exp, [kr[d], "sa16"], [ku[d]], scale=sa16[:, pidx_[d]:pidx_[d] + 1])
                          act(rr2[d][:, 0:L], rr2[d][:, 0:L], AF.Exp, [kr[d], "sa8"], [kr[d]], scale=sa8[:, pidx_[d]:pidx_[d] + 1])
                      for d in range(2):
                          act(uu2[d][:, 0:L], uu2[d][:, 0:L], AF.Sqrt, [ku[d], "c_one"], [ku[d]], bias=c_one[:, 0:1], scale=-1.0)
                      for d in range(2):
                          tt("dve", uu2[d][:, 0:L], uu2[d][:, 0:L], ii2[d][:, 0:L], ALU.mult, [ku[d], ki_[d]], [ku[d]])
                      for sq_ in range(nseq):
                          b0 = sq_ * Ls
                          for d in range(2):
                              hd = ii2[d]
                              init = C("h0", pidx_[d], 1) if samp else 0.0
                              if d == 0:
                                  V(lambda e, hd=hd, init=init, b0=b0, Ls=Ls, rr2=rr2, uu2=uu2: e.tensor_tensor_scan(out=hd[:, b0:b0 + Ls],
                                    data0=rr2[0][:, b0:b0 + Ls], data1=uu2[0][:, b0:b0 + Ls], initial=init, op0=ALU.mult, op1=ALU.add),
                                    [kr[0], ku[0], "cst"], [ki_[0]])
                                  fin = hd[:, b0 + Ls - 1:b0 + Ls]
                              else:
                                  V(lambda e, hd=hd, init=init, b0=b0, Ls=Ls, rr2=rr2, uu2=uu2: e.tensor_tensor_scan(out=hd[:, b0:b0 + Ls][:, ::-1],
                                    data0=rr2[1][:, b0:b0 + Ls][:, ::-1], data1=uu2[1][:, b0:b0 + Ls][:, ::-1],
                                    initial=init, op0=ALU.mult, op1=ALU.add), [kr[1], ku[1], "cst"], [ki_[1]])
                                  fin = hd[:, b0:b0 + 1]
                              if not samp:
                                  cp("pool", slo2[sq_][:, d * 2 + cc: d * 2 + cc + 1], fin, [ki_[d]], ["slo%d" % sq_])
                      tt("dve", ii2[0][:, 0:L], ii2[0][:, 0:L], ii2[1][:, 0:L], ALU.add, [ki_[0], ki_[1]], [ki_[0]])
                      tt("dve", yg[:, cc, c0:c0 + L], ii2[0][:, 0:L], lgt[:, cc, c0:c0 + L], ALU.mult, [ki_[0], "lgt"], ["yg"])
              for sj in range(2):
                  P.dma(oslru_o[:, (sj * DEPTH + l) * 4:(sj * DEPTH + l) * 4 + 4], slo2[sj][:, 0:4], r=["slo%d" % sj])
              (P.dma(dbg_o["d_y_lru"], yg[:].rearrange("p a b -> p (a b)"), r=["yg"]) if (debug and l == 0) else None)
              out_proj(pi["lruO"], G1)

        barrier()
        with contextlib.ExitStack() as st:
          if ST('swa'):
              def sba(name, shape, dt=F32):
                  rr["ev"] += 1
                  return st.enter_context(nc.sbuf_tensor("a%d_" % rr["ev"] + name, list(shape), dt))
              sq = sba("sq", [128, 2, T], BF16)
              sk = sba("sk", [128, T], BF16)
              kms = [sba("kms%d" % i, [128, 256 + T], BF16) for i in range(2)]
              svaug = sba("svaug", [128, NT, 2, 80], BF16)
              vctx = sba("vctx", [128, 2, 2, 80], BF16)
              ckf = sba("ckf", [128, 256])
              cvf = sba("cvf", [128, 2, 128])
              pT = [sba("pT%d" % i, [128, 640], BF16) for i in range(2)]
              Osw = [sba("Osw", [128, NT, 65])] * 2
              den = sba("den", [128, NT])
              ytok = sba("ytok", [128, NT, 256], BF16)
              kvst = [sba("kvst%d" % i, [128, 256]) for i in range(2)]
              cp("dve", svaug[:, :, :, 64:66], ones[:, 0:NT * 4].rearrange("p (a b c) -> p a b c", a=NT, b=2), ["ones"], ["svaug"])
              cp("dve", vctx[:, :, :, 64:66], ones[:, 0:8].rearrange("p (a b c) -> p a b c", a=2, b=2), ["ones"], ["vctx"])

              def ev_sq(mi, n, b):
                  cp("act", sq[:, mi, n * 512:(n + 1) * 512], PS(b), [pk(b)], ["sq%d" % mi])
              proj_F(pi["swaF"][0], hT, hk, 2, ev_sq)

              def ev_sk(mi, n, b):
                  cp("act", sk[:, n * 512:(n + 1) * 512], PS(b), [pk(b)], ["sk"])
              proj_F(pi["swaF"][1], hT, hk, 1, ev_sk)

              def ev_swt(t, b):
                  for hv in range(2):
                      cp("act", svaug[:, t, hv, 0:64], PS(b, 64, 128 + hv * 64), [pk(b)], ["svaug"])
                  if t >= 8:
                      kb = kvst[t % 2]
                      cp("dve", kb[:], PS(b, 256), [pk(b)], ["kvst%d" % (t % 2)])
                      P.dma(okv_o[t - 8, :, l, 0:256], kb[:], r=["kvst%d" % (t % 2)])
              proj_T(pi["swaT"][0], range(NT), ev_swt)
              rope = sba("rope", [128, 2048], BF16)
              if ST('swa_rope'):
                  rope_load(rope, 0)
                  rope_apply(sq[:, 0, :], "sq0", R64, "R64", rope)
                  rope_apply(sq[:, 1, :], "sq1", R64, "R64", rope)
                  rope_apply(sk[:], "sk", R64, "R64", rope)
              P.dma(ckf[:], cks_d[:, l, :], w=["ckf"])
              P.dma(cvf[:], cvs_d[:, l, :, :], w=["cvf"])
              for cb_ in range(2):
                  for hv in range(2):
                      cp("dve", vctx[:, cb_, hv, 0:64], cvf[:, cb_, hv * 64:(hv + 1) * 64], ["cvf"], ["vctx"])
              for m in range(2):
                  ts("dve", kms[m][:, 0:256], ckf[:], C("hm", m, 1), None, ALU.mult, None, ["ckf", "cst"], ["kms%d" % m])
                  ts("dve", kms[m][:, 256:256 + T], sk[:], C("hm", m, 1), None, ALU.mult, None, ["sk", "cst"], ["kms%d" % m])
              if debug and l == 0:
                  P.dma(dbg_o["d_sq"], sq[:].rearrange("p a b -> p (a b)"), r=["sq0", "sq1"])
                  P.dma(dbg_o["d_kms0"], kms[0][:], r=["kms0"])
                  P.dma(dbg_o["d_kms1"], kms[1][:], r=["kms1"])
                  P.dma(dbg_o["d_sva"], svaug[:].rearrange("p a b c -> p (a b c)"), r=["svaug"])
                  P.dma(dbg_o["d_vctx"], vctx[:].rearrange("p a b c -> p (a b c)"), r=["vctx"])
              apipe = AttnPipe(pT, "pT")
              for h in (range(4) if ST('swa_attn') else []):
                  qc = 0 if h in (0, 3) else 1
                  kvh = h // 2
                  for t in list(range(8)) + [8, 10]:
                      nq = 1 if t < 8 else 2
                      qT = sq[:, qc, t * 128:(t + nq) * 128]
                      kb = []
                      if t < 8:
                          for cb in range(2):
                              kb.append((kms[kvh][:, cb * 128:(cb + 1) * 128], "kms%d" % kvh, vctx[:, cb, kvh, 0:65], "vctx", None))
                          for dt_ in (-1, 0, 1):
                              tk = t + dt_
                              if tk < 0 or tk > 7:
                                  continue
                              mask = None if dt_ == 0 else ((mlo, "mlo") if dt_ == -1 else (mhi, "mhi"))
                              kb.append((kms[kvh][:, 256 + tk * 128: 256 + (tk + 1) * 128], "kms%d" % kvh, svaug[:, tk, kvh, 0:65], "svaug", mask))
                      else:
                          for tk in (t, t + 1):
                              kb.append((kms[kvh][:, 256 + tk * 128: 256 + (tk + 1) * 128], "kms%d" % kvh, svaug[:, tk, kvh, 0:65], "svaug", None))

                      def post(ob, t=t, h=h, nq=nq):
                          Oh = Osw[h % 2]
                          for qi in range(nq):
                              cp("dve", Oh[:, t + qi, :], PS(ob, 65, qi * 128), [pk(ob)], ["Osw"])
                          if t + nq == NT:
                              ts("dve", den[:], Oh[:, :, 64], esink[:, l * 4 + h: l * 4 + h + 1], None, ALU.add, None,
                                 ["Osw", "esink"], ["den"])
                              V(lambda e: e.reciprocal(out=den[:], in_=den[:]), ["den"], ["den"])
                              tt("dve", ytok[:, :, h * 64:(h + 1) * 64], Oh[:, :, 0:64], den[:].unsqueeze(2).to_broadcast([128, NT, 64]),
                                 ALU.mult, ["Osw", "den"], ["ytok"])
                              if h % 2 == 1:
                                  transpose_part(ytok, "ytok", range(6), [h // 2])
                      apipe.push(qT, "sq%d" % qc, kb, 0.125, post, nq=nq)
              apipe.flush()
              (P.dma(dbg_o["d_y_swa"], yg[:].rearrange("p a b -> p (a b)"), r=["yg"]) if (debug and l == 0) else None)
              out_proj(pi["swaO"], G1)

        barrier()
        with contextlib.ExitStack() as st:
          if ST('diff'):
              def sba(name, shape, dt=F32):
                  rr["ev"] += 1
                  return st.enter_context(nc.sbuf_tensor("a%d_" % rr["ev"] + name, list(shape), dt))
              dq = sba("dq", [128, 2, T], BF16)
              dk = sba("dk", [128, 2, T], BF16)
              kmd = [sba("kmd%d" % i, [128, 256 + T], BF16) for i in range(2)]
              dvaug = sba("dvaug", [128, NT, 4, 80], BF16)
              vctx = sba("vctxd", [128, 2, 4, 80], BF16)
              ckf = sba("ckfd", [128, 2, 256])
              cvf = sba("cvfd", [128, 2, 256])
              pT = [sba("pTd%d" % i, [128, 1280], BF16) for i in range(2)]
              Oall = [sba("Oall", [128, 2, NT, 65])] * 2
              rec = sba("rec", [128, 2, NT])
              y0 = sba("y0", [128, NT, 64])
              y1 = sba("y1", [128, NT, 64])
              ss = sba("ss", [128, NT])
              ytok = sba("ytokd", [128, NT, 256], BF16)
              kvst = [sba("kvstd%d" % i, [128, 256]) for i in range(2)]
              cp("dve", dvaug[:, :, :, 64:66], ones[:, 0:NT * 8].rearrange("p (a b c) -> p a b c", a=NT, b=4), ["ones"], ["dvaug"])
              cp("dve", vctx[:, :, :, 64:66], ones[:, 0:16].rearrange("p (a b c) -> p a b c", a=2, b=4), ["ones"], ["vctxd"])
              for (dst, key, pid) in ((dq, "dq", pi["difF"][0]), (dk, "dk", pi["difF"][1])):
                  def ev(mi, n, b, dst=dst, key=key):
                      cp("act", dst[:, mi, n * 512:(n + 1) * 512], PS(b), [pk(b)], [key + str(mi)])
                  proj_F(pid, hT, hk, 2, ev)

              def ev_dkt(t, b):
                  kb = kvst[t % 2]
                  cp("dve", kb[:], PS(b, 256), [pk(b)], ["kvstd%d" % (t % 2)])
                  P.dma(okv_o[t - 8, :, l, 256:512], kb[:], r=["kvstd%d" % (t % 2)])
              proj_T(pi["difT"][0], range(8, NT), ev_dkt)

              def ev_dvt(t, b):
                  for hv in range(4):
                      cp("act", dvaug[:, t, hv, 0:64], PS(b, 64, hv * 64), [pk(b)], ["dvaug"])
                  if t >= 8:
                      kb = kvst[t % 2]
                      cp("dve", kb[:], PS(b, 256), [pk(b)], ["kvstd%d" % (t % 2)])
                      P.dma(okv_o[t - 8, :, l, 512:768], kb[:], r=["kvstd%d" % (t % 2)])
              proj_T(pi["difT"][1], range(NT), ev_dvt)
              rope = sba("roped", [128, 2048], BF16)
              rope_load(rope, 1)
              for cc in range(2):
                  rope_apply(dq[:, cc, :], "dq%d" % cc, R32, "R32", rope)
                  rope_apply(dk[:, cc, :], "dk%d" % cc, R32, "R32", rope)
              P.dma(ckf[:], ckd_d[:, l, :, :], w=["ckfd"])
              P.dma(cvf[:], cvd_d[:, l, :, :], w=["cvfd"])
              for cb_ in range(2):
                  for hv in range(4):
                      cp("dve", vctx[:, cb_, hv, 0:64], cvf[:, cb_, hv * 64:(hv + 1) * 64], ["cvfd"], ["vctxd"])
              ki = 0
              apipe = AttnPipe(pT, "pTd")
              for h in range(4):
                  cc = h // 2
                  hh = h % 2
                  for comp in range(2):
                      km = kmd[ki % 2]
                      kk = "kmd%d" % (ki % 2)
                      ki += 1
                      mcol = C("cm4", hh * 2 + comp, 1)
                      ts("dve", km[:, 0:256], ckf[:, cc, :], mcol, None, ALU.mult, None, ["ckfd", "cst"], [kk])
                      ts("dve", km[:, 256:256 + T], dk[:, cc, :], mcol, None, ALU.mult, None, ["dk%d" % cc, "cst"], [kk])
                      for t in list(range(8)) + [8, 10]:
                          nq = 1 if t < 8 else 2
                          qT = dq[:, cc, t * 128:(t + nq) * 128]
                          kb = []
                          if t < 8:
                              for cb in range(2):
                                  kb.append((km[:, cb * 128:(cb + 1) * 128], kk, vctx[:, cb, h, 0:65], "vctxd", None))
                              for tk in range(8):
                                  kb.append((km[:, 256 + tk * 128: 256 + (tk + 1) * 128], kk, dvaug[:, tk, h, 0:65], "dvaug", None))
                          else:
                              for tk in (t, t + 1):
                                  kb.append((km[:, 256 + tk * 128: 256 + (tk + 1) * 128], kk, dvaug[:, tk, h, 0:65], "dvaug", None))

                          def post(ob, t=t, h=h, comp=comp, nq=nq):
                              Oa = Oall[h % 2]
                              ok = "Oall"
                              for qi in range(nq):
                                  cp("dve", Oa[:, comp, t + qi, :], PS(ob, 65, qi * 128), [pk(ob)], [ok])
                              if t + nq == NT and comp == 1:
                                  V(lambda e: e.reciprocal(out=rec[:], in_=Oa[:, :, :, 64]), [ok], ["rec"])
                                  tt("dve", y0[:], Oa[:, 0, :, 0:64], rec[:, 0, :].unsqueeze(2).to_broadcast([128, NT, 64]), ALU.mult, [ok, "rec"], ["y0"])
                                  tt("dve", y1[:], Oa[:, 1, :, 0:64], rec[:, 1, :].unsqueeze(2).to_broadcast([128, NT, 64]), ALU.mult, [ok, "rec"], ["y1"])
                                  stt("dve", y0[:], y1[:], nlam[:, l:l + 1], y0[:], ALU.mult, ALU.add, ["y1", "nlam", "y0"], ["y0"])
                                  tt("pool", y1[:], y0[:], y0[:], ALU.mult, ["y0"], ["y1"])
                                  V(lambda e: e.tensor_reduce(out=ss[:], in_=y1[:], axis=AX.X, op=ALU.add), ["y1"], ["ss"])
                                  act(ss[:], ss[:], AF.Sqrt, ["ss", "c_eps"], ["ss"], bias=c_eps[:, 0:1], scale=1.0 / 64)
                                  V(lambda e: e.reciprocal(out=ss[:], in_=ss[:]), ["ss"], ["ss"])
                                  tt("dve", y0[:], y0[:], ss[:].unsqueeze(2).to_broadcast([128, NT, 64]), ALU.mult, ["y0", "ss"], ["y0"])
                                  tt("dve", ytok[:, :, h * 64:(h + 1) * 64], y0[:], dgt[:, l, :].unsqueeze(1).to_broadcast([128, NT, 64]),
                                     ALU.mult, ["y0", "dgt"], ["ytokd"])
                          apipe.push(qT, "dq%d" % cc, kb, 32 ** -0.5, post, nq=nq)
              apipe.flush()
              transpose_y(ytok, "ytokd")
              (P.dma(dbg_o["d_y_diff"], yg[:].rearrange("p a b -> p (a b)"), r=["yg"]) if (debug and l == 0) else None)
              out_proj(pi["difO"], G1)

        barrier()
        if ST('ffn'):
            norm(A2, B2, lambda c, n: hT[:, c, n * 512:(n + 1) * 512], hk)
        with contextlib.ExitStack() as st:
          if ST('ffn'):
              hid = st.enter_context(nc.sbuf_tensor("a_hid_%d" % l, [128, 8, T], BF16))
              rl = [st.enter_context(nc.sbuf_tensor("a_rl%d_%d" % (i, l), [128, 512], BF16)) for i in range(2)]
              xs = [st.enter_context(nc.sbuf_tensor("a_xstg%d_%d" % (i, l), [128, 2048], F32)) for i in range(3)]
              xw = [st.enter_context(nc.sbuf_tensor("a_xwbf%d_%d" % (i, l), [128, 2048], BF16)) for i in range(2)]
              w_extra_begin(xs, xw)
              nada = [0]

              def ada_tick():
                  if l + 1 < depth and nada[0] < 24 and ST('ada'):
                      ada_piece(l + 1, nada[0]); nada[0] += 1
              for q in range(4):
                  f1, f2 = pi["ff"][q]
                  for j in range(4):
                      def ev1(mi, n, b, j=j):
                          r_ = rl[(mi + n) % 2]
                          act(r_[:], PS(b), AF.Relu, [pk(b)], ["rl%d" % ((mi + n) % 2)])
                          tt("pool" if (mi + n) % 2 == 0 else "dve", hid[:, j * 2 + mi, n * 512:(n + 1) * 512], r_[:], r_[:], ALU.mult, ["rl%d" % ((mi + n) % 2)], ["hid"])
                      proj_F(f1[j], hT, hk, 2, ev1)
                      ada_tick()
                  for j in range(4):
                      def ev2(mi, n, b, j=j):
                          m = j * 2 + mi
                          v = 0 if n < 2 else 1
                          xs = x[:, m, n * 512:(n + 1) * 512]
                          stt("dve", xs, PS(b), G2(m, v), xs, ALU.mult, ALU.add, [pk(b), xk[m]] + MODK, [xk[m]])
                      proj_F(f2[j], hid, ["hid"], 2, ev2)
                      ada_tick()
              if l + 1 < depth and ST('ada'):
                  ada_finish(l + 1)
              w_extra_end()

    P.scope = None
    ostg = [sb("ostg%d" % i, [128, 512]) for i in range(2)]
    okeys = ["ostg%d" % (c % 2) for c in range(8)]
    cnt = {"i": 0}
    for n in range(3):
        blk = slice(n * 512, (n + 1) * 512)
        for c in range(8):
            s = sqt[c % 2]
            act(s[:], x[:, c, blk], AF.Square, [xk[c]], ["sqt%d" % (c % 2)])
            mm(PS(5), ones[:], s[:], c == 0, c == 7, ["ones", "sqt%d" % (c % 2)], [pk(5)])
        act(rstd[:], PS(5), AF.Sqrt, [pk(5), "c_eps"], ["rstd"], bias=c_eps[:, 0:1], scale=1.0 / 1024.0)
        V(lambda e: e.reciprocal(out=rstd[:], in_=rstd[:]), ["rstd"], ["rstd"])
        for c in range(8):
            o = ostg[c % 2]
            stt("dve", o[:], x[:, c, blk], C("g_fin", c, 1), rstd[:], ALU.mult, ALU.mult, [xk[c], "rstd", "cst"], ["ostg%d" % (c % 2)])
            P.dma(yT_o[c * 128:(c + 1) * 128, blk], o[:], r=["ostg%d" % (c % 2)])
    P.emit()
    P.stats['minfree'] = minfree['v']
    return nc, P


def _consts():
    coff, CW = cst_layout()
    c = np.zeros((128, CW), np.float32)

    def put(name, arr):
        o, w = coff[name]
        c[:, o:o + w] = np.asarray(arr, np.float32).reshape(128, w)
    put("ident", np.eye(128))
    R64 = np.zeros((128, 128)); R32 = np.zeros((128, 128))
    for blk in range(2):
        for i in range(64):
            if i < 32:
                R64[blk * 64 + i, blk * 64 + i + 32] = -1.0
            else:
                R64[blk * 64 + i, blk * 64 + i - 32] = 1.0
    for blk in range(4):
        for i in range(32):
            if i < 16:
                R32[blk * 32 + i, blk * 32 + i + 16] = -1.0
            else:
                R32[blk * 32 + i, blk * 32 + i - 16] = 1.0
    put("R64T", R64.T); put("R32T", R32.T)
    j = np.arange(128)[:, None]; i = np.arange(128)[None, :]
    put("mlo", np.where(j >= i, 0.0, -480.0))
    put("mhi", np.where(j <= i, 0.0, -480.0))
    m = np.arange(128)[:, None]; n = np.arange(128)[None, :]
    put("Ef", np.maximum(n - m, 0)); put("Eb", np.maximum(m - n, 0))
    put("Lf8", (n >= m) / 8.0); put("Lb8", (m >= n) / 8.0)
    bd = np.zeros((128, 128)); bd[:64, :64] = 1; bd[64:, 64:] = 1
    put("BD", bd)
    put("idx1", np.broadcast_to(np.arange(128)[None, :] + 1.0, (128, 128)))
    put("idxr", np.broadcast_to(128.0 - np.arange(128)[None, :], (128, 128)))
    put("cmf", (127.0 - np.arange(128))[:, None]); put("cmb", (np.arange(128) * 1.0)[:, None])
    p = np.arange(128)
    put("hm", np.stack([p < 64, p >= 64], 1))
    put("cm4", np.stack([(p // 32) == k for k in range(4)], 1))
    pos = np.arange(1024)
    row = (pos // 64).astype(np.float32); col = (pos % 64).astype(np.float32)

    def tab(dim):
        nn = dim // 4
        inv = (10000.0 ** (-np.arange(nn, dtype=np.float32) / nn)).astype(np.float32)
        ang = np.concatenate([row[:, None] * inv, col[:, None] * inv], -1).astype(np.float32)
        idx = (np.arange(128) % dim) % (dim // 2)
        return np.cos(ang)[:, idx].T.astype(np.float32), np.sin(ang)[:, idx].T.astype(np.float32)
    c64, s64 = tab(64); c32, s32 = tab(32)
    rope = np.concatenate([c64, s64, c32, s32], 1).astype(np.float32)
    return c, rope, coff


def _fm(v):
    v = np.asarray(v)
    n = v.shape[-1] // 128
    lead = v.shape[:-1]
    return np.moveaxis(v.reshape(lead + (n, 128)), -1, 0)


_CACHE = {}


def kernel(**inp):
    I = {k: np.asarray(v) for k, v in inp.items()}
    if "prog" not in _CACHE:
        _CACHE["prog"] = build_program()
    nc, P = _CACHE["prog"]
    in_maps = prep_inputs(I)
    res = run_bass_kernel_spmd(nc, in_maps, core_ids=list(range(8)))
    return assemble(res.results)


def prep_inputs(I, cores=range(8)):
    cbase, rope, coff = _consts()

    wts = np.zeros((DEPTH * PPL, 128, 2048), np.float32)

    def kpiece(W, cols):
        out = np.zeros((128, 8, 256), np.float32)
        cols = np.asarray(cols)
        valid = cols >= 0
        Wc = W[:, cols[valid]].reshape(8, 128, -1)
        out[:, :, np.nonzero(valid)[0]] = np.transpose(Wc, (1, 0, 2))
        return out.reshape(128, 2048)
    ar = np.arange
    for l in range(DEPTH):
        pi = piece_ids(l)
        wa, wi, wo, w1, w2 = I["w_ada"][l], I["w_in"][l], I["w_out"][l], I["w_ff1"][l], I["w_ff2"][l]
        for j in range(24):
            wts[pi["ada"][j]] = kpiece(wa, ar(j * 256, (j + 1) * 256))
        wts[pi["retF"][0]] = kpiece(wi, ar(0, 256))
        neg = -np.ones(64, int)
        wts[pi["retF"][1]] = kpiece(wi, np.concatenate([ar(256, 320), neg, neg, ar(320, 384)]))
        wts[pi["retF"][2]] = kpiece(wi, np.concatenate([ar(384, 448), neg, neg, ar(448, 512)]))
        wts[pi["retT"][0]] = kpiece(wi, ar(256, 512))
        wts[pi["retT"][1]] = kpiece(wi, ar(512, 768))
        wts[pi["retT"][2]] = kpiece(wi, ar(768, 1024))
        wts[pi["lruF"][0]] = kpiece(wi, ar(1024, 1280))
        wts[pi["lruF"][1]] = kpiece(wi, ar(1280, 1536))
        sqb = 1536
        wts[pi["swaF"][0]] = kpiece(wi, np.concatenate([ar(sqb, sqb + 64), ar(sqb + 192, sqb + 256), ar(sqb + 64, sqb + 192)]))
        wts[pi["swaF"][1]] = kpiece(wi, np.concatenate([ar(1792, 1920), -np.ones(128, int)]))
        wts[pi["swaT"][0]] = kpiece(wi, ar(1792, 2048))
        wts[pi["difF"][0]] = kpiece(wi, ar(2048, 2304))
        wts[pi["difF"][1]] = kpiece(wi, ar(2304, 2560))
        wts[pi["difT"][0]] = kpiece(wi, ar(2304, 2560))
        wts[pi["difT"][1]] = kpiece(wi, ar(2560, 2816))
        for g, name in enumerate(("retO", "lruO", "swaO", "difO")):
            blkw = wo[g * 256:(g + 1) * 256, :].reshape(2, 128, 1024)
            wts[pi[name]] = np.transpose(blkw, (1, 0, 2)).reshape(128, 2048)
        for q in range(4):
            f1, f2 = pi["ff"][q]
            for j in range(4):
                wts[f1[j]] = kpiece(w1, ar(q * 1024 + j * 256, q * 1024 + (j + 1) * 256))
                wts[f2[j]] = kpiece(w2[q * 1024:(q + 1) * 1024, :], ar(j * 256, (j + 1) * 256))
    lruw = np.zeros((128, DEPTH, 8, 128), np.float32)
    for l in range(DEPTH):
        for d in range(2):
            for gsel, nm in enumerate(("lru_w_a", "lru_w_x")):
                for cc in range(2):
                    for bb in range(2):
                        blk = I[nm][l, d, cc * 2 + bb]
                        lruw[bb * 64:(bb + 1) * 64, l, d * 4 + gsel * 2 + cc, bb * 64:(bb + 1) * 64] = blk
    lruw = lruw.reshape(128, DEPTH * 8 * 128)

    def putc(c, name, arr):
        o, w = coff[name]
        c[:, o:o + w] = np.asarray(arr, np.float32).reshape(128, w)

    in_maps = []
    for core in cores:
        c = cbase.copy()
        putc(c, "b_ada", _fm(I["b_ada"]))
        putc(c, "g_mix", _fm(I["norm_mix_g"])); putc(c, "g_mlp", _fm(I["norm_mlp_g"])); putc(c, "g_fin", _fm(I["final_norm_g"]))
        cv = np.stack([I["c"][core], I["c_ctx"]], 0)
        putc(c, "cT", np.transpose(_fm(cv), (0, 2, 1)))
        putc(c, "conv_w", _fm(I["lru_conv_w"])); putc(c, "conv_b", _fm(I["lru_conv_b"]))
        putc(c, "b_a", _fm(I["lru_b_a"])); putc(c, "b_x", _fm(I["lru_b_x"])); putc(c, "lam", _fm(I["lru_lambda"]))
        putc(c, "h0", _fm(I["state_lru"][core]))
        rd = I["ret_decay"]
        putc(c, "rd_row", np.broadcast_to(rd.reshape(1, 32), (128, 32)))
        rp = np.zeros((128, DEPTH, 2, 2), np.float32)
        for j in range(2):
            rp[:64, :, :, j] = rd[:, :, 2 * j]; rp[64:, :, :, j] = rd[:, :, 2 * j + 1]
        putc(c, "rd_pair", rp)
        putc(c, "sink", np.broadcast_to(I["swa_sink"].reshape(1, 16), (128, 16)))
        putc(c, "dlam", np.broadcast_to(I["diff_lambda"].reshape(1, -1), (128, DEPTH * 128)))
        putc(c, "dng", np.broadcast_to(I["diff_norm_g"].reshape(1, -1), (128, DEPTH * 64)))
        putc(c, "gng", np.broadcast_to(I["ret_gn_g"].reshape(1, -1), (128, DEPTH * 256)))
        xall = np.concatenate([I["x_sample"][core], I["x_prompt"][2 * core], I["x_prompt"][2 * core + 1]], 0)
        cks = np.transpose(I["cache_swa_k"][core].reshape(DEPTH, 256, 128), (2, 0, 1))
        cvs = np.transpose(I["cache_swa_v"][core].reshape(DEPTH, 2, 128, 128), (2, 0, 1, 3))
        ckd = np.transpose(I["cache_diff_k"][core].reshape(DEPTH, 256, 2, 128), (3, 0, 2, 1))
        cvd = np.transpose(I["cache_diff_v"][core].reshape(DEPTH, 2, 128, 256), (2, 0, 1, 3))
        sr = I["state_ret"][core]
        sret = np.zeros((128, DEPTH, 4, 128), np.float32)
        for d in range(2):
            for j in range(2):
                sret[:64, :, d * 2 + j, :64] = np.transpose(sr[:, d, 2 * j], (1, 0, 2))
                sret[64:, :, d * 2 + j, 64:] = np.transpose(sr[:, d, 2 * j + 1], (1, 0, 2))
        in_maps.append({"xT": np.ascontiguousarray(xall.T), "wts": wts, "cst": c, "rope": rope, "lruw": lruw,
                        "cks": np.ascontiguousarray(cks), "cvs": np.ascontiguousarray(cvs), "ckd": np.ascontiguousarray(ckd),
                        "cvd": np.ascontiguousarray(cvd), "sret": sret})
    return in_maps


def assemble(R):
    y_prompt = np.zeros((16, 256, 1024), np.float32); y_sample = np.zeros((8, 1024, 1024), np.float32)
    nsr = np.zeros((16, DEPTH, 2, 4, 64, 64), np.float32); nsl = np.zeros((16, DEPTH, 2, 256), np.float32)
    nsk = np.zeros((16, DEPTH, 256, 2, 64), np.float32); nsv = np.zeros((16, DEPTH, 256, 2, 64), np.float32)
    ndk = np.zeros((16, DEPTH, 256, 4, 64), np.float32); ndv = np.zeros((16, DEPTH, 256, 4, 64), np.float32)
    for core in range(8):
        r = R[core]
        y = r["yT"].T
        y_sample[core] = y[0:1024]
        osr = r["o_sret"]
        osl = r["o_slru"].reshape(128, 2, DEPTH, 2, 2)
        okv = r["o_kv"]
        for sj in range(2):
            b = 2 * core + sj
            y_prompt[b] = y[1024 + 256 * sj: 1280 + 256 * sj]
            for d in range(2):
                for j in range(2):
                    nsr[b, :, d, 2 * j] = np.transpose(osr[:64, sj, :, d * 2 + j, :64], (1, 0, 2))
                    nsr[b, :, d, 2 * j + 1] = np.transpose(osr[64:, sj, :, d * 2 + j, 64:], (1, 0, 2))
            nsl[b] = np.transpose(osl[:, sj], (1, 2, 3, 0)).reshape(DEPTH, 2, 256)
            kv = np.transpose(okv[2 * sj:2 * sj + 2], (2, 0, 1, 3)).reshape(DEPTH, 256, 768)
            nsk[b] = kv[:, :, 0:128].reshape(DEPTH, 256, 2, 64)
            nsv[b] = kv[:, :, 128:256].reshape(DEPTH, 256, 2, 64)
            ndk[b] = kv[:, :, 256:512].reshape(DEPTH, 256, 4, 64)
            ndv[b] = kv[:, :, 512:768].reshape(DEPTH, 256, 4, 64)
    return (y_prompt, y_sample, nsr, nsl, nsk, nsv, ndk, ndv)
```

```python
import contextlib
import numpy as np
import ml_dtypes
import concourse.bass as bass
import concourse.mybir as mybir
from concourse.bass_utils import run_bass_kernel_spmd

F32 = mybir.dt.float32
BF16 = mybir.dt.bfloat16
AF = mybir.ActivationFunctionType
ALU = mybir.AluOpType
AX = mybir.AxisListType

ENGS = ("pe", "act", "dve", "pool", "sp")
DMA_WIN = 8
DEPTH = 4
T = 1536
NT = 12
EPS = 1e-6
NSTG = 3
NWBF = 3
PPL = 24 + 3 + 3 + 1 + 2 + 1 + 2 + 1 + 1 + 2 + 2 + 1 + 32


class Op:
    __slots__ = ("idx", "eng", "fn", "deps", "dma", "signal", "sem", "val", "waits")

    def __init__(self, idx, eng, fn, dma):
        self.idx = idx; self.eng = eng; self.fn = fn; self.dma = dma
        self.deps = set(); self.signal = False; self.sem = None; self.val = 0; self.waits = []


class Prog:
    def __init__(self, nc):
        self.nc = nc
        self.ops = []
        self.last_w = {}
        self.readers = {}
        self.scope = None

    def add(self, eng, fn, r=(), w=(), dma=False, noscope=False):
        op = Op(len(self.ops), eng, fn, dma)
        w = list(w) + [k for k in r if isinstance(k, str) and k.startswith("ps")]
        r = [k for k in r if not (isinstance(k, str) and k.startswith("ps"))]
        if self.scope is not None and not noscope:
            r.append(self.scope)
        for k in r:
            lw = self.last_w.get(k)
            if lw is not None:
                op.deps.add((lw, "raw"))
        for k in w:
            lw = self.last_w.get(k)
            if lw is not None:
                op.deps.add((lw, "waw"))
            for rd in self.readers.get(k, ()):
                op.deps.add((rd, "war"))
        for k in r:
            self.readers.setdefault(k, []).append(op.idx)
        for k in w:
            self.last_w[k] = op.idx
            self.readers[k] = []
        self.ops.append(op)
        return op

    def dma(self, out, in_, r=(), w=(), noscope=False):
        return self.add("sp", lambda e: e.dma_start(out=out, in_=in_), r=r, w=w, dma=True, noscope=noscope)

    def emit(self):
        nc = self.nc
        ops = self.ops
        for op in ops:
            nd = set()
            for (d, kind) in op.deps:
                dop = ops[d]
                if d == op.idx:
                    continue
                if dop.eng == op.eng and not dop.dma and not op.dma and kind != "raw":
                    continue
                if dop.eng == op.eng and op.eng == "pe":
                    continue
                nd.add(d)
            op.deps = nd
            for d in nd:
                ops[d].signal = True
        stack = contextlib.ExitStack()
        eng_sem = {}; eng_cnt = {}; dma_sems = {}; dma_cnt = {}
        for e in ENGS:
            eng_sem[e] = stack.enter_context(nc.semaphore("c_" + e))
            eng_cnt[e] = 0; dma_sems[e] = None; dma_cnt[e] = 0
        extra_waits = {}
        final_waits = []
        for op in ops:
            if op.dma:
                if dma_sems[op.eng] is None:
                    dma_sems[op.eng] = [stack.enter_context(nc.semaphore("d_%s%d" % (op.eng, i)))
                                        for i in range(DMA_WIN)]
                k = dma_cnt[op.eng]; dma_cnt[op.eng] += 1
                op.sem = dma_sems[op.eng][k % DMA_WIN]
                op.val = 16 * (k // DMA_WIN + 1)
                op.signal = True
                if k >= DMA_WIN:
                    extra_waits[op.idx] = [(op.sem, op.val - 16)]
            elif op.signal:
                eng_cnt[op.eng] += 1
                op.sem = eng_sem[op.eng]; op.val = eng_cnt[op.eng]
        for e in ENGS:
            if dma_sems[e] is not None:
                k = dma_cnt[e]
                for slot in range(min(k, DMA_WIN)):
                    n = (k - 1 - slot) // DMA_WIN + 1
                    final_waits.append((dma_sems[e][slot], 16 * n))
        waited = {e: {} for e in ENGS}
        for op in ops:
            need = {}
            for d in op.deps:
                dop = ops[d]
                if need.get(dop.sem, (None, 0))[1] < dop.val:
                    need[dop.sem] = (dop.sem, dop.val)
            for (s, v) in extra_waits.get(op.idx, ()):
                if need.get(s, (None, 0))[1] < v:
                    need[s] = (s, v)
            wl = []
            for key, (s, v) in need.items():
                if waited[op.eng].get(key, 0) >= v:
                    continue
                waited[op.eng][key] = v
                wl.append((s, v))
            op.waits = wl
        by_eng = {e: [op for op in ops if op.eng == e] for e in ENGS}
        self.stats = {e: len(by_eng[e]) for e in ENGS}
        self.stats["waits"] = sum(len(op.waits) for op in ops)
        self.stats["maxsem"] = dict(eng_cnt)

        def run(eh, ename, final=False):
            for op in by_eng[ename]:
                for (s, v) in op.waits:
                    eh.wait_ge(s, v)
                inst = op.fn(eh)
                if op.signal:
                    inst.then_inc(op.sem, 16 if op.dma else 1)
            if final:
                for (s, v) in final_waits:
                    eh.wait_ge(s, v)

        with stack:
            with nc.Block() as block:
                @block.tensor
                def _(e):
                    run(e, "pe")

                @block.scalar
                def _(e):
                    run(e, "act")

                @block.vector
                def _(e):
                    run(e, "dve")

                @block.gpsimd
                def _(e):
                    run(e, "pool")

                @block.sync
                def _(e):
                    run(e, "sp", final=True)


def cst_layout():
    entA = [("BD", 128), ("hm", 2), ("cm4", 4),
            ("b_ada", DEPTH * 48), ("g_mix", DEPTH * 8), ("g_mlp", DEPTH * 8), ("g_fin", 8),
            ("conv_w", DEPTH * 8), ("conv_b", DEPTH * 2), ("b_a", DEPTH * 4), ("b_x", DEPTH * 4),
            ("h0", DEPTH * 4), ("gng", DEPTH * 256)]
    entB = [("ident", 128), ("R64T", 128), ("R32T", 128), ("mlo", 128), ("mhi", 128), ("Ef", 128), ("Eb", 128),
            ("Lf8", 128), ("Lb8", 128), ("idx1", 128), ("idxr", 128), ("cmf", 1), ("cmb", 1), ("cT", 16),
            ("lam", DEPTH * 4), ("rd_row", 32), ("rd_pair", 16), ("sink", 16), ("dlam", DEPTH * 128), ("dng", DEPTH * 64)]
    off = {}
    o = 0
    for n, w in entA + entB:
        off[n] = (o, w)
        o += w
    off["_WA"] = (sum(w for _, w in entA), 0)
    return off, o


def piece_ids(l):
    b = l * PPL
    d = {}
    d["ada"] = list(range(b, b + 24)); b += 24
    d["retF"] = [b, b + 1, b + 2]; b += 3
    d["retT"] = [b, b + 1, b + 2]; b += 3
    d["retO"] = b; b += 1
    d["lruF"] = [b, b + 1]; b += 2
    d["lruO"] = b; b += 1
    d["swaF"] = [b, b + 1]; b += 2
    d["swaT"] = [b]; b += 1
    d["swaO"] = b; b += 1
    d["difF"] = [b, b + 1]; b += 2
    d["difT"] = [b, b + 1]; b += 2
    d["difO"] = b; b += 1
    d["ff"] = []
    for q in range(4):
        d["ff"].append((list(range(b, b + 4)), list(range(b + 4, b + 8)))); b += 8
    assert b == (l + 1) * PPL + 0 or True
    return d


def build_program(depth=DEPTH, stages=None, debug=False):
    ST = (lambda name: True) if stages is None else (lambda name: name in stages)
    nc = bass.Bass("TRN2", target_bir_lowering=False)
    P = Prog(nc)
    coff, CW = cst_layout()

    def din(name, shape):
        return nc.dram_tensor(name, list(shape), F32, kind="ExternalInput").ap()

    def dout(name, shape):
        return nc.dram_tensor(name, list(shape), F32, kind="ExternalOutput").ap()

    xT_d = din("xT", [1024, T])
    wts_d = din("wts", [DEPTH * PPL, 128, 2048])
    cst_d = din("cst", [128, CW])
    rope_d = din("rope", [128, 4096])
    lruw_d = din("lruw", [128, DEPTH * 8 * 128])
    cks_d = din("cks", [128, DEPTH, 256])
    cvs_d = din("cvs", [128, DEPTH, 2, 128])
    ckd_d = din("ckd", [128, DEPTH, 2, 256])
    cvd_d = din("cvd", [128, DEPTH, 2, 256])
    sret_d = din("sret", [128, DEPTH, 4, 128])
    yT_o = dout("yT", [1024, T])
    osret_o = dout("o_sret", [128, 2, DEPTH, 4, 128])
    oslru_o = dout("o_slru", [128, 2 * DEPTH * 4])
    okv_o = dout("o_kv", [4, 128, DEPTH, 768])

    def sb(name, shape, dt=F32):
        return nc.alloc_sbuf_tensor("s_" + name, list(shape), dt)

    dbg_o = {}
    if debug:
        dbg_o["d_hT"] = nc.dram_tensor("d_hT", [128, 8 * T], BF16, kind="ExternalOutput").ap()
        dbg_o["d_mod"] = nc.dram_tensor("d_mod", [128, 96], F32, kind="ExternalOutput").ap()
        dbg_o["d_sq"] = nc.dram_tensor("d_sq", [128, 2 * T], BF16, kind="ExternalOutput").ap()
        dbg_o["d_kms0"] = nc.dram_tensor("d_kms0", [128, 256 + T], BF16, kind="ExternalOutput").ap()
        dbg_o["d_kms1"] = nc.dram_tensor("d_kms1", [128, 256 + T], BF16, kind="ExternalOutput").ap()
        dbg_o["d_sva"] = nc.dram_tensor("d_sva", [128, NT * 2 * 80], BF16, kind="ExternalOutput").ap()
        dbg_o["d_vctx"] = nc.dram_tensor("d_vctx", [128, 2 * 2 * 80], BF16, kind="ExternalOutput").ap()
        dbg_o["d_osw"] = nc.dram_tensor("d_osw", [128, NT * 65], F32, kind="ExternalOutput").ap()
        for g_ in ("ret", "lru", "swa", "diff"):
            dbg_o["d_y_" + g_] = nc.dram_tensor("d_y_" + g_, [128, 2 * T], BF16, kind="ExternalOutput").ap()

    psum = nc.alloc_psum_tensor("psum", [128, 4096], F32)

    def PS(b, n=512, o=0):
        return psum[:, b * 512 + o: b * 512 + o + n]

    def pk(b):
        return "ps%d" % b

    x = sb("x", [128, 8, T])
    hT = sb("hT", [128, 8, T], BF16)
    yg = sb("yg", [128, 2, T], BF16)
    WA = coff["_WA"][0]
    cst = sb("cst", [128, WA])
    stg = [sb("stg%d" % i, [128, 2048]) for i in range(NSTG)]
    wbf = [sb("wbf%d" % i, [128, 2048], BF16) for i in range(NWBF)]
    ident = sb("ident", [128, 128], BF16)
    ones = sb("ones", [128, 128], BF16)
    R64 = sb("R64", [128, 128], BF16)
    R32 = sb("R32", [128, 128], BF16)
    mlo = sb("mlo", [128, 128], BF16)
    mhi = sb("mhi", [128, 128], BF16)
    DTm = sb("DTm", [128, DEPTH * 4, 128], BF16)
    Xi = sb("Xi", [128, DEPTH * 4, 128], BF16)
    zeta = sb("zeta", [128, DEPTH * 2, 4])
    Gp = sb("Gp", [128, DEPTH * 4])
    lgrow = sb("lgrow", [128, 32])
    lgpair = sb("lgpair", [128, 16])
    esink = sb("esink", [128, 16])
    nlam = sb("nlam", [128, DEPTH])
    dgt = sb("dgt", [128, DEPTH, 64])
    sa8 = sb("sa8", [128, DEPTH * 4])
    sa16 = sb("sa16", [128, DEPTH * 4])
    csil = sb("csil", [128, 8, 2], BF16)
    mods = [sb("mod%d" % i, [128, 48, 2]) for i in range(2)]
    modAs = [sb("modA%d" % i, [128, 2, 8, 2]) for i in range(2)]
    MODK = ["mod0", "mod1", "modA0", "modA1"]
    c_eps = sb("c_eps", [128, 1])
    c_one = sb("c_one", [128, 1])
    bar_t = sb("bar_t", [128, 1])
    rstd = sb("rstd", [128, 512])
    rstd2 = [rstd, sb("rstd1", [128, 512])]
    sqt = [sb("sqt%d" % i, [128, 512], BF16) for i in range(2)]
    ntmp = [sb("ntmp%d" % i, [128, 512]) for i in range(2)]
    adarow = [sb("adarow%d" % i, [2, 256]) for i in range(2)]
    identf = sb("identf", [2, 2])
    setup_stack = contextlib.ExitStack()
    cstB = setup_stack.enter_context(nc.sbuf_tensor("s_cstB", [128, CW - WA], F32))

    def C(name, lo=0, n=None):
        o, w = coff[name]
        if n is None:
            n = w - lo
        if o >= WA:
            return cstB[:, o - WA + lo: o - WA + lo + n]
        return cst[:, o + lo: o + lo + n]

    rr = {"ev": 0}

    def V(fn, r, w):
        return P.add("dve", fn, r, w)

    def G(fn, r, w):
        return P.add("pool", fn, r, w)

    def A(fn, r, w):
        return P.add("act", fn, r, w)

    def M(fn, r, w):
        return P.add("pe", fn, r, w)

    def act(out, in_, func, r, w, bias=None, scale=1.0):
        kw = {}
        if bias is not None:
            kw["bias"] = bias
        return A(lambda e: e.activation(out=out, in_=in_, func=func, scale=scale, **kw), r, w)

    def tt(eng, out, a, b, op, r, w):
        return P.add(eng, lambda e: e.tensor_tensor(out=out, in0=a, in1=b, op=op), r, w)

    def ts(eng, out, a, s1, s2, op0, op1, r, w):
        if s2 is None:
            return P.add(eng, lambda e: e.tensor_scalar(out=out, in0=a, scalar1=s1, scalar2=None, op0=op0), r, w)
        return P.add(eng, lambda e: e.tensor_scalar(out=out, in0=a, scalar1=s1, scalar2=s2, op0=op0, op1=op1), r, w)

    def stt(eng, out, a, s, b, op0, op1, r, w):
        return P.add(eng, lambda e: e.scalar_tensor_tensor(out=out, in0=a, scalar=s, in1=b, op0=op0, op1=op1), r, w)

    def cp(eng, out, in_, r, w):
        if eng == "act":
            return A(lambda e: e.activation(out=out, in_=in_, func=AF.Copy), r, w)
        return P.add(eng, lambda e: e.tensor_copy(out=out, in_=in_), r, w)

    def mm(out, lhsT, rhs, start, stop, r, w):
        return M(lambda e: e.matmul(out, lhsT=lhsT, rhs=rhs, start=start, stop=stop), r, w)

    def mmk(out, lhs_fn, rhs_fn, nk, r, w):
        def f(e):
            for kc in range(nk):
                ins = e.matmul(out, lhsT=lhs_fn(kc), rhs=rhs_fn(kc), start=(kc == 0), stop=(kc == nk - 1))
            return ins
        return M(f, r, w)

    minfree = {"v": 1 << 30}

    def barrier():
        minfree["v"] = min(minfree["v"], nc.sbuf_bytes_remaining)
        P.scope = None
        G(lambda e: e.memset(bar_t[:], 0.0), [], ["ARENA", "bar_t"])
        P.scope = "ARENA"

    order = []
    for l in range(depth):
        pi = piece_ids(l)
        if l == 0:
            order += pi["ada"]
        order += pi["retF"] + pi["retT"] + [pi["retO"]] + pi["lruF"] + [pi["lruO"]] + pi["swaF"] + pi["swaT"] + \
            [pi["swaO"]] + pi["difF"] + pi["difT"] + [pi["difO"]]
        nada = 0
        nxt = piece_ids(l + 1)["ada"] if l + 1 < depth else []
        for q in range(4):
            for p_ in pi["ff"][q][0] + pi["ff"][q][1]:
                order.append(p_)
                if nada < len(nxt):
                    order.append(nxt[nada]); nada += 1
    ffn_piece = set()
    for l in range(depth):
        pi = piece_ids(l)
        for q in range(4):
            ffn_piece.update(pi["ff"][q][0] + pi["ff"][q][1])
        if l + 1 < depth:
            ffn_piece.update(piece_ids(l + 1)["ada"])
    ws = {"issued": 0, "used": 0, "stg_rr": 0, "slot": {}, "free": list(range(NWBF)), "xstg": [], "xwbf": []}
    cast_eng = ["dve", "act", "dve", "act"]

    def stg_of(i):
        return (stg[i], "stg%d" % i) if i < NSTG else (ws["xstg"][i - NSTG], "xstg%d" % (i - NSTG))

    def wbf_of(i):
        return (wbf[i], "wbf%d" % i) if i < NWBF else (ws["xwbf"][i - NWBF], "xwbf%d" % (i - NWBF))

    def w_issue():
        k = ws["issued"]
        if k >= len(order):
            return False
        pid = order[k]
        big = (pid in ffn_piece) and len(ws["xstg"]) > 0
        cand = [b for b in ws["free"] if b < NWBF or big]
        if not cand:
            return False
        b = cand[0]
        ws["free"].remove(b)
        nst = NSTG + (len(ws["xstg"]) if big else 0)
        si = ws["stg_rr"] % nst
        ws["stg_rr"] += 1
        st_, skey = stg_of(si)
        wb_, wkey = wbf_of(b)
        ws["slot"][k] = b
        nsc = not (si >= NSTG or b >= NWBF)
        P.dma(st_[:], wts_d[pid], r=[], w=[skey], noscope=nsc)
        eng = cast_eng[k % 4]
        if eng == "act":
            P.add("act", lambda e: e.activation(out=wb_[:], in_=st_[:], func=AF.Copy), [skey], [wkey], noscope=nsc)
        else:
            P.add(eng, lambda e: e.tensor_copy(out=wb_[:], in_=st_[:]), [skey], [wkey], noscope=nsc)
        ws["issued"] += 1
        return True

    def w_next(expect):
        while order[ws["used"]] != expect:
            ws["used"] += 1
        k = ws["used"]
        for kk in list(ws["slot"].keys()):
            if kk < k:
                ws["free"].append(ws["slot"].pop(kk))
        assert ws["issued"] >= k or ws["issued"] == k or True
        if ws["issued"] < k:
            ws["issued"] = k
        while ws["issued"] < len(order) and ws["issued"] < k + 8:
            if not w_issue():
                break
        assert k in ws["slot"], (k, ws)
        ws["used"] += 1
        return wbf_of(ws["slot"][k])

    def w_extra_begin(xs, xw):
        ws["xstg"] = xs
        ws["xwbf"] = xw
        ws["free"] += [NWBF + i for i in range(len(xw))]

    def w_extra_end():
        for kk, b in ws["slot"].items():
            assert b < NWBF, "extra bf16 slot still in use at phase end"
        ws["free"] = [b for b in ws["free"] if b < NWBF]
        ws["xstg"] = []
        ws["xwbf"] = []

    P.scope = "ARENA"
    P.dma(cst[:], cst_d[:, 0:WA], w=["cst"])
    P.dma(cstB[:], cst_d[:, WA:CW], w=["cst"])
    for c in range(8):
        P.dma(x[:, c, :], xT_d[c * 128:(c + 1) * 128, :], w=["x%d" % c])
    G(lambda e: e.memset(ones[:], 1.0), [], ["ones"])
    G(lambda e: e.memset(c_eps[:], EPS), [], ["c_eps"])
    G(lambda e: e.memset(c_one[:], 1.0), [], ["c_one"])
    cp("dve", ident[:], C("ident"), ["cst"], ["ident"])
    cp("dve", identf[:], C("ident")[0:2, 0:2], ["cst"], ["identf"])
    cp("dve", R64[:], C("R64T"), ["cst"], ["R64"])
    cp("dve", R32[:], C("R32T"), ["cst"], ["R32"])
    cp("dve", mlo[:], C("mlo"), ["cst"], ["mlo"])
    cp("dve", mhi[:], C("mhi"), ["cst"], ["mhi"])
    act(csil[:].rearrange("p a b -> p (a b)"), C("cT"), AF.Silu, ["cst"], ["csil"])
    for (dst, src, n, key) in ((lgrow, "rd_row", 32, "lgrow"), (lgpair, "rd_pair", 16, "lgpair")):
        act(dst[:], C(src), AF.Exp, ["cst"], [key], scale=-1.0)
        act(dst[:], dst[:], AF.Ln, [key, "c_one"], [key], bias=c_one[:, 0:1])
        ts("dve", dst[:], dst[:], -1.0, None, ALU.mult, None, [key], [key])
    etmp = setup_stack.enter_context(nc.sbuf_tensor("s_etmp", [128, 128], F32))
    etmp2 = setup_stack.enter_context(nc.sbuf_tensor("s_etmp2", [128, 128], F32))
    for l in range(depth):
        for h in range(4):
            act(etmp[:], C("Ef"), AF.Exp, ["cst", "lgrow"], ["etmp"], scale=lgrow[:, l * 8 + h: l * 8 + h + 1])
            tt("dve", etmp[:], etmp[:], C("Lf8"), ALU.mult, ["etmp", "cst"], ["etmp"])
            act(etmp2[:], C("Eb"), AF.Exp, ["cst", "lgrow"], ["etmp2"], scale=lgrow[:, l * 8 + 4 + h: l * 8 + 4 + h + 1])
            tt("dve", etmp2[:], etmp2[:], C("Lb8"), ALU.mult, ["etmp2", "cst"], ["etmp2"])
            tt("dve", DTm[:, l * 4 + h, :], etmp[:], etmp2[:], ALU.add, ["etmp", "etmp2"], ["DTm"])
        for d in range(2):
            for j in range(2):
                i = l * 4 + d * 2 + j
                act(Xi[:, i, :], C("idx1") if d == 0 else C("idxr"), AF.Exp, ["cst", "lgpair"], ["Xi"], scale=lgpair[:, i:i + 1])
                act(Gp[:, i:i + 1], lgpair[:, i:i + 1], AF.Exp, ["lgpair"], ["Gp"], scale=128.0)
            act(zeta[:, l * 2 + d, :], lgrow[:, l * 8 + d * 4: l * 8 + d * 4 + 4], AF.Exp, ["lgrow", "cst"], ["zeta"],
                scale=(C("cmf") if d == 0 else C("cmb")))
    ts("dve", zeta[:], zeta[:], 0.125, None, ALU.mult, None, ["zeta"], ["zeta"])
    act(esink[:], C("sink"), AF.Exp, ["cst"], ["esink"])
    dl1 = setup_stack.enter_context(nc.sbuf_tensor("s_dl1", [128, DEPTH, 2, 32], F32))
    dl2 = setup_stack.enter_context(nc.sbuf_tensor("s_dl2", [128, DEPTH * 2], F32))
    dlv = C("dlam").rearrange("p (l a b c) -> p l a b c", l=DEPTH, a=2, b=2)
    tt("dve", dl1[:], dlv[:, :, :, 0, :], dlv[:, :, :, 1, :], ALU.mult, ["cst"], ["dl1"])
    V(lambda e: e.tensor_reduce(out=dl2[:], in_=dl1[:].rearrange("p l a c -> p (l a) c"), axis=AX.X, op=ALU.add), ["dl1"], ["dl2"])
    act(dl2[:], dl2[:], AF.Exp, ["dl2"], ["dl2"])
    for l in range(depth):
        import math
        li = 0.8 - 0.6 * math.exp(-0.3 * l)
        stt("dve", nlam[:, l:l + 1], dl2[:, 2 * l + 1: 2 * l + 2], -li, dl2[:, 2 * l: 2 * l + 1], ALU.add, ALU.subtract, ["dl2"], ["nlam"])
        ts("dve", dgt[:, l, :], C("dng", l * 64, 64), 1.0 - li, None, ALU.mult, None, ["cst"], ["dgt"])
    act(sa8[:], C("lam"), AF.Exp, ["cst"], ["sa8"], scale=-1.0)
    act(sa8[:], sa8[:], AF.Ln, ["sa8", "c_one"], ["sa8"], bias=c_one[:, 0:1])
    ts("dve", sa16[:], sa8[:], -16.0, None, ALU.mult, None, ["sa8"], ["sa16"])
    ts("dve", sa8[:], sa8[:], -8.0, None, ALU.mult, None, ["sa8", "sa16"], ["sa8"])

    setup_stack.close()
    P.scope = None
    xk = ["x%d" % c for c in range(8)]
    hk = ["h%d" % c for c in range(8)]

    def norm(Aap, Bap, dst_fn, dst_keys, final=False):
        def stats(n):
            blk = slice(n * 512, (n + 1) * 512)
            bank = 5 if n % 2 == 0 else 2
            rs = rstd2[n % 2]
            rk_ = "rstd%d" % (n % 2)
            for c in range(8):
                s = sqt[c % 2]
                act(s[:], x[:, c, blk], AF.Square, [xk[c]], ["sqt%d" % (c % 2)])
                mm(PS(bank), ones[:], s[:], c == 0, c == 7, ["ones", "sqt%d" % (c % 2)], [pk(bank)])
            act(rs[:], PS(bank), AF.Sqrt, [pk(bank), "c_eps"], [rk_], bias=c_eps[:, 0:1], scale=1.0 / 1024.0)
            V(lambda e: e.reciprocal(out=rs[:], in_=rs[:]), [rk_], [rk_])

        def apply(n):
            v = 0 if n < 2 else 1
            blk = slice(n * 512, (n + 1) * 512)
            rs = rstd2[n % 2]
            rk_ = "rstd%d" % (n % 2)
            for c in range(8):
                tmp = ntmp[c % 2]
                tt("dve", tmp[:], x[:, c, blk], rs[:], ALU.mult, [xk[c], rk_], ["ntmp%d" % (c % 2)])
                o = dst_fn(c, n)
                if c % 2 == 0:
                    A(lambda e, o=o, tmp=tmp, sa=Aap(c, v), ba=Bap(c, v): e.activation(out=o, in_=tmp[:], func=AF.Identity, scale=sa, bias=ba),
                      ["ntmp%d" % (c % 2)] + MODK, [dst_keys[c]])
                else:
                    ts("pool", o, tmp[:], Aap(c, v), Bap(c, v), ALU.mult, ALU.add, ["ntmp%d" % (c % 2)] + MODK, [dst_keys[c]])
        stats(0)
        stats(1)
        apply(0)
        stats(2)
        apply(1)
        apply(2)

    dstate = {"b": 0}

    def dbank():
        dstate["b"] ^= 1
        return 3 + dstate["b"]

    def proj_F(pid, src, srck, nchunks, evac):
        wv, wkey = w_next(pid)
        w3 = wv[:].rearrange("p (k c) -> p k c", k=8)
        for mi in range(nchunks):
            for n in range(3):
                b = dbank()
                mmk(PS(b), lambda kc, mi=mi: w3[:, kc, mi * 128:(mi + 1) * 128],
                    lambda kc, n=n: src[:, kc, n * 512:(n + 1) * 512], 8, [wkey] + srck, [pk(b)])
                evac(mi, n, b)

    def proj_T(pid, tiles, evac, ncols=256):
        wv, wkey = w_next(pid)
        w3 = wv[:].rearrange("p (k c) -> p k c", k=8)
        for t in tiles:
            b = dbank()
            mmk(PS(b, ncols), lambda kc, t=t: hT[:, kc, t * 128:(t + 1) * 128], lambda kc: w3[:, kc, 0:ncols], 8,
                [wkey] + hk, [pk(b)])
            evac(t, b)

    def out_proj(pid, G1):
        wv, wkey = w_next(pid)
        w3 = wv[:].rearrange("p (k c) -> p k c", k=2)
        for n in range(3):
            for m in range(8):
                v = 0 if n < 2 else 1
                b = dbank()
                mmk(PS(b), lambda kc, m=m: w3[:, kc, m * 128:(m + 1) * 128],
                    lambda kc, n=n: yg[:, kc, n * 512:(n + 1) * 512], 2, [wkey, "yg"], [pk(b)])
                xs = x[:, m, n * 512:(n + 1) * 512]
                stt("dve", xs, PS(b), G1(m, v), xs, ALU.mult, ALU.add, [pk(b), xk[m]] + MODK, [xk[m]])

    def transpose_part(ytok, ykey, t2s, fcs):
        for t2 in t2s:
            for fc in fcs:
                for i in range(2):
                    t = t2 * 2 + i
                    mm(PS(5, 128, (fc * 2 + i) * 128), ytok[:, t, fc * 128:(fc + 1) * 128], ident[:], True, True,
                       [ykey, "ident"], [pk(5)])
            for fc in fcs:
                cp("act" if fc == 0 else "dve", yg[:, fc, t2 * 256:(t2 + 1) * 256], PS(5, 256, fc * 256), [pk(5)], ["yg"])

    def transpose_y(ytok, ykey):
        transpose_part(ytok, ykey, range(6), [0, 1])

    def rope_load(rope, half):
        P.dma(stg[0][:], rope_d[:, half * 2048:(half + 1) * 2048], w=["stg0"], noscope=True)
        cp("dve", rope[:], stg[0][:], ["stg0"], ["rope"])

    def rope_apply(z, zkey, R, Rk, rope):
        tab0 = 0
        for n in range(2):
            blk = slice(n * 512, (n + 1) * 512)
            b = dbank()
            mm(PS(b), R[:], z[:, blk], True, True, [Rk, zkey], [pk(b)])
            t1 = ntmp[0]
            t2 = ntmp[1]
            tt("dve", t1[:], z[:, blk], rope[:, tab0 + n * 512: tab0 + (n + 1) * 512], ALU.mult, [zkey, "rope"], ["ntmp0"])
            tt("dve", t2[:], PS(b), rope[:, tab0 + 1024 + n * 512: tab0 + 1024 + (n + 1) * 512], ALU.mult, [pk(b), "rope"], ["ntmp1"])
            tt("pool", z[:, blk], t1[:], t2[:], ALU.add, ["ntmp0", "ntmp1"], [zkey])

    astate = {"i": 0}

    class AttnPipe:
        def __init__(self, pTs, pkey):
            self.pTs = pTs; self.pkey = pkey; self.n = 0; self.pending = None

        def push(self, qT, qkey, kblocks, scale, post, nq=1):
            i = self.n; self.n += 1
            base = 0 if (i % 2 == 0) else 3
            pT = self.pTs[i % 2]; pkey = self.pkey + "%d_" % (i % 2); ob = 6 + (i % 2)
            nk = len(kblocks)
            w = nq * 128
            assert nk * w <= 1536 and (512 % w == 0)
            banks = sorted(set(base + (j * w) // 512 for j in range(nk)))
            for j, (kT, kkey, vap, vkey, mask) in enumerate(kblocks):
                o = psum[:, base * 512 + j * w: base * 512 + (j + 1) * w]
                bk = pk(base + (j * w) // 512)
                mm(o, kT, qT, True, mask is None, [kkey, qkey], [bk])
                if mask is not None:
                    assert nq == 1
                    mm(o, ident[:], mask[0][:], False, True, ["ident", mask[1]], [bk])
            for bi, b in enumerate(banks):
                c0 = bi * 512
                c1 = min(nk * w, c0 + 512)
                act(pT[:, c0:c1], psum[:, base * 512 + c0: base * 512 + c1], AF.Exp, [pk(b)], [pkey + str(bi)], scale=scale)
            prev = self.pending
            self.pending = (kblocks, pT, pkey, ob, post, nq)
            if prev is not None:
                self._pv(prev)

        def _pv(self, call):
            kblocks, pT, pkey, ob, post, nq = call
            nk = len(kblocks)
            w = nq * 128
            for qi in range(nq):
                for j, (kT, kkey, vap, vkey, mask) in enumerate(kblocks):
                    c = j * w + qi * 128
                    mm(PS(ob, 65, qi * 128), pT[:, c:c + 128], vap, j == 0, j == nk - 1, [pkey + str(c // 512), vkey], [pk(ob)])
            post(ob)

        def flush(self):
            if self.pending is not None:
                self._pv(self.pending)
                self.pending = None

    def ada_piece(l, j):
        wv, wkey = w_next(piece_ids(l)["ada"][j])
        w3 = wv[:].rearrange("p (k c) -> p k c", k=8)
        o = psum[0:2, 2 * 512 + (j % 2) * 256: 2 * 512 + (j % 2) * 256 + 256]
        ar_ = adarow[j % 2]
        mmk(o, lambda kc: csil[:, kc, :], lambda kc, w3=w3: w3[:, kc, 0:256], 8, [wkey, "csil"], [pk(2)])
        cp("act" if j % 2 == 0 else "dve", ar_[:], o, [pk(2)], ["adarow%d" % (j % 2)])
        for mi in range(2):
            mc = j * 2 + mi
            mm(PS(5, 2, mc * 2), ar_[:, mi * 128:(mi + 1) * 128], identf[:], True, True, ["adarow%d" % (j % 2), "identf"], [pk(5)])

    def ada_finish(l):
        mod = mods[l % 2]
        modA = modAs[l % 2]
        badd = C("b_ada", l * 48, 48)
        tt("dve", mod[:], PS(5, 96).rearrange("p (a b) -> p a b", b=2), badd.unsqueeze(2).to_broadcast([128, 48, 2]), ALU.add,
           [pk(5), "cst"], ["mod%d" % (l % 2)])
        for w_, (gname, sci) in enumerate((("g_mix", 1), ("g_mlp", 4))):
            gv = C(gname, l * 8, 8).unsqueeze(2).to_broadcast([128, 8, 2])
            stt("dve", modA[:, w_, :, :], mod[:, sci * 8:(sci + 1) * 8, :], 1.0, gv, ALU.add, ALU.mult,
                ["mod%d" % (l % 2), "cst"], ["modA%d" % (l % 2)])

    for l in range(depth):
        pi = piece_ids(l)
        P.scope = None
        mod = mods[l % 2]
        modA = modAs[l % 2]
        if l == 0 and ST('ada'):
            for j in range(24):
                ada_piece(0, j)
            ada_finish(0)
        A1 = lambda c, v: modA[:, 0, c, v:v + 1]
        B1 = lambda c, v: mod[:, 0 * 8 + c, v:v + 1]
        G1 = lambda c, v: mod[:, 2 * 8 + c, v:v + 1]
        A2 = lambda c, v: modA[:, 1, c, v:v + 1]
        B2 = lambda c, v: mod[:, 3 * 8 + c, v:v + 1]
        G2 = lambda c, v: mod[:, 5 * 8 + c, v:v + 1]
        if ST('norm'):
            norm(A1, B1, lambda c, n: hT[:, c, n * 512:(n + 1) * 512], hk)
        if debug and l == 0:
            P.dma(dbg_o["d_hT"], hT[:].rearrange("p a b -> p (a b)"), r=hk)
            P.dma(dbg_o["d_mod"], mod[:].rearrange("p a b -> p (a b)"), r=["mod"])

        barrier()
        with contextlib.ExitStack() as st:
          if ST('ret'):
              def sba(name, shape, dt=F32):
                  rr["ev"] += 1
                  return st.enter_context(nc.sbuf_tensor("a%d_" % rr["ev"] + name, list(shape), dt))
              rq = sba("rq", [128, 2, T], BF16)
              rkm = sba("rkm", [128, 4, T], BF16)
              rkT = sba("rkT", [128, NT, 256], BF16)
              rvT = sba("rvT", [128, NT, 256], BF16)
              rgT = sba("rgT", [128, NT, 256], BF16)
              ytok = sba("ytok", [128, NT, 256], BF16)
              Sbd = sba("Sbd", [128, 4, 128])
              Sbf = sba("Sbf", [128, 4, 8, 128], BF16)
              Ob = sba("Ob", [128, 2, 256])
              Oq = sba("Oq", [128, 2, 256])
              st4 = sba("st4", [128, 6, 8])
              qx8 = [sba("qx%d" % i, [128, 128], BF16) for i in range(8)]
              ktx = [sba("ktx%d" % i, [128, 256], BF16) for i in range(2)]
              dsb4 = [sba("dsb%d" % i, [128, 128]) for i in range(4)]
              scT8 = [sba("scT%d" % i, [128, 128], BF16) for i in range(8)]

              def ev_rq(mi, n, b):
                  cp("act", rq[:, mi, n * 512:(n + 1) * 512], PS(b), [pk(b)], ["rq"])
              proj_F(pi["retF"][0], hT, hk, 2, ev_rq)
              for pp in range(2):
                  def ev_rk(mi, n, b, pp=pp):
                      cp("act" if mi == 0 else "dve", rkm[:, pp * 2 + mi, n * 512:(n + 1) * 512], PS(b), [pk(b)], ["rkm"])
                  proj_F(pi["retF"][1 + pp], hT, hk, 2, ev_rk)
              for (dst, key, pid) in ((rkT, "rkT", pi["retT"][0]), (rvT, "rvT", pi["retT"][1]), (rgT, "rgT", pi["retT"][2])):
                  def ev_t(t, b, dst=dst, key=key):
                      cp("act" if t % 2 == 0 else "dve", dst[:, t, :], PS(b, 256), [pk(b)], [key])
                  proj_T(pid, range(NT), ev_t)
              G(lambda e: e.memset(Sbf[:], 0.0), [], ["Sbf"])
              act(rgT[:].rearrange("p a b -> p (a b)"), rgT[:].rearrange("p a b -> p (a b)"), AF.Silu, ["rgT"], ["rgT"])
              tt("pool", rgT[:], rgT[:], C("gng", l * 256, 256).unsqueeze(1).to_broadcast([128, NT, 256]), ALU.mult, ["rgT", "cst"], ["rgT"])
              segs = [(0, 8, True, 0), (8, 2, False, 0), (10, 2, False, 1)]
              for (t0, ncnk, samp, sj) in segs:
                  if samp:
                      P.dma(Sbd[:], sret_d[:, l, :, :], w=["Sbd0", "Sbd1", "Sbd2", "Sbd3"])
                  else:
                      G(lambda e: e.memset(Sbd[:], 0.0), [], ["Sbd0", "Sbd1", "Sbd2", "Sbd3"])
                  sbank = [0, 1, 2, 6]
                  for step in range(ncnk):
                      for d in range(2):
                          i = step if d == 0 else ncnk - 1 - step
                          t = t0 + i
                          kt = ktx[d]
                          tt("dve", kt[:].rearrange("p (h e) -> p h e", h=4), rkT[:, t, :].rearrange("p (h e) -> p h e", h=4),
                             zeta[:, l * 2 + d, :].unsqueeze(2).to_broadcast([128, 4, 64]), ALU.mult, ["rkT", "zeta"], ["ktx%d" % d])
                          for j in range(2):
                              si = d * 2 + j
                              sk_ = "Sbd%d" % si
                              for hb in range(2):
                                  A(lambda e, si=si, i=i, hb=hb: e.activation(out=Sbf[hb * 64:(hb + 1) * 64, si, i, hb * 64:(hb + 1) * 64],
                                    in_=Sbd[hb * 64:(hb + 1) * 64, si, hb * 64:(hb + 1) * 64], func=AF.Copy), [sk_], ["Sbf"])
                              mm(PS(sbank[si], 128), kt[:, j * 128:(j + 1) * 128], rvT[:, t, j * 128:(j + 1) * 128], True, True,
                                 ["ktx%d" % d, "rvT"], [pk(sbank[si])])
                              stt("dve", Sbd[:, si, :], Sbd[:, si, :], Gp[:, l * 4 + si: l * 4 + si + 1], PS(sbank[si], 128), ALU.mult, ALU.add,
                                  [sk_, "Gp", pk(sbank[si])], [sk_])
                  if not samp:
                      P.dma(osret_o[:, sj, l, :, :], Sbd[:], r=["Sbd0", "Sbd1", "Sbd2", "Sbd3"])
                  def stage_a(t):
                      cols = slice(t * 128, (t + 1) * 128)
                      par = t % 2
                      for j in range(2):
                          for d in range(2):
                              q_ = qx8[par * 4 + d * 2 + j]
                              tt("dve" if d == 0 else "pool", q_[:], rq[:, j, cols], Xi[:, l * 4 + d * 2 + j, :], ALU.mult,
                                 ["rq", "Xi"], ["qx%d" % (par * 4 + d * 2 + j)])
                      for h in range(4):
                          mm(PS(par, 128, h * 128), rkm[:, h, cols], rq[:, h // 2, cols], True, True, ["rkm", "rq"], [pk(par)])
                      for h in range(4):
                          tt("dve", scT8[par * 4 + h][:], PS(par, 128, h * 128), DTm[:, l * 4 + h, :], ALU.mult, [pk(par), "DTm"],
                             ["scT%d" % (par * 4 + h)])
                  stage_a(t0)
                  for i in range(ncnk):
                      t = t0 + i
                      if i + 1 < ncnk:
                          stage_a(t + 1)
                      par = t % 2
                      ob = 6 + (t % 2)
                      for j in range(2):
                          mm(PS(ob, 128, j * 128), qx8[par * 4 + j][:], Sbf[:, j, i, :], True, False, ["qx%d" % (par * 4 + j), "Sbf"], [pk(ob)])
                          mm(PS(ob, 128, j * 128), qx8[par * 4 + 2 + j][:], Sbf[:, 2 + j, i, :], False, False,
                             ["qx%d" % (par * 4 + 2 + j), "Sbf"], [pk(ob)])
                          for hh in range(2):
                              h = j * 2 + hh
                              mm(PS(ob, 64, h * 64), scT8[par * 4 + h][:], rvT[:, t, h * 64:(h + 1) * 64], False, hh == 1,
                                 ["scT%d" % (par * 4 + h), "rvT"], [pk(ob)])
                      cp("act", Ob[:, t % 2, :], PS(ob, 256), [pk(ob)], ["Ob"])
                      if t % 2 == 1:
                          tb = t - 1
                          O3 = Ob[:].rearrange("p t (h e) -> p (t h) e", h=4)
                          Q3 = Oq[:].rearrange("p t (h e) -> p (t h) e", h=4)
                          s1 = st4[:, 0, :]; s2 = st4[:, 1, :]; mean = st4[:, 2, :]; var = st4[:, 3, :]; rs = st4[:, 4, :]
                          V(lambda e: e.tensor_reduce(out=s1, in_=O3, axis=AX.X, op=ALU.add), ["Ob"], ["st4a"])
                          tt("pool", Oq[:], Ob[:], Ob[:], ALU.mult, ["Ob"], ["Oq"])
                          V(lambda e: e.tensor_reduce(out=s2, in_=Q3, axis=AX.X, op=ALU.add), ["Oq"], ["st4b"])
                          ts("dve", mean, s1, 1.0 / 64, None, ALU.mult, None, ["st4a"], ["st4c"])
                          tt("dve", var, mean, mean, ALU.mult, ["st4c"], ["st4d"])
                          stt("dve", var, s2, 1.0 / 64, var, ALU.mult, ALU.subtract, ["st4b", "st4d"], ["st4d"])
                          act(rs, var, AF.Sqrt, ["st4d", "c_eps"], ["st4e"], bias=c_eps[:, 0:1])
                          V(lambda e: e.reciprocal(out=rs, in_=rs), ["st4e"], ["st4e"])
                          tt("dve", Q3, O3, mean.unsqueeze(2).to_broadcast([128, 8, 64]), ALU.subtract, ["Ob", "st4c", "Oq"], ["Oq"])
                          tt("dve", Q3, Q3, rs.unsqueeze(2).to_broadcast([128, 8, 64]), ALU.mult, ["Oq", "st4e"], ["Oq"])
                          tt("pool", ytok[:, tb:tb + 2, :], Oq[:], rgT[:, tb:tb + 2, :], ALU.mult, ["Oq", "rgT"], ["ytok"])
                          transpose_part(ytok, "ytok", [tb // 2], [0, 1])
              (P.dma(dbg_o["d_y_ret"], yg[:].rearrange("p a b -> p (a b)"), r=["yg"]) if (debug and l == 0) else None)
              out_proj(pi["retO"], G1)

        barrier()
        with contextlib.ExitStack() as st:
          if ST('lru'):
              def sba(name, shape, dt=F32):
                  rr["ev"] += 1
                  return st.enter_context(nc.sbuf_tensor("a%d_" % rr["ev"] + name, list(shape), dt))
              lx = sba("lx", [128, 2, T], BF16)
              lgt = sba("lgt", [128, 2, T], BF16)
              BUF = {}
              for sfx, Lb in (("", 1024), ("P", 512)):
                  BUF[sfx] = dict(xc=sba("xc" + sfx, [128, Lb]), xcb=sba("xcb" + sfx, [128, Lb], BF16),
                                  rr=[sba("rr%d%s" % (i, sfx), [128, Lb]) for i in range(2)],
                                  ii=[sba("ii%d%s" % (i, sfx), [128, Lb]) for i in range(2)],
                                  uu=[sba("uu%d%s" % (i, sfx), [128, Lb]) for i in range(2)])
              slo2 = [sba("slo%d" % i, [128, 2 * 4]) for i in range(2)]
              lruwb = sba("lruwb", [128, 8, 128], BF16)
              for (dst, key, pid) in ((lx, "lx", pi["lruF"][0]), (lgt, "lgt", pi["lruF"][1])):
                  def ev(mi, n, b, dst=dst, key=key):
                      cp("act" if n % 2 == 0 else "dve", dst[:, mi, n * 512:(n + 1) * 512], PS(b), [pk(b)], [key])
                  proj_F(pid, hT, hk, 2, ev)
              P.dma(stg[0][:, 0:1024], lruw_d[:, l * 1024:(l + 1) * 1024], w=["stg0"], noscope=True)
              cp("dve", lruwb[:].rearrange("p a b -> p (a b)"), stg[0][:, 0:1024], ["stg0"], ["lruwb"])
              for cc in range(2):
                  act(lgt[:, cc, :], lgt[:, cc, :], AF.Gelu_apprx_tanh, ["lgt"], ["lgt"])
              for cc in range(2):
                for (c0, L, samp, nseq) in ((0, 1024, True, 1), (1024, 512, False, 2)):
                  sfx = "" if samp else "P"
                  B_ = BUF[sfx]
                  xc = B_["xc"]; xcb = B_["xcb"]; rr2 = B_["rr"]; ii2 = B_["ii"]; uu2 = B_["uu"]
                  kx = "xc" + sfx; kxb = "xcb" + sfx
                  kr = ["rr0" + sfx, "rr1" + sfx]; ki_ = ["ii0" + sfx, "ii1" + sfx]; ku = ["uu0" + sfx, "uu1" + sfx]
                  Ls = L // nseq
                  if True:
                      xi_ = lx[:, cc, c0:c0 + L]
                      cw = lambda k: C("conv_w", l * 8 + k * 2 + cc, 1)
                      ts("dve", xc[:, 0:L], xi_, cw(2), C("conv_b", l * 2 + cc, 1), ALU.mult, ALU.add, ["lx", "cst"], [kx])
                      for sq_ in range(nseq):
                          b0 = sq_ * Ls
                          stt("dve", xc[:, b0 + 2:b0 + Ls], xi_[:, b0:b0 + Ls - 2], cw(0), xc[:, b0 + 2:b0 + Ls], ALU.mult, ALU.add, ["lx", "cst", kx], [kx])
                          stt("dve", xc[:, b0 + 1:b0 + Ls], xi_[:, b0:b0 + Ls - 1], cw(1), xc[:, b0 + 1:b0 + Ls], ALU.mult, ALU.add, ["lx", "cst", kx], [kx])
                          stt("dve", xc[:, b0:b0 + Ls - 1], xi_[:, b0 + 1:b0 + Ls], cw(3), xc[:, b0:b0 + Ls - 1], ALU.mult, ALU.add, ["lx", "cst", kx], [kx])
                      cp("pool", xcb[:, 0:L], xc[:, 0:L], [kx], [kxb])
                      pidx_ = [l * 4 + d * 2 + cc for d in range(2)]
                      for d in range(2):
                          for (dst, key, gsel, bname) in ((rr2[d], kr[d], 0, "b_a"), (ii2[d], ki_[d], 1, "b_x")):
                              for n in range((L + 511) // 512):
                                  w_ = min(512, L - n * 512)
                                  b = dbank()
                                  mm(PS(b, w_), lruwb[:, d * 4 + gsel * 2 + cc, :], xcb[:, n * 512:n * 512 + w_], True, True,
                                     ["lruwb", kxb], [pk(b)])
                                  act(dst[:, n * 512:n * 512 + w_], PS(b, w_), AF.Sigmoid, [pk(b), "cst"], [key], bias=C(bname, pidx_[d], 1))
                      for d in range(2):
                          tt("pool" if d == 1 else "dve", ii2[d][:, 0:L], ii2[d][:, 0:L], xc[:, 0:L], ALU.mult, [ki_[d], kx], [ki_[d]])
                      for d in range(2):
                          act(uu2[d][:, 0:L], rr2[d][:, 0:L], AF.Exp, [kr[d], "sa16"], [ku[d]], scale=sa16[:, pidx_[d]:pidx_[d] + 1])
                          act(rr2[d][:, 0:L], rr2[d][:, 0:L], AF.Exp, [kr[d], "sa8"], [kr[d]], scale=sa8[:, pidx_[d]:pidx_[d] + 1])
                      for d in range(2):
                          act(uu2[d][:, 0:L], uu2[d][:, 0:L], AF.Sqrt, [ku[d], "c_one"], [ku[d]], bias=c_one[:, 0:1], scale=-1.0)
                      for d in range(2):
                          tt("dve", uu2[d][:, 0:L], uu2[d][:, 0:L], ii2[d][:, 0:L], ALU.mult, [ku[d], ki_[d]], [ku[d]])
                      for sq_ in range(nseq):
                          b0 = sq_ * Ls
                          for d in range(2):
                              hd = ii2[d]
                              init = C("h0", pidx_[d], 1) if samp else 0.0
                              if d == 0:
                                  V(lambda e, hd=hd, init=init, b0=b0, Ls=Ls, rr2=rr2, uu2=uu2: e.tensor_tensor_scan(out=hd[:, b0:b0 + Ls],
                                    data0=rr2[0][:, b0:b0 + Ls], data1=uu2[0][:, b0:b0 + Ls], initial=init, op0=ALU.mult, op1=ALU.add),
                                    [kr[0], ku[0], "cst"], [ki_[0]])
                                  fin = hd[:, b0 + Ls - 1:b0 + Ls]
                              else:
                                  V(lambda e, hd=hd, init=init, b0=b0, Ls=Ls, rr2=rr2, uu2=uu2: e.tensor_tensor_scan(out=hd[:, b0:b0 + Ls][:, ::-1],
                                    data0=rr2[1][:, b0:b0 + Ls][:, ::-1], data1=uu2[1][:, b0:b0 + Ls][:, ::-1],
                                    initial=init, op0=ALU.mult, op1=ALU.add), [kr[1], ku[1], "cst"], [ki_[1]])
                                  fin = hd[:, b0:b0 + 1]
                              if not samp:
                                  cp("pool", slo2[sq_][:, d * 2 + cc: d * 2 + cc + 1], fin, [ki_[d]], ["slo%d" % sq_])
                      tt("dve", ii2[0][:, 0:L], ii2[0][:, 0:L], ii2[1][:, 0:L], ALU.add, [ki_[0], ki_[1]], [ki_[0]])
                      tt("dve", yg[:, cc, c0:c0 + L], ii2[0][:, 0:L], lgt[:, cc, c0:c0 + L], ALU.mult, [ki_[0], "lgt"], ["yg"])
              for sj in range(2):
                  P.dma(oslru_o[:, (sj * DEPTH + l) * 4:(sj * DEPTH + l) * 4 + 4], slo2[sj][:, 0:4], r=["slo%d" % sj])
              (P.dma(dbg_o["d_y_lru"], yg[:].rearrange("p a b -> p (a b)"), r=["yg"]) if (debug and l == 0) else None)
              out_proj(pi["lruO"], G1)

        barrier()
        with contextlib.ExitStack() as st:
          if ST('swa'):
              def sba(name, shape, dt=F32):
                  rr["ev"] += 1
                  return st.enter_context(nc.sbuf_tensor("a%d_" % rr["ev"] + name, list(shape), dt))
              sq = sba("sq", [128, 2, T], BF16)
              sk = sba("sk", [128, T], BF16)
              kms = [sba("kms%d" % i, [128, 256 + T], BF16) for i in range(2)]
              svaug = sba("svaug", [128, NT, 2, 80], BF16)
              vctx = sba("vctx", [128, 2, 2, 80], BF16)
              ckf = sba("ckf", [128, 256])
              cvf = sba("cvf", [128, 2, 128])
              pT = [sba("pT%d" % i, [128, 640], BF16) for i in range(2)]
              Osw = [sba("Osw", [128, NT, 65])] * 2
              den = sba("den", [128, NT])
              ytok = sba("ytok", [128, NT, 256], BF16)
              kvst = [sba("kvst%d" % i, [128, 256]) for i in range(2)]
              cp("dve", svaug[:, :, :, 64:66], ones[:, 0:NT * 4].rearrange("p (a b c) -> p a b c", a=NT, b=2), ["ones"], ["svaug"])
              cp("dve", vctx[:, :, :, 64:66], ones[:, 0:8].rearrange("p (a b c) -> p a b c", a=2, b=2), ["ones"], ["vctx"])

              def ev_sq(mi, n, b):
                  cp("act", sq[:, mi, n * 512:(n + 1) * 512], PS(b), [pk(b)], ["sq%d" % mi])
              proj_F(pi["swaF"][0], hT, hk, 2, ev_sq)

              def ev_sk(mi, n, b):
                  cp("act", sk[:, n * 512:(n + 1) * 512], PS(b), [pk(b)], ["sk"])
              proj_F(pi["swaF"][1], hT, hk, 1, ev_sk)

              def ev_swt(t, b):
                  for hv in range(2):
                      cp("act", svaug[:, t, hv, 0:64], PS(b, 64, 128 + hv * 64), [pk(b)], ["svaug"])
                  if t >= 8:
                      kb = kvst[t % 2]
                      cp("dve", kb[:], PS(b, 256), [pk(b)], ["kvst%d" % (t % 2)])
                      P.dma(okv_o[t - 8, :, l, 0:256], kb[:], r=["kvst%d" % (t % 2)])
              proj_T(pi["swaT"][0], range(NT), ev_swt)
              rope = sba("rope", [128, 2048], BF16)
              if ST('swa_rope'):
                  rope_load(rope, 0)
                  rope_apply(sq[:, 0, :], "sq0", R64, "R64", rope)
                  rope_apply(sq[:, 1, :], "sq1", R64, "R64", rope)
                  rope_apply(sk[:], "sk", R64, "R64", rope)
              P.dma(ckf[:], cks_d[:, l, :], w=["ckf"])
              P.dma(cvf[:], cvs_d[:, l, :, :], w=["cvf"])
              for cb_ in range(2):
                  for hv in range(2):
                      cp("dve", vctx[:, cb_, hv, 0:64], cvf[:, cb_, hv * 64:(hv + 1) * 64], ["cvf"], ["vctx"])
              for m in range(2):
                  ts("dve", kms[m][:, 0:256], ckf[:], C("hm", m, 1), None, ALU.mult, None, ["ckf", "cst"], ["kms%d" % m])
                  ts("dve", kms[m][:, 256:256 + T], sk[:], C("hm", m, 1), None, ALU.mult, None, ["sk", "cst"], ["kms%d" % m])
              if debug and l == 0:
                  P.dma(dbg_o["d_sq"], sq[:].rearrange("p a b -> p (a b)"), r=["sq0", "sq1"])
                  P.dma(dbg_o["d_kms0"], kms[0][:], r=["kms0"])
                  P.dma(dbg_o["d_kms1"], kms[1][:], r=["kms1"])
                  P.dma(dbg_o["d_sva"], svaug[:].rearrange("p a b c -> p (a b c)"), r=["svaug"])
                  P.dma(dbg_o["d_vctx"], vctx[:].rearrange("p a b c -> p (a b c)"), r=["vctx"])
              apipe = AttnPipe(pT, "pT")
              for h in (range(4) if ST('swa_attn') else []):
                  qc = 0 if h in (0, 3) else 1
                  kvh = h // 2
                  for t in list(range(8)) + [8, 10]:
                      nq = 1 if t < 8 else 2
                      qT = sq[:, qc, t * 128:(t + nq) * 128]
                      kb = []
                      if t < 8:
                          for cb in range(2):
                              kb.append((kms[kvh][:, cb * 128:(cb + 1) * 128], "kms%d" % kvh, vctx[:, cb, kvh, 0:65], "vctx", None))
                          for dt_ in (-1, 0, 1):
                              tk = t + dt_
                              if tk < 0 or tk > 7:
                                  continue
                              mask = None if dt_ == 0 else ((mlo, "mlo") if dt_ == -1 else (mhi, "mhi"))
                              kb.append((kms[kvh][:, 256 + tk * 128: 256 + (tk + 1) * 128], "kms%d" % kvh, svaug[:, tk, kvh, 0:65], "svaug", mask))
                      else:
                          for tk in (t, t + 1):
                              kb.append((kms[kvh][:, 256 + tk * 128: 256 + (tk + 1) * 128], "kms%d" % kvh, svaug[:, tk, kvh, 0:65], "svaug", None))

                      def post(ob, t=t, h=h, nq=nq):
                          Oh = Osw[h % 2]
                          for qi in range(nq):
                              cp("dve", Oh[:, t + qi, :], PS(ob, 65, qi * 128), [pk(ob)], ["Osw"])
                          if t + nq == NT:
                              ts("dve", den[:], Oh[:, :, 64], esink[:, l * 4 + h: l * 4 + h + 1], None, ALU.add, None,
                                 ["Osw", "esink"], ["den"])
                              V(lambda e: e.reciprocal(out=den[:], in_=den[:]), ["den"], ["den"])
                              tt("dve", ytok[:, :, h * 64:(h + 1) * 64], Oh[:, :, 0:64], den[:].unsqueeze(2).to_broadcast([128, NT, 64]),
                                 ALU.mult, ["Osw", "den"], ["ytok"])
                              if h % 2 == 1:
                                  transpose_part(ytok, "ytok", range(6), [h // 2])
                      apipe.push(qT, "sq%d" % qc, kb, 0.125, post, nq=nq)
              apipe.flush()
              (P.dma(dbg_o["d_y_swa"], yg[:].rearrange("p a b -> p (a b)"), r=["yg"]) if (debug and l == 0) else None)
              out_proj(pi["swaO"], G1)

        barrier()
        with contextlib.ExitStack() as st:
          if ST('diff'):
              def sba(name, shape, dt=F32):
                  rr["ev"] += 1
                  return st.enter_context(nc.sbuf_tensor("a%d_" % rr["ev"] + name, list(shape), dt))
              dq = sba("dq", [128, 2, T], BF16)
              dk = sba("dk", [128, 2, T], BF16)
              kmd = [sba("kmd%d" % i, [128, 256 + T], BF16) for i in range(2)]
              dvaug = sba("dvaug", [128, NT, 4, 80], BF16)
              vctx = sba("vctxd", [128, 2, 4, 80], BF16)
              ckf = sba("ckfd", [128, 2, 256])
              cvf = sba("cvfd", [128, 2, 256])
              pT = [sba("pTd%d" % i, [128, 1280], BF16) for i in range(2)]
              Oall = [sba("Oall", [128, 2, NT, 65])] * 2
              rec = sba("rec", [128, 2, NT])
              y0 = sba("y0", [128, NT, 64])
              y1 = sba("y1", [128, NT, 64])
              ss = sba("ss", [128, NT])
              ytok = sba("ytokd", [128, NT, 256], BF16)
              kvst = [sba("kvstd%d" % i, [128, 256]) for i in range(2)]
              cp("dve", dvaug[:, :, :, 64:66], ones[:, 0:NT * 8].rearrange("p (a b c) -> p a b c", a=NT, b=4), ["ones"], ["dvaug"])
              cp("dve", vctx[:, :, :, 64:66], ones[:, 0:16].rearrange("p (a b c) -> p a b c", a=2, b=4), ["ones"], ["vctxd"])
              for (dst, key, pid) in ((dq, "dq", pi["difF"][0]), (dk, "dk", pi["difF"][1])):
                  def ev(mi, n, b, dst=dst, key=key):
                      cp("act", dst[:, mi, n * 512:(n + 1) * 512], PS(b), [pk(b)], [key + str(mi)])
                  proj_F(pid, hT, hk, 2, ev)

              def ev_dkt(t, b):
                  kb = kvst[t % 2]
                  cp("dve", kb[:], PS(b, 256), [pk(b)], ["kvstd%d" % (t % 2)])
                  P.dma(okv_o[t - 8, :, l, 256:512], kb[:], r=["kvstd%d" % (t % 2)])
              proj_T(pi["difT"][0], range(8, NT), ev_dkt)

              def ev_dvt(t, b):
                  for hv in range(4):
                      cp("act", dvaug[:, t, hv, 0:64], PS(b, 64, hv * 64), [pk(b)], ["dvaug"])
                  if t >= 8:
                      kb = kvst[t % 2]
                      cp("dve", kb[:], PS(b, 256), [pk(b)], ["kvstd%d" % (t % 2)])
                      P.dma(okv_o[t - 8, :, l, 512:768], kb[:], r=["kvstd%d" % (t % 2)])
              proj_T(pi["difT"][1], range(NT), ev_dvt)
              rope = sba("roped", [128, 2048], BF16)
              rope_load(rope, 1)
              for cc in range(2):
                  rope_apply(dq[:, cc, :], "dq%d" % cc, R32, "R32", rope)
                  rope_apply(dk[:, cc, :], "dk%d" % cc, R32, "R32", rope)
              P.dma(ckf[:], ckd_d[:, l, :, :], w=["ckfd"])
              P.dma(cvf[:], cvd_d[:, l, :, :], w=["cvfd"])
              for cb_ in range(2):
                  for hv in range(4):
                      cp("dve", vctx[:, cb_, hv, 0:64], cvf[:, cb_, hv * 64:(hv + 1) * 64], ["cvfd"], ["vctxd"])
              ki = 0
              apipe = AttnPipe(pT, "pTd")
              for h in range(4):
                  cc = h // 2
                  hh = h % 2
                  for comp in range(2):
                      km = kmd[ki % 2]
                      kk = "kmd%d" % (ki % 2)
                      ki += 1
                      mcol = C("cm4", hh * 2 + comp, 1)
                      ts("dve", km[:, 0:256], ckf[:, cc, :], mcol, None, ALU.mult, None, ["ckfd", "cst"], [kk])
                      ts("dve", km[:, 256:256 + T], dk[:, cc, :], mcol, None, ALU.mult, None, ["dk%d" % cc, "cst"], [kk])
                      for t in list(range(8)) + [8, 10]:
                          nq = 1 if t < 8 else 2
                          qT = dq[:, cc, t * 128:(t + nq) * 128]
                          kb = []
                          if t < 8:
                              for cb in range(2):
                                  kb.append((km[:, cb * 128:(cb + 1) * 128], kk, vctx[:, cb, h, 0:65], "vctxd", None))
                              for tk in range(8):
                                  kb.append((km[:, 256 + tk * 128: 256 + (tk + 1) * 128], kk, dvaug[:, tk, h, 0:65], "dvaug", None))
                          else:
                              for tk in (t, t + 1):
                                  kb.append((km[:, 256 + tk * 128: 256 + (tk + 1) * 128], kk, dvaug[:, tk, h, 0:65], "dvaug", None))

                          def post(ob, t=t, h=h, comp=comp, nq=nq):
                              Oa = Oall[h % 2]
                              ok = "Oall"
                              for qi in range(nq):
                                  cp("dve", Oa[:, comp, t + qi, :], PS(ob, 65, qi * 128), [pk(ob)], [ok])
                              if t + nq == NT and comp == 1:
                                  V(lambda e: e.reciprocal(out=rec[:], in_=Oa[:, :, :, 64]), [ok], ["rec"])
                                  tt("dve", y0[:], Oa[:, 0, :, 0:64], rec[:, 0, :].unsqueeze(2).to_broadcast([128, NT, 64]), ALU.mult, [ok, "rec"], ["y0"])
                                  tt("dve", y1[:], Oa[:, 1, :, 0:64], rec[:, 1, :].unsqueeze(2).to_broadcast([128, NT, 64]), ALU.mult, [ok, "rec"], ["y1"])
                                  stt("dve", y0[:], y1[:], nlam[:, l:l + 1], y0[:], ALU.mult, ALU.add, ["y1", "nlam", "y0"], ["y0"])
                                  tt("pool", y1[:], y0[:], y0[:], ALU.mult, ["y0"], ["y1"])
                                  V(lambda e: e.tensor_reduce(out=ss[:], in_=y1[:], axis=AX.X, op=ALU.add), ["y1"], ["ss"])
                                  act(ss[:], ss[:], AF.Sqrt, ["ss", "c_eps"], ["ss"], bias=c_eps[:, 0:1], scale=1.0 / 64)
                                  V(lambda e: e.reciprocal(out=ss[:], in_=ss[:]), ["ss"], ["ss"])
                                  tt("dve", y0[:], y0[:], ss[:].unsqueeze(2).to_broadcast([128, NT, 64]), ALU.mult, ["y0", "ss"], ["y0"])
                                  tt("dve", ytok[:, :, h * 64:(h + 1) * 64], y0[:], dgt[:, l, :].unsqueeze(1).to_broadcast([128, NT, 64]),
                                     ALU.mult, ["y0", "dgt"], ["ytokd"])
                          apipe.push(qT, "dq%d" % cc, kb, 32 ** -0.5, post, nq=nq)
              apipe.flush()
              transpose_y(ytok, "ytokd")
              (P.dma(dbg_o["d_y_diff"], yg[:].rearrange("p a b -> p (a b)"), r=["yg"]) if (debug and l == 0) else None)
              out_proj(pi["difO"], G1)

        barrier()
        if ST('ffn'):
            norm(A2, B2, lambda c, n: hT[:, c, n * 512:(n + 1) * 512], hk)
        with contextlib.ExitStack() as st:
          if ST('ffn'):
              hid = st.enter_context(nc.sbuf_tensor("a_hid_%d" % l, [128, 8, T], BF16))
              rl = [st.enter_context(nc.sbuf_tensor("a_rl%d_%d" % (i, l), [128, 512], BF16)) for i in range(2)]
              xs = [st.enter_context(nc.sbuf_tensor("a_xstg%d_%d" % (i, l), [128, 2048], F32)) for i in range(3)]
              xw = [st.enter_context(nc.sbuf_tensor("a_xwbf%d_%d" % (i, l), [128, 2048], BF16)) for i in range(2)]
              w_extra_begin(xs, xw)
              nada = [0]

              def ada_tick():
                  if l + 1 < depth and nada[0] < 24 and ST('ada'):
                      ada_piece(l + 1, nada[0]); nada[0] += 1
              for q in range(4):
                  f1, f2 = pi["ff"][q]
                  for j in range(4):
                      def ev1(mi, n, b, j=j):
                          r_ = rl[(mi + n) % 2]
                          act(r_[:], PS(b), AF.Relu, [pk(b)], ["rl%d" % ((mi + n) % 2)])
                          tt("pool" if (mi + n) % 2 == 0 else "dve", hid[:, j * 2 + mi, n * 512:(n + 1) * 512], r_[:], r_[:], ALU.mult, ["rl%d" % ((mi + n) % 2)], ["hid"])
                      proj_F(f1[j], hT, hk, 2, ev1)
                      ada_tick()
                  for j in range(4):
                      def ev2(mi, n, b, j=j):
                          m = j * 2 + mi
                          v = 0 if n < 2 else 1
                          xs = x[:, m, n * 512:(n + 1) * 512]
                          stt("dve", xs, PS(b), G2(m, v), xs, ALU.mult, ALU.add, [pk(b), xk[m]] + MODK, [xk[m]])
                      proj_F(f2[j], hid, ["hid"], 2, ev2)
                      ada_tick()
              if l + 1 < depth and ST('ada'):
                  ada_finish(l + 1)
              w_extra_end()

    P.scope = None
    NOST = 8
    ostg = [sb("ostg%d" % i, [128, 512]) for i in range(NOST)]
    okeys = ["ostg%d" % (c % 2) for c in range(8)]
    cnt = {"i": 0}
    for n in range(3):
        blk = slice(n * 512, (n + 1) * 512)
        for c in range(8):
            s = sqt[c % 2]
            act(s[:], x[:, c, blk], AF.Square, [xk[c]], ["sqt%d" % (c % 2)])
            mm(PS(5), ones[:], s[:], c == 0, c == 7, ["ones", "sqt%d" % (c % 2)], [pk(5)])
        act(rstd[:], PS(5), AF.Sqrt, [pk(5), "c_eps"], ["rstd"], bias=c_eps[:, 0:1], scale=1.0 / 1024.0)
        V(lambda e: e.reciprocal(out=rstd[:], in_=rstd[:]), ["rstd"], ["rstd"])
        for c in range(8):
            oi = (n * 8 + c) % NOST
            o = ostg[oi]
            stt("dve", o[:], x[:, c, blk], C("g_fin", c, 1), rstd[:], ALU.mult, ALU.mult, [xk[c], "rstd", "cst"], ["ostg%d" % oi])
            P.dma(yT_o[c * 128:(c + 1) * 128, blk], o[:], r=["ostg%d" % oi])
    P.emit()
    P.stats['minfree'] = minfree['v']
    return nc, P


def _consts():
    coff, CW = cst_layout()
    c = np.zeros((128, CW), np.float32)

    def put(name, arr):
        o, w = coff[name]
        c[:, o:o + w] = np.asarray(arr, np.float32).reshape(128, w)
    put("ident", np.eye(128))
    R64 = np.zeros((128, 128)); R32 = np.zeros((128, 128))
    for blk in range(2):
        for i in range(64):
            if i < 32:
                R64[blk * 64 + i, blk * 64 + i + 32] = -1.0
            else:
                R64[blk * 64 + i, blk * 64 + i - 32] = 1.0
    for blk in range(4):
        for i in range(32):
            if i < 16:
                R32[blk * 32 + i, blk * 32 + i + 16] = -1.0
            else:
                R32[blk * 32 + i, blk * 32 + i - 16] = 1.0
    put("R64T", R64.T); put("R32T", R32.T)
    j = np.arange(128)[:, None]; i = np.arange(128)[None, :]
    put("mlo", np.where(j >= i, 0.0, -480.0))
    put("mhi", np.where(j <= i, 0.0, -480.0))
    m = np.arange(128)[:, None]; n = np.arange(128)[None, :]
    put("Ef", np.maximum(n - m, 0)); put("Eb", np.maximum(m - n, 0))
    put("Lf8", (n >= m) / 8.0); put("Lb8", (m >= n) / 8.0)
    bd = np.zeros((128, 128)); bd[:64, :64] = 1; bd[64:, 64:] = 1
    put("BD", bd)
    put("idx1", np.broadcast_to(np.arange(128)[None, :] + 1.0, (128, 128)))
    put("idxr", np.broadcast_to(128.0 - np.arange(128)[None, :], (128, 128)))
    put("cmf", (127.0 - np.arange(128))[:, None]); put("cmb", (np.arange(128) * 1.0)[:, None])
    p = np.arange(128)
    put("hm", np.stack([p < 64, p >= 64], 1))
    put("cm4", np.stack([(p // 32) == k for k in range(4)], 1))
    pos = np.arange(1024)
    row = (pos // 64).astype(np.float32); col = (pos % 64).astype(np.float32)

    def tab(dim):
        nn = dim // 4
        inv = (10000.0 ** (-np.arange(nn, dtype=np.float32) / nn)).astype(np.float32)
        ang = np.concatenate([row[:, None] * inv, col[:, None] * inv], -1).astype(np.float32)
        idx = (np.arange(128) % dim) % (dim // 2)
        return np.cos(ang)[:, idx].T.astype(np.float32), np.sin(ang)[:, idx].T.astype(np.float32)
    c64, s64 = tab(64); c32, s32 = tab(32)
    rope = np.concatenate([c64, s64, c32, s32], 1).astype(np.float32)
    return c, rope, coff


def _fm(v):
    v = np.asarray(v)
    n = v.shape[-1] // 128
    lead = v.shape[:-1]
    return np.moveaxis(v.reshape(lead + (n, 128)), -1, 0)


_CACHE = {}


def kernel(**inp):
    I = {k: np.asarray(v) for k, v in inp.items()}
    if "prog" not in _CACHE:
        _CACHE["prog"] = build_program()
    nc, P = _CACHE["prog"]
    in_maps = prep_inputs(I)
    res = run_bass_kernel_spmd(nc, in_maps, core_ids=list(range(8)))
    return assemble(res.results)


def prep_inputs(I, cores=range(8)):
    cbase, rope, coff = _consts()

    wts = np.zeros((DEPTH * PPL, 128, 2048), np.float32)

    def kpiece(W, cols):
        out = np.zeros((128, 8, 256), np.float32)
        cols = np.asarray(cols)
        valid = cols >= 0
        Wc = W[:, cols[valid]].reshape(8, 128, -1)
        out[:, :, np.nonzero(valid)[0]] = np.transpose(Wc, (1, 0, 2))
        return out.reshape(128, 2048)
    ar = np.arange
    for l in range(DEPTH):
        pi = piece_ids(l)
        wa, wi, wo, w1, w2 = I["w_ada"][l], I["w_in"][l], I["w_out"][l], I["w_ff1"][l], I["w_ff2"][l]
        for j in range(24):
            wts[pi["ada"][j]] = kpiece(wa, ar(j * 256, (j + 1) * 256))
        wts[pi["retF"][0]] = kpiece(wi, ar(0, 256))
        neg = -np.ones(64, int)
        wts[pi["retF"][1]] = kpiece(wi, np.concatenate([ar(256, 320), neg, neg, ar(320, 384)]))
        wts[pi["retF"][2]] = kpiece(wi, np.concatenate([ar(384, 448), neg, neg, ar(448, 512)]))
        wts[pi["retT"][0]] = kpiece(wi, ar(256, 512))
        wts[pi["retT"][1]] = kpiece(wi, ar(512, 768))
        wts[pi["retT"][2]] = kpiece(wi, ar(768, 1024))
        wts[pi["lruF"][0]] = kpiece(wi, ar(1024, 1280))
        wts[pi["lruF"][1]] = kpiece(wi, ar(1280, 1536))
        sqb = 1536
        wts[pi["swaF"][0]] = kpiece(wi, np.concatenate([ar(sqb, sqb + 64), ar(sqb + 192, sqb + 256), ar(sqb + 64, sqb + 192)]))
        wts[pi["swaF"][1]] = kpiece(wi, np.concatenate([ar(1792, 1920), -np.ones(128, int)]))
        wts[pi["swaT"][0]] = kpiece(wi, ar(1792, 2048))
        wts[pi["difF"][0]] = kpiece(wi, ar(2048, 2304))
        wts[pi["difF"][1]] = kpiece(wi, ar(2304, 2560))
        wts[pi["difT"][0]] = kpiece(wi, ar(2304, 2560))
        wts[pi["difT"][1]] = kpiece(wi, ar(2560, 2816))
        for g, name in enumerate(("retO", "lruO", "swaO", "difO")):
            blkw = wo[g * 256:(g + 1) * 256, :].reshape(2, 128, 1024)
            wts[pi[name]] = np.transpose(blkw, (1, 0, 2)).reshape(128, 2048)
        for q in range(4):
            f1, f2 = pi["ff"][q]
            for j in range(4):
                wts[f1[j]] = kpiece(w1, ar(q * 1024 + j * 256, q * 1024 + (j + 1) * 256))
                wts[f2[j]] = kpiece(w2[q * 1024:(q + 1) * 1024, :], ar(j * 256, (j + 1) * 256))
    lruw = np.zeros((128, DEPTH, 8, 128), np.float32)
    for l in range(DEPTH):
        for d in range(2):
            for gsel, nm in enumerate(("lru_w_a", "lru_w_x")):
                for cc in range(2):
                    for bb in range(2):
                        blk = I[nm][l, d, cc * 2 + bb]
                        lruw[bb * 64:(bb + 1) * 64, l, d * 4 + gsel * 2 + cc, bb * 64:(bb + 1) * 64] = blk
    lruw = lruw.reshape(128, DEPTH * 8 * 128)

    def putc(c, name, arr):
        o, w = coff[name]
        c[:, o:o + w] = np.asarray(arr, np.float32).reshape(128, w)

    in_maps = []
    for core in cores:
        c = cbase.copy()
        putc(c, "b_ada", _fm(I["b_ada"]))
        putc(c, "g_mix", _fm(I["norm_mix_g"])); putc(c, "g_mlp", _fm(I["norm_mlp_g"])); putc(c, "g_fin", _fm(I["final_norm_g"]))
        cv = np.stack([I["c"][core], I["c_ctx"]], 0)
        putc(c, "cT", np.transpose(_fm(cv), (0, 2, 1)))
        putc(c, "conv_w", _fm(I["lru_conv_w"])); putc(c, "conv_b", _fm(I["lru_conv_b"]))
        putc(c, "b_a", _fm(I["lru_b_a"])); putc(c, "b_x", _fm(I["lru_b_x"])); putc(c, "lam", _fm(I["lru_lambda"]))
        putc(c, "h0", _fm(I["state_lru"][core]))
        rd = I["ret_decay"]
        putc(c, "rd_row", np.broadcast_to(rd.reshape(1, 32), (128, 32)))
        rp = np.zeros((128, DEPTH, 2, 2), np.float32)
        for j in range(2):
            rp[:64, :, :, j] = rd[:, :, 2 * j]; rp[64:, :, :, j] = rd[:, :, 2 * j + 1]
        putc(c, "rd_pair", rp)
        putc(c, "sink", np.broadcast_to(I["swa_sink"].reshape(1, 16), (128, 16)))
        putc(c, "dlam", np.broadcast_to(I["diff_lambda"].reshape(1, -1), (128, DEPTH * 128)))
        putc(c, "dng", np.broadcast_to(I["diff_norm_g"].reshape(1, -1), (128, DEPTH * 64)))
        putc(c, "gng", np.broadcast_to(I["ret_gn_g"].reshape(1, -1), (128, DEPTH * 256)))
        xall = np.concatenate([I["x_sample"][core], I["x_prompt"][2 * core], I["x_prompt"][2 * core + 1]], 0)
        cks = np.transpose(I["cache_swa_k"][core].reshape(DEPTH, 256, 128), (2, 0, 1))
        cvs = np.transpose(I["cache_swa_v"][core].reshape(DEPTH, 2, 128, 128), (2, 0, 1, 3))
        ckd = np.transpose(I["cache_diff_k"][core].reshape(DEPTH, 256, 2, 128), (3, 0, 2, 1))
        cvd = np.transpose(I["cache_diff_v"][core].reshape(DEPTH, 2, 128, 256), (2, 0, 1, 3))
        sr = I["state_ret"][core]
        sret = np.zeros((128, DEPTH, 4, 128), np.float32)
        for d in range(2):
            for j in range(2):
                sret[:64, :, d * 2 + j, :64] = np.transpose(sr[:, d, 2 * j], (1, 0, 2))
                sret[64:, :, d * 2 + j, 64:] = np.transpose(sr[:, d, 2 * j + 1], (1, 0, 2))
        in_maps.append({"xT": np.ascontiguousarray(xall.T), "wts": wts, "cst": c, "rope": rope, "lruw": lruw,
                        "cks": np.ascontiguousarray(cks), "cvs": np.ascontiguousarray(cvs), "ckd": np.ascontiguousarray(ckd),
                        "cvd": np.ascontiguousarray(cvd), "sret": sret})
    return in_maps


def assemble(R):
    y_prompt = np.zeros((16, 256, 1024), np.float32); y_sample = np.zeros((8, 1024, 1024), np.float32)
    nsr = np.zeros((16, DEPTH, 2, 4, 64, 64), np.float32); nsl = np.zeros((16, DEPTH, 2, 256), np.float32)
    nsk = np.zeros((16, DEPTH, 256, 2, 64), np.float32); nsv = np.zeros((16, DEPTH, 256, 2, 64), np.float32)
    ndk = np.zeros((16, DEPTH, 256, 4, 64), np.float32); ndv = np.zeros((16, DEPTH, 256, 4, 64), np.float32)
    for core in range(8):
        r = R[core]
        y = r["yT"].T
        y_sample[core] = y[0:1024]
        osr = r["o_sret"]
        osl = r["o_slru"].reshape(128, 2, DEPTH, 2, 2)
        okv = r["o_kv"]
        for sj in range(2):
            b = 2 * core + sj
            y_prompt[b] = y[1024 + 256 * sj: 1280 + 256 * sj]
            for d in range(2):
                for j in range(2):
                    nsr[b, :, d, 2 * j] = np.transpose(osr[:64, sj, :, d * 2 + j, :64], (1, 0, 2))
                    nsr[b, :, d, 2 * j + 1] = np.transpose(osr[64:, sj, :, d * 2 + j, 64:], (1, 0, 2))
            nsl[b] = np.transpose(osl[:, sj], (1, 2, 3, 0)).reshape(DEPTH, 2, 256)
            kv = np.transpose(okv[2 * sj:2 * sj + 2], (2, 0, 1, 3)).reshape(DEPTH, 256, 768)
            nsk[b] = kv[:, :, 0:128].reshape(DEPTH, 256, 2, 64)
            nsv[b] = kv[:, :, 128:256].reshape(DEPTH, 256, 2, 64)
            ndk[b] = kv[:, :, 256:512].reshape(DEPTH, 256, 4, 64)
            ndv[b] = kv[:, :, 512:768].reshape(DEPTH, 256, 4, 64)
    return (y_prompt, y_sample, nsr, nsl, nsk, nsv, ndk, ndv)
```

```python
import contextlib
import numpy as np
import ml_dtypes
import concourse.bass as bass
import concourse.mybir as mybir
from concourse.bass_utils import run_bass_kernel_spmd

F32 = mybir.dt.float32
BF16 = mybir.dt.bfloat16
AF = mybir.ActivationFunctionType
ALU = mybir.AluOpType
AX = mybir.AxisListType

ENGS = ("pe", "act", "dve", "pool", "sp")
DMA_WIN = 8
DEPTH = 4
T = 1536
NT = 12
EPS = 1e-6
NSTG = 3
NWBF = 3
PPL = 24 + 3 + 3 + 1 + 2 + 1 + 2 + 1 + 1 + 2 + 2 + 1 + 32


class Op:
    __slots__ = ("idx", "eng", "fn", "deps", "dma", "signal", "sem", "val", "waits")

    def __init__(self, idx, eng, fn, dma):
        self.idx = idx; self.eng = eng; self.fn = fn; self.dma = dma
        self.deps = set(); self.signal = False; self.sem = None; self.val = 0; self.waits = []


class Prog:
    def __init__(self, nc):
        self.nc = nc
        self.ops = []
        self.last_w = {}
        self.readers = {}
        self.scope = None

    def add(self, eng, fn, r=(), w=(), dma=False, noscope=False):
        op = Op(len(self.ops), eng, fn, dma)
        w = list(w) + [k for k in r if isinstance(k, str) and k.startswith("ps")]
        r = [k for k in r if not (isinstance(k, str) and k.startswith("ps"))]
        if self.scope is not None and not noscope:
            r.append(self.scope)
        for k in r:
            lw = self.last_w.get(k)
            if lw is not None:
                op.deps.add((lw, "raw"))
        for k in w:
            lw = self.last_w.get(k)
            if lw is not None:
                op.deps.add((lw, "waw"))
            for rd in self.readers.get(k, ()):
                op.deps.add((rd, "war"))
        for k in r:
            self.readers.setdefault(k, []).append(op.idx)
        for k in w:
            self.last_w[k] = op.idx
            self.readers[k] = []
        self.ops.append(op)
        return op

    def dma(self, out, in_, r=(), w=(), noscope=False):
        return self.add("sp", lambda e: e.dma_start(out=out, in_=in_), r=r, w=w, dma=True, noscope=noscope)

    def emit(self):
        nc = self.nc
        ops = self.ops
        for op in ops:
            nd = set()
            for (d, kind) in op.deps:
                dop = ops[d]
                if d == op.idx:
                    continue
                if dop.eng == op.eng and not dop.dma and not op.dma and kind != "raw":
                    continue
                if dop.eng == op.eng and op.eng == "pe":
                    continue
                nd.add(d)
            op.deps = nd
            for d in nd:
                ops[d].signal = True
        stack = contextlib.ExitStack()
        eng_sem = {}; eng_cnt = {}; dma_sems = {}; dma_cnt = {}
        for e in ENGS:
            eng_sem[e] = stack.enter_context(nc.semaphore("c_" + e))
            eng_cnt[e] = 0; dma_sems[e] = None; dma_cnt[e] = 0
        extra_waits = {}
        final_waits = []
        for op in ops:
            if op.dma:
                if dma_sems[op.eng] is None:
                    dma_sems[op.eng] = [stack.enter_context(nc.semaphore("d_%s%d" % (op.eng, i)))
                                        for i in range(DMA_WIN)]
                k = dma_cnt[op.eng]; dma_cnt[op.eng] += 1
                op.sem = dma_sems[op.eng][k % DMA_WIN]
                op.val = 16 * (k // DMA_WIN + 1)
                op.signal = True
                if k >= DMA_WIN:
                    extra_waits[op.idx] = [(op.sem, op.val - 16)]
            elif op.signal:
                eng_cnt[op.eng] += 1
                op.sem = eng_sem[op.eng]; op.val = eng_cnt[op.eng]
        for e in ENGS:
            if dma_sems[e] is not None:
                k = dma_cnt[e]
                for slot in range(min(k, DMA_WIN)):
                    n = (k - 1 - slot) // DMA_WIN + 1
                    final_waits.append((dma_sems[e][slot], 16 * n))
        waited = {e: {} for e in ENGS}
        for op in ops:
            need = {}
            for d in op.deps:
                dop = ops[d]
                if need.get(dop.sem, (None, 0))[1] < dop.val:
                    need[dop.sem] = (dop.sem, dop.val)
            for (s, v) in extra_waits.get(op.idx, ()):
                if need.get(s, (None, 0))[1] < v:
                    need[s] = (s, v)
            wl = []
            for key, (s, v) in need.items():
                if waited[op.eng].get(key, 0) >= v:
                    continue
                waited[op.eng][key] = v
                wl.append((s, v))
            op.waits = wl
        by_eng = {e: [op for op in ops if op.eng == e] for e in ENGS}
        self.stats = {e: len(by_eng[e]) for e in ENGS}
        self.stats["waits"] = sum(len(op.waits) for op in ops)
        self.stats["maxsem"] = dict(eng_cnt)

        def run(eh, ename, final=False):
            for op in by_eng[ename]:
                for (s, v) in op.waits:
                    eh.wait_ge(s, v)
                inst = op.fn(eh)
                if op.signal:
                    inst.then_inc(op.sem, 16 if op.dma else 1)
            if final:
                for (s, v) in final_waits:
                    eh.wait_ge(s, v)

        with stack:
            with nc.Block() as block:
                @block.tensor
                def _(e):
                    run(e, "pe")

                @block.scalar
                def _(e):
                    run(e, "act")

                @block.vector
                def _(e):
                    run(e, "dve")

                @block.gpsimd
                def _(e):
                    run(e, "pool")

                @block.sync
                def _(e):
                    run(e, "sp", final=True)


def cst_layout():
    entA = [("BD", 128), ("hm", 2), ("cm4", 4),
            ("b_ada", DEPTH * 48), ("g_mix", DEPTH * 8), ("g_mlp", DEPTH * 8), ("g_fin", 8),
            ("conv_w", DEPTH * 8), ("conv_b", DEPTH * 2), ("b_a", DEPTH * 4), ("b_x", DEPTH * 4),
            ("h0", DEPTH * 4), ("gng", DEPTH * 256)]
    entB = [("ident", 128), ("R64T", 128), ("R32T", 128), ("mlo", 128), ("mhi", 128), ("Ef", 128), ("Eb", 128),
            ("Lf8", 128), ("Lb8", 128), ("idx1", 128), ("idxr", 128), ("cmf", 1), ("cmb", 1), ("cT", 16),
            ("lam", DEPTH * 4), ("rd_row", 32), ("rd_pair", 16), ("sink", 16), ("dlam", DEPTH * 128), ("dng", DEPTH * 64)]
    off = {}
    o = 0
    for n, w in entA + entB:
        off[n] = (o, w)
        o += w
    off["_WA"] = (sum(w for _, w in entA), 0)
    return off, o


def piece_ids(l):
    b = l * PPL
    d = {}
    d["ada"] = list(range(b, b + 24)); b += 24
    d["retF"] = [b, b + 1, b + 2]; b += 3
    d["retT"] = [b, b + 1, b + 2]; b += 3
    d["retO"] = b; b += 1
    d["lruF"] = [b, b + 1]; b += 2
    d["lruO"] = b; b += 1
    d["swaF"] = [b, b + 1]; b += 2
    d["swaT"] = [b]; b += 1
    d["swaO"] = b; b += 1
    d["difF"] = [b, b + 1]; b += 2
    d["difT"] = [b, b + 1]; b += 2
    d["difO"] = b; b += 1
    d["ff"] = []
    for q in range(4):
        d["ff"].append((list(range(b, b + 4)), list(range(b + 4, b + 8)))); b += 8
    assert b == (l + 1) * PPL + 0 or True
    return d


def build_program(depth=DEPTH, stages=None, debug=False):
    ST = (lambda name: True) if stages is None else (lambda name: name in stages)
    nc = bass.Bass("TRN2", target_bir_lowering=False)
    P = Prog(nc)
    coff, CW = cst_layout()

    def din(name, shape):
        return nc.dram_tensor(name, list(shape), F32, kind="ExternalInput").ap()

    def dout(name, shape):
        return nc.dram_tensor(name, list(shape), F32, kind="ExternalOutput").ap()

    xT_d = din("xT", [1024, T])
    wts_d = din("wts", [DEPTH * PPL, 128, 2048])
    cst_d = din("cst", [128, CW])
    rope_d = din("rope", [128, 4096])
    lruw_d = din("lruw", [128, DEPTH * 8 * 128])
    cks_d = din("cks", [128, DEPTH, 256])
    cvs_d = din("cvs", [128, DEPTH, 2, 128])
    ckd_d = din("ckd", [128, DEPTH, 2, 256])
    cvd_d = din("cvd", [128, DEPTH, 2, 256])
    sret_d = din("sret", [128, DEPTH, 4, 128])
    yT_o = dout("yT", [1024, T])
    osret_o = dout("o_sret", [128, 2, DEPTH, 4, 128])
    oslru_o = dout("o_slru", [128, 2 * DEPTH * 4])
    okv_o = dout("o_kv", [4, 128, DEPTH, 768])

    def sb(name, shape, dt=F32):
        return nc.alloc_sbuf_tensor("s_" + name, list(shape), dt)

    dbg_o = {}
    if debug:
        dbg_o["d_hT"] = nc.dram_tensor("d_hT", [128, 8 * T], BF16, kind="ExternalOutput").ap()
        dbg_o["d_mod"] = nc.dram_tensor("d_mod", [128, 96], F32, kind="ExternalOutput").ap()
        dbg_o["d_sq"] = nc.dram_tensor("d_sq", [128, 2 * T], BF16, kind="ExternalOutput").ap()
        dbg_o["d_kms0"] = nc.dram_tensor("d_kms0", [128, 256 + T], BF16, kind="ExternalOutput").ap()
        dbg_o["d_kms1"] = nc.dram_tensor("d_kms1", [128, 256 + T], BF16, kind="ExternalOutput").ap()
        dbg_o["d_sva"] = nc.dram_tensor("d_sva", [128, NT * 2 * 80], BF16, kind="ExternalOutput").ap()
        dbg_o["d_vctx"] = nc.dram_tensor("d_vctx", [128, 2 * 2 * 80], BF16, kind="ExternalOutput").ap()
        dbg_o["d_osw"] = nc.dram_tensor("d_osw", [128, NT * 65], F32, kind="ExternalOutput").ap()
        for g_ in ("ret", "lru", "swa", "diff"):
            dbg_o["d_y_" + g_] = nc.dram_tensor("d_y_" + g_, [128, 2 * T], BF16, kind="ExternalOutput").ap()

    psum = nc.alloc_psum_tensor("psum", [128, 4096], F32)

    def PS(b, n=512, o=0):
        return psum[:, b * 512 + o: b * 512 + o + n]

    def pk(b):
        return "ps%d" % b

    x = sb("x", [128, 8, T])
    hT = sb("hT", [128, 8, T], BF16)
    yg = sb("yg", [128, 2, T], BF16)
    WA = coff["_WA"][0]
    cst = sb("cst", [128, WA])
    stg = [sb("stg%d" % i, [128, 2048]) for i in range(NSTG)]
    wbf = [sb("wbf%d" % i, [128, 2048], BF16) for i in range(NWBF)]
    ident = sb("ident", [128, 128], BF16)
    ones = sb("ones", [128, 128], BF16)
    R64 = sb("R64", [128, 128], BF16)
    R32 = sb("R32", [128, 128], BF16)
    mlo = sb("mlo", [128, 128], BF16)
    mhi = sb("mhi", [128, 128], BF16)
    DTm = sb("DTm", [128, DEPTH * 4, 128], BF16)
    Xi = sb("Xi", [128, DEPTH * 4, 128], BF16)
    zeta = sb("zeta", [128, DEPTH * 2, 4])
    Gp = sb("Gp", [128, DEPTH * 4])
    lgrow = sb("lgrow", [128, 32])
    lgpair = sb("lgpair", [128, 16])
    esink = sb("esink", [128, 16])
    nlam = sb("nlam", [128, DEPTH])
    dgt = sb("dgt", [128, DEPTH, 64])
    sa8 = sb("sa8", [128, DEPTH * 4])
    sa16 = sb("sa16", [128, DEPTH * 4])
    csil = sb("csil", [128, 8, 2], BF16)
    mods = [sb("mod%d" % i, [128, 48, 2]) for i in range(2)]
    modAs = [sb("modA%d" % i, [128, 2, 8, 2]) for i in range(2)]
    MODK = ["mod0", "mod1", "modA0", "modA1"]
    c_eps = sb("c_eps", [128, 1])
    c_one = sb("c_one", [128, 1])
    bar_t = sb("bar_t", [128, 1])
    rstd = sb("rstd", [128, 512])
    rstd2 = [rstd, sb("rstd1", [128, 512])]
    sqt = [sb("sqt%d" % i, [128, 512], BF16) for i in range(2)]
    ntmp = [sb("ntmp%d" % i, [128, 512]) for i in range(2)]
    adarow = [sb("adarow%d" % i, [2, 256]) for i in range(2)]
    identf = sb("identf", [2, 2])
    setup_stack = contextlib.ExitStack()
    cstB = setup_stack.enter_context(nc.sbuf_tensor("s_cstB", [128, CW - WA], F32))

    def C(name, lo=0, n=None):
        o, w = coff[name]
        if n is None:
            n = w - lo
        if o >= WA:
            return cstB[:, o - WA + lo: o - WA + lo + n]
        return cst[:, o + lo: o + lo + n]

    rr = {"ev": 0}

    def V(fn, r, w):
        return P.add("dve", fn, r, w)

    def G(fn, r, w):
        return P.add("pool", fn, r, w)

    def A(fn, r, w):
        return P.add("act", fn, r, w)

    def M(fn, r, w):
        return P.add("pe", fn, r, w)

    def act(out, in_, func, r, w, bias=None, scale=1.0):
        kw = {}
        if bias is not None:
            kw["bias"] = bias
        return A(lambda e: e.activation(out=out, in_=in_, func=func, scale=scale, **kw), r, w)

    def tt(eng, out, a, b, op, r, w):
        return P.add(eng, lambda e: e.tensor_tensor(out=out, in0=a, in1=b, op=op), r, w)

    def ts(eng, out, a, s1, s2, op0, op1, r, w):
        if s2 is None:
            return P.add(eng, lambda e: e.tensor_scalar(out=out, in0=a, scalar1=s1, scalar2=None, op0=op0), r, w)
        return P.add(eng, lambda e: e.tensor_scalar(out=out, in0=a, scalar1=s1, scalar2=s2, op0=op0, op1=op1), r, w)

    def stt(eng, out, a, s, b, op0, op1, r, w):
        return P.add(eng, lambda e: e.scalar_tensor_tensor(out=out, in0=a, scalar=s, in1=b, op0=op0, op1=op1), r, w)

    def cp(eng, out, in_, r, w):
        if eng == "act":
            return A(lambda e: e.activation(out=out, in_=in_, func=AF.Copy), r, w)
        return P.add(eng, lambda e: e.tensor_copy(out=out, in_=in_), r, w)

    def mm(out, lhsT, rhs, start, stop, r, w):
        return M(lambda e: e.matmul(out, lhsT=lhsT, rhs=rhs, start=start, stop=stop), r, w)

    def mmk(out, lhs_fn, rhs_fn, nk, r, w):
        def f(e):
            for kc in range(nk):
                ins = e.matmul(out, lhsT=lhs_fn(kc), rhs=rhs_fn(kc), start=(kc == 0), stop=(kc == nk - 1))
            return ins
        return M(f, r, w)

    minfree = {"v": 1 << 30}

    def barrier():
        minfree["v"] = min(minfree["v"], nc.sbuf_bytes_remaining)
        P.scope = None
        G(lambda e: e.memset(bar_t[:], 0.0), [], ["ARENA", "bar_t"])
        P.scope = "ARENA"

    order = []
    for l in range(depth):
        pi = piece_ids(l)
        if l == 0:
            order += pi["ada"]
        order += pi["retF"] + pi["retT"] + [pi["retO"]] + pi["lruF"] + [pi["lruO"]] + pi["swaF"] + pi["swaT"] + \
            [pi["swaO"]] + pi["difF"] + pi["difT"] + [pi["difO"]]
        nada = 0
        nxt = piece_ids(l + 1)["ada"] if l + 1 < depth else []
        for q in range(4):
            for p_ in pi["ff"][q][0] + pi["ff"][q][1]:
                order.append(p_)
                if nada < len(nxt):
                    order.append(nxt[nada]); nada += 1
    ffn_piece = set()
    for l in range(depth):
        pi = piece_ids(l)
        for q in range(4):
            ffn_piece.update(pi["ff"][q][0] + pi["ff"][q][1])
        if l + 1 < depth:
            ffn_piece.update(piece_ids(l + 1)["ada"])
    ws = {"issued": 0, "used": 0, "stg_rr": 0, "slot": {}, "free": list(range(NWBF)), "xstg": [], "xwbf": []}
    cast_eng = ["dve", "act", "dve", "act"]

    def stg_of(i):
        return (stg[i], "stg%d" % i) if i < NSTG else (ws["xstg"][i - NSTG], "xstg%d" % (i - NSTG))

    def wbf_of(i):
        return (wbf[i], "wbf%d" % i) if i < NWBF else (ws["xwbf"][i - NWBF], "xwbf%d" % (i - NWBF))

    def w_issue():
        k = ws["issued"]
        if k >= len(order):
            return False
        pid = order[k]
        big = (pid in ffn_piece) and len(ws["xstg"]) > 0
        cand = [b for b in ws["free"] if b < NWBF or big]
        if not cand:
            return False
        b = cand[0]
        ws["free"].remove(b)
        nst = NSTG + (len(ws["xstg"]) if big else 0)
        si = ws["stg_rr"] % nst
        ws["stg_rr"] += 1
        st_, skey = stg_of(si)
        wb_, wkey = wbf_of(b)
        ws["slot"][k] = b
        nsc = not (si >= NSTG or b >= NWBF)
        P.dma(st_[:], wts_d[pid], r=[], w=[skey], noscope=nsc)
        eng = cast_eng[k % 4]
        if eng == "act":
            P.add("act", lambda e: e.activation(out=wb_[:], in_=st_[:], func=AF.Copy), [skey], [wkey], noscope=nsc)
        else:
            P.add(eng, lambda e: e.tensor_copy(out=wb_[:], in_=st_[:]), [skey], [wkey], noscope=nsc)
        ws["issued"] += 1
        return True

    def w_next(expect):
        while order[ws["used"]] != expect:
            ws["used"] += 1
        k = ws["used"]
        for kk in list(ws["slot"].keys()):
            if kk < k:
                ws["free"].append(ws["slot"].pop(kk))
        assert ws["issued"] >= k or ws["issued"] == k or True
        if ws["issued"] < k:
            ws["issued"] = k
        while ws["issued"] < len(order) and ws["issued"] < k + 8:
            if not w_issue():
                break
        assert k in ws["slot"], (k, ws)
        ws["used"] += 1
        return wbf_of(ws["slot"][k])

    def w_extra_begin(xs, xw):
        ws["xstg"] = xs
        ws["xwbf"] = xw
        ws["free"] += [NWBF + i for i in range(len(xw))]

    def w_extra_end():
        for kk, b in ws["slot"].items():
            assert b < NWBF, "extra bf16 slot still in use at phase end"
        ws["free"] = [b for b in ws["free"] if b < NWBF]
        ws["xstg"] = []
        ws["xwbf"] = []

    P.scope = "ARENA"
    P.dma(cst[:], cst_d[:, 0:WA], w=["cst"])
    P.dma(cstB[:], cst_d[:, WA:CW], w=["cst"])
    for c in range(8):
        P.dma(x[:, c, :], xT_d[c * 128:(c + 1) * 128, :], w=["x%d" % c])
    G(lambda e: e.memset(ones[:], 1.0), [], ["ones"])
    G(lambda e: e.memset(c_eps[:], EPS), [], ["c_eps"])
    G(lambda e: e.memset(c_one[:], 1.0), [], ["c_one"])
    cp("dve", ident[:], C("ident"), ["cst"], ["ident"])
    cp("dve", identf[:], C("ident")[0:2, 0:2], ["cst"], ["identf"])
    cp("dve", R64[:], C("R64T"), ["cst"], ["R64"])
    cp("dve", R32[:], C("R32T"), ["cst"], ["R32"])
    cp("dve", mlo[:], C("mlo"), ["cst"], ["mlo"])
    cp("dve", mhi[:], C("mhi"), ["cst"], ["mhi"])
    act(csil[:].rearrange("p a b -> p (a b)"), C("cT"), AF.Silu, ["cst"], ["csil"])
    for (dst, src, n, key) in ((lgrow, "rd_row", 32, "lgrow"), (lgpair, "rd_pair", 16, "lgpair")):
        act(dst[:], C(src), AF.Exp, ["cst"], [key], scale=-1.0)
        act(dst[:], dst[:], AF.Ln, [key, "c_one"], [key], bias=c_one[:, 0:1])
        ts("dve", dst[:], dst[:], -1.0, None, ALU.mult, None, [key], [key])
    etmp = setup_stack.enter_context(nc.sbuf_tensor("s_etmp", [128, 128], F32))
    etmp2 = setup_stack.enter_context(nc.sbuf_tensor("s_etmp2", [128, 128], F32))
    for l in range(depth):
        for h in range(4):
            act(etmp[:], C("Ef"), AF.Exp, ["cst", "lgrow"], ["etmp"], scale=lgrow[:, l * 8 + h: l * 8 + h + 1])
            tt("dve", etmp[:], etmp[:], C("Lf8"), ALU.mult, ["etmp", "cst"], ["etmp"])
            act(etmp2[:], C("Eb"), AF.Exp, ["cst", "lgrow"], ["etmp2"], scale=lgrow[:, l * 8 + 4 + h: l * 8 + 4 + h + 1])
            tt("dve", etmp2[:], etmp2[:], C("Lb8"), ALU.mult, ["etmp2", "cst"], ["etmp2"])
            tt("dve", DTm[:, l * 4 + h, :], etmp[:], etmp2[:], ALU.add, ["etmp", "etmp2"], ["DTm"])
        for d in range(2):
            for j in range(2):
                i = l * 4 + d * 2 + j
                act(Xi[:, i, :], C("idx1") if d == 0 else C("idxr"), AF.Exp, ["cst", "lgpair"], ["Xi"], scale=lgpair[:, i:i + 1])
                act(Gp[:, i:i + 1], lgpair[:, i:i + 1], AF.Exp, ["lgpair"], ["Gp"], scale=128.0)
            act(zeta[:, l * 2 + d, :], lgrow[:, l * 8 + d * 4: l * 8 + d * 4 + 4], AF.Exp, ["lgrow", "cst"], ["zeta"],
                scale=(C("cmf") if d == 0 else C("cmb")))
    ts("dve", zeta[:], zeta[:], 0.125, None, ALU.mult, None, ["zeta"], ["zeta"])
    act(esink[:], C("sink"), AF.Exp, ["cst"], ["esink"])
    dl1 = setup_stack.enter_context(nc.sbuf_tensor("s_dl1", [128, DEPTH, 2, 32], F32))
    dl2 = setup_stack.enter_context(nc.sbuf_tensor("s_dl2", [128, DEPTH * 2], F32))
    dlv = C("dlam").rearrange("p (l a b c) -> p l a b c", l=DEPTH, a=2, b=2)
    tt("dve", dl1[:], dlv[:, :, :, 0, :], dlv[:, :, :, 1, :], ALU.mult, ["cst"], ["dl1"])
    V(lambda e: e.tensor_reduce(out=dl2[:], in_=dl1[:].rearrange("p l a c -> p (l a) c"), axis=AX.X, op=ALU.add), ["dl1"], ["dl2"])
    act(dl2[:], dl2[:], AF.Exp, ["dl2"], ["dl2"])
    for l in range(depth):
        import math
        li = 0.8 - 0.6 * math.exp(-0.3 * l)
        stt("dve", nlam[:, l:l + 1], dl2[:, 2 * l + 1: 2 * l + 2], -li, dl2[:, 2 * l: 2 * l + 1], ALU.add, ALU.subtract, ["dl2"], ["nlam"])
        ts("dve", dgt[:, l, :], C("dng", l * 64, 64), 1.0 - li, None, ALU.mult, None, ["cst"], ["dgt"])
    act(sa8[:], C("lam"), AF.Exp, ["cst"], ["sa8"], scale=-1.0)
    act(sa8[:], sa8[:], AF.Ln, ["sa8", "c_one"], ["sa8"], bias=c_one[:, 0:1])
    ts("dve", sa16[:], sa8[:], -16.0, None, ALU.mult, None, ["sa8"], ["sa16"])
    ts("dve", sa8[:], sa8[:], -8.0, None, ALU.mult, None, ["sa8", "sa16"], ["sa8"])

    setup_stack.close()
    P.scope = None
    xk = ["x%d" % c for c in range(8)]
    hk = ["h%d" % c for c in range(8)]

    def norm(Aap, Bap, dst_fn, dst_keys, final=False):
        def stats(n):
            blk = slice(n * 512, (n + 1) * 512)
            bank = 5 if n % 2 == 0 else 2
            rs = rstd2[n % 2]
            rk_ = "rstd%d" % (n % 2)
            for c in range(8):
                s = sqt[c % 2]
                act(s[:], x[:, c, blk], AF.Square, [xk[c]], ["sqt%d" % (c % 2)])
                mm(PS(bank), ones[:], s[:], c == 0, c == 7, ["ones", "sqt%d" % (c % 2)], [pk(bank)])
            act(rs[:], PS(bank), AF.Sqrt, [pk(bank), "c_eps"], [rk_], bias=c_eps[:, 0:1], scale=1.0 / 1024.0)
            V(lambda e: e.reciprocal(out=rs[:], in_=rs[:]), [rk_], [rk_])

        def apply(n):
            v = 0 if n < 2 else 1
            blk = slice(n * 512, (n + 1) * 512)
            rs = rstd2[n % 2]
            rk_ = "rstd%d" % (n % 2)
            for c in range(8):
                tmp = ntmp[c % 2]
                tt("dve", tmp[:], x[:, c, blk], rs[:], ALU.mult, [xk[c], rk_], ["ntmp%d" % (c % 2)])
                o = dst_fn(c, n)
                if c % 2 == 0:
                    A(lambda e, o=o, tmp=tmp, sa=Aap(c, v), ba=Bap(c, v): e.activation(out=o, in_=tmp[:], func=AF.Identity, scale=sa, bias=ba),
                      ["ntmp%d" % (c % 2)] + MODK, [dst_keys[c]])
                else:
                    ts("pool", o, tmp[:], Aap(c, v), Bap(c, v), ALU.mult, ALU.add, ["ntmp%d" % (c % 2)] + MODK, [dst_keys[c]])
        stats(0)
        stats(1)
        apply(0)
        stats(2)
        apply(1)
        apply(2)

    dstate = {"b": 0}

    def dbank():
        dstate["b"] ^= 1
        return 3 + dstate["b"]

    def proj_F(pid, src, srck, nchunks, evac):
        wv, wkey = w_next(pid)
        w3 = wv[:].rearrange("p (k c) -> p k c", k=8)
        for mi in range(nchunks):
            for n in range(3):
                b = dbank()
                mmk(PS(b), lambda kc, mi=mi: w3[:, kc, mi * 128:(mi + 1) * 128],
                    lambda kc, n=n: src[:, kc, n * 512:(n + 1) * 512], 8, [wkey] + srck, [pk(b)])
                evac(mi, n, b)

    def proj_T(pid, tiles, evac, ncols=256):
        wv, wkey = w_next(pid)
        w3 = wv[:].rearrange("p (k c) -> p k c", k=8)
        for t in tiles:
            b = dbank()
            mmk(PS(b, ncols), lambda kc, t=t: hT[:, kc, t * 128:(t + 1) * 128], lambda kc: w3[:, kc, 0:ncols], 8,
                [wkey] + hk, [pk(b)])
            evac(t, b)

    def out_proj(pid, G1):
        wv, wkey = w_next(pid)
        w3 = wv[:].rearrange("p (k c) -> p k c", k=2)
        for n in range(3):
            for m in range(8):
                v = 0 if n < 2 else 1
                b = dbank()
                mmk(PS(b), lambda kc, m=m: w3[:, kc, m * 128:(m + 1) * 128],
                    lambda kc, n=n: yg[:, kc, n * 512:(n + 1) * 512], 2, [wkey, "yg"], [pk(b)])
                xs = x[:, m, n * 512:(n + 1) * 512]
                stt("dve", xs, PS(b), G1(m, v), xs, ALU.mult, ALU.add, [pk(b), xk[m]] + MODK, [xk[m]])

    def transpose_part(ytok, ykey, t2s, fcs):
        for t2 in t2s:
            for fc in fcs:
                for i in range(2):
                    t = t2 * 2 + i
                    mm(PS(5, 128, (fc * 2 + i) * 128), ytok[:, t, fc * 128:(fc + 1) * 128], ident[:], True, True,
                       [ykey, "ident"], [pk(5)])
            for fc in fcs:
                cp("act" if fc == 0 else "dve", yg[:, fc, t2 * 256:(t2 + 1) * 256], PS(5, 256, fc * 256), [pk(5)], ["yg"])

    def transpose_y(ytok, ykey):
        transpose_part(ytok, ykey, range(6), [0, 1])

    def rope_load(rope, half):
        P.dma(stg[0][:], rope_d[:, half * 2048:(half + 1) * 2048], w=["stg0"], noscope=True)
        cp("dve", rope[:], stg[0][:], ["stg0"], ["rope"])

    def rope_apply(z, zkey, R, Rk, rope):
        tab0 = 0
        for n in range(2):
            blk = slice(n * 512, (n + 1) * 512)
            b = dbank()
            mm(PS(b), R[:], z[:, blk], True, True, [Rk, zkey], [pk(b)])
            t1 = ntmp[0]
            t2 = ntmp[1]
            tt("dve", t1[:], z[:, blk], rope[:, tab0 + n * 512: tab0 + (n + 1) * 512], ALU.mult, [zkey, "rope"], ["ntmp0"])
            tt("dve", t2[:], PS(b), rope[:, tab0 + 1024 + n * 512: tab0 + 1024 + (n + 1) * 512], ALU.mult, [pk(b), "rope"], ["ntmp1"])
            tt("pool", z[:, blk], t1[:], t2[:], ALU.add, ["ntmp0", "ntmp1"], [zkey])

    astate = {"i": 0}

    class AttnPipe:
        def __init__(self, pTs, pkey):
            self.pTs = pTs; self.pkey = pkey; self.n = 0; self.pending = None

        def push(self, qT, qkey, kblocks, scale, post, nq=1):
            i = self.n; self.n += 1
            base = 0 if (i % 2 == 0) else 3
            pT = self.pTs[i % 2]; pkey = self.pkey + "%d_" % (i % 2); ob = 6 + (i % 2)
            nk = len(kblocks)
            w = nq * 128
            assert nk * w <= 1536 and (512 % w == 0)
            banks = sorted(set(base + (j * w) // 512 for j in range(nk)))
            for j, (kT, kkey, vap, vkey, mask) in enumerate(kblocks):
                o = psum[:, base * 512 + j * w: base * 512 + (j + 1) * w]
                bk = pk(base + (j * w) // 512)
                mm(o, kT, qT, True, mask is None, [kkey, qkey], [bk])
                if mask is not None:
                    assert nq == 1
                    mm(o, ident[:], mask[0][:], False, True, ["ident", mask[1]], [bk])
            for bi, b in enumerate(banks):
                c0 = bi * 512
                c1 = min(nk * w, c0 + 512)
                act(pT[:, c0:c1], psum[:, base * 512 + c0: base * 512 + c1], AF.Exp, [pk(b)], [pkey + str(bi)], scale=scale)
            prev = self.pending
            self.pending = (kblocks, pT, pkey, ob, post, nq)
            if prev is not None:
                self._pv(prev)

        def _pv(self, call):
            kblocks, pT, pkey, ob, post, nq = call
            nk = len(kblocks)
            w = nq * 128
            for qi in range(nq):
                for j, (kT, kkey, vap, vkey, mask) in enumerate(kblocks):
                    c = j * w + qi * 128
                    mm(PS(ob, 65, qi * 128), pT[:, c:c + 128], vap, j == 0, j == nk - 1, [pkey + str(c // 512), vkey], [pk(ob)])
            post(ob)

        def flush(self):
            if self.pending is not None:
                self._pv(self.pending)
                self.pending = None

    def ada_piece(l, j):
        wv, wkey = w_next(piece_ids(l)["ada"][j])
        w3 = wv[:].rearrange("p (k c) -> p k c", k=8)
        o = psum[0:2, 2 * 512 + (j % 2) * 256: 2 * 512 + (j % 2) * 256 + 256]
        ar_ = adarow[j % 2]
        mmk(o, lambda kc: csil[:, kc, :], lambda kc, w3=w3: w3[:, kc, 0:256], 8, [wkey, "csil"], [pk(2)])
        cp("act" if j % 2 == 0 else "dve", ar_[:], o, [pk(2)], ["adarow%d" % (j % 2)])
        for mi in range(2):
            mc = j * 2 + mi
            mm(PS(5, 2, mc * 2), ar_[:, mi * 128:(mi + 1) * 128], identf[:], True, True, ["adarow%d" % (j % 2), "identf"], [pk(5)])

    def ada_finish(l):
        mod = mods[l % 2]
        modA = modAs[l % 2]
        badd = C("b_ada", l * 48, 48)
        tt("dve", mod[:], PS(5, 96).rearrange("p (a b) -> p a b", b=2), badd.unsqueeze(2).to_broadcast([128, 48, 2]), ALU.add,
           [pk(5), "cst"], ["mod%d" % (l % 2)])
        for w_, (gname, sci) in enumerate((("g_mix", 1), ("g_mlp", 4))):
            gv = C(gname, l * 8, 8).unsqueeze(2).to_broadcast([128, 8, 2])
            stt("dve", modA[:, w_, :, :], mod[:, sci * 8:(sci + 1) * 8, :], 1.0, gv, ALU.add, ALU.mult,
                ["mod%d" % (l % 2), "cst"], ["modA%d" % (l % 2)])

    for l in range(depth):
        pi = piece_ids(l)
        P.scope = None
        mod = mods[l % 2]
        modA = modAs[l % 2]
        if l == 0 and ST('ada'):
            for j in range(24):
                ada_piece(0, j)
            ada_finish(0)
        A1 = lambda c, v: modA[:, 0, c, v:v + 1]
        B1 = lambda c, v: mod[:, 0 * 8 + c, v:v + 1]
        G1 = lambda c, v: mod[:, 2 * 8 + c, v:v + 1]
        A2 = lambda c, v: modA[:, 1, c, v:v + 1]
        B2 = lambda c, v: mod[:, 3 * 8 + c, v:v + 1]
        G2 = lambda c, v: mod[:, 5 * 8 + c, v:v + 1]
        if ST('norm'):
            norm(A1, B1, lambda c, n: hT[:, c, n * 512:(n + 1) * 512], hk)
        if debug and l == 0:
            P.dma(dbg_o["d_hT"], hT[:].rearrange("p a b -> p (a b)"), r=hk)
            P.dma(dbg_o["d_mod"], mod[:].rearrange("p a b -> p (a b)"), r=["mod"])

        barrier()
        with contextlib.ExitStack() as st:
          if ST('ret'):
              def sba(name, shape, dt=F32):
                  rr["ev"] += 1
                  return st.enter_context(nc.sbuf_tensor("a%d_" % rr["ev"] + name, list(shape), dt))
              rq = sba("rq", [128, 2, T], BF16)
              rkm = sba("rkm", [128, 4, T], BF16)
              rkT = sba("rkT", [128, NT, 256], BF16)
              rvT = sba("rvT", [128, NT, 256], BF16)
              rgT = sba("rgT", [128, NT, 256], BF16)
              ytok = sba("ytok", [128, NT, 256], BF16)
              Sbd = sba("Sbd", [128, 4, 128])
              Sbf = sba("Sbf", [128, 4, 8, 128], BF16)
              Ob = sba("Ob", [128, 2, 256])
              Oq = sba("Oq", [128, 2, 256])
              st4 = sba("st4", [128, 6, 8])
              qx8 = [sba("qx%d" % i, [128, 128], BF16) for i in range(8)]
              ktx = [sba("ktx%d" % i, [128, 256], BF16) for i in range(2)]
              dsb4 = [sba("dsb%d" % i, [128, 128]) for i in range(4)]
              scT8 = [sba("scT%d" % i, [128, 128], BF16) for i in range(8)]

              def ev_rq(mi, n, b):
                  cp("act", rq[:, mi, n * 512:(n + 1) * 512], PS(b), [pk(b)], ["rq"])
              proj_F(pi["retF"][0], hT, hk, 2, ev_rq)
              for pp in range(2):
                  def ev_rk(mi, n, b, pp=pp):
                      cp("act" if mi == 0 else "dve", rkm[:, pp * 2 + mi, n * 512:(n + 1) * 512], PS(b), [pk(b)], ["rkm"])
                  proj_F(pi["retF"][1 + pp], hT, hk, 2, ev_rk)
              for (dst, key, pid) in ((rkT, "rkT", pi["retT"][0]), (rvT, "rvT", pi["retT"][1]), (rgT, "rgT", pi["retT"][2])):
                  def ev_t(t, b, dst=dst, key=key):
                      cp("act" if t % 2 == 0 else "dve", dst[:, t, :], PS(b, 256), [pk(b)], [key])
                  proj_T(pid, range(NT), ev_t)
              G(lambda e: e.memset(Sbf[:], 0.0), [], ["Sbf"])
              act(rgT[:].rearrange("p a b -> p (a b)"), rgT[:].rearrange("p a b -> p (a b)"), AF.Silu, ["rgT"], ["rgT"])
              tt("pool", rgT[:], rgT[:], C("gng", l * 256, 256).unsqueeze(1).to_broadcast([128, NT, 256]), ALU.mult, ["rgT", "cst"], ["rgT"])
              segs = [(0, 8, True, 0), (8, 2, False, 0), (10, 2, False, 1)]
              for (t0, ncnk, samp, sj) in segs:
                  if samp:
                      P.dma(Sbd[:], sret_d[:, l, :, :], w=["Sbd0", "Sbd1", "Sbd2", "Sbd3"])
                  else:
                      G(lambda e: e.memset(Sbd[:], 0.0), [], ["Sbd0", "Sbd1", "Sbd2", "Sbd3"])
                  sbank = [0, 1, 2, 6]
                  for step in range(ncnk):
                      for d in range(2):
                          i = step if d == 0 else ncnk - 1 - step
                          t = t0 + i
                          kt = ktx[d]
                          tt("dve", kt[:].rearrange("p (h e) -> p h e", h=4), rkT[:, t, :].rearrange("p (h e) -> p h e", h=4),
                             zeta[:, l * 2 + d, :].unsqueeze(2).to_broadcast([128, 4, 64]), ALU.mult, ["rkT", "zeta"], ["ktx%d" % d])
                          for j in range(2):
                              si = d * 2 + j
                              sk_ = "Sbd%d" % si
                              for hb in range(2):
                                  A(lambda e, si=si, i=i, hb=hb: e.activation(out=Sbf[hb * 64:(hb + 1) * 64, si, i, hb * 64:(hb + 1) * 64],
                                    in_=Sbd[hb * 64:(hb + 1) * 64, si, hb * 64:(hb + 1) * 64], func=AF.Copy), [sk_], ["Sbf"])
                              mm(PS(sbank[si], 128), kt[:, j * 128:(j + 1) * 128], rvT[:, t, j * 128:(j + 1) * 128], True, True,
                                 ["ktx%d" % d, "rvT"], [pk(sbank[si])])
                              stt("dve", Sbd[:, si, :], Sbd[:, si, :], Gp[:, l * 4 + si: l * 4 + si + 1], PS(sbank[si], 128), ALU.mult, ALU.add,
                                  [sk_, "Gp", pk(sbank[si])], [sk_])
                  if not samp:
                      P.dma(osret_o[:, sj, l, :, :], Sbd[:], r=["Sbd0", "Sbd1", "Sbd2", "Sbd3"])
                  def stage_a(t):
                      cols = slice(t * 128, (t + 1) * 128)
                      par = t % 2
                      for j in range(2):
                          for d in range(2):
                              q_ = qx8[par * 4 + d * 2 + j]
                              tt("dve" if d == 0 else "pool", q_[:], rq[:, j, cols], Xi[:, l * 4 + d * 2 + j, :], ALU.mult,
                                 ["rq", "Xi"], ["qx%d" % (par * 4 + d * 2 + j)])
                      for h in range(4):
                          mm(PS(par, 128, h * 128), rkm[:, h, cols], rq[:, h // 2, cols], True, True, ["rkm", "rq"], [pk(par)])
                      for h in range(4):
                          tt("dve", scT8[par * 4 + h][:], PS(par, 128, h * 128), DTm[:, l * 4 + h, :], ALU.mult, [pk(par), "DTm"],
                             ["scT%d" % (par * 4 + h)])
                  stage_a(t0)
                  for i in range(ncnk):
                      t = t0 + i
                      if i + 1 < ncnk:
                          stage_a(t + 1)
                      par = t % 2
                      ob = 6 + (t % 2)
                      for j in range(2):
                          mm(PS(ob, 128, j * 128), qx8[par * 4 + j][:], Sbf[:, j, i, :], True, False, ["qx%d" % (par * 4 + j), "Sbf"], [pk(ob)])
                          mm(PS(ob, 128, j * 128), qx8[par * 4 + 2 + j][:], Sbf[:, 2 + j, i, :], False, False,
                             ["qx%d" % (par * 4 + 2 + j), "Sbf"], [pk(ob)])
                          for hh in range(2):
                              h = j * 2 + hh
                              mm(PS(ob, 64, h * 64), scT8[par * 4 + h][:], rvT[:, t, h * 64:(h + 1) * 64], False, hh == 1,
                                 ["scT%d" % (par * 4 + h), "rvT"], [pk(ob)])
                      cp("act", Ob[:, t % 2, :], PS(ob, 256), [pk(ob)], ["Ob"])
                      if t % 2 == 1:
                          tb = t - 1
                          O3 = Ob[:].rearrange("p t (h e) -> p (t h) e", h=4)
                          Q3 = Oq[:].rearrange("p t (h e) -> p (t h) e", h=4)
                          s1 = st4[:, 0, :]; s2 = st4[:, 1, :]; mean = st4[:, 2, :]; var = st4[:, 3, :]; rs = st4[:, 4, :]
                          V(lambda e: e.tensor_reduce(out=s1, in_=O3, axis=AX.X, op=ALU.add), ["Ob"], ["st4a"])
                          tt("pool", Oq[:], Ob[:], Ob[:], ALU.mult, ["Ob"], ["Oq"])
                          V(lambda e: e.tensor_reduce(out=s2, in_=Q3, axis=AX.X, op=ALU.add), ["Oq"], ["st4b"])
                          ts("dve", mean, s1, 1.0 / 64, None, ALU.mult, None, ["st4a"], ["st4c"])
                          tt("dve", var, mean, mean, ALU.mult, ["st4c"], ["st4d"])
                          stt("dve", var, s2, 1.0 / 64, var, ALU.mult, ALU.subtract, ["st4b", "st4d"], ["st4d"])
                          act(rs, var, AF.Sqrt, ["st4d", "c_eps"], ["st4e"], bias=c_eps[:, 0:1])
                          V(lambda e: e.reciprocal(out=rs, in_=rs), ["st4e"], ["st4e"])
                          tt("dve", Q3, O3, mean.unsqueeze(2).to_broadcast([128, 8, 64]), ALU.subtract, ["Ob", "st4c", "Oq"], ["Oq"])
                          tt("dve", Q3, Q3, rs.unsqueeze(2).to_broadcast([128, 8, 64]), ALU.mult, ["Oq", "st4e"], ["Oq"])
                          tt("pool", ytok[:, tb:tb + 2, :], Oq[:], rgT[:, tb:tb + 2, :], ALU.mult, ["Oq", "rgT"], ["ytok"])
                          transpose_part(ytok, "ytok", [tb // 2], [0, 1])
              (P.dma(dbg_o["d_y_ret"], yg[:].rearrange("p a b -> p (a b)"), r=["yg"]) if (debug and l == 0) else None)
              out_proj(pi["retO"], G1)

        barrier()
        with contextlib.ExitStack() as st:
          if ST('lru'):
              def sba(name, shape, dt=F32):
                  rr["ev"] += 1
                  return st.enter_context(nc.sbuf_tensor("a%d_" % rr["ev"] + name, list(shape), dt))
              lx = sba("lx", [128, 2, T], BF16)
              lgt = sba("lgt", [128, 2, T], BF16)
              BUF = {}
              for sfx, Lb in (("", 1024), ("P", 512)):
                  BUF[sfx] = dict(xc=sba("xc" + sfx, [128, Lb]), xcb=sba("xcb" + sfx, [128, Lb], BF16),
                                  rr=[sba("rr%d%s" % (i, sfx), [128, Lb]) for i in range(2)],
                                  ii=[sba("ii%d%s" % (i, sfx), [128, Lb]) for i in range(2)],
                                  uu=[sba("uu%d%s" % (i, sfx), [128, Lb]) for i in range(2)])
              slo2 = [sba("slo%d" % i, [128, 2 * 4]) for i in range(2)]
              lruwb = sba("lruwb", [128, 8, 128], BF16)
              for (dst, key, pid) in ((lx, "lx", pi["lruF"][0]), (lgt, "lgt", pi["lruF"][1])):
                  def ev(mi, n, b, dst=dst, key=key):
                      cp("act" if n % 2 == 0 else "dve", dst[:, mi, n * 512:(n + 1) * 512], PS(b), [pk(b)], [key])
                  proj_F(pid, hT, hk, 2, ev)
              P.dma(stg[0][:, 0:1024], lruw_d[:, l * 1024:(l + 1) * 1024], w=["stg0"], noscope=True)
              cp("dve", lruwb[:].rearrange("p a b -> p (a b)"), stg[0][:, 0:1024], ["stg0"], ["lruwb"])
              for cc in range(2):
                  act(lgt[:, cc, :], lgt[:, cc, :], AF.Gelu_apprx_tanh, ["lgt"], ["lgt"])
              for cc in range(2):
                for (c0, L, samp, nseq) in ((0, 1024, True, 1), (1024, 512, False, 2)):
                  sfx = "" if samp else "P"
                  B_ = BUF[sfx]
                  xc = B_["xc"]; xcb = B_["xcb"]; rr2 = B_["rr"]; ii2 = B_["ii"]; uu2 = B_["uu"]
                  kx = "xc" + sfx; kxb = "xcb" + sfx
                  kr = ["rr0" + sfx, "rr1" + sfx]; ki_ = ["ii0" + sfx, "ii1" + sfx]; ku = ["uu0" + sfx, "uu1" + sfx]
                  Ls = L // nseq
                  if True:
                      xi_ = lx[:, cc, c0:c0 + L]
                      cw = lambda k: C("conv_w", l * 8 + k * 2 + cc, 1)
                      ts("dve", xc[:, 0:L], xi_, cw(2), C("conv_b", l * 2 + cc, 1), ALU.mult, ALU.add, ["lx", "cst"], [kx])
                      for sq_ in range(nseq):
                          b0 = sq_ * Ls
                          stt("dve", xc[:, b0 + 2:b0 + Ls], xi_[:, b0:b0 + Ls - 2], cw(0), xc[:, b0 + 2:b0 + Ls], ALU.mult, ALU.add, ["lx", "cst", kx], [kx])
                          stt("dve", xc[:, b0 + 1:b0 + Ls], xi_[:, b0:b0 + Ls - 1], cw(1), xc[:, b0 + 1:b0 + Ls], ALU.mult, ALU.add, ["lx", "cst", kx], [kx])
                          stt("dve", xc[:, b0:b0 + Ls - 1], xi_[:, b0 + 1:b0 + Ls], cw(3), xc[:, b0:b0 + Ls - 1], ALU.mult, ALU.add, ["lx", "cst", kx], [kx])
                      cp("pool", xcb[:, 0:L], xc[:, 0:L], [kx], [kxb])
                      pidx_ = [l * 4 + d * 2 + cc for d in range(2)]
                      for d in range(2):
                          for (dst, key, gsel, bname) in ((rr2[d], kr[d], 0, "b_a"), (ii2[d], ki_[d], 1, "b_x")):
                              for n in range((L + 511) // 512):
                                  w_ = min(512, L - n * 512)
                                  b = dbank()
                                  mm(PS(b, w_), lruwb[:, d * 4 + gsel * 2 + cc, :], xcb[:, n * 512:n * 512 + w_], True, True,
                                     ["lruwb", kxb], [pk(b)])
                                  act(dst[:, n * 512:n * 512 + w_], PS(b, w_), AF.Sigmoid, [pk(b), "cst"], [key], bias=C(bname, pidx_[d], 1))
                      for d in range(2):
                          tt("pool" if d == 1 else "dve", ii2[d][:, 0:L], ii2[d][:, 0:L], xc[:, 0:L], ALU.mult, [ki_[d], kx], [ki_[d]])
                      for d in range(2):
                          act(uu2[d][:, 0:L], rr2[d][:, 0:L], AF.Exp, [kr[d], "sa16"], [ku[d]], scale=sa16[:, pidx_[d]:pidx_[d] + 1])
                          act(rr2[d][:, 0:L], rr2[d][:, 0:L], AF.Exp, [kr[d], "sa8"], [kr[d]], scale=sa8[:, pidx_[d]:pidx_[d] + 1])
                      for d in range(2):
                          act(uu2[d][:, 0:L], uu2[d][:, 0:L], AF.Sqrt, [ku[d], "c_one"], [ku[d]], bias=c_one[:, 0:1], scale=-1.0)
                      for d in range(2):
                          tt("dve", uu2[d][:, 0:L], uu2[d][:, 0:L], ii2[d][:, 0:L], ALU.mult, [ku[d], ki_[d]], [ku[d]])
                      for sq_ in range(nseq):
                          b0 = sq_ * Ls
                          for d in range(2):
                              hd = ii2[d]
                              init = C("h0", pidx_[d], 1) if samp else 0.0
                              if d == 0:
                                  V(lambda e, hd=hd, init=init, b0=b0, Ls=Ls, rr2=rr2, uu2=uu2: e.tensor_tensor_scan(out=hd[:, b0:b0 + Ls],
                                    data0=rr2[0][:, b0:b0 + Ls], data1=uu2[0][:, b0:b0 + Ls], initial=init, op0=ALU.mult, op1=ALU.add),
                                    [kr[0], ku[0], "cst"], [ki_[0]])
                                  fin = hd[:, b0 + Ls - 1:b0 + Ls]
                              else:
                                  V(lambda e, hd=hd, init=init, b0=b0, Ls=Ls, rr2=rr2, uu2=uu2: e.tensor_tensor_scan(out=hd[:, b0:b0 + Ls][:, ::-1],
                                    data0=rr2[1][:, b0:b0 + Ls][:, ::-1], data1=uu2[1][:, b0:b0 + Ls][:, ::-1],
                                    initial=init, op0=ALU.mult, op1=ALU.add), [kr[1], ku[1], "cst"], [ki_[1]])
                                  fin = hd[:, b0:b0 + 1]
                              if not samp:
                                  cp("pool", slo2[sq_][:, d * 2 + cc: d * 2 + cc + 1], fin, [ki_[d]], ["slo%d" % sq_])
                      tt("dve", ii2[0][:, 0:L], ii2[0][:, 0:L], ii2[1][:, 0:L], ALU.add, [ki_[0], ki_[1]], [ki_[0]])
                      tt("dve", yg[:, cc, c0:c0 + L], ii2[0][:, 0:L], lgt[:, cc, c0:c0 + L], ALU.mult, [ki_[0], "lgt"], ["yg"])
              for sj in range(2):
                  P.dma(oslru_o[:, (sj * DEPTH + l) * 4:(sj * DEPTH + l) * 4 + 4], slo2[sj][:, 0:4], r=["slo%d" % sj])
              (P.dma(dbg_o["d_y_lru"], yg[:].rearrange("p a b -> p (a b)"), r=["yg"]) if (debug and l == 0) else None)
              out_proj(pi["lruO"], G1)

        barrier()
        with contextlib.ExitStack() as st:
          if ST('swa'):
              def sba(name, shape, dt=F32):
                  rr["ev"] += 1
                  return st.enter_context(nc.sbuf_tensor("a%d_" % rr["ev"] + name, list(shape), dt))
              sq = sba("sq", [128, 2, T], BF16)
              sk = sba("sk", [128, T], BF16)
              kms = [sba("kms%d" % i, [128, 256 + T], BF16) for i in range(2)]
              svaug = sba("svaug", [128, NT, 2, 80], BF16)
              vctx = sba("vctx", [128, 2, 2, 80], BF16)
              ckf = sba("ckf", [128, 256])
              cvf = sba("cvf", [128, 2, 128])
              pT = [sba("pT%d" % i, [128, 640], BF16) for i in range(2)]
              Osw = [sba("Osw", [128, NT, 65])] * 2
              den = sba("den", [128, NT])
              ytok = sba("ytok", [128, NT, 256], BF16)
              kvst = [sba("kvst%d" % i, [128, 256]) for i in range(2)]
              cp("dve", svaug[:, :, :, 64:66], ones[:, 0:NT * 4].rearrange("p (a b c) -> p a b c", a=NT, b=2), ["ones"], ["svaug"])
              cp("dve", vctx[:, :, :, 64:66], ones[:, 0:8].rearrange("p (a b c) -> p a b c", a=2, b=2), ["ones"], ["vctx"])

              def ev_sq(mi, n, b):
                  cp("act", sq[:, mi, n * 512:(n + 1) * 512], PS(b), [pk(b)], ["sq%d" % mi])
              proj_F(pi["swaF"][0], hT, hk, 2, ev_sq)

              def ev_sk(mi, n, b):
                  cp("act", sk[:, n * 512:(n + 1) * 512], PS(b), [pk(b)], ["sk"])
              proj_F(pi["swaF"][1], hT, hk, 1, ev_sk)

              def ev_swt(t, b):
                  for hv in range(2):
                      cp("act", svaug[:, t, hv, 0:64], PS(b, 64, 128 + hv * 64), [pk(b)], ["svaug"])
                  if t >= 8:
                      kb = kvst[t % 2]
                      cp("dve", kb[:], PS(b, 256), [pk(b)], ["kvst%d" % (t % 2)])
                      P.dma(okv_o[t - 8, :, l, 0:256], kb[:], r=["kvst%d" % (t % 2)])
              proj_T(pi["swaT"][0], range(NT), ev_swt)
              rope = sba("rope", [128, 2048], BF16)
              if ST('swa_rope'):
                  rope_load(rope, 0)
                  rope_apply(sq[:, 0, :], "sq0", R64, "R64", rope)
                  rope_apply(sq[:, 1, :], "sq1", R64, "R64", rope)
                  rope_apply(sk[:], "sk", R64, "R64", rope)
              P.dma(ckf[:], cks_d[:, l, :], w=["ckf"])
              P.dma(cvf[:], cvs_d[:, l, :, :], w=["cvf"])
              for cb_ in range(2):
                  for hv in range(2):
                      cp("dve", vctx[:, cb_, hv, 0:64], cvf[:, cb_, hv * 64:(hv + 1) * 64], ["cvf"], ["vctx"])
              for m in range(2):
                  ts("dve", kms[m][:, 0:256], ckf[:], C("hm", m, 1), None, ALU.mult, None, ["ckf", "cst"], ["kms%d" % m])
                  ts("dve", kms[m][:, 256:256 + T], sk[:], C("hm", m, 1), None, ALU.mult, None, ["sk", "cst"], ["kms%d" % m])
              if debug and l == 0:
                  P.dma(dbg_o["d_sq"], sq[:].rearrange("p a b -> p (a b)"), r=["sq0", "sq1"])
                  P.dma(dbg_o["d_kms0"], kms[0][:], r=["kms0"])
                  P.dma(dbg_o["d_kms1"], kms[1][:], r=["kms1"])
                  P.dma(dbg_o["d_sva"], svaug[:].rearrange("p a b c -> p (a b c)"), r=["svaug"])
                  P.dma(dbg_o["d_vctx"], vctx[:].rearrange("p a b c -> p (a b c)"), r=["vctx"])
              apipe = AttnPipe(pT, "pT")
              for h in (range(4) if ST('swa_attn') else []):
                  qc = 0 if h in (0, 3) else 1
                  kvh = h // 2
                  for t in list(range(8)) + [8, 10]:
                      nq = 1 if t < 8 else 2
                      qT = sq[:, qc, t * 128:(t + nq) * 128]
                      kb = []
                      if t < 8:
                          for cb in range(2):
                              kb.append((kms[kvh][:, cb * 128:(cb + 1) * 128], "kms%d" % kvh, vctx[:, cb, kvh, 0:65], "vctx", None))
                          for dt_ in (-1, 0, 1):
                              tk = t + dt_
                              if tk < 0 or tk > 7:
                                  continue
                              mask = None if dt_ == 0 else ((mlo, "mlo") if dt_ == -1 else (mhi, "mhi"))
                              kb.append((kms[kvh][:, 256 + tk * 128: 256 + (tk + 1) * 128], "kms%d" % kvh, svaug[:, tk, kvh, 0:65], "svaug", mask))
                      else:
                          for tk in (t, t + 1):
                              kb.append((kms[kvh][:, 256 + tk * 128: 256 + (tk + 1) * 128], "kms%d" % kvh, svaug[:, tk, kvh, 0:65], "svaug", None))

                      def post(ob, t=t, h=h, nq=nq):
                          Oh = Osw[h % 2]
                          for qi in range(nq):
                              cp("dve", Oh[:, t + qi, :], PS(ob, 65, qi * 128), [pk(ob)], ["Osw"])
                          if t + nq == NT:
                              ts("dve", den[:], Oh[:, :, 64], esink[:, l * 4 + h: l * 4 + h + 1], None, ALU.add, None,
                                 ["Osw", "esink"], ["den"])
                              V(lambda e: e.reciprocal(out=den[:], in_=den[:]), ["den"], ["den"])
                              tt("dve", ytok[:, :, h * 64:(h + 1) * 64], Oh[:, :, 0:64], den[:].unsqueeze(2).to_broadcast([128, NT, 64]),
                                 ALU.mult, ["Osw", "den"], ["ytok"])
                              if h % 2 == 1:
                                  transpose_part(ytok, "ytok", range(6), [h // 2])
                      apipe.push(qT, "sq%d" % qc, kb, 0.125, post, nq=nq)
              apipe.flush()
              (P.dma(dbg_o["d_y_swa"], yg[:].rearrange("p a b -> p (a b)"), r=["yg"]) if (debug and l == 0) else None)
              out_proj(pi["swaO"], G1)

        barrier()
        with contextlib.ExitStack() as st:
          if ST('diff'):
              def sba(name, shape, dt=F32):
                  rr["ev"] += 1
                  return st.enter_context(nc.sbuf_tensor("a%d_" % rr["ev"] + name, list(shape), dt))
              dq = sba("dq", [128, 2, T], BF16)
              dk = sba("dk", [128, 2, T], BF16)
              kmd = [sba("kmd%d" % i, [128, 256 + T], BF16) for i in range(2)]
              dvaug = sba("dvaug", [128, NT, 4, 80], BF16)
              vctx = sba("vctxd", [128, 2, 4, 80], BF16)
              ckf = sba("ckfd", [128, 2, 256])
              cvf = sba("cvfd", [128, 2, 256])
              pT = [sba("pTd%d" % i, [128, 1280], BF16) for i in range(2)]
              Oall = [sba("Oall", [128, 2, NT, 65])] * 2
              rec = sba("rec", [128, 2, NT])
              y0 = sba("y0", [128, NT, 64])
              y1 = sba("y1", [128, NT, 64])
              ss = sba("ss", [128, NT])
              ytok = sba("ytokd", [128, NT, 256], BF16)
              kvst = [sba("kvstd%d" % i, [128, 256]) for i in range(2)]
              cp("dve", dvaug[:, :, :, 64:66], ones[:, 0:NT * 8].rearrange("p (a b c) -> p a b c", a=NT, b=4), ["ones"], ["dvaug"])
              cp("dve", vctx[:, :, :, 64:66], ones[:, 0:16].rearrange("p (a b c) -> p a b c", a=2, b=4), ["ones"], ["vctxd"])
              for (dst, key, pid) in ((dq, "dq", pi["difF"][0]), (dk, "dk", pi["difF"][1])):
                  def ev(mi, n, b, dst=dst, key=key):
                      cp("act", dst[:, mi, n * 512:(n + 1) * 512], PS(b), [pk(b)], [key + str(mi)])
                  proj_F(pid, hT, hk, 2, ev)

              def ev_dkt(t, b):
                  kb = kvst[t % 2]
                  cp("dve", kb[:], PS(b, 256), [pk(b)], ["kvstd%d" % (t % 2)])
                  P.dma(okv_o[t - 8, :, l, 256:512], kb[:], r=["kvstd%d" % (t % 2)])
              proj_T(pi["difT"][0], range(8, NT), ev_dkt)

              def ev_dvt(t, b):
                  for hv in range(4):
                      cp("act", dvaug[:, t, hv, 0:64], PS(b, 64, hv * 64), [pk(b)], ["dvaug"])
                  if t >= 8:
                      kb = kvst[t % 2]
                      cp("dve", kb[:], PS(b, 256), [pk(b)], ["kvstd%d" % (t % 2)])
                      P.dma(okv_o[t - 8, :, l, 512:768], kb[:], r=["kvstd%d" % (t % 2)])
              proj_T(pi["difT"][1], range(NT), ev_dvt)
              rope = sba("roped", [128, 2048], BF16)
              rope_load(rope, 1)
              for cc in range(2):
                  rope_apply(dq[:, cc, :], "dq%d" % cc, R32, "R32", rope)
                  rope_apply(dk[:, cc, :], "dk%d" % cc, R32, "R32", rope)
              P.dma(ckf[:], ckd_d[:, l, :, :], w=["ckfd"])
              P.dma(cvf[:], cvd_d[:, l, :, :], w=["cvfd"])
              for cb_ in range(2):
                  for hv in range(4):
                      cp("dve", vctx[:, cb_, hv, 0:64], cvf[:, cb_, hv * 64:(hv + 1) * 64], ["cvfd"], ["vctxd"])
              ki = 0
              apipe = AttnPipe(pT, "pTd")
              for h in range(4):
                  cc = h // 2
                  hh = h % 2
                  for comp in range(2):
                      km = kmd[ki % 2]
                      kk = "kmd%d" % (ki % 2)
                      ki += 1
                      mcol = C("cm4", hh * 2 + comp, 1)
                      ts("dve", km[:, 0:256], ckf[:, cc, :], mcol, None, ALU.mult, None, ["ckfd", "cst"], [kk])
                      ts("dve", km[:, 256:256 + T], dk[:, cc, :], mcol, None, ALU.mult, None, ["dk%d" % cc, "cst"], [kk])
                      for t in list(range(8)) + [8, 10]:
                          nq = 1 if t < 8 else 2
                          qT = dq[:, cc, t * 128:(t + nq) * 128]
                          kb = []
                          if t < 8:
                              for cb in range(2):
                                  kb.append((km[:, cb * 128:(cb + 1) * 128], kk, vctx[:, cb, h, 0:65], "vctxd", None))
                              for tk in range(8):
                                  kb.append((km[:, 256 + tk * 128: 256 + (tk + 1) * 128], kk, dvaug[:, tk, h, 0:65], "dvaug", None))
                          else:
                              for tk in (t, t + 1):
                                  kb.append((km[:, 256 + tk * 128: 256 + (tk + 1) * 128], kk, dvaug[:, tk, h, 0:65], "dvaug", None))

                          def post(ob, t=t, h=h, comp=comp, nq=nq):
                              Oa = Oall[h % 2]
                              ok = "Oall"
                              for qi in range(nq):
                                  cp("dve", Oa[:, comp, t + qi, :], PS(ob, 65, qi * 128), [pk(ob)], [ok])
                              if t + nq == NT and comp == 1:
                                  V(lambda e: e.reciprocal(out=rec[:], in_=Oa[:, :, :, 64]), [ok], ["rec"])
                                  tt("dve", y0[:], Oa[:, 0, :, 0:64], rec[:, 0, :].unsqueeze(2).to_broadcast([128, NT, 64]), ALU.mult, [ok, "rec"], ["y0"])
                                  tt("dve", y1[:], Oa[:, 1, :, 0:64], rec[:, 1, :].unsqueeze(2).to_broadcast([128, NT, 64]), ALU.mult, [ok, "rec"], ["y1"])
                                  stt("dve", y0[:], y1[:], nlam[:, l:l + 1], y0[:], ALU.mult, ALU.add, ["y1", "nlam", "y0"], ["y0"])
                                  tt("pool", y1[:], y0[:], y0[:], ALU.mult, ["y0"], ["y1"])
                                  V(lambda e: e.tensor_reduce(out=ss[:], in_=y1[:], axis=AX.X, op=ALU.add), ["y1"], ["ss"])
                                  act(ss[:], ss[:], AF.Sqrt, ["ss", "c_eps"], ["ss"], bias=c_eps[:, 0:1], scale=1.0 / 64)
                                  V(lambda e: e.reciprocal(out=ss[:], in_=ss[:]), ["ss"], ["ss"])
                                  tt("dve", y0[:], y0[:], ss[:].unsqueeze(2).to_broadcast([128, NT, 64]), ALU.mult, ["y0", "ss"], ["y0"])
                                  tt("dve", ytok[:, :, h * 64:(h + 1) * 64], y0[:], dgt[:, l, :].unsqueeze(1).to_broadcast([128, NT, 64]),
                                     ALU.mult, ["y0", "dgt"], ["ytokd"])
                          apipe.push(qT, "dq%d" % cc, kb, 32 ** -0.5, post, nq=nq)
              apipe.flush()
              transpose_y(ytok, "ytokd")
              (P.dma(dbg_o["d_y_diff"], yg[:].rearrange("p a b -> p (a b)"), r=["yg"]) if (debug and l == 0) else None)
              out_proj(pi["difO"], G1)

        barrier()
        if ST('ffn'):
            norm(A2, B2, lambda c, n: hT[:, c, n * 512:(n + 1) * 512], hk)
        with contextlib.ExitStack() as st:
          if ST('ffn'):
              hid = st.enter_context(nc.sbuf_tensor("a_hid_%d" % l, [128, 8, T], BF16))
              rl = [st.enter_context(nc.sbuf_tensor("a_rl%d_%d" % (i, l), [128, 512], BF16)) for i in range(2)]
              xs = [st.enter_context(nc.sbuf_tensor("a_xstg%d_%d" % (i, l), [128, 2048], F32)) for i in range(3)]
              xw = [st.enter_context(nc.sbuf_tensor("a_xwbf%d_%d" % (i, l), [128, 2048], BF16)) for i in range(2)]
              w_extra_begin(xs, xw)
              nada = [0]

              def ada_tick():
                  if l + 1 < depth and nada[0] < 24 and ST('ada'):
                      ada_piece(l + 1, nada[0]); nada[0] += 1
              for q in range(4):
                  f1, f2 = pi["ff"][q]
                  for j in range(4):
                      def ev1(mi, n, b, j=j):
                          r_ = rl[(mi + n) % 2]
                          act(r_[:], PS(b), AF.Relu, [pk(b)], ["rl%d" % ((mi + n) % 2)])
                          tt("pool" if (mi + n) % 2 == 0 else "dve", hid[:, j * 2 + mi, n * 512:(n + 1) * 512], r_[:], r_[:], ALU.mult, ["rl%d" % ((mi + n) % 2)], ["hid"])
                      proj_F(f1[j], hT, hk, 2, ev1)
                      ada_tick()
                  for j in range(4):
                      def ev2(mi, n, b, j=j):
                          m = j * 2 + mi
                          v = 0 if n < 2 else 1
                          xs = x[:, m, n * 512:(n + 1) * 512]
                          stt("dve", xs, PS(b), G2(m, v), xs, ALU.mult, ALU.add, [pk(b), xk[m]] + MODK, [xk[m]])
                      proj_F(f2[j], hid, ["hid"], 2, ev2)
                      ada_tick()
              if l + 1 < depth and ST('ada'):
                  ada_finish(l + 1)
              w_extra_end()

    P.scope = None
    NOST = 8
    ostg = [sb("ostg%d" % i, [128, 512]) for i in range(NOST)]
    okeys = ["ostg%d" % (c % 2) for c in range(8)]
    def fin_stats(n):
        blk = slice(n * 512, (n + 1) * 512)
        bank = 5 if n % 2 == 0 else 2
        rs = rstd2[n % 2]
        rk_ = "rstd%d" % (n % 2)
        for c in range(8):
            s_ = sqt[c % 2]
            act(s_[:], x[:, c, blk], AF.Square, [xk[c]], ["sqt%d" % (c % 2)])
            mm(PS(bank), ones[:], s_[:], c == 0, c == 7, ["ones", "sqt%d" % (c % 2)], [pk(bank)])
        act(rs[:], PS(bank), AF.Sqrt, [pk(bank), "c_eps"], [rk_], bias=c_eps[:, 0:1], scale=1.0 / 1024.0)
        V(lambda e, rs=rs: e.reciprocal(out=rs[:], in_=rs[:]), [rk_], [rk_])

    def fin_apply(n):
        blk = slice(n * 512, (n + 1) * 512)
        rs = rstd2[n % 2]
        rk_ = "rstd%d" % (n % 2)
        for c in range(8):
            oi = (n * 8 + c) % NOST
            o = ostg[oi]
            stt("dve", o[:], x[:, c, blk], C("g_fin", c, 1), rs[:], ALU.mult, ALU.mult, [xk[c], rk_, "cst"], ["ostg%d" % oi])
            P.dma(yT_o[c * 128:(c + 1) * 128, blk], o[:], r=["ostg%d" % oi])
    fin_stats(0)
    fin_stats(1)
    fin_apply(0)
    fin_stats(2)
    fin_apply(1)
    fin_apply(2)
    P.emit()
    P.stats['minfree'] = minfree['v']
    return nc, P


def _consts():
    coff, CW = cst_layout()
    c = np.zeros((128, CW), np.float32)

    def put(name, arr):
        o, w = coff[name]
        c[:, o:o + w] = np.asarray(arr, np.float32).reshape(128, w)
    put("ident", np.eye(128))
    R64 = np.zeros((128, 128)); R32 = np.zeros((128, 128))
    for blk in range(2):
        for i in range(64):
            if i < 32:
                R64[blk * 64 + i, blk * 64 + i + 32] = -1.0
            else:
                R64[blk * 64 + i, blk * 64 + i - 32] = 1.0
    for blk in range(4):
        for i in range(32):
            if i < 16:
                R32[blk * 32 + i, blk * 32 + i + 16] = -1.0
            else:
                R32[blk * 32 + i, blk * 32 + i - 16] = 1.0
    put("R64T", R64.T); put("R32T", R32.T)
    j = np.arange(128)[:, None]; i = np.arange(128)[None, :]
    put("mlo", np.where(j >= i, 0.0, -480.0))
    put("mhi", np.where(j <= i, 0.0, -480.0))
    m = np.arange(128)[:, None]; n = np.arange(128)[None, :]
    put("Ef", np.maximum(n - m, 0)); put("Eb", np.maximum(m - n, 0))
    put("Lf8", (n >= m) / 8.0); put("Lb8", (m >= n) / 8.0)
    bd = np.zeros((128, 128)); bd[:64, :64] = 1; bd[64:, 64:] = 1
    put("BD", bd)
    put("idx1", np.broadcast_to(np.arange(128)[None, :] + 1.0, (128, 128)))
    put("idxr", np.broadcast_to(128.0 - np.arange(128)[None, :], (128, 128)))
    put("cmf", (127.0 - np.arange(128))[:, None]); put("cmb", (np.arange(128) * 1.0)[:, None])
    p = np.arange(128)
    put("hm", np.stack([p < 64, p >= 64], 1))
    put("cm4", np.stack([(p // 32) == k for k in range(4)], 1))
    pos = np.arange(1024)
    row = (pos // 64).astype(np.float32); col = (pos % 64).astype(np.float32)

    def tab(dim):
        nn = dim // 4
        inv = (10000.0 ** (-np.arange(nn, dtype=np.float32) / nn)).astype(np.float32)
        ang = np.concatenate([row[:, None] * inv, col[:, None] * inv], -1).astype(np.float32)
        idx = (np.arange(128) % dim) % (dim // 2)
        return np.cos(ang)[:, idx].T.astype(np.float32), np.sin(ang)[:, idx].T.astype(np.float32)
    c64, s64 = tab(64); c32, s32 = tab(32)
    rope = np.concatenate([c64, s64, c32, s32], 1).astype(np.float32)
    return c, rope, coff


def _fm(v):
    v = np.asarray(v)
    n = v.shape[-1] // 128
    lead = v.shape[:-1]
    return np.moveaxis(v.reshape(lead + (n, 128)), -1, 0)


_CACHE = {}


def kernel(**inp):
    I = {k: np.asarray(v) for k, v in inp.items()}
    if "prog" not in _CACHE:
        _CACHE["prog"] = build_program()
    nc, P = _CACHE["prog"]
    in_maps = prep_inputs(I)
    res = run_bass_kernel_spmd(nc, in_maps, core_ids=list(range(8)))
    return assemble(res.results)


def prep_inputs(I, cores=range(8)):
    cbase, rope, coff = _consts()

    wts = np.zeros((DEPTH * PPL, 128, 2048), np.float32)

    def kpiece(W, cols):
        out = np.zeros((128, 8, 256), np.float32)
        cols = np.asarray(cols)
        valid = cols >= 0
        Wc = W[:, cols[valid]].reshape(8, 128, -1)
        out[:, :, np.nonzero(valid)[0]] = np.transpose(Wc, (1, 0, 2))
        return out.reshape(128, 2048)
    ar = np.arange
    for l in range(DEPTH):
        pi = piece_ids(l)
        wa, wi, wo, w1, w2 = I["w_ada"][l], I["w_in"][l], I["w_out"][l], I["w_ff1"][l], I["w_ff2"][l]
        for j in range(24):
            wts[pi["ada"][j]] = kpiece(wa, ar(j * 256, (j + 1) * 256))
        wts[pi["retF"][0]] = kpiece(wi, ar(0, 256))
        neg = -np.ones(64, int)
        wts[pi["retF"][1]] = kpiece(wi, np.concatenate([ar(256, 320), neg, neg, ar(320, 384)]))
        wts[pi["retF"][2]] = kpiece(wi, np.concatenate([ar(384, 448), neg, neg, ar(448, 512)]))
        wts[pi["retT"][0]] = kpiece(wi, ar(256, 512))
        wts[pi["retT"][1]] = kpiece(wi, ar(512, 768))
        wts[pi["retT"][2]] = kpiece(wi, ar(768, 1024))
        wts[pi["lruF"][0]] = kpiece(wi, ar(1024, 1280))
        wts[pi["lruF"][1]] = kpiece(wi, ar(1280, 1536))
        sqb = 1536
        wts[pi["swaF"][0]] = kpiece(wi, np.concatenate([ar(sqb, sqb + 64), ar(sqb + 192, sqb + 256), ar(sqb + 64, sqb + 192)]))
        wts[pi["swaF"][1]] = kpiece(wi, np.concatenate([ar(1792, 1920), -np.ones(128, int)]))
        wts[pi["swaT"][0]] = kpiece(wi, ar(1792, 2048))
        wts[pi["difF"][0]] = kpiece(wi, ar(2048, 2304))
        wts[pi["difF"][1]] = kpiece(wi, ar(2304, 2560))
        wts[pi["difT"][0]] = kpiece(wi, ar(2304, 2560))
        wts[pi["difT"][1]] = kpiece(wi, ar(2560, 2816))
        for g, name in enumerate(("retO", "lruO", "swaO", "difO")):
            blkw = wo[g * 256:(g + 1) * 256, :].reshape(2, 128, 1024)
            wts[pi[name]] = np.transpose(blkw, (1, 0, 2)).reshape(128, 2048)
        for q in range(4):
            f1, f2 = pi["ff"][q]
            for j in range(4):
                wts[f1[j]] = kpiece(w1, ar(q * 1024 + j * 256, q * 1024 + (j + 1) * 256))
                wts[f2[j]] = kpiece(w2[q * 1024:(q + 1) * 1024, :], ar(j * 256, (j + 1) * 256))
    lruw = np.zeros((128, DEPTH, 8, 128), np.float32)
    for l in range(DEPTH):
        for d in range(2):
            for gsel, nm in enumerate(("lru_w_a", "lru_w_x")):
                for cc in range(2):
                    for bb in range(2):
                        blk = I[nm][l, d, cc * 2 + bb]
                        lruw[bb * 64:(bb + 1) * 64, l, d * 4 + gsel * 2 + cc, bb * 64:(bb + 1) * 64] = blk
    lruw = lruw.reshape(128, DEPTH * 8 * 128)

    def putc(c, name, arr):
        o, w = coff[name]
        c[:, o:o + w] = np.asarray(arr, np.float32).reshape(128, w)

    in_maps = []
    for core in cores:
        c = cbase.copy()
        putc(c, "b_ada", _fm(I["b_ada"]))
        putc(c, "g_mix", _fm(I["norm_mix_g"])); putc(c, "g_mlp", _fm(I["norm_mlp_g"])); putc(c, "g_fin", _fm(I["final_norm_g"]))
        cv = np.stack([I["c"][core], I["c_ctx"]], 0)
        putc(c, "cT", np.transpose(_fm(cv), (0, 2, 1)))
        putc(c, "conv_w", _fm(I["lru_conv_w"])); putc(c, "conv_b", _fm(I["lru_conv_b"]))
        putc(c, "b_a", _fm(I["lru_b_a"])); putc(c, "b_x", _fm(I["lru_b_x"])); putc(c, "lam", _fm(I["lru_lambda"]))
        putc(c, "h0", _fm(I["state_lru"][core]))
        rd = I["ret_decay"]
        putc(c, "rd_row", np.broadcast_to(rd.reshape(1, 32), (128, 32)))
        rp = np.zeros((128, DEPTH, 2, 2), np.float32)
        for j in range(2):
            rp[:64, :, :, j] = rd[:, :, 2 * j]; rp[64:, :, :, j] = rd[:, :, 2 * j + 1]
        putc(c, "rd_pair", rp)
        putc(c, "sink", np.broadcast_to(I["swa_sink"].reshape(1, 16), (128, 16)))
        putc(c, "dlam", np.broadcast_to(I["diff_lambda"].reshape(1, -1), (128, DEPTH * 128)))
        putc(c, "dng", np.broadcast_to(I["diff_norm_g"].reshape(1, -1), (128, DEPTH * 64)))
        putc(c, "gng", np.broadcast_to(I["ret_gn_g"].reshape(1, -1), (128, DEPTH * 256)))
        xall = np.concatenate([I["x_sample"][core], I["x_prompt"][2 * core], I["x_prompt"][2 * core + 1]], 0)
        cks = np.transpose(I["cache_swa_k"][core].reshape(DEPTH, 256, 128), (2, 0, 1))
        cvs = np.transpose(I["cache_swa_v"][core].reshape(DEPTH, 2, 128, 128), (2, 0, 1, 3))
        ckd = np.transpose(I["cache_diff_k"][core].reshape(DEPTH, 256, 2, 128), (3, 0, 2, 1))
        cvd = np.transpose(I["cache_diff_v"][core].reshape(DEPTH, 2, 128, 256), (2, 0, 1, 3))
        sr = I["state_ret"][core]
        sret = np.zeros((128, DEPTH, 4, 128), np.float32)
        for d in range(2):
            for j in range(2):
                sret[:64, :, d * 2 + j, :64] = np.transpose(sr[:, d, 2 * j], (1, 0, 2))
                sret[64:, :, d * 2 + j, 64:] = np.transpose(sr[:, d, 2 * j + 1], (1, 0, 2))
        in_maps.append({"xT": np.ascontiguousarray(xall.T), "wts": wts, "cst": c, "rope": rope, "lruw": lruw,
                        "cks": np.ascontiguousarray(cks), "cvs": np.ascontiguousarray(cvs), "ckd": np.ascontiguousarray(ckd),
                        "cvd": np.ascontiguousarray(cvd), "sret": sret})
    return in_maps


def assemble(R):
    y_prompt = np.zeros((16, 256, 1024), np.float32); y_sample = np.zeros((8, 1024, 1024), np.float32)
    nsr = np.zeros((16, DEPTH, 2, 4, 64, 64), np.float32); nsl = np.zeros((16, DEPTH, 2, 256), np.float32)
    nsk = np.zeros((16, DEPTH, 256, 2, 64), np.float32); nsv = np.zeros((16, DEPTH, 256, 2, 64), np.float32)
    ndk = np.zeros((16, DEPTH, 256, 4, 64), np.float32); ndv = np.zeros((16, DEPTH, 256, 4, 64), np.float32)
    for core in range(8):
        r = R[core]
        y = r["yT"].T
        y_sample[core] = y[0:1024]
        osr = r["o_sret"]
        osl = r["o_slru"].reshape(128, 2, DEPTH, 2, 2)
        okv = r["o_kv"]
        for sj in range(2):
            b = 2 * core + sj
            y_prompt[b] = y[1024 + 256 * sj: 1280 + 256 * sj]
            for d in range(2):
                for j in range(2):
                    nsr[b, :, d, 2 * j] = np.transpose(osr[:64, sj, :, d * 2 + j, :64], (1, 0, 2))
                    nsr[b, :, d, 2 * j + 1] = np.transpose(osr[64:, sj, :, d * 2 + j, 64:], (1, 0, 2))
            nsl[b] = np.transpose(osl[:, sj], (1, 2, 3, 0)).reshape(DEPTH, 2, 256)
            kv = np.transpose(okv[2 * sj:2 * sj + 2], (2, 0, 1, 3)).reshape(DEPTH, 256, 768)
            nsk[b] = kv[:, :, 0:128].reshape(DEPTH, 256, 2, 64)
            nsv[b] = kv[:, :, 128:256].reshape(DEPTH, 256, 2, 64)
            ndk[b] = kv[:, :, 256:512].reshape(DEPTH, 256, 4, 64)
            ndv[b] = kv[:, :, 512:768].reshape(DEPTH, 256, 4, 64)
    return (y_prompt, y_sample, nsr, nsl, nsk, nsv, ndk, ndv)
```

```python
import contextlib
import numpy as np
import ml_dtypes
import concourse.bass as bass
import concourse.mybir as mybir
from concourse.bass_utils import run_bass_kernel_spmd

F32 = mybir.dt.float32
BF16 = mybir.dt.bfloat16
AF = mybir.ActivationFunctionType
ALU = mybir.AluOpType
AX = mybir.AxisListType

ENGS = ("pe", "act", "dve", "pool", "sp")
DMA_WIN = 12
DEPTH = 4
T = 1536
NT = 12
EPS = 1e-6
NSTG = 3
NWBF = 3
PPL = 24 + 3 + 3 + 1 + 2 + 1 + 2 + 1 + 1 + 2 + 2 + 1 + 32


class Op:
    __slots__ = ("idx", "eng", "fn", "deps", "dma", "signal", "sem", "val", "waits")

    def __init__(self, idx, eng, fn, dma):
        self.idx = idx; self.eng = eng; self.fn = fn; self.dma = dma
        self.deps = set(); self.signal = False; self.sem = None; self.val = 0; self.waits = []


class Prog:
    def __init__(self, nc):
        self.nc = nc
        self.ops = []
        self.last_w = {}
        self.readers = {}
        self.scope = None

    def add(self, eng, fn, r=(), w=(), dma=False, noscope=False):
        op = Op(len(self.ops), eng, fn, dma)
        w = list(w) + [k for k in r if isinstance(k, str) and k.startswith("ps")]
        r = [k for k in r if not (isinstance(k, str) and k.startswith("ps"))]
        if self.scope is not None and not noscope:
            r.append(self.scope)
        for k in r:
            lw = self.last_w.get(k)
            if lw is not None:
                op.deps.add((lw, "raw"))
        for k in w:
            lw = self.last_w.get(k)
            if lw is not None:
                op.deps.add((lw, "waw"))
            for rd in self.readers.get(k, ()):
                op.deps.add((rd, "war"))
        for k in r:
            self.readers.setdefault(k, []).append(op.idx)
        for k in w:
            self.last_w[k] = op.idx
            self.readers[k] = []
        self.ops.append(op)
        return op

    def dma(self, out, in_, r=(), w=(), noscope=False):
        return self.add("sp", lambda e: e.dma_start(out=out, in_=in_), r=r, w=w, dma=True, noscope=noscope)

    def emit(self):
        nc = self.nc
        ops = self.ops
        for op in ops:
            nd = set()
            for (d, kind) in op.deps:
                dop = ops[d]
                if d == op.idx:
                    continue
                if dop.eng == op.eng and not dop.dma and not op.dma and kind != "raw":
                    continue
                if dop.eng == op.eng and op.eng == "pe":
                    continue
                nd.add(d)
            op.deps = nd
            for d in nd:
                ops[d].signal = True
        stack = contextlib.ExitStack()
        eng_sem = {}; eng_cnt = {}; dma_sems = {}; dma_cnt = {}
        for e in ENGS:
            eng_sem[e] = stack.enter_context(nc.semaphore("c_" + e))
            eng_cnt[e] = 0; dma_sems[e] = None; dma_cnt[e] = 0
        extra_waits = {}
        final_waits = []
        for op in ops:
            if op.dma:
                if dma_sems[op.eng] is None:
                    dma_sems[op.eng] = [stack.enter_context(nc.semaphore("d_%s%d" % (op.eng, i)))
                                        for i in range(DMA_WIN)]
                k = dma_cnt[op.eng]; dma_cnt[op.eng] += 1
                op.sem = dma_sems[op.eng][k % DMA_WIN]
                op.val = 16 * (k // DMA_WIN + 1)
                op.signal = True
                if k >= DMA_WIN:
                    extra_waits[op.idx] = [(op.sem, op.val - 16)]
            elif op.signal:
                eng_cnt[op.eng] += 1
                op.sem = eng_sem[op.eng]; op.val = eng_cnt[op.eng]
        for e in ENGS:
            if dma_sems[e] is not None:
                k = dma_cnt[e]
                for slot in range(min(k, DMA_WIN)):
                    n = (k - 1 - slot) // DMA_WIN + 1
                    final_waits.append((dma_sems[e][slot], 16 * n))
        waited = {e: {} for e in ENGS}
        for op in ops:
            need = {}
            for d in op.deps:
                dop = ops[d]
                if need.get(dop.sem, (None, 0))[1] < dop.val:
                    need[dop.sem] = (dop.sem, dop.val)
            for (s, v) in extra_waits.get(op.idx, ()):
                if need.get(s, (None, 0))[1] < v:
                    need[s] = (s, v)
            wl = []
            for key, (s, v) in need.items():
                if waited[op.eng].get(key, 0) >= v:
                    continue
                waited[op.eng][key] = v
                wl.append((s, v))
            op.waits = wl
        by_eng = {e: [op for op in ops if op.eng == e] for e in ENGS}
        self.stats = {e: len(by_eng[e]) for e in ENGS}
        self.stats["waits"] = sum(len(op.waits) for op in ops)
        self.stats["maxsem"] = dict(eng_cnt)

        def run(eh, ename, final=False):
            for op in by_eng[ename]:
                for (s, v) in op.waits:
                    eh.wait_ge(s, v)
                inst = op.fn(eh)
                if op.signal:
                    inst.then_inc(op.sem, 16 if op.dma else 1)
            if final:
                for (s, v) in final_waits:
                    eh.wait_ge(s, v)

        with stack:
            with nc.Block() as block:
                @block.tensor
                def _(e):
                    run(e, "pe")

                @block.scalar
                def _(e):
                    run(e, "act")

                @block.vector
                def _(e):
                    run(e, "dve")

                @block.gpsimd
                def _(e):
                    run(e, "pool")

                @block.sync
                def _(e):
                    run(e, "sp", final=True)


def cst_layout():
    entA = [("BD", 128), ("hm", 2), ("cm4", 4),
            ("b_ada", DEPTH * 48), ("g_mix", DEPTH * 8), ("g_mlp", DEPTH * 8), ("g_fin", 8),
            ("conv_w", DEPTH * 8), ("conv_b", DEPTH * 2), ("b_a", DEPTH * 4), ("b_x", DEPTH * 4),
            ("h0", DEPTH * 4), ("gng", DEPTH * 256)]
    entB = [("ident", 128), ("R64T", 128), ("R32T", 128), ("mlo", 128), ("mhi", 128), ("Ef", 128), ("Eb", 128),
            ("Lf8", 128), ("Lb8", 128), ("idx1", 128), ("idxr", 128), ("cmf", 1), ("cmb", 1), ("cT", 16),
            ("lam", DEPTH * 4), ("rd_row", 32), ("rd_pair", 16), ("sink", 16), ("dlam", DEPTH * 128), ("dng", DEPTH * 64)]
    off = {}
    o = 0
    for n, w in entA + entB:
        off[n] = (o, w)
        o += w
    off["_WA"] = (sum(w for _, w in entA), 0)
    return off, o


def piece_ids(l):
    b = l * PPL
    d = {}
    d["ada"] = list(range(b, b + 24)); b += 24
    d["retF"] = [b, b + 1, b + 2]; b += 3
    d["retT"] = [b, b + 1, b + 2]; b += 3
    d["retO"] = b; b += 1
    d["lruF"] = [b, b + 1]; b += 2
    d["lruO"] = b; b += 1
    d["swaF"] = [b, b + 1]; b += 2
    d["swaT"] = [b]; b += 1
    d["swaO"] = b; b += 1
    d["difF"] = [b, b + 1]; b += 2
    d["difT"] = [b, b + 1]; b += 2
    d["difO"] = b; b += 1
    d["ff"] = []
    for q in range(4):
        d["ff"].append((list(range(b, b + 4)), list(range(b + 4, b + 8)))); b += 8
    assert b == (l + 1) * PPL + 0 or True
    return d


def build_program(depth=DEPTH, stages=None, debug=False):
    ST = (lambda name: True) if stages is None else (lambda name: name in stages)
    nc = bass.Bass("TRN2", target_bir_lowering=False)
    P = Prog(nc)
    coff, CW = cst_layout()

    def din(name, shape):
        return nc.dram_tensor(name, list(shape), F32, kind="ExternalInput").ap()

    def dout(name, shape):
        return nc.dram_tensor(name, list(shape), F32, kind="ExternalOutput").ap()

    xT_d = din("xT", [1024, T])
    wts_d = din("wts", [DEPTH * PPL, 128, 2048])
    cst_d = din("cst", [128, CW])
    rope_d = din("rope", [128, 4096])
    lruw_d = din("lruw", [128, DEPTH * 8 * 128])
    cks_d = din("cks", [128, DEPTH, 256])
    cvs_d = din("cvs", [128, DEPTH, 2, 128])
    ckd_d = din("ckd", [128, DEPTH, 2, 256])
    cvd_d = din("cvd", [128, DEPTH, 2, 256])
    sret_d = din("sret", [128, DEPTH, 4, 128])
    yT_o = dout("yT", [1024, T])
    osret_o = dout("o_sret", [128, 2, DEPTH, 4, 128])
    oslru_o = dout("o_slru", [128, 2 * DEPTH * 4])
    okv_o = dout("o_kv", [4, 128, DEPTH, 768])

    def sb(name, shape, dt=F32):
        return nc.alloc_sbuf_tensor("s_" + name, list(shape), dt)

    dbg_o = {}
    if debug:
        dbg_o["d_hT"] = nc.dram_tensor("d_hT", [128, 8 * T], BF16, kind="ExternalOutput").ap()
        dbg_o["d_mod"] = nc.dram_tensor("d_mod", [128, 96], F32, kind="ExternalOutput").ap()
        dbg_o["d_sq"] = nc.dram_tensor("d_sq", [128, 2 * T], BF16, kind="ExternalOutput").ap()
        dbg_o["d_kms0"] = nc.dram_tensor("d_kms0", [128, 256 + T], BF16, kind="ExternalOutput").ap()
        dbg_o["d_kms1"] = nc.dram_tensor("d_kms1", [128, 256 + T], BF16, kind="ExternalOutput").ap()
        dbg_o["d_sva"] = nc.dram_tensor("d_sva", [128, NT * 2 * 80], BF16, kind="ExternalOutput").ap()
        dbg_o["d_vctx"] = nc.dram_tensor("d_vctx", [128, 2 * 2 * 80], BF16, kind="ExternalOutput").ap()
        dbg_o["d_osw"] = nc.dram_tensor("d_osw", [128, NT * 65], F32, kind="ExternalOutput").ap()
        for g_ in ("ret", "lru", "swa", "diff"):
            dbg_o["d_y_" + g_] = nc.dram_tensor("d_y_" + g_, [128, 2 * T], BF16, kind="ExternalOutput").ap()

    psum = nc.alloc_psum_tensor("psum", [128, 4096], F32)

    def PS(b, n=512, o=0):
        return psum[:, b * 512 + o: b * 512 + o + n]

    def pk(b):
        return "ps%d" % b

    x = sb("x", [128, 8, T])
    hT = sb("hT", [128, 8, T], BF16)
    yg = sb("yg", [128, 2, T], BF16)
    WA = coff["_WA"][0]
    cst = sb("cst", [128, WA])
    stg = [sb("stg%d" % i, [128, 2048]) for i in range(NSTG)]
    wbf = [sb("wbf%d" % i, [128, 2048], BF16) for i in range(NWBF)]
    ident = sb("ident", [128, 128], BF16)
    ones = sb("ones", [128, 128], BF16)
    R64 = sb("R64", [128, 128], BF16)
    R32 = sb("R32", [128, 128], BF16)
    mlo = sb("mlo", [128, 128], BF16)
    mhi = sb("mhi", [128, 128], BF16)
    DTm = sb("DTm", [128, DEPTH * 4, 128], BF16)
    Xi = sb("Xi", [128, DEPTH * 4, 128], BF16)
    zeta = sb("zeta", [128, DEPTH * 2, 4])
    Gp = sb("Gp", [128, DEPTH * 4])
    lgrow = sb("lgrow", [128, 32])
    lgpair = sb("lgpair", [128, 16])
    esink = sb("esink", [128, 16])
    nlam = sb("nlam", [128, DEPTH])
    dgt = sb("dgt", [128, DEPTH, 64])
    sa8 = sb("sa8", [128, DEPTH * 4])
    sa16 = sb("sa16", [128, DEPTH * 4])
    csil = sb("csil", [128, 8, 2], BF16)
    mods = [sb("mod%d" % i, [128, 48, 2]) for i in range(2)]
    modAs = [sb("modA%d" % i, [128, 2, 8, 2]) for i in range(2)]
    MODK = ["mod0", "mod1", "modA0", "modA1"]
    c_eps = sb("c_eps", [128, 1])
    c_one = sb("c_one", [128, 1])
    bar_t = sb("bar_t", [128, 1])
    rstd = sb("rstd", [128, 512])
    rstd2 = [rstd, sb("rstd1", [128, 512])]
    sqt = [sb("sqt%d" % i, [128, 512], BF16) for i in range(2)]
    ntmp = [sb("ntmp%d" % i, [128, 512]) for i in range(2)]
    adarow = [sb("adarow%d" % i, [2, 256]) for i in range(2)]
    identf = sb("identf", [2, 2])
    setup_stack = contextlib.ExitStack()
    cstB = setup_stack.enter_context(nc.sbuf_tensor("s_cstB", [128, CW - WA], F32))

    def C(name, lo=0, n=None):
        o, w = coff[name]
        if n is None:
            n = w - lo
        if o >= WA:
            return cstB[:, o - WA + lo: o - WA + lo + n]
        return cst[:, o + lo: o + lo + n]

    rr = {"ev": 0}

    def V(fn, r, w):
        return P.add("dve", fn, r, w)

    def G(fn, r, w):
        return P.add("pool", fn, r, w)

    def A(fn, r, w):
        return P.add("act", fn, r, w)

    def M(fn, r, w):
        return P.add("pe", fn, r, w)

    def act(out, in_, func, r, w, bias=None, scale=1.0):
        kw = {}
        if bias is not None:
            kw["bias"] = bias
        return A(lambda e: e.activation(out=out, in_=in_, func=func, scale=scale, **kw), r, w)

    def tt(eng, out, a, b, op, r, w):
        return P.add(eng, lambda e: e.tensor_tensor(out=out, in0=a, in1=b, op=op), r, w)

    def ts(eng, out, a, s1, s2, op0, op1, r, w):
        if s2 is None:
            return P.add(eng, lambda e: e.tensor_scalar(out=out, in0=a, scalar1=s1, scalar2=None, op0=op0), r, w)
        return P.add(eng, lambda e: e.tensor_scalar(out=out, in0=a, scalar1=s1, scalar2=s2, op0=op0, op1=op1), r, w)

    def stt(eng, out, a, s, b, op0, op1, r, w):
        return P.add(eng, lambda e: e.scalar_tensor_tensor(out=out, in0=a, scalar=s, in1=b, op0=op0, op1=op1), r, w)

    def cp(eng, out, in_, r, w):
        if eng == "act":
            return A(lambda e: e.activation(out=out, in_=in_, func=AF.Copy), r, w)
        return P.add(eng, lambda e: e.tensor_copy(out=out, in_=in_), r, w)

    def mm(out, lhsT, rhs, start, stop, r, w):
        return M(lambda e: e.matmul(out, lhsT=lhsT, rhs=rhs, start=start, stop=stop), r, w)

    def mmk(out, lhs_fn, rhs_fn, nk, r, w):
        def f(e):
            for kc in range(nk):
                ins = e.matmul(out, lhsT=lhs_fn(kc), rhs=rhs_fn(kc), start=(kc == 0), stop=(kc == nk - 1))
            return ins
        return M(f, r, w)

    minfree = {"v": 1 << 30}

    def barrier():
        minfree["v"] = min(minfree["v"], nc.sbuf_bytes_remaining)
        P.scope = None
        G(lambda e: e.memset(bar_t[:], 0.0), [], ["ARENA", "bar_t"])
        P.scope = "ARENA"

    order = []
    for l in range(depth):
        pi = piece_ids(l)
        if l == 0:
            order += pi["ada"]
        order += pi["retF"] + pi["retT"] + [pi["retO"]] + pi["lruF"] + [pi["lruO"]] + pi["swaF"] + pi["swaT"] + \
            [pi["swaO"]] + pi["difF"] + pi["difT"] + [pi["difO"]]
        nada = 0
        nxt = piece_ids(l + 1)["ada"] if l + 1 < depth else []
        for q in range(4):
            for p_ in pi["ff"][q][0] + pi["ff"][q][1]:
                order.append(p_)
                if nada < len(nxt):
                    order.append(nxt[nada]); nada += 1
    ffn_piece = set()
    for l in range(depth):
        pi = piece_ids(l)
        for q in range(4):
            ffn_piece.update(pi["ff"][q][0] + pi["ff"][q][1])
        if l + 1 < depth:
            ffn_piece.update(piece_ids(l + 1)["ada"])
    ws = {"issued": 0, "used": 0, "stg_rr": 0, "slot": {}, "free": list(range(NWBF)), "xstg": [], "xwbf": []}
    cast_eng = ["dve", "act", "dve", "act"]

    def stg_of(i):
        return (stg[i], "stg%d" % i) if i < NSTG else (ws["xstg"][i - NSTG], "xstg%d" % (i - NSTG))

    def wbf_of(i):
        return (wbf[i], "wbf%d" % i) if i < NWBF else (ws["xwbf"][i - NWBF], "xwbf%d" % (i - NWBF))

    def w_issue():
        k = ws["issued"]
        if k >= len(order):
            return False
        pid = order[k]
        big = (pid in ffn_piece) and len(ws["xstg"]) > 0
        cand = [b for b in ws["free"] if b < NWBF or big]
        if not cand:
            return False
        b = cand[0]
        ws["free"].remove(b)
        nst = NSTG + (len(ws["xstg"]) if big else 0)
        si = ws["stg_rr"] % nst
        ws["stg_rr"] += 1
        st_, skey = stg_of(si)
        wb_, wkey = wbf_of(b)
        ws["slot"][k] = b
        nsc = not (si >= NSTG or b >= NWBF)
        P.dma(st_[:], wts_d[pid], r=[], w=[skey], noscope=nsc)
        eng = cast_eng[k % 4]
        if eng == "act":
            P.add("act", lambda e: e.activation(out=wb_[:], in_=st_[:], func=AF.Copy), [skey], [wkey], noscope=nsc)
        else:
            P.add(eng, lambda e: e.tensor_copy(out=wb_[:], in_=st_[:]), [skey], [wkey], noscope=nsc)
        ws["issued"] += 1
        return True

    def w_next(expect):
        while order[ws["used"]] != expect:
            ws["used"] += 1
        k = ws["used"]
        for kk in list(ws["slot"].keys()):
            if kk < k:
                ws["free"].append(ws["slot"].pop(kk))
        assert ws["issued"] >= k or ws["issued"] == k or True
        if ws["issued"] < k:
            ws["issued"] = k
        while ws["issued"] < len(order) and ws["issued"] < k + 8:
            if not w_issue():
                break
        assert k in ws["slot"], (k, ws)
        ws["used"] += 1
        return wbf_of(ws["slot"][k])

    def w_extra_begin(xs, xw):
        ws["xstg"] = xs
        ws["xwbf"] = xw
        ws["free"] += [NWBF + i for i in range(len(xw))]

    def w_extra_end():
        for kk, b in ws["slot"].items():
            assert b < NWBF, "extra bf16 slot still in use at phase end"
        ws["free"] = [b for b in ws["free"] if b < NWBF]
        ws["xstg"] = []
        ws["xwbf"] = []

    P.scope = "ARENA"
    P.dma(cst[:], cst_d[:, 0:WA], w=["cst"])
    P.dma(cstB[:], cst_d[:, WA:CW], w=["cst"])
    for c in range(8):
        P.dma(x[:, c, :], xT_d[c * 128:(c + 1) * 128, :], w=["x%d" % c])
    G(lambda e: e.memset(ones[:], 1.0), [], ["ones"])
    G(lambda e: e.memset(c_eps[:], EPS), [], ["c_eps"])
    G(lambda e: e.memset(c_one[:], 1.0), [], ["c_one"])
    cp("dve", ident[:], C("ident"), ["cst"], ["ident"])
    cp("dve", identf[:], C("ident")[0:2, 0:2], ["cst"], ["identf"])
    cp("dve", R64[:], C("R64T"), ["cst"], ["R64"])
    cp("dve", R32[:], C("R32T"), ["cst"], ["R32"])
    cp("dve", mlo[:], C("mlo"), ["cst"], ["mlo"])
    cp("dve", mhi[:], C("mhi"), ["cst"], ["mhi"])
    act(csil[:].rearrange("p a b -> p (a b)"), C("cT"), AF.Silu, ["cst"], ["csil"])
    for (dst, src, n, key) in ((lgrow, "rd_row", 32, "lgrow"), (lgpair, "rd_pair", 16, "lgpair")):
        act(dst[:], C(src), AF.Exp, ["cst"], [key], scale=-1.0)
        act(dst[:], dst[:], AF.Ln, [key, "c_one"], [key], bias=c_one[:, 0:1])
        ts("dve", dst[:], dst[:], -1.0, None, ALU.mult, None, [key], [key])
    etmp = setup_stack.enter_context(nc.sbuf_tensor("s_etmp", [128, 128], F32))
    etmp2 = setup_stack.enter_context(nc.sbuf_tensor("s_etmp2", [128, 128], F32))
    for l in range(depth):
        for h in range(4):
            act(etmp[:], C("Ef"), AF.Exp, ["cst", "lgrow"], ["etmp"], scale=lgrow[:, l * 8 + h: l * 8 + h + 1])
            tt("dve", etmp[:], etmp[:], C("Lf8"), ALU.mult, ["etmp", "cst"], ["etmp"])
            act(etmp2[:], C("Eb"), AF.Exp, ["cst", "lgrow"], ["etmp2"], scale=lgrow[:, l * 8 + 4 + h: l * 8 + 4 + h + 1])
            tt("dve", etmp2[:], etmp2[:], C("Lb8"), ALU.mult, ["etmp2", "cst"], ["etmp2"])
            tt("dve", DTm[:, l * 4 + h, :], etmp[:], etmp2[:], ALU.add, ["etmp", "etmp2"], ["DTm"])
        for d in range(2):
            for j in range(2):
                i = l * 4 + d * 2 + j
                act(Xi[:, i, :], C("idx1") if d == 0 else C("idxr"), AF.Exp, ["cst", "lgpair"], ["Xi"], scale=lgpair[:, i:i + 1])
                act(Gp[:, i:i + 1], lgpair[:, i:i + 1], AF.Exp, ["lgpair"], ["Gp"], scale=128.0)
            act(zeta[:, l * 2 + d, :], lgrow[:, l * 8 + d * 4: l * 8 + d * 4 + 4], AF.Exp, ["lgrow", "cst"], ["zeta"],
                scale=(C("cmf") if d == 0 else C("cmb")))
    ts("dve", zeta[:], zeta[:], 0.125, None, ALU.mult, None, ["zeta"], ["zeta"])
    act(esink[:], C("sink"), AF.Exp, ["cst"], ["esink"])
    dl1 = setup_stack.enter_context(nc.sbuf_tensor("s_dl1", [128, DEPTH, 2, 32], F32))
    dl2 = setup_stack.enter_context(nc.sbuf_tensor("s_dl2", [128, DEPTH * 2], F32))
    dlv = C("dlam").rearrange("p (l a b c) -> p l a b c", l=DEPTH, a=2, b=2)
    tt("dve", dl1[:], dlv[:, :, :, 0, :], dlv[:, :, :, 1, :], ALU.mult, ["cst"], ["dl1"])
    V(lambda e: e.tensor_reduce(out=dl2[:], in_=dl1[:].rearrange("p l a c -> p (l a) c"), axis=AX.X, op=ALU.add), ["dl1"], ["dl2"])
    act(dl2[:], dl2[:], AF.Exp, ["dl2"], ["dl2"])
    for l in range(depth):
        import math
        li = 0.8 - 0.6 * math.exp(-0.3 * l)
        stt("dve", nlam[:, l:l + 1], dl2[:, 2 * l + 1: 2 * l + 2], -li, dl2[:, 2 * l: 2 * l + 1], ALU.add, ALU.subtract, ["dl2"], ["nlam"])
        ts("dve", dgt[:, l, :], C("dng", l * 64, 64), 1.0 - li, None, ALU.mult, None, ["cst"], ["dgt"])
    act(sa8[:], C("lam"), AF.Exp, ["cst"], ["sa8"], scale=-1.0)
    act(sa8[:], sa8[:], AF.Ln, ["sa8", "c_one"], ["sa8"], bias=c_one[:, 0:1])
    ts("dve", sa16[:], sa8[:], -16.0, None, ALU.mult, None, ["sa8"], ["sa16"])
    ts("dve", sa8[:], sa8[:], -8.0, None, ALU.mult, None, ["sa8", "sa16"], ["sa8"])

    setup_stack.close()
    P.scope = None
    xk = ["x%d" % c for c in range(8)]
    hk = ["h%d" % c for c in range(8)]

    def norm(Aap, Bap, dst_fn, dst_keys, final=False):
        def stats(n):
            blk = slice(n * 512, (n + 1) * 512)
            bank = 5 if n % 2 == 0 else 2
            rs = rstd2[n % 2]
            rk_ = "rstd%d" % (n % 2)
            for c in range(8):
                s = sqt[c % 2]
                act(s[:], x[:, c, blk], AF.Square, [xk[c]], ["sqt%d" % (c % 2)])
                mm(PS(bank), ones[:], s[:], c == 0, c == 7, ["ones", "sqt%d" % (c % 2)], [pk(bank)])
            act(rs[:], PS(bank), AF.Sqrt, [pk(bank), "c_eps"], [rk_], bias=c_eps[:, 0:1], scale=1.0 / 1024.0)
            V(lambda e: e.reciprocal(out=rs[:], in_=rs[:]), [rk_], [rk_])

        def apply(n):
            v = 0 if n < 2 else 1
            blk = slice(n * 512, (n + 1) * 512)
            rs = rstd2[n % 2]
            rk_ = "rstd%d" % (n % 2)
            for c in range(8):
                tmp = ntmp[c % 2]
                tt("dve", tmp[:], x[:, c, blk], rs[:], ALU.mult, [xk[c], rk_], ["ntmp%d" % (c % 2)])
                o = dst_fn(c, n)
                if c % 2 == 0:
                    A(lambda e, o=o, tmp=tmp, sa=Aap(c, v), ba=Bap(c, v): e.activation(out=o, in_=tmp[:], func=AF.Identity, scale=sa, bias=ba),
                      ["ntmp%d" % (c % 2)] + MODK, [dst_keys[c]])
                else:
                    ts("pool", o, tmp[:], Aap(c, v), Bap(c, v), ALU.mult, ALU.add, ["ntmp%d" % (c % 2)] + MODK, [dst_keys[c]])
        stats(0)
        stats(1)
        apply(0)
        stats(2)
        apply(1)
        apply(2)

    dstate = {"b": 0}

    def dbank():
        dstate["b"] ^= 1
        return 3 + dstate["b"]

    def proj_F(pid, src, srck, nchunks, evac):
        wv, wkey = w_next(pid)
        w3 = wv[:].rearrange("p (k c) -> p k c", k=8)
        for mi in range(nchunks):
            for n in range(3):
                b = dbank()
                mmk(PS(b), lambda kc, mi=mi: w3[:, kc, mi * 128:(mi + 1) * 128],
                    lambda kc, n=n: src[:, kc, n * 512:(n + 1) * 512], 8, [wkey] + srck, [pk(b)])
                evac(mi, n, b)

    def proj_T(pid, tiles, evac, ncols=256):
        wv, wkey = w_next(pid)
        w3 = wv[:].rearrange("p (k c) -> p k c", k=8)
        for t in tiles:
            b = dbank()
            mmk(PS(b, ncols), lambda kc, t=t: hT[:, kc, t * 128:(t + 1) * 128], lambda kc: w3[:, kc, 0:ncols], 8,
                [wkey] + hk, [pk(b)])
            evac(t, b)

    def out_proj(pid, G1):
        wv, wkey = w_next(pid)
        w3 = wv[:].rearrange("p (k c) -> p k c", k=2)
        for n in range(3):
            for m in range(8):
                v = 0 if n < 2 else 1
                b = dbank()
                mmk(PS(b), lambda kc, m=m: w3[:, kc, m * 128:(m + 1) * 128],
                    lambda kc, n=n: yg[:, kc, n * 512:(n + 1) * 512], 2, [wkey, "yg"], [pk(b)])
                xs = x[:, m, n * 512:(n + 1) * 512]
                stt("dve", xs, PS(b), G1(m, v), xs, ALU.mult, ALU.add, [pk(b), xk[m]] + MODK, [xk[m]])

    def transpose_part(ytok, ykey, t2s, fcs):
        for t2 in t2s:
            for fc in fcs:
                for i in range(2):
                    t = t2 * 2 + i
                    mm(PS(5, 128, (fc * 2 + i) * 128), ytok[:, t, fc * 128:(fc + 1) * 128], ident[:], True, True,
                       [ykey, "ident"], [pk(5)])
            for fc in fcs:
                cp("act" if fc == 0 else "dve", yg[:, fc, t2 * 256:(t2 + 1) * 256], PS(5, 256, fc * 256), [pk(5)], ["yg"])

    def transpose_y(ytok, ykey):
        transpose_part(ytok, ykey, range(6), [0, 1])

    def rope_load(rope, half):
        P.dma(stg[0][:], rope_d[:, half * 2048:(half + 1) * 2048], w=["stg0"], noscope=True)
        cp("dve", rope[:], stg[0][:], ["stg0"], ["rope"])

    def rope_apply(z, zkey, R, Rk, rope):
        tab0 = 0
        for n in range(2):
            blk = slice(n * 512, (n + 1) * 512)
            b = dbank()
            mm(PS(b), R[:], z[:, blk], True, True, [Rk, zkey], [pk(b)])
            t1 = ntmp[0]
            t2 = ntmp[1]
            tt("dve", t1[:], z[:, blk], rope[:, tab0 + n * 512: tab0 + (n + 1) * 512], ALU.mult, [zkey, "rope"], ["ntmp0"])
            tt("dve", t2[:], PS(b), rope[:, tab0 + 1024 + n * 512: tab0 + 1024 + (n + 1) * 512], ALU.mult, [pk(b), "rope"], ["ntmp1"])
            tt("pool", z[:, blk], t1[:], t2[:], ALU.add, ["ntmp0", "ntmp1"], [zkey])

    astate = {"i": 0}

    class AttnPipe:
        def __init__(self, pTs, pkey):
            self.pTs = pTs; self.pkey = pkey; self.n = 0; self.pending = None

        def push(self, qT, qkey, kblocks, scale, post, nq=1):
            i = self.n; self.n += 1
            base = 0 if (i % 2 == 0) else 3
            pT = self.pTs[i % 2]; pkey = self.pkey + "%d_" % (i % 2); ob = 6 + (i % 2)
            nk = len(kblocks)
            w = nq * 128
            assert nk * w <= 1536 and (512 % w == 0)
            banks = sorted(set(base + (j * w) // 512 for j in range(nk)))
            for j, (kT, kkey, vap, vkey, mask) in enumerate(kblocks):
                o = psum[:, base * 512 + j * w: base * 512 + (j + 1) * w]
                bk = pk(base + (j * w) // 512)
                mm(o, kT, qT, True, mask is None, [kkey, qkey], [bk])
                if mask is not None:
                    assert nq == 1
                    mm(o, ident[:], mask[0][:], False, True, ["ident", mask[1]], [bk])
            for bi, b in enumerate(banks):
                c0 = bi * 512
                c1 = min(nk * w, c0 + 512)
                act(pT[:, c0:c1], psum[:, base * 512 + c0: base * 512 + c1], AF.Exp, [pk(b)], [pkey + str(bi)], scale=scale)
            prev = self.pending
            self.pending = (kblocks, pT, pkey, ob, post, nq)
            if prev is not None:
                self._pv(prev)

        def _pv(self, call):
            kblocks, pT, pkey, ob, post, nq = call
            nk = len(kblocks)
            w = nq * 128
            for qi in range(nq):
                for j, (kT, kkey, vap, vkey, mask) in enumerate(kblocks):
                    c = j * w + qi * 128
                    mm(PS(ob, 65, qi * 128), pT[:, c:c + 128], vap, j == 0, j == nk - 1, [pkey + str(c // 512), vkey], [pk(ob)])
            post(ob)

        def flush(self):
            if self.pending is not None:
                self._pv(self.pending)
                self.pending = None

    def ada_piece(l, j):
        wv, wkey = w_next(piece_ids(l)["ada"][j])
        w3 = wv[:].rearrange("p (k c) -> p k c", k=8)
        o = psum[0:2, 2 * 512 + (j % 2) * 256: 2 * 512 + (j % 2) * 256 + 256]
        ar_ = adarow[j % 2]
        mmk(o, lambda kc: csil[:, kc, :], lambda kc, w3=w3: w3[:, kc, 0:256], 8, [wkey, "csil"], [pk(2)])
        cp("act" if j % 2 == 0 else "dve", ar_[:], o, [pk(2)], ["adarow%d" % (j % 2)])
        for mi in range(2):
            mc = j * 2 + mi
            mm(PS(5, 2, mc * 2), ar_[:, mi * 128:(mi + 1) * 128], identf[:], True, True, ["adarow%d" % (j % 2), "identf"], [pk(5)])

    def ada_finish(l):
        mod = mods[l % 2]
        modA = modAs[l % 2]
        badd = C("b_ada", l * 48, 48)
        tt("dve", mod[:], PS(5, 96).rearrange("p (a b) -> p a b", b=2), badd.unsqueeze(2).to_broadcast([128, 48, 2]), ALU.add,
           [pk(5), "cst"], ["mod%d" % (l % 2)])
        for w_, (gname, sci) in enumerate((("g_mix", 1), ("g_mlp", 4))):
            gv = C(gname, l * 8, 8).unsqueeze(2).to_broadcast([128, 8, 2])
            stt("dve", modA[:, w_, :, :], mod[:, sci * 8:(sci + 1) * 8, :], 1.0, gv, ALU.add, ALU.mult,
                ["mod%d" % (l % 2), "cst"], ["modA%d" % (l % 2)])

    for l in range(depth):
        pi = piece_ids(l)
        P.scope = None
        mod = mods[l % 2]
        modA = modAs[l % 2]
        if l == 0 and ST('ada'):
            for j in range(24):
                ada_piece(0, j)
            ada_finish(0)
        A1 = lambda c, v: modA[:, 0, c, v:v + 1]
        B1 = lambda c, v: mod[:, 0 * 8 + c, v:v + 1]
        G1 = lambda c, v: mod[:, 2 * 8 + c, v:v + 1]
        A2 = lambda c, v: modA[:, 1, c, v:v + 1]
        B2 = lambda c, v: mod[:, 3 * 8 + c, v:v + 1]
        G2 = lambda c, v: mod[:, 5 * 8 + c, v:v + 1]
        if ST('norm'):
            norm(A1, B1, lambda c, n: hT[:, c, n * 512:(n + 1) * 512], hk)
        if debug and l == 0:
            P.dma(dbg_o["d_hT"], hT[:].rearrange("p a b -> p (a b)"), r=hk)
            P.dma(dbg_o["d_mod"], mod[:].rearrange("p a b -> p (a b)"), r=["mod"])

        barrier()
        with contextlib.ExitStack() as st:
          if ST('ret'):
              def sba(name, shape, dt=F32):
                  rr["ev"] += 1
                  return st.enter_context(nc.sbuf_tensor("a%d_" % rr["ev"] + name, list(shape), dt))
              rq = sba("rq", [128, 2, T], BF16)
              rkm = sba("rkm", [128, 4, T], BF16)
              rkT = sba("rkT", [128, NT, 256], BF16)
              rvT = sba("rvT", [128, NT, 256], BF16)
              rgT = sba("rgT", [128, NT, 256], BF16)
              ytok = sba("ytok", [128, NT, 256], BF16)
              Sbd = sba("Sbd", [128, 4, 128])
              Sbf = sba("Sbf", [128, 4, 8, 128], BF16)
              Ob = sba("Ob", [128, 2, 256])
              Oq = sba("Oq", [128, 2, 256])
              st4 = sba("st4", [128, 6, 8])
              qx8 = [sba("qx%d" % i, [128, 128], BF16) for i in range(8)]
              ktx = [sba("ktx%d" % i, [128, 256], BF16) for i in range(2)]
              dsb4 = [sba("dsb%d" % i, [128, 128]) for i in range(4)]
              scT8 = [sba("scT%d" % i, [128, 128], BF16) for i in range(8)]

              def ev_rq(mi, n, b):
                  cp("act", rq[:, mi, n * 512:(n + 1) * 512], PS(b), [pk(b)], ["rq"])
              proj_F(pi["retF"][0], hT, hk, 2, ev_rq)
              for pp in range(2):
                  def ev_rk(mi, n, b, pp=pp):
                      cp("act" if mi == 0 else "dve", rkm[:, pp * 2 + mi, n * 512:(n + 1) * 512], PS(b), [pk(b)], ["rkm"])
                  proj_F(pi["retF"][1 + pp], hT, hk, 2, ev_rk)
              for (dst, key, pid) in ((rkT, "rkT", pi["retT"][0]), (rvT, "rvT", pi["retT"][1]), (rgT, "rgT", pi["retT"][2])):
                  def ev_t(t, b, dst=dst, key=key):
                      cp("act" if t % 2 == 0 else "dve", dst[:, t, :], PS(b, 256), [pk(b)], [key])
                  proj_T(pid, range(NT), ev_t)
              G(lambda e: e.memset(Sbf[:], 0.0), [], ["Sbf"])
              act(rgT[:].rearrange("p a b -> p (a b)"), rgT[:].rearrange("p a b -> p (a b)"), AF.Silu, ["rgT"], ["rgT"])
              tt("pool", rgT[:], rgT[:], C("gng", l * 256, 256).unsqueeze(1).to_broadcast([128, NT, 256]), ALU.mult, ["rgT", "cst"], ["rgT"])
              segs = [(0, 8, True, 0), (8, 2, False, 0), (10, 2, False, 1)]
              for (t0, ncnk, samp, sj) in segs:
                  if samp:
                      P.dma(Sbd[:], sret_d[:, l, :, :], w=["Sbd0", "Sbd1", "Sbd2", "Sbd3"])
                  else:
                      G(lambda e: e.memset(Sbd[:], 0.0), [], ["Sbd0", "Sbd1", "Sbd2", "Sbd3"])
                  sbank = [0, 1, 2, 6]
                  for step in range(ncnk):
                      for d in range(2):
                          i = step if d == 0 else ncnk - 1 - step
                          t = t0 + i
                          kt = ktx[d]
                          tt("dve", kt[:].rearrange("p (h e) -> p h e", h=4), rkT[:, t, :].rearrange("p (h e) -> p h e", h=4),
                             zeta[:, l * 2 + d, :].unsqueeze(2).to_broadcast([128, 4, 64]), ALU.mult, ["rkT", "zeta"], ["ktx%d" % d])
                          for j in range(2):
                              si = d * 2 + j
                              sk_ = "Sbd%d" % si
                              for hb in range(2):
                                  A(lambda e, si=si, i=i, hb=hb: e.activation(out=Sbf[hb * 64:(hb + 1) * 64, si, i, hb * 64:(hb + 1) * 64],
                                    in_=Sbd[hb * 64:(hb + 1) * 64, si, hb * 64:(hb + 1) * 64], func=AF.Copy), [sk_], ["Sbf"])
                              mm(PS(sbank[si], 128), kt[:, j * 128:(j + 1) * 128], rvT[:, t, j * 128:(j + 1) * 128], True, True,
                                 ["ktx%d" % d, "rvT"], [pk(sbank[si])])
                              stt("dve", Sbd[:, si, :], Sbd[:, si, :], Gp[:, l * 4 + si: l * 4 + si + 1], PS(sbank[si], 128), ALU.mult, ALU.add,
                                  [sk_, "Gp", pk(sbank[si])], [sk_])
                  if not samp:
                      P.dma(osret_o[:, sj, l, :, :], Sbd[:], r=["Sbd0", "Sbd1", "Sbd2", "Sbd3"])
                  def stage_a(t):
                      cols = slice(t * 128, (t + 1) * 128)
                      par = t % 2
                      for j in range(2):
                          for d in range(2):
                              q_ = qx8[par * 4 + d * 2 + j]
                              tt("dve" if d == 0 else "pool", q_[:], rq[:, j, cols], Xi[:, l * 4 + d * 2 + j, :], ALU.mult,
                                 ["rq", "Xi"], ["qx%d" % (par * 4 + d * 2 + j)])
                      for h in range(4):
                          mm(PS(par, 128, h * 128), rkm[:, h, cols], rq[:, h // 2, cols], True, True, ["rkm", "rq"], [pk(par)])
                      for h in range(4):
                          tt("dve", scT8[par * 4 + h][:], PS(par, 128, h * 128), DTm[:, l * 4 + h, :], ALU.mult, [pk(par), "DTm"],
                             ["scT%d" % (par * 4 + h)])
                  stage_a(t0)
                  for i in range(ncnk):
                      t = t0 + i
                      if i + 1 < ncnk:
                          stage_a(t + 1)
                      par = t % 2
                      ob = 6 + (t % 2)
                      for j in range(2):
                          mm(PS(ob, 128, j * 128), qx8[par * 4 + j][:], Sbf[:, j, i, :], True, False, ["qx%d" % (par * 4 + j), "Sbf"], [pk(ob)])
                          mm(PS(ob, 128, j * 128), qx8[par * 4 + 2 + j][:], Sbf[:, 2 + j, i, :], False, False,
                             ["qx%d" % (par * 4 + 2 + j), "Sbf"], [pk(ob)])
                          for hh in range(2):
                              h = j * 2 + hh
                              mm(PS(ob, 64, h * 64), scT8[par * 4 + h][:], rvT[:, t, h * 64:(h + 1) * 64], False, hh == 1,
                                 ["scT%d" % (par * 4 + h), "rvT"], [pk(ob)])
                      cp("act", Ob[:, t % 2, :], PS(ob, 256), [pk(ob)], ["Ob"])
                      if t % 2 == 1:
                          tb = t - 1
                          O3 = Ob[:].rearrange("p t (h e) -> p (t h) e", h=4)
                          Q3 = Oq[:].rearrange("p t (h e) -> p (t h) e", h=4)
                          s1 = st4[:, 0, :]; s2 = st4[:, 1, :]; mean = st4[:, 2, :]; var = st4[:, 3, :]; rs = st4[:, 4, :]
                          V(lambda e: e.tensor_reduce(out=s1, in_=O3, axis=AX.X, op=ALU.add), ["Ob"], ["st4a"])
                          tt("pool", Oq[:], Ob[:], Ob[:], ALU.mult, ["Ob"], ["Oq"])
                          V(lambda e: e.tensor_reduce(out=s2, in_=Q3, axis=AX.X, op=ALU.add), ["Oq"], ["st4b"])
                          ts("dve", mean, s1, 1.0 / 64, None, ALU.mult, None, ["st4a"], ["st4c"])
                          tt("dve", var, mean, mean, ALU.mult, ["st4c"], ["st4d"])
                          stt("dve", var, s2, 1.0 / 64, var, ALU.mult, ALU.subtract, ["st4b", "st4d"], ["st4d"])
                          act(rs, var, AF.Sqrt, ["st4d", "c_eps"], ["st4e"], bias=c_eps[:, 0:1])
                          V(lambda e: e.reciprocal(out=rs, in_=rs), ["st4e"], ["st4e"])
                          tt("dve", Q3, O3, mean.unsqueeze(2).to_broadcast([128, 8, 64]), ALU.subtract, ["Ob", "st4c", "Oq"], ["Oq"])
                          tt("dve", Q3, Q3, rs.unsqueeze(2).to_broadcast([128, 8, 64]), ALU.mult, ["Oq", "st4e"], ["Oq"])
                          tt("pool", ytok[:, tb:tb + 2, :], Oq[:], rgT[:, tb:tb + 2, :], ALU.mult, ["Oq", "rgT"], ["ytok"])
                          transpose_part(ytok, "ytok", [tb // 2], [0, 1])
              (P.dma(dbg_o["d_y_ret"], yg[:].rearrange("p a b -> p (a b)"), r=["yg"]) if (debug and l == 0) else None)
              out_proj(pi["retO"], G1)

        barrier()
        with contextlib.ExitStack() as st:
          if ST('lru'):
              def sba(name, shape, dt=F32):
                  rr["ev"] += 1
                  return st.enter_context(nc.sbuf_tensor("a%d_" % rr["ev"] + name, list(shape), dt))
              lx = sba("lx", [128, 2, T], BF16)
              lgt = sba("lgt", [128, 2, T], BF16)
              BUF = {}
              for sfx, Lb in (("", 1024), ("P", 512)):
                  BUF[sfx] = dict(xc=sba("xc" + sfx, [128, Lb]), xcb=sba("xcb" + sfx, [128, Lb], BF16),
                                  rr=[sba("rr%d%s" % (i, sfx), [128, Lb]) for i in range(2)],
                                  ii=[sba("ii%d%s" % (i, sfx), [128, Lb]) for i in range(2)],
                                  uu=[sba("uu%d%s" % (i, sfx), [128, Lb]) for i in range(2)])
              slo2 = [sba("slo%d" % i, [128, 2 * 4]) for i in range(2)]
              lruwb = sba("lruwb", [128, 8, 128], BF16)
              for (dst, key, pid) in ((lx, "lx", pi["lruF"][0]), (lgt, "lgt", pi["lruF"][1])):
                  def ev(mi, n, b, dst=dst, key=key):
                      cp("act" if n % 2 == 0 else "dve", dst[:, mi, n * 512:(n + 1) * 512], PS(b), [pk(b)], [key])
                  proj_F(pid, hT, hk, 2, ev)
              P.dma(stg[0][:, 0:1024], lruw_d[:, l * 1024:(l + 1) * 1024], w=["stg0"], noscope=True)
              cp("dve", lruwb[:].rearrange("p a b -> p (a b)"), stg[0][:, 0:1024], ["stg0"], ["lruwb"])
              for cc in range(2):
                  act(lgt[:, cc, :], lgt[:, cc, :], AF.Gelu_apprx_tanh, ["lgt"], ["lgt"])
              for cc in range(2):
                for (c0, L, samp, nseq) in ((0, 1024, True, 1), (1024, 512, False, 2)):
                  sfx = "" if samp else "P"
                  B_ = BUF[sfx]
                  xc = B_["xc"]; xcb = B_["xcb"]; rr2 = B_["rr"]; ii2 = B_["ii"]; uu2 = B_["uu"]
                  kx = "xc" + sfx; kxb = "xcb" + sfx
                  kr = ["rr0" + sfx, "rr1" + sfx]; ki_ = ["ii0" + sfx, "ii1" + sfx]; ku = ["uu0" + sfx, "uu1" + sfx]
                  Ls = L // nseq
                  if True:
                      xi_ = lx[:, cc, c0:c0 + L]
                      cw = lambda k: C("conv_w", l * 8 + k * 2 + cc, 1)
                      ts("dve", xc[:, 0:L], xi_, cw(2), C("conv_b", l * 2 + cc, 1), ALU.mult, ALU.add, ["lx", "cst"], [kx])
                      for sq_ in range(nseq):
                          b0 = sq_ * Ls
                          stt("dve", xc[:, b0 + 2:b0 + Ls], xi_[:, b0:b0 + Ls - 2], cw(0), xc[:, b0 + 2:b0 + Ls], ALU.mult, ALU.add, ["lx", "cst", kx], [kx])
                          stt("dve", xc[:, b0 + 1:b0 + Ls], xi_[:, b0:b0 + Ls - 1], cw(1), xc[:, b0 + 1:b0 + Ls], ALU.mult, ALU.add, ["lx", "cst", kx], [kx])
                          stt("dve", xc[:, b0:b0 + Ls - 1], xi_[:, b0 + 1:b0 + Ls], cw(3), xc[:, b0:b0 + Ls - 1], ALU.mult, ALU.add, ["lx", "cst", kx], [kx])
                      cp("pool", xcb[:, 0:L], xc[:, 0:L], [kx], [kxb])
                      pidx_ = [l * 4 + d * 2 + cc for d in range(2)]
                      for d in range(2):
                          for (dst, key, gsel, bname) in ((rr2[d], kr[d], 0, "b_a"), (ii2[d], ki_[d], 1, "b_x")):
                              for n in range((L + 511) // 512):
                                  w_ = min(512, L - n * 512)
                                  b = dbank()
                                  mm(PS(b, w_), lruwb[:, d * 4 + gsel * 2 + cc, :], xcb[:, n * 512:n * 512 + w_], True, True,
                                     ["lruwb", kxb], [pk(b)])
                                  act(dst[:, n * 512:n * 512 + w_], PS(b, w_), AF.Sigmoid, [pk(b), "cst"], [key], bias=C(bname, pidx_[d], 1))
                      for d in range(2):
                          tt("pool" if d == 1 else "dve", ii2[d][:, 0:L], ii2[d][:, 0:L], xc[:, 0:L], ALU.mult, [ki_[d], kx], [ki_[d]])
                      for d in range(2):
                          act(uu2[d][:, 0:L], rr2[d][:, 0:L], AF.Exp, [kr[d], "sa16"], [ku[d]], scale=sa16[:, pidx_[d]:pidx_[d] + 1])
                          act(rr2[d][:, 0:L], rr2[d][:, 0:L], AF.Exp, [kr[d], "sa8"], [kr[d]], scale=sa8[:, pidx_[d]:pidx_[d] + 1])
                      for d in range(2):
                          act(uu2[d][:, 0:L], uu2[d][:, 0:L], AF.Sqrt, [ku[d], "c_one"], [ku[d]], bias=c_one[:, 0:1], scale=-1.0)
                      for d in range(2):
                          tt("dve", uu2[d][:, 0:L], uu2[d][:, 0:L], ii2[d][:, 0:L], ALU.mult, [ku[d], ki_[d]], [ku[d]])
                      for sq_ in range(nseq):
                          b0 = sq_ * Ls
                          for d in range(2):
                              hd = ii2[d]
                              init = C("h0", pidx_[d], 1) if samp else 0.0
                              if d == 0:
                                  V(lambda e, hd=hd, init=init, b0=b0, Ls=Ls, rr2=rr2, uu2=uu2: e.tensor_tensor_scan(out=hd[:, b0:b0 + Ls],
                                    data0=rr2[0][:, b0:b0 + Ls], data1=uu2[0][:, b0:b0 + Ls], initial=init, op0=ALU.mult, op1=ALU.add),
                                    [kr[0], ku[0], "cst"], [ki_[0]])
                                  fin = hd[:, b0 + Ls - 1:b0 + Ls]
                              else:
                                  V(lambda e, hd=hd, init=init, b0=b0, Ls=Ls, rr2=rr2, uu2=uu2: e.tensor_tensor_scan(out=hd[:, b0:b0 + Ls][:, ::-1],
                                    data0=rr2[1][:, b0:b0 + Ls][:, ::-1], data1=uu2[1][:, b0:b0 + Ls][:, ::-1],
                                    initial=init, op0=ALU.mult, op1=ALU.add), [kr[1], ku[1], "cst"], [ki_[1]])
                                  fin = hd[:, b0:b0 + 1]
                              if not samp:
                                  cp("pool", slo2[sq_][:, d * 2 + cc: d * 2 + cc + 1], fin, [ki_[d]], ["slo%d" % sq_])
                      tt("dve", ii2[0][:, 0:L], ii2[0][:, 0:L], ii2[1][:, 0:L], ALU.add, [ki_[0], ki_[1]], [ki_[0]])
                      tt("dve", yg[:, cc, c0:c0 + L], ii2[0][:, 0:L], lgt[:, cc, c0:c0 + L], ALU.mult, [ki_[0], "lgt"], ["yg"])
              for sj in range(2):
                  P.dma(oslru_o[:, (sj * DEPTH + l) * 4:(sj * DEPTH + l) * 4 + 4], slo2[sj][:, 0:4], r=["slo%d" % sj])
              (P.dma(dbg_o["d_y_lru"], yg[:].rearrange("p a b -> p (a b)"), r=["yg"]) if (debug and l == 0) else None)
              out_proj(pi["lruO"], G1)

        barrier()
        with contextlib.ExitStack() as st:
          if ST('swa'):
              def sba(name, shape, dt=F32):
                  rr["ev"] += 1
                  return st.enter_context(nc.sbuf_tensor("a%d_" % rr["ev"] + name, list(shape), dt))
              sq = sba("sq", [128, 2, T], BF16)
              sk = sba("sk", [128, T], BF16)
              kms = [sba("kms%d" % i, [128, 256 + T], BF16) for i in range(2)]
              svaug = sba("svaug", [128, NT, 2, 80], BF16)
              vctx = sba("vctx", [128, 2, 2, 80], BF16)
              ckf = sba("ckf", [128, 256])
              cvf = sba("cvf", [128, 2, 128])
              pT = [sba("pT%d" % i, [128, 640], BF16) for i in range(2)]
              Osw = [sba("Osw", [128, NT, 65])] * 2
              den = sba("den", [128, NT])
              ytok = sba("ytok", [128, NT, 256], BF16)
              kvst = [sba("kvst%d" % i, [128, 256]) for i in range(2)]
              cp("dve", svaug[:, :, :, 64:66], ones[:, 0:NT * 4].rearrange("p (a b c) -> p a b c", a=NT, b=2), ["ones"], ["svaug"])
              cp("dve", vctx[:, :, :, 64:66], ones[:, 0:8].rearrange("p (a b c) -> p a b c", a=2, b=2), ["ones"], ["vctx"])

              def ev_sq(mi, n, b):
                  cp("act", sq[:, mi, n * 512:(n + 1) * 512], PS(b), [pk(b)], ["sq%d" % mi])
              proj_F(pi["swaF"][0], hT, hk, 2, ev_sq)

              def ev_sk(mi, n, b):
                  cp("act", sk[:, n * 512:(n + 1) * 512], PS(b), [pk(b)], ["sk"])
              proj_F(pi["swaF"][1], hT, hk, 1, ev_sk)

              def ev_swt(t, b):
                  for hv in range(2):
                      cp("act", svaug[:, t, hv, 0:64], PS(b, 64, 128 + hv * 64), [pk(b)], ["svaug"])
                  if t >= 8:
                      kb = kvst[t % 2]
                      cp("dve", kb[:], PS(b, 256), [pk(b)], ["kvst%d" % (t % 2)])
                      P.dma(okv_o[t - 8, :, l, 0:256], kb[:], r=["kvst%d" % (t % 2)])
              proj_T(pi["swaT"][0], range(NT), ev_swt)
              rope = sba("rope", [128, 2048], BF16)
              if ST('swa_rope'):
                  rope_load(rope, 0)
                  rope_apply(sq[:, 0, :], "sq0", R64, "R64", rope)
                  rope_apply(sq[:, 1, :], "sq1", R64, "R64", rope)
                  rope_apply(sk[:], "sk", R64, "R64", rope)
              P.dma(ckf[:], cks_d[:, l, :], w=["ckf"])
              P.dma(cvf[:], cvs_d[:, l, :, :], w=["cvf"])
              for cb_ in range(2):
                  for hv in range(2):
                      cp("dve", vctx[:, cb_, hv, 0:64], cvf[:, cb_, hv * 64:(hv + 1) * 64], ["cvf"], ["vctx"])
              for m in range(2):
                  ts("dve", kms[m][:, 0:256], ckf[:], C("hm", m, 1), None, ALU.mult, None, ["ckf", "cst"], ["kms%d" % m])
                  ts("dve", kms[m][:, 256:256 + T], sk[:], C("hm", m, 1), None, ALU.mult, None, ["sk", "cst"], ["kms%d" % m])
              if debug and l == 0:
                  P.dma(dbg_o["d_sq"], sq[:].rearrange("p a b -> p (a b)"), r=["sq0", "sq1"])
                  P.dma(dbg_o["d_kms0"], kms[0][:], r=["kms0"])
                  P.dma(dbg_o["d_kms1"], kms[1][:], r=["kms1"])
                  P.dma(dbg_o["d_sva"], svaug[:].rearrange("p a b c -> p (a b c)"), r=["svaug"])
                  P.dma(dbg_o["d_vctx"], vctx[:].rearrange("p a b c -> p (a b c)"), r=["vctx"])
              apipe = AttnPipe(pT, "pT")
              for h in (range(4) if ST('swa_attn') else []):
                  qc = 0 if h in (0, 3) else 1
                  kvh = h // 2
                  for t in list(range(8)) + [8, 10]:
                      nq = 1 if t < 8 else 2
                      qT = sq[:, qc, t * 128:(t + nq) * 128]
                      kb = []
                      if t < 8:
                          for cb in range(2):
                              kb.append((kms[kvh][:, cb * 128:(cb + 1) * 128], "kms%d" % kvh, vctx[:, cb, kvh, 0:65], "vctx", None))
                          for dt_ in (-1, 0, 1):
                              tk = t + dt_
                              if tk < 0 or tk > 7:
                                  continue
                              mask = None if dt_ == 0 else ((mlo, "mlo") if dt_ == -1 else (mhi, "mhi"))
                              kb.append((kms[kvh][:, 256 + tk * 128: 256 + (tk + 1) * 128], "kms%d" % kvh, svaug[:, tk, kvh, 0:65], "svaug", mask))
                      else:
                          for tk in (t, t + 1):
                              kb.append((kms[kvh][:, 256 + tk * 128: 256 + (tk + 1) * 128], "kms%d" % kvh, svaug[:, tk, kvh, 0:65], "svaug", None))

                      def post(ob, t=t, h=h, nq=nq):
                          Oh = Osw[h % 2]
                          for qi in range(nq):
                              cp("dve", Oh[:, t + qi, :], PS(ob, 65, qi * 128), [pk(ob)], ["Osw"])
                          if t + nq == NT:
                              ts("dve", den[:], Oh[:, :, 64], esink[:, l * 4 + h: l * 4 + h + 1], None, ALU.add, None,
                                 ["Osw", "esink"], ["den"])
                              V(lambda e: e.reciprocal(out=den[:], in_=den[:]), ["den"], ["den"])
                              tt("dve", ytok[:, :, h * 64:(h + 1) * 64], Oh[:, :, 0:64], den[:].unsqueeze(2).to_broadcast([128, NT, 64]),
                                 ALU.mult, ["Osw", "den"], ["ytok"])
                              if h % 2 == 1:
                                  transpose_part(ytok, "ytok", range(6), [h // 2])
                      apipe.push(qT, "sq%d" % qc, kb, 0.125, post, nq=nq)
              apipe.flush()
              (P.dma(dbg_o["d_y_swa"], yg[:].rearrange("p a b -> p (a b)"), r=["yg"]) if (debug and l == 0) else None)
              out_proj(pi["swaO"], G1)

        barrier()
        with contextlib.ExitStack() as st:
          if ST('diff'):
              def sba(name, shape, dt=F32):
                  rr["ev"] += 1
                  return st.enter_context(nc.sbuf_tensor("a%d_" % rr["ev"] + name, list(shape), dt))
              dq = sba("dq", [128, 2, T], BF16)
              dk = sba("dk", [128, 2, T], BF16)
              kmd = [sba("kmd%d" % i, [128, 256 + T], BF16) for i in range(2)]
              dvaug = sba("dvaug", [128, NT, 4, 80], BF16)
              vctx = sba("vctxd", [128, 2, 4, 80], BF16)
              ckf = sba("ckfd", [128, 2, 256])
              cvf = sba("cvfd", [128, 2, 256])
              pT = [sba("pTd%d" % i, [128, 1280], BF16) for i in range(2)]
              Oall = [sba("Oall", [128, 2, NT, 65])] * 2
              rec = sba("rec", [128, 2, NT])
              y0 = sba("y0", [128, NT, 64])
              y1 = sba("y1", [128, NT, 64])
              ss = sba("ss", [128, NT])
              ytok = sba("ytokd", [128, NT, 256], BF16)
              kvst = [sba("kvstd%d" % i, [128, 256]) for i in range(2)]
              cp("dve", dvaug[:, :, :, 64:66], ones[:, 0:NT * 8].rearrange("p (a b c) -> p a b c", a=NT, b=4), ["ones"], ["dvaug"])
              cp("dve", vctx[:, :, :, 64:66], ones[:, 0:16].rearrange("p (a b c) -> p a b c", a=2, b=4), ["ones"], ["vctxd"])
              for (dst, key, pid) in ((dq, "dq", pi["difF"][0]), (dk, "dk", pi["difF"][1])):
                  def ev(mi, n, b, dst=dst, key=key):
                      cp("act", dst[:, mi, n * 512:(n + 1) * 512], PS(b), [pk(b)], [key + str(mi)])
                  proj_F(pid, hT, hk, 2, ev)

              def ev_dkt(t, b):
                  kb = kvst[t % 2]
                  cp("dve", kb[:], PS(b, 256), [pk(b)], ["kvstd%d" % (t % 2)])
                  P.dma(okv_o[t - 8, :, l, 256:512], kb[:], r=["kvstd%d" % (t % 2)])
              proj_T(pi["difT"][0], range(8, NT), ev_dkt)

              def ev_dvt(t, b):
                  for hv in range(4):
                      cp("act", dvaug[:, t, hv, 0:64], PS(b, 64, hv * 64), [pk(b)], ["dvaug"])
                  if t >= 8:
                      kb = kvst[t % 2]
                      cp("dve", kb[:], PS(b, 256), [pk(b)], ["kvstd%d" % (t % 2)])
                      P.dma(okv_o[t - 8, :, l, 512:768], kb[:], r=["kvstd%d" % (t % 2)])
              proj_T(pi["difT"][1], range(NT), ev_dvt)
              rope = sba("roped", [128, 2048], BF16)
              rope_load(rope, 1)
              for cc in range(2):
                  rope_apply(dq[:, cc, :], "dq%d" % cc, R32, "R32", rope)
                  rope_apply(dk[:, cc, :], "dk%d" % cc, R32, "R32", rope)
              P.dma(ckf[:], ckd_d[:, l, :, :], w=["ckfd"])
              P.dma(cvf[:], cvd_d[:, l, :, :], w=["cvfd"])
              for cb_ in range(2):
                  for hv in range(4):
                      cp("dve", vctx[:, cb_, hv, 0:64], cvf[:, cb_, hv * 64:(hv + 1) * 64], ["cvfd"], ["vctxd"])
              ki = 0
              apipe = AttnPipe(pT, "pTd")
              for h in range(4):
                  cc = h // 2
                  hh = h % 2
                  for comp in range(2):
                      km = kmd[ki % 2]
                      kk = "kmd%d" % (ki % 2)
                      ki += 1
                      mcol = C("cm4", hh * 2 + comp, 1)
                      ts("dve", km[:, 0:256], ckf[:, cc, :], mcol, None, ALU.mult, None, ["ckfd", "cst"], [kk])
                      ts("dve", km[:, 256:256 + T], dk[:, cc, :], mcol, None, ALU.mult, None, ["dk%d" % cc, "cst"], [kk])
                      for t in list(range(8)) + [8, 10]:
                          nq = 1 if t < 8 else 2
                          qT = dq[:, cc, t * 128:(t + nq) * 128]
                          kb = []
                          if t < 8:
                              for cb in range(2):
                                  kb.append((km[:, cb * 128:(cb + 1) * 128], kk, vctx[:, cb, h, 0:65], "vctxd", None))
                              for tk in range(8):
                                  kb.append((km[:, 256 + tk * 128: 256 + (tk + 1) * 128], kk, dvaug[:, tk, h, 0:65], "dvaug", None))
                          else:
                              for tk in (t, t + 1):
                                  kb.append((km[:, 256 + tk * 128: 256 + (tk + 1) * 128], kk, dvaug[:, tk, h, 0:65], "dvaug", None))

                          def post(ob, t=t, h=h, comp=comp, nq=nq):
                              Oa = Oall[h % 2]
                              ok = "Oall"
                              for qi in range(nq):
                                  cp("dve", Oa[:, comp, t + qi, :], PS(ob, 65, qi * 128), [pk(ob)], [ok])
                              if t + nq == NT and comp == 1:
                                  V(lambda e: e.reciprocal(out=rec[:], in_=Oa[:, :, :, 64]), [ok], ["rec"])
                                  tt("dve", y0[:], Oa[:, 0, :, 0:64], rec[:, 0, :].unsqueeze(2).to_broadcast([128, NT, 64]), ALU.mult, [ok, "rec"], ["y0"])
                                  tt("dve", y1[:], Oa[:, 1, :, 0:64], rec[:, 1, :].unsqueeze(2).to_broadcast([128, NT, 64]), ALU.mult, [ok, "rec"], ["y1"])
                                  stt("dve", y0[:], y1[:], nlam[:, l:l + 1], y0[:], ALU.mult, ALU.add, ["y1", "nlam", "y0"], ["y0"])
                                  tt("pool", y1[:], y0[:], y0[:], ALU.mult, ["y0"], ["y1"])
                                  V(lambda e: e.tensor_reduce(out=ss[:], in_=y1[:], axis=AX.X, op=ALU.add), ["y1"], ["ss"])
                                  act(ss[:], ss[:], AF.Sqrt, ["ss", "c_eps"], ["ss"], bias=c_eps[:, 0:1], scale=1.0 / 64)
                                  V(lambda e: e.reciprocal(out=ss[:], in_=ss[:]), ["ss"], ["ss"])
                                  tt("dve", y0[:], y0[:], ss[:].unsqueeze(2).to_broadcast([128, NT, 64]), ALU.mult, ["y0", "ss"], ["y0"])
                                  tt("dve", ytok[:, :, h * 64:(h + 1) * 64], y0[:], dgt[:, l, :].unsqueeze(1).to_broadcast([128, NT, 64]),
                                     ALU.mult, ["y0", "dgt"], ["ytokd"])
                          apipe.push(qT, "dq%d" % cc, kb, 32 ** -0.5, post, nq=nq)
              apipe.flush()
              transpose_y(ytok, "ytokd")
              (P.dma(dbg_o["d_y_diff"], yg[:].rearrange("p a b -> p (a b)"), r=["yg"]) if (debug and l == 0) else None)
              out_proj(pi["difO"], G1)

        barrier()
        if ST('ffn'):
            norm(A2, B2, lambda c, n: hT[:, c, n * 512:(n + 1) * 512], hk)
        with contextlib.ExitStack() as st:
          if ST('ffn'):
              hid = st.enter_context(nc.sbuf_tensor("a_hid_%d" % l, [128, 8, T], BF16))
              rl = [st.enter_context(nc.sbuf_tensor("a_rl%d_%d" % (i, l), [128, 512], BF16)) for i in range(2)]
              xs = [st.enter_context(nc.sbuf_tensor("a_xstg%d_%d" % (i, l), [128, 2048], F32)) for i in range(3)]
              xw = [st.enter_context(nc.sbuf_tensor("a_xwbf%d_%d" % (i, l), [128, 2048], BF16)) for i in range(2)]
              w_extra_begin(xs, xw)
              nada = [0]

              def ada_tick():
                  if l + 1 < depth and nada[0] < 24 and ST('ada'):
                      ada_piece(l + 1, nada[0]); nada[0] += 1
              for q in range(4):
                  f1, f2 = pi["ff"][q]
                  for j in range(4):
                      def ev1(mi, n, b, j=j):
                          r_ = rl[(mi + n) % 2]
                          act(r_[:], PS(b), AF.Relu, [pk(b)], ["rl%d" % ((mi + n) % 2)])
                          tt("pool" if (mi + n) % 2 == 0 else "dve", hid[:, j * 2 + mi, n * 512:(n + 1) * 512], r_[:], r_[:], ALU.mult, ["rl%d" % ((mi + n) % 2)], ["hid"])
                      proj_F(f1[j], hT, hk, 2, ev1)
                      ada_tick()
                  for j in range(4):
                      def ev2(mi, n, b, j=j):
                          m = j * 2 + mi
                          v = 0 if n < 2 else 1
                          xs = x[:, m, n * 512:(n + 1) * 512]
                          stt("dve", xs, PS(b), G2(m, v), xs, ALU.mult, ALU.add, [pk(b), xk[m]] + MODK, [xk[m]])
                      proj_F(f2[j], hid, ["hid"], 2, ev2)
                      ada_tick()
              if l + 1 < depth and ST('ada'):
                  ada_finish(l + 1)
              w_extra_end()

    P.scope = None
    NOST = 12
    ostg = [sb("ostg%d" % i, [128, 512]) for i in range(NOST)]
    okeys = ["ostg%d" % (c % 2) for c in range(8)]
    cnt = {"i": 0}
    for n in range(3):
        blk = slice(n * 512, (n + 1) * 512)
        for c in range(8):
            s = sqt[c % 2]
            act(s[:], x[:, c, blk], AF.Square, [xk[c]], ["sqt%d" % (c % 2)])
            mm(PS(5), ones[:], s[:], c == 0, c == 7, ["ones", "sqt%d" % (c % 2)], [pk(5)])
        act(rstd[:], PS(5), AF.Sqrt, [pk(5), "c_eps"], ["rstd"], bias=c_eps[:, 0:1], scale=1.0 / 1024.0)
        V(lambda e: e.reciprocal(out=rstd[:], in_=rstd[:]), ["rstd"], ["rstd"])
        for c in range(8):
            oi = (n * 8 + c) % NOST
            o = ostg[oi]
            stt("dve", o[:], x[:, c, blk], C("g_fin", c, 1), rstd[:], ALU.mult, ALU.mult, [xk[c], "rstd", "cst"], ["ostg%d" % oi])
            P.dma(yT_o[c * 128:(c + 1) * 128, blk], o[:], r=["ostg%d" % oi])
    P.emit()
    P.stats['minfree'] = minfree['v']
    return nc, P


def _consts():
    coff, CW = cst_layout()
    c = np.zeros((128, CW), np.float32)

    def put(name, arr):
        o, w = coff[name]
        c[:, o:o + w] = np.asarray(arr, np.float32).reshape(128, w)
    put("ident", np.eye(128))
    R64 = np.zeros((128, 128)); R32 = np.zeros((128, 128))
    for blk in range(2):
        for i in range(64):
            if i < 32:
                R64[blk * 64 + i, blk * 64 + i + 32] = -1.0
            else:
                R64[blk * 64 + i, blk * 64 + i - 32] = 1.0
    for blk in range(4):
        for i in range(32):
            if i < 16:
                R32[blk * 32 + i, blk * 32 + i + 16] = -1.0
            else:
                R32[blk * 32 + i, blk * 32 + i - 16] = 1.0
    put("R64T", R64.T); put("R32T", R32.T)
    j = np.arange(128)[:, None]; i = np.arange(128)[None, :]
    put("mlo", np.where(j >= i, 0.0, -480.0))
    put("mhi", np.where(j <= i, 0.0, -480.0))
    m = np.arange(128)[:, None]; n = np.arange(128)[None, :]
    put("Ef", np.maximum(n - m, 0)); put("Eb", np.maximum(m - n, 0))
    put("Lf8", (n >= m) / 8.0); put("Lb8", (m >= n) / 8.0)
    bd = np.zeros((128, 128)); bd[:64, :64] = 1; bd[64:, 64:] = 1
    put("BD", bd)
    put("idx1", np.broadcast_to(np.arange(128)[None, :] + 1.0, (128, 128)))
    put("idxr", np.broadcast_to(128.0 - np.arange(128)[None, :], (128, 128)))
    put("cmf", (127.0 - np.arange(128))[:, None]); put("cmb", (np.arange(128) * 1.0)[:, None])
    p = np.arange(128)
    put("hm", np.stack([p < 64, p >= 64], 1))
    put("cm4", np.stack([(p // 32) == k for k in range(4)], 1))
    pos = np.arange(1024)
    row = (pos // 64).astype(np.float32); col = (pos % 64).astype(np.float32)

    def tab(dim):
        nn = dim // 4
        inv = (10000.0 ** (-np.arange(nn, dtype=np.float32) / nn)).astype(np.float32)
        ang = np.concatenate([row[:, None] * inv, col[:, None] * inv], -1).astype(np.float32)
        idx = (np.arange(128) % dim) % (dim // 2)
        return np.cos(ang)[:, idx].T.astype(np.float32), np.sin(ang)[:, idx].T.astype(np.float32)
    c64, s64 = tab(64); c32, s32 = tab(32)
    rope = np.concatenate([c64, s64, c32, s32], 1).astype(np.float32)
    return c, rope, coff


def _fm(v):
    v = np.asarray(v)
    n = v.shape[-1] // 128
    lead = v.shape[:-1]
    return np.moveaxis(v.reshape(lead + (n, 128)), -1, 0)


_CACHE = {}


def kernel(**inp):
    I = {k: np.asarray(v) for k, v in inp.items()}
    if "prog" not in _CACHE:
        _CACHE["prog"] = build_program()
    nc, P = _CACHE["prog"]
    in_maps = prep_inputs(I)
    res = run_bass_kernel_spmd(nc, in_maps, core_ids=list(range(8)))
    return assemble(res.results)


def prep_inputs(I, cores=range(8)):
    cbase, rope, coff = _consts()

    wts = np.zeros((DEPTH * PPL, 128, 2048), np.float32)

    def kpiece(W, cols):
        out = np.zeros((128, 8, 256), np.float32)
        cols = np.asarray(cols)
        valid = cols >= 0
        Wc = W[:, cols[valid]].reshape(8, 128, -1)
        out[:, :, np.nonzero(valid)[0]] = np.transpose(Wc, (1, 0, 2))
        return out.reshape(128, 2048)
    ar = np.arange
    for l in range(DEPTH):
        pi = piece_ids(l)
        wa, wi, wo, w1, w2 = I["w_ada"][l], I["w_in"][l], I["w_out"][l], I["w_ff1"][l], I["w_ff2"][l]
        for j in range(24):
            wts[pi["ada"][j]] = kpiece(wa, ar(j * 256, (j + 1) * 256))
        wts[pi["retF"][0]] = kpiece(wi, ar(0, 256))
        neg = -np.ones(64, int)
        wts[pi["retF"][1]] = kpiece(wi, np.concatenate([ar(256, 320), neg, neg, ar(320, 384)]))
        wts[pi["retF"][2]] = kpiece(wi, np.concatenate([ar(384, 448), neg, neg, ar(448, 512)]))
        wts[pi["retT"][0]] = kpiece(wi, ar(256, 512))
        wts[pi["retT"][1]] = kpiece(wi, ar(512, 768))
        wts[pi["retT"][2]] = kpiece(wi, ar(768, 1024))
        wts[pi["lruF"][0]] = kpiece(wi, ar(1024, 1280))
        wts[pi["lruF"][1]] = kpiece(wi, ar(1280, 1536))
        sqb = 1536
        wts[pi["swaF"][0]] = kpiece(wi, np.concatenate([ar(sqb, sqb + 64), ar(sqb + 192, sqb + 256), ar(sqb + 64, sqb + 192)]))
        wts[pi["swaF"][1]] = kpiece(wi, np.concatenate([ar(1792, 1920), -np.ones(128, int)]))
        wts[pi["swaT"][0]] = kpiece(wi, ar(1792, 2048))
        wts[pi["difF"][0]] = kpiece(wi, ar(2048, 2304))
        wts[pi["difF"][1]] = kpiece(wi, ar(2304, 2560))
        wts[pi["difT"][0]] = kpiece(wi, ar(2304, 2560))
        wts[pi["difT"][1]] = kpiece(wi, ar(2560, 2816))
        for g, name in enumerate(("retO", "lruO", "swaO", "difO")):
            blkw = wo[g * 256:(g + 1) * 256, :].reshape(2, 128, 1024)
            wts[pi[name]] = np.transpose(blkw, (1, 0, 2)).reshape(128, 2048)
        for q in range(4):
            f1, f2 = pi["ff"][q]
            for j in range(4):
                wts[f1[j]] = kpiece(w1, ar(q * 1024 + j * 256, q * 1024 + (j + 1) * 256))
                wts[f2[j]] = kpiece(w2[q * 1024:(q + 1) * 1024, :], ar(j * 256, (j + 1) * 256))
    lruw = np.zeros((128, DEPTH, 8, 128), np.float32)
    for l in range(DEPTH):
        for d in range(2):
            for gsel, nm in enumerate(("lru_w_a", "lru_w_x")):
                for cc in range(2):
                    for bb in range(2):
                        blk = I[nm][l, d, cc * 2 + bb]
                        lruw[bb * 64:(bb + 1) * 64, l, d * 4 + gsel * 2 + cc, bb * 64:(bb + 1) * 64] = blk
    lruw = lruw.reshape(128, DEPTH * 8 * 128)

    def putc(c, name, arr):
        o, w = coff[name]
        c[:, o:o + w] = np.asarray(arr, np.float32).reshape(128, w)

    in_maps = []
    for core in cores:
        c = cbase.copy()
        putc(c, "b_ada", _fm(I["b_ada"]))
        putc(c, "g_mix", _fm(I["norm_mix_g"])); putc(c, "g_mlp", _fm(I["norm_mlp_g"])); putc(c, "g_fin", _fm(I["final_norm_g"]))
        cv = np.stack([I["c"][core], I["c_ctx"]], 0)
        putc(c, "cT", np.transpose(_fm(cv), (0, 2, 1)))
        putc(c, "conv_w", _fm(I["lru_conv_w"])); putc(c, "conv_b", _fm(I["lru_conv_b"]))
        putc(c, "b_a", _fm(I["lru_b_a"])); putc(c, "b_x", _fm(I["lru_b_x"])); putc(c, "lam", _fm(I["lru_lambda"]))
        putc(c, "h0", _fm(I["state_lru"][core]))
        rd = I["ret_decay"]
        putc(c, "rd_row", np.broadcast_to(rd.reshape(1, 32), (128, 32)))
        rp = np.zeros((128, DEPTH, 2, 2), np.float32)
        for j in range(2):
            rp[:64, :, :, j] = rd[:, :, 2 * j]; rp[64:, :, :, j] = rd[:, :, 2 * j + 1]
        putc(c, "rd_pair", rp)
        putc(c, "sink", np.broadcast_to(I["swa_sink"].reshape(1, 16), (128, 16)))
        putc(c, "dlam", np.broadcast_to(I["diff_lambda"].reshape(1, -1), (128, DEPTH * 128)))
        putc(c, "dng", np.broadcast_to(I["diff_norm_g"].reshape(1, -1), (128, DEPTH * 64)))
        putc(c, "gng", np.broadcast_to(I["ret_gn_g"].reshape(1, -1), (128, DEPTH * 256)))
        xall = np.concatenate([I["x_sample"][core], I["x_prompt"][2 * core], I["x_prompt"][2 * core + 1]], 0)
        cks = np.transpose(I["cache_swa_k"][core].reshape(DEPTH, 256, 128), (2, 0, 1))
        cvs = np.transpose(I["cache_swa_v"][core].reshape(DEPTH, 2, 128, 128), (2, 0, 1, 3))
        ckd = np.transpose(I["cache_diff_k"][core].reshape(DEPTH, 256, 2, 128), (3, 0, 2, 1))
        cvd = np.transpose(I["cache_diff_v"][core].reshape(DEPTH, 2, 128, 256), (2, 0, 1, 3))
        sr = I["state_ret"][core]
        sret = np.zeros((128, DEPTH, 4, 128), np.float32)
        for d in range(2):
            for j in range(2):
                sret[:64, :, d * 2 + j, :64] = np.transpose(sr[:, d, 2 * j], (1, 0, 2))
                sret[64:, :, d * 2 + j, 64:] = np.transpose(sr[:, d, 2 * j + 1], (1, 0, 2))
        in_maps.append({"xT": np.ascontiguousarray(xall.T), "wts": wts, "cst": c, "rope": rope, "lruw": lruw,
                        "cks": np.ascontiguousarray(cks), "cvs": np.ascontiguousarray(cvs), "ckd": np.ascontiguousarray(ckd),
                        "cvd": np.ascontiguousarray(cvd), "sret": sret})
    return in_maps


def assemble(R):
    y_prompt = np.zeros((16, 256, 1024), np.float32); y_sample = np.zeros((8, 1024, 1024), np.float32)
    nsr = np.zeros((16, DEPTH, 2, 4, 64, 64), np.float32); nsl = np.zeros((16, DEPTH, 2, 256), np.float32)
    nsk = np.zeros((16, DEPTH, 256, 2, 64), np.float32); nsv = np.zeros((16, DEPTH, 256, 2, 64), np.float32)
    ndk = np.zeros((16, DEPTH, 256, 4, 64), np.float32); ndv = np.zeros((16, DEPTH, 256, 4, 64), np.float32)
    for core in range(8):
        r = R[core]
        y = r["yT"].T
        y_sample[core] = y[0:1024]
        osr = r["o_sret"]
        osl = r["o_slru"].reshape(128, 2, DEPTH, 2, 2)
        okv = r["o_kv"]
        for sj in range(2):
            b = 2 * core + sj
            y_prompt[b] = y[1024 + 256 * sj: 1280 + 256 * sj]
            for d in range(2):
                for j in range(2):
                    nsr[b, :, d, 2 * j] = np.transpose(osr[:64, sj, :, d * 2 + j, :64], (1, 0, 2))
                    nsr[b, :, d, 2 * j + 1] = np.transpose(osr[64:, sj, :, d * 2 + j, 64:], (1, 0, 2))
            nsl[b] = np.transpose(osl[:, sj], (1, 2, 3, 0)).reshape(DEPTH, 2, 256)
            kv = np.transpose(okv[2 * sj:2 * sj + 2], (2, 0, 1, 3)).reshape(DEPTH, 256, 768)
            nsk[b] = kv[:, :, 0:128].reshape(DEPTH, 256, 2, 64)
            nsv[b] = kv[:, :, 128:256].reshape(DEPTH, 256, 2, 64)
            ndk[b] = kv[:, :, 256:512].reshape(DEPTH, 256, 4, 64)
            ndv[b] = kv[:, :, 512:768].reshape(DEPTH, 256, 4, 64)
    return (y_prompt, y_sample, nsr, nsl, nsk, nsv, ndk, ndv)
```
